# Optimizing a Trainium2 kernel written in Bass

```python
import math
import jax, jax.numpy as jnp
from jax import lax
import numpy as np

D_MODEL = 1024
BATCH = 2
SEQ = 16384
DEPTH = 2

N_META = 16
MIX_WIDTH = D_MODEL
HEAD_DIM = 64
SB_WIDTH = MIX_WIDTH // 4
SB_HEADS = SB_WIDTH // HEAD_DIM
SB_BLOCK = 128
DN_WIDTH = MIX_WIDTH // 4
DN_HEADS = DN_WIDTH // HEAD_DIM
DN_CONV = 4
DN_CHUNK = 64
S5_WIDTH = MIX_WIDTH - SB_WIDTH - DN_WIDTH
S5_GROUP = 16
S5_GROUPS = S5_WIDTH // S5_GROUP
S5_STATE = 64
D_FF = ((8 * D_MODEL // 3 + 127) // 128) * 128
IN_SPLITS = (SB_WIDTH, SB_WIDTH, SB_WIDTH, 3 * DN_WIDTH, DN_WIDTH, DN_HEADS, DN_HEADS, S5_WIDTH)
IN_WIDTH = sum(IN_SPLITS)
EPS = 1e-6

kernel_name = 'hymba_sb_deltanet_s5_macaron'


def rmsnorm(x, g):
    xf = x.astype(jnp.float32)
    y = xf * lax.rsqrt(jnp.mean(xf * xf, axis=-1, keepdims=True) + EPS)
    return (y * g.astype(jnp.float32)).astype(x.dtype)


def l2norm(x):
    xf = x.astype(jnp.float32)
    return xf * lax.rsqrt(jnp.sum(xf * xf, axis=-1, keepdims=True) + EPS)


def swiglu(x, w_gate, w_up, w_down):
    return (jax.nn.silu(x @ w_gate) * (x @ w_up)) @ w_down


def front_pad(x, n):
    return jnp.pad(x, [(0, 0), (n, 0)] + [(0, 0)] * (x.ndim - 2))


def causal_depthwise_conv(x, w):
    k = w.shape[0]
    return lax.conv_general_dilated(
        x, w[:, None, :], window_strides=(1,), padding=[(k - 1, 0)],
        dimension_numbers=('NWC', 'WIO', 'NWC'), feature_group_count=x.shape[-1])


def stick_breaking_attention(q, k, v):
    bsz, length, nh, hd = q.shape
    pad = (-N_META) % SB_BLOCK
    q, k, v = (jnp.transpose(front_pad(t, pad), (0, 2, 1, 3)) for t in (q, k, v))
    lp = length + pad
    n_blocks = lp // SB_BLOCK
    key_pos = jnp.arange(lp)
    scale = hd ** -0.5

    def block(i):
        q_blk = lax.dynamic_slice_in_dim(q, i * SB_BLOCK, SB_BLOCK, axis=2)
        q_pos = i * SB_BLOCK + jnp.arange(SB_BLOCK)
        z = jnp.einsum('bhqd,bhkd->bhqk', q_blk, k, preferred_element_type=jnp.float32) * scale
        valid = (key_pos[None, :] < q_pos[:, None]) & (key_pos[None, :] >= pad)
        log_keep = jnp.where(valid, jax.nn.log_sigmoid(-z), 0.0)
        after = lax.cumsum(log_keep, axis=3, reverse=True) - log_keep
        w = jnp.where(valid, jnp.exp(jax.nn.log_sigmoid(z) + after), 0.0)
        return jnp.einsum('bhqk,bhkd->bhqd', w.astype(v.dtype), v)

    out = lax.map(block, jnp.arange(n_blocks))
    out = jnp.transpose(out, (1, 0, 3, 2, 4)).reshape(bsz, lp, nh, hd)
    return out[:, pad:]


def chunk_gated_delta_rule(q, k, v, g, beta):
    bsz, nh, length, dk = q.shape
    dv = v.shape[-1]
    c = DN_CHUNK
    n = length // c
    q = q * dk ** -0.5
    qc = q.reshape(bsz, nh, n, c, dk)
    kc = k.reshape(bsz, nh, n, c, dk)
    vc = v.reshape(bsz, nh, n, c, dv)
    bc = beta.reshape(bsz, nh, n, c, 1)
    gc = jnp.cumsum(g.reshape(bsz, nh, n, c), axis=-1)
    incl = jnp.tril(jnp.ones((c, c), dtype=bool))
    strict = jnp.tril(jnp.ones((c, c), dtype=bool), -1)
    decay = jnp.exp(jnp.where(incl, gc[..., :, None] - gc[..., None, :], -jnp.inf))
    kb = kc * bc
    lmat = jnp.where(strict, jnp.einsum('bhnid,bhnjd->bhnij', kb, kc) * decay, 0.0)
    eye = jnp.eye(c, dtype=jnp.float32)
    t_inv = lax.linalg.triangular_solve(lmat + eye, jnp.broadcast_to(eye, lmat.shape),
                                        left_side=True, lower=True, unit_diagonal=True)
    u = jnp.einsum('bhnij,bhnjd->bhnid', t_inv, vc * bc)
    w = jnp.einsum('bhnij,bhnjd->bhnid', t_inv, kb * jnp.exp(gc)[..., None])
    attn = jnp.where(incl, jnp.einsum('bhnid,bhnjd->bhnij', qc, kc) * decay, 0.0)

    def step(state, inp):
        q_i, k_i, u_i, w_i, a_i, g_i = inp
        v_new = u_i - jnp.einsum('bhcd,bhde->bhce', w_i, state)
        o_i = (jnp.einsum('bhcd,bhde->bhce', q_i * jnp.exp(g_i)[..., None], state)
               + jnp.einsum('bhij,bhje->bhie', a_i, v_new))
        g_last = g_i[..., -1:]
        state = (state * jnp.exp(g_last)[..., None]
                 + jnp.einsum('bhcd,bhce->bhde', k_i * jnp.exp(g_last - g_i)[..., None], v_new))
        return state, o_i

    chunks = tuple(jnp.moveaxis(t, 2, 0) for t in (qc, kc, u, w, attn, gc))
    state0 = jnp.zeros((bsz, nh, dk, dv), jnp.float32)
    _, out = lax.scan(step, state0, chunks)
    return jnp.moveaxis(out, 0, 2).reshape(bsz, nh, length, dv)


def gated_deltanet(qkv, z, b, a, conv_w, a_log, dt_bias, out_norm):
    bsz, length, _ = qkv.shape
    qkv = jax.nn.silu(causal_depthwise_conv(qkv, conv_w))
    q, k, v = (t.reshape(bsz, length, DN_HEADS, HEAD_DIM) for t in jnp.split(qkv, 3, axis=-1))
    q, k, v = l2norm(q), l2norm(k), v.astype(jnp.float32)
    beta = jax.nn.sigmoid(b.astype(jnp.float32))
    g = -jnp.exp(a_log.astype(jnp.float32)) * jax.nn.softplus(a.astype(jnp.float32) + dt_bias.astype(jnp.float32))
    pad = (-N_META) % DN_CHUNK
    q, k, v, g, beta = (front_pad(t, pad) for t in (q, k, v, g, beta))
    o = chunk_gated_delta_rule(jnp.moveaxis(q, 2, 1), jnp.moveaxis(k, 2, 1), jnp.moveaxis(v, 2, 1),
                               jnp.moveaxis(g, 2, 1), jnp.moveaxis(beta, 2, 1))
    o = jnp.moveaxis(o, 1, 2)[:, pad:]
    o = rmsnorm(o, out_norm) * jax.nn.silu(z.astype(jnp.float32).reshape(bsz, length, DN_HEADS, HEAD_DIM))
    return o.reshape(bsz, length, DN_WIDTH)


def s5_mixer(u, a_re, a_im, log_dt, b_re, b_im, c_re, c_im, d, w_glu, b_glu):
    bsz, length, _ = u.shape
    f32 = jnp.float32
    uf = u.astype(f32).reshape(bsz, length, S5_GROUPS, S5_GROUP)
    lam = lax.complex(a_re.astype(f32), a_im.astype(f32))
    dt = jnp.exp(log_dt.astype(f32))[:, None]
    log_abar = lam * dt
    abar = jnp.exp(log_abar)
    b_bar = ((abar - 1.0) / lam)[..., None] * lax.complex(b_re.astype(f32), b_im.astype(f32))
    bu = jnp.einsum('blgc,gpc->blgp', uf.astype(jnp.complex64), b_bar)
    steps = jnp.ones((1, length, 1, 1), f32)

    def combine(e1, e2):
        n1, x1 = e1
        n2, x2 = e2
        return n1 + n2, x1 * jnp.exp(n2 * log_abar) + x2

    _, states = lax.associative_scan(combine, (steps, bu), axis=1)
    c_cplx = lax.complex(c_re.astype(f32), c_im.astype(f32))
    y = jnp.real(jnp.einsum('blgp,gcp->blgc', states, c_cplx)) + d.astype(f32).reshape(S5_GROUPS, S5_GROUP) * uf
    y = jax.nn.gelu(y.reshape(bsz, length, S5_WIDTH))
    return y * jax.nn.sigmoid(y @ w_glu.astype(f32) + b_glu.astype(f32))


def hybrid_mixer(h, w_in, sb_out_norm, dn_conv_w, dn_a_log, dn_dt_bias, dn_out_norm,
                 s5_a_re, s5_a_im, s5_log_dt, s5_b_re, s5_b_im, s5_c_re, s5_c_im,
                 s5_d, s5_w_glu, s5_b_glu, s5_out_norm, w_out):
    bsz, length, _ = h.shape
    proj = h @ w_in
    sb_q, sb_k, sb_v, dn_qkv, dn_z, dn_b, dn_a, s5_u = jnp.split(
        proj, np.cumsum(IN_SPLITS)[:-1].tolist(), axis=-1)
    heads = lambda t: t.reshape(bsz, length, SB_HEADS, HEAD_DIM)
    o_sb = stick_breaking_attention(heads(sb_q), heads(sb_k), heads(sb_v))
    o_sb = rmsnorm(o_sb, sb_out_norm).reshape(bsz, length, SB_WIDTH)
    o_dn = gated_deltanet(dn_qkv, dn_z, dn_b, dn_a, dn_conv_w, dn_a_log, dn_dt_bias, dn_out_norm)
    o_s5 = rmsnorm(s5_mixer(s5_u, s5_a_re, s5_a_im, s5_log_dt, s5_b_re, s5_b_im, s5_c_re, s5_c_im,
                            s5_d, s5_w_glu, s5_b_glu), s5_out_norm)
    mixed = jnp.concatenate([o_sb.astype(h.dtype), o_dn.astype(h.dtype), o_s5.astype(h.dtype)], axis=-1)
    return mixed @ w_out


def setup_inputs(seed: int = 0) -> dict:
    key = jax.random.key(seed)
    ks = iter(jax.random.split(key, 40))
    f32 = jnp.float32
    nrm = lambda shape, scale: scale * jax.random.normal(next(ks), shape, f32)
    gain = lambda shape: 1.0 + nrm(shape, 0.02)
    unif = lambda shape, lo, hi: jax.random.uniform(next(ks), shape, f32, minval=lo, maxval=hi)
    dn_dt = jnp.exp(unif((DEPTH, DN_HEADS), math.log(1e-3), math.log(1e-1)))
    return {
        'x': nrm((BATCH, SEQ, D_MODEL), 1.0),
        'meta_tokens': nrm((N_META, D_MODEL), 1.0),
        'ffn1_norm': gain((DEPTH, D_MODEL)),
        'ffn1_w_gate': nrm((DEPTH, D_MODEL, D_FF), D_MODEL ** -0.5),
        'ffn1_w_up': nrm((DEPTH, D_MODEL, D_FF), D_MODEL ** -0.5),
        'ffn1_w_down': nrm((DEPTH, D_FF, D_MODEL), D_FF ** -0.5),
        'mix_norm': gain((DEPTH, D_MODEL)),
        'w_in': nrm((DEPTH, D_MODEL, IN_WIDTH), D_MODEL ** -0.5),
        'sb_out_norm': gain((DEPTH, HEAD_DIM)),
        'dn_conv_w': nrm((DEPTH, DN_CONV, 3 * DN_WIDTH), DN_CONV ** -0.5),
        'dn_a_log': jnp.log(unif((DEPTH, DN_HEADS), 1.0, 16.0)),
        'dn_dt_bias': dn_dt + jnp.log(-jnp.expm1(-dn_dt)),
        'dn_out_norm': gain((DEPTH, HEAD_DIM)),
        's5_a_re': -0.5 + nrm((DEPTH, S5_GROUPS, S5_STATE), 0.01),
        's5_a_im': math.pi * jnp.arange(S5_STATE, dtype=f32) + nrm((DEPTH, S5_GROUPS, S5_STATE), 0.01),
        's5_log_dt': unif((DEPTH, S5_GROUPS), math.log(1e-3), math.log(1e-1)),
        's5_b_re': nrm((DEPTH, S5_GROUPS, S5_STATE, S5_GROUP), (2 * S5_GROUP) ** -0.5),
        's5_b_im': nrm((DEPTH, S5_GROUPS, S5_STATE, S5_GROUP), (2 * S5_GROUP) ** -0.5),
        's5_c_re': nrm((DEPTH, S5_GROUPS, S5_GROUP, S5_STATE), (2 * S5_STATE) ** -0.5),
        's5_c_im': nrm((DEPTH, S5_GROUPS, S5_GROUP, S5_STATE), (2 * S5_STATE) ** -0.5),
        's5_d': nrm((DEPTH, S5_WIDTH), 1.0),
        's5_w_glu': nrm((DEPTH, S5_WIDTH, S5_WIDTH), S5_WIDTH ** -0.5),
        's5_b_glu': nrm((DEPTH, S5_WIDTH), 0.02),
        's5_out_norm': gain((DEPTH, S5_WIDTH)),
        'w_out': nrm((DEPTH, MIX_WIDTH, D_MODEL), MIX_WIDTH ** -0.5),
        'ffn2_norm': gain((DEPTH, D_MODEL)),
        'ffn2_w_gate': nrm((DEPTH, D_MODEL, D_FF), D_MODEL ** -0.5),
        'ffn2_w_up': nrm((DEPTH, D_MODEL, D_FF), D_MODEL ** -0.5),
        'ffn2_w_down': nrm((DEPTH, D_FF, D_MODEL), D_FF ** -0.5),
        'final_norm': gain((D_MODEL,)),
    }


def reference(x, meta_tokens, ffn1_norm, ffn1_w_gate, ffn1_w_up, ffn1_w_down, mix_norm, w_in,
              sb_out_norm, dn_conv_w, dn_a_log, dn_dt_bias, dn_out_norm,
              s5_a_re, s5_a_im, s5_log_dt, s5_b_re, s5_b_im, s5_c_re, s5_c_im,
              s5_d, s5_w_glu, s5_b_glu, s5_out_norm, w_out,
              ffn2_norm, ffn2_w_gate, ffn2_w_up, ffn2_w_down, final_norm):
    bsz = x.shape[0]
    meta = jnp.broadcast_to(meta_tokens[None].astype(x.dtype), (bsz, N_META, D_MODEL))
    h = jnp.concatenate([meta, x], axis=1)
    for l in range(DEPTH):
        h = h + 0.5 * swiglu(rmsnorm(h, ffn1_norm[l]), ffn1_w_gate[l], ffn1_w_up[l], ffn1_w_down[l])
        h = h + hybrid_mixer(rmsnorm(h, mix_norm[l]), w_in[l], sb_out_norm[l], dn_conv_w[l],
                             dn_a_log[l], dn_dt_bias[l], dn_out_norm[l],
                             s5_a_re[l], s5_a_im[l], s5_log_dt[l], s5_b_re[l], s5_b_im[l],
                             s5_c_re[l], s5_c_im[l], s5_d[l], s5_w_glu[l], s5_b_glu[l],
                             s5_out_norm[l], w_out[l])
        h = h + 0.5 * swiglu(rmsnorm(h, ffn2_norm[l]), ffn2_w_gate[l], ffn2_w_up[l], ffn2_w_down[l])
    return rmsnorm(h, final_norm)[:, N_META:]
```

```python
from contextlib import ExitStack
import numpy as np
import concourse.bass as bass
import concourse.mybir as mybir

F32 = mybir.dt.float32
BF16 = mybir.dt.bfloat16
AF = mybir.ActivationFunctionType
ALU = mybir.AluOpType
AX = mybir.AxisListType

N_DMA_SEMS = 24


class Trk:
    __slots__ = ("name", "w", "r")

    def __init__(self, name=""):
        self.name = name
        self.w = None
        self.r = []


class Sched:
    ENG = ("pe", "act", "dve", "pool", "sp")

    def __init__(self, nc, stack):
        self.nc = nc
        self.stack = stack
        self.prog = {e: [] for e in self.ENG}
        self.cnt = {e: 0 for e in ("pe", "act", "dve", "pool")}
        self.sems = {}
        for e in ("pe", "act", "dve", "pool"):
            self.sems[e] = stack.enter_context(nc.semaphore("s_" + e))
        self.dsems = [stack.enter_context(nc.semaphore("d%d" % i)) for i in range(N_DMA_SEMS)]
        self.dcnt = [0] * N_DMA_SEMS
        self.dnext = 0
        self.seen = {e: {} for e in self.ENG}
        self.n_wait = 0

    def sb(self, name, shape, dt=F32):
        self.uid = getattr(self, "uid", 0) + 1
        return self.stack.enter_context(self.nc.sbuf_tensor("%s_%d" % (name, self.uid), list(shape), dt))

    def ps(self, name, shape, dt=F32):
        self.uid = getattr(self, "uid", 0) + 1
        return self.stack.enter_context(self.nc.psum_tensor("%s_%d" % (name, self.uid), list(shape), dt))

    def _semobj(self, key):
        return self.sems[key] if isinstance(key, str) else self.dsems[key]

    def _need(self, eng, ev, waits):
        if ev is None:
            return
        key, val, src = ev
        if eng == "pe" and src == "pe":
            return
        if self.seen[eng].get(key, 0) >= val:
            return
        waits[key] = max(waits.get(key, 0), val)

    def _collect(self, eng, reads, writes):
        waits = {}
        for t in reads:
            self._need(eng, t.w, waits)
        for t in writes:
            self._need(eng, t.w, waits)
            for ev in t.r:
                self._need(eng, ev, waits)
        for key, val in waits.items():
            self.seen[eng][key] = val
        return list(waits.items())

    def _record(self, ev, reads, writes):
        for t in reads:
            t.r.append(ev)
            if len(t.r) > 64:
                best = {}
                for k, v, s in t.r:
                    if k not in best or best[k][1] < v:
                        best[k] = (k, v, s)
                t.r = list(best.values())
        for t in writes:
            t.w = ev
            t.r = []

    def op(self, eng, fn, reads=(), writes=()):
        waits = self._collect(eng, reads, writes)
        self.cnt[eng] += 1
        n = self.cnt[eng]
        sem = self.sems[eng]
        wl = [(self._semobj(k), v) for k, v in waits]
        self.n_wait += len(wl)

        def emit(e, wl=wl, fn=fn, sem=sem):
            for s, v in wl:
                e.wait_ge(s, v)
            fn(e).then_inc(sem, 1)
        self.prog[eng].append(emit)
        self._record((eng, n, eng), reads, writes)

    def dma(self, q, out, in_, reads=(), writes=(), **kw):
        i = self.dnext
        self.dnext = (self.dnext + 1) % N_DMA_SEMS
        waits = dict(self._collect(q, reads, writes))
        if self.dcnt[i] > 0 and self.seen[q].get(i, 0) < self.dcnt[i]:
            waits[i] = max(waits.get(i, 0), self.dcnt[i])
            self.seen[q][i] = self.dcnt[i]
        self.dcnt[i] += 16
        val = self.dcnt[i]
        sem = self.dsems[i]
        wl = [(self._semobj(k), v) for k, v in waits.items()]

        def emit(e, wl=wl, sem=sem, out=out, in_=in_, kw=kw):
            for s, v in wl:
                e.wait_ge(s, v)
            e.dma_start(out=out, in_=in_, **kw).then_inc(sem, 16)
        self.prog[q].append(emit)
        self._record((i, val, "dma"), reads, writes)

    def barrier(self):
        for eng in self.ENG:
            waits = {}
            for e in ("pe", "act", "dve", "pool"):
                if e != eng and self.cnt[e] > 0 and self.seen[eng].get(e, 0) < self.cnt[e]:
                    waits[e] = self.cnt[e]
            for i in range(N_DMA_SEMS):
                if self.dcnt[i] > 0 and self.seen[eng].get(i, 0) < self.dcnt[i]:
                    waits[i] = self.dcnt[i]
            for k, v in waits.items():
                self.seen[eng][k] = v
            wl = [(self._semobj(k), v) for k, v in waits.items()]

            def emit(e, wl=wl):
                for s, v in wl:
                    e.wait_ge(s, v)
            self.prog[eng].append(emit)

    def finish(self, out_trackers):
        nc = self.nc
        waits = {}
        for t in out_trackers:
            self._need("sp", t.w, waits)
        for i in range(N_DMA_SEMS):
            if self.dcnt[i] > 0 and self.seen["sp"].get(i, 0) < self.dcnt[i]:
                waits[i] = max(waits.get(i, 0), self.dcnt[i])
        for e in ("pe", "act", "dve", "pool"):
            if self.cnt[e] > 0:
                waits[e] = max(waits.get(e, 0), self.cnt[e])
        wl = [(self._semobj(k), v) for k, v in waits.items()]

        def emit(e, wl=wl):
            for s, v in wl:
                e.wait_ge(s, v)
        self.prog["sp"].append(emit)

        prog = self.prog
        with nc.Block() as block:
            @block.tensor
            def _(e):
                for f in prog["pe"]:
                    f(e)

            @block.scalar
            def _(e):
                for f in prog["act"]:
                    f(e)

            @block.vector
            def _(e):
                for f in prog["dve"]:
                    f(e)

            @block.gpsimd
            def _(e):
                for f in prog["pool"]:
                    f(e)

            @block.sync
            def _(e):
                for f in prog["sp"]:
                    f(e)

from contextlib import ExitStack

D = 1024
KD = 8
DFF = 2816
KF = 22
INW = 2312
INP = 2432
KP = 19
EPS = 1e-6
WCOLS = 22528
STG = 1408


def build_tok(mode, NT=4100, TT=205):
    assert NT % TT == 0
    NTILE = NT // TT
    nc = bass.Bass("TRN2", target_bir_lowering=False)

    def din(name, shape):
        return nc.dram_tensor(name, list(shape), F32, kind="ExternalInput").ap()

    def dout(name, shape):
        return nc.dram_tensor(name, list(shape), F32, kind="ExternalOutput").ap()

    def dint(name, shape):
        return nc.dram_tensor(name, list(shape), F32, kind="Internal").ap()

    h_in = din("h_in", [D, NT])
    do_epi = mode in ("CA", "C1")
    do_a = mode in ("A0", "CA")
    do_fin = mode == "C1"
    if do_epi:
        osb = din("osb", [256, NT]); odn = din("odn", [256, NT]); dnz = din("dnz", [256, NT])
        ys5 = din("ys5", [512, NT])
        sbn = din("sbn", [128, 1]); dnn = din("dnn", [128, 1])
        wglu = din("wglu", [512, 512]); bglu = din("bglu", [128, 4]); s5n = din("s5n", [128, 4])
        w_out = din("w_out", [D, D])
        wg2 = din("wg2", [D, DFF]); wu2 = din("wu2", [D, DFF]); wd2 = din("wd2", [DFF, D]); n2 = din("n2", [128, KD])
        hs_a = dint("hs_a", [D, NT])
    if do_a:
        wg1 = din("wg1", [D, DFF]); wu1 = din("wu1", [D, DFF]); wd1 = din("wd1", [DFF, D]); n1 = din("n1", [128, KD])
        w_in = din("w_in", [D, INW]); nmix = din("nmix", [128, KD])
        h_out = dout("h_out", [D, NT])
        proj = dout("proj", [INP, NT])
        if do_epi:
            hs_b = dint("hs_b", [D, NT])
    if do_fin:
        nfin = din("nfin", [128, KD])
        y_out = dout("y_out", [D, NT])

    out_trks = []
    with ExitStack() as st:
        S = Sched(nc, st)
        WA = S.sb("WA", [128, WCOLS], BF16)
        WB = S.sb("WB", [128, WCOLS], BF16)
        WC = S.sb("WC", [128, WCOLS], BF16)
        stage = [S.sb("stg%d" % i, [128, STG]) for i in range(2)]
        Tstage = [Trk("stg%d" % i) for i in range(2)]
        sidx = [0]
        ones_bf = S.sb("ones_bf", [128, 128], BF16)
        blk_bf = S.sb("blk_bf", [128, 128], BF16)
        gains = S.sb("gains", [128, 64])
        Tconst = Trk("const")
        Tg = Trk("gains")
        S.op("pool", lambda e: e.memset(ones_bf[:], 1.0), writes=[Tconst])
        S.op("pool", lambda e: e.memset(blk_bf[:], 0.0), writes=[Tconst])
        S.op("pool", lambda e: e.memset(blk_bf[0:64, 0:64], 1.0), writes=[Tconst])
        S.op("pool", lambda e: e.memset(blk_bf[64:128, 64:128], 1.0), writes=[Tconst])
        if do_a:
            S.dma("sp", gains[:, 0:8], n1[:, :], writes=[Tg])
            S.dma("sp", gains[:, 8:16], nmix[:, :], writes=[Tg])
        if do_epi:
            S.dma("sp", gains[:, 16:24], n2[:, :], writes=[Tg])
            S.dma("sp", gains[:, 32:33], sbn[:, :], writes=[Tg])
            S.dma("sp", gains[:, 33:34], dnn[:, :], writes=[Tg])
            S.dma("sp", gains[:, 34:38], bglu[:, :], writes=[Tg])
            S.dma("sp", gains[:, 38:42], s5n[:, :], writes=[Tg])
        if do_fin:
            S.dma("sp", gains[:, 24:32], nfin[:, :], writes=[Tg])

        TW = {"A": [Trk("WA%d" % k) for k in range(KF)], "B": [Trk("WB%d" % k) for k in range(KF)],
              "C": [Trk("WC%d" % k) for k in range(KF)]}

        def load_w(wtile, trk, dst_c0, src_ap, ncols, scale_ap):
            for c0 in range(0, ncols, STG):
                n = min(STG, ncols - c0)
                i = sidx[0]; sidx[0] ^= 1
                stg = stage[i]
                S.dma("sp", stg[:, 0:n], src_ap[:, c0:c0 + n], writes=[Tstage[i]])
                dst = wtile[:, dst_c0 + c0: dst_c0 + c0 + n]
                if scale_ap is None:
                    S.op("pool", lambda e, dst=dst, stg=stg, n=n: e.tensor_copy(out=dst, in_=stg[:, 0:n]),
                         reads=[Tstage[i]], writes=[trk])
                else:
                    S.op("pool", lambda e, dst=dst, stg=stg, n=n, sc=scale_ap: e.tensor_scalar(
                        out=dst, in0=stg[:, 0:n], scalar1=sc, scalar2=1.0, op0=ALU.mult, op1=ALU.mult),
                        reads=[Tstage[i], Tg], writes=[trk])

        def load_ffn_weights(wg, wu, wd, gcol):
            for k in range(KD):
                load_w(WA, TW["A"][k], k * DFF, wg[k * 128:(k + 1) * 128, :], DFF, gains[:, gcol + k: gcol + k + 1])
                load_w(WB, TW["B"][k], k * DFF, wu[k * 128:(k + 1) * 128, :], DFF, gains[:, gcol + k: gcol + k + 1])
            for f in range(KF):
                load_w(WC, TW["C"][f], f * D, wd[f * 128:(f + 1) * 128, :], D, None)

        def tok_view(ap, nch):
            return ap.rearrange("(k p) n -> p k n", p=128)

        def norm_stats(ph, src_tile, Tsrc, nk, lhs_ones, inv_n, sq, Tsq, ps_stat, Tps, lnv, Tln, rstd, Trs):
            S.op("act", lambda e: e.activation(out=sq[:, 0:nk, :], in_=src_tile[:, 0:nk, :], func=AF.Square),
                 reads=[Tsrc], writes=[Tsq])
            for k in range(nk):
                S.op("pe", lambda e, k=k: e.matmul(ps_stat[:, 0:TT], lhsT=lhs_ones[:], rhs=sq[:, k, :],
                                                    start=(k == 0), stop=(k == nk - 1)),
                     reads=[Tsq, Tconst], writes=[Tps])
            S.op("act", lambda e: e.activation(out=lnv[:], in_=ps_stat[:, 0:TT], func=AF.Ln, scale=inv_n, bias=eps_t[:, 0:1]),
                 reads=[Tps, Tconst], writes=[Tln])
            S.op("act", lambda e: e.activation(out=rstd[:], in_=lnv[:], func=AF.Exp, scale=-0.5),
                 reads=[Tln], writes=[Trs])

        eps_t = S.sb("eps_t", [128, 1])
        S.op("pool", lambda e: e.memset(eps_t[:], EPS), writes=[Tconst])

        def phase_ffn(src, dst, wg, wu, wd, gcol, fin_gcol=None, fin_dst=None):
            load_ffn_weights(wg, wu, wd, gcol)
            with ExitStack() as ps_:
                S.stack = ps_
                hb = [S.sb("hb%d" % i, [128, KD, TT]) for i in range(2)]
                Th = [Trk("hb%d" % i) for i in range(2)]
                xn = [S.sb("xn%d" % i, [128, KD, TT], BF16) for i in range(2)]
                Txn = [Trk("xn%d" % i) for i in range(2)]
                sq = S.sb("sq", [128, KD, TT], BF16); Tsq = Trk("sq")
                lnv = S.sb("lnv", [128, TT]); Tln = Trk("lnv")
                rstd = S.sb("rstd", [128, TT]); Trs = Trk("rstd")
                hmid = S.sb("hmid", [128, KF, TT], BF16)
                Thm = [Trk("hm%d" % f) for f in range(KF)]
                sgt = [S.sb("sgt%d" % i, [128, TT]) for i in range(2)]
                Tsg = [Trk("sgt%d" % i) for i in range(2)]
                ps_stat = S.ps("ps_stat", [128, 512]); Tps = Trk("ps_stat")
                psg = [S.ps("psg%d" % i, [128, 512]) for i in range(2)]
                psu = [S.ps("psu%d" % i, [128, 512]) for i in range(2)]
                psd = [S.ps("psd%d" % i, [128, 512]) for i in range(2)]
                Tpg = [Trk() for i in range(2)]; Tpu = [Trk() for i in range(2)]; Tpd = [Trk() for i in range(2)]
                if fin_dst is not None:
                    ob = S.sb("ob", [128, KD, TT]); Tob = Trk("ob")
                srcv = tok_view(src, KD); dstv = tok_view(dst, KD) if dst is not None else None
                finv = tok_view(fin_dst, KD) if fin_dst is not None else None
                Tdst = Trk("dst")

                def pre(t):
                    i = t % 2
                    c0 = t * TT
                    S.dma("sp", hb[i][:], srcv[:, :, c0:c0 + TT], writes=[Th[i]])
                    norm_stats(None, hb[i], Th[i], KD, ones_bf, 1.0 / D, sq, Tsq, ps_stat, Tps, lnv, Tln, rstd, Trs)
                    S.op("dve", lambda e, i=i: e.tensor_tensor(
                        out=xn[i][:], in0=hb[i][:], in1=rstd[:].unsqueeze(1).broadcast_to([128, KD, TT]), op=ALU.mult),
                        reads=[Th[i], Trs], writes=[Txn[i]])

                gcnt = [0]

                def gateup(t):
                    i = t % 2
                    for f in range(KF):
                        j = gcnt[0] % 2; gcnt[0] += 1
                        for k in range(KD):
                            S.op("pe", lambda e, k=k, f=f, j=j: e.matmul(
                                psg[j][:, 0:TT], lhsT=WA[:, k * DFF + f * 128: k * DFF + (f + 1) * 128],
                                rhs=xn[i][:, k, :], start=(k == 0), stop=(k == KD - 1)),
                                reads=[TW["A"][k], Txn[i]], writes=[Tpg[j]])
                        for k in range(KD):
                            S.op("pe", lambda e, k=k, f=f, j=j: e.matmul(
                                psu[j][:, 0:TT], lhsT=WB[:, k * DFF + f * 128: k * DFF + (f + 1) * 128],
                                rhs=xn[i][:, k, :], start=(k == 0), stop=(k == KD - 1)),
                                reads=[TW["B"][k], Txn[i]], writes=[Tpu[j]])
                        S.op("act", lambda e, j=j: e.activation(out=sgt[j][:], in_=psg[j][:, 0:TT], func=AF.Silu),
                             reads=[Tpg[j]], writes=[Tsg[j]])
                        S.op("dve", lambda e, j=j, f=f: e.tensor_tensor(
                            out=hmid[:, f, :], in0=sgt[j][:], in1=psu[j][:, 0:TT], op=ALU.mult),
                            reads=[Tsg[j], Tpu[j]], writes=[Thm[f]])

                dcnt = [0]

                def down(t):
                    i = t % 2
                    c0 = t * TT
                    for dc in range(KD):
                        j = dcnt[0] % 2; dcnt[0] += 1
                        for f in range(KF):
                            S.op("pe", lambda e, f=f, dc=dc, j=j: e.matmul(
                                psd[j][:, 0:TT], lhsT=WC[:, f * D + dc * 128: f * D + (dc + 1) * 128],
                                rhs=hmid[:, f, :], start=(f == 0), stop=(f == KF - 1)),
                                reads=[TW["C"][f], Thm[f]], writes=[Tpd[j]])
                        S.op("dve", lambda e, dc=dc, j=j, i=i: e.scalar_tensor_tensor(
                            out=hb[i][:, dc, :], in0=psd[j][:, 0:TT], scalar=0.5, in1=hb[i][:, dc, :],
                            op0=ALU.mult, op1=ALU.add),
                            reads=[Tpd[j], Th[i]], writes=[Th[i]])
                    if dstv is not None:
                        S.dma("sp", dstv[:, :, c0:c0 + TT], hb[i][:], reads=[Th[i]], writes=[Tdst])
                    if fin_dst is not None:
                        norm_stats(None, hb[i], Th[i], KD, ones_bf, 1.0 / D, sq, Tsq, ps_stat, Tps, lnv, Tln, rstd, Trs)
                        for k in range(KD):
                            S.op("dve", lambda e, k=k, i=i: e.scalar_tensor_tensor(
                                out=ob[:, k, :], in0=hb[i][:, k, :], scalar=gains[:, fin_gcol + k: fin_gcol + k + 1],
                                in1=rstd[:], op0=ALU.mult, op1=ALU.mult),
                                reads=[Th[i], Trs, Tg], writes=[Tob])
                        S.dma("sp", finv[:, :, c0:c0 + TT], ob[:], reads=[Tob], writes=[Tdst])

                pre(0)
                for t in range(NTILE):
                    gateup(t)
                    if t + 1 < NTILE:
                        pre(t + 1)
                    down(t)
                S.barrier()
                S.stack = st
            return Tdst

        def phase_proj(src, dstp, w, gcol):
            for k in range(KD):
                load_w(WA, TW["A"][k], k * INP, w[k * 128:(k + 1) * 128, :], INW, gains[:, gcol + k: gcol + k + 1])
            with ExitStack() as ps_:
                S.stack = ps_
                hb = [S.sb("hb%d" % i, [128, KD, TT]) for i in range(2)]
                Th = [Trk() for i in range(2)]
                xn = [S.sb("xn%d" % i, [128, KD, TT], BF16) for i in range(2)]
                Txn = [Trk() for i in range(2)]
                sq = S.sb("sq", [128, KD, TT], BF16); Tsq = Trk("sq")
                lnv = S.sb("lnv", [128, TT]); Tln = Trk("lnv")
                rstd = S.sb("rstd", [128, TT]); Trs = Trk("rstd")
                obt = [S.sb("obt%d" % i, [128, KP, TT]) for i in range(2)]
                Tobt = [Trk() for i in range(2)]
                for i_ in range(2):
                    S.op("pool", lambda e, i_=i_: e.memset(obt[i_][:, KP - 1, :], 0.0), writes=[Tobt[i_]])
                dstv3 = dstp.rearrange("(c p) n -> p c n", p=128)
                ps_stat = S.ps("ps_stat", [128, 512]); Tps = Trk("ps_stat")
                pp = [S.ps("pp%d" % i, [128, 512]) for i in range(4)]
                Tpp = [Trk() for i in range(4)]
                srcv = tok_view(src, KD)
                Tdst = Trk("projdst")
                cntb = [0]

                def ld(t):
                    i = t % 2
                    c0 = t * TT
                    S.dma("sp", hb[i][:], srcv[:, :, c0:c0 + TT], writes=[Th[i]])

                def pre(t):
                    i = t % 2
                    norm_stats(None, hb[i], Th[i], KD, ones_bf, 1.0 / D, sq, Tsq, ps_stat, Tps, lnv, Tln, rstd, Trs)
                    S.op("dve", lambda e, i=i: e.tensor_tensor(
                        out=xn[i][:], in0=hb[i][:], in1=rstd[:].unsqueeze(1).broadcast_to([128, KD, TT]), op=ALU.mult),
                        reads=[Th[i], Trs], writes=[Txn[i]])

                def body(t):
                    i = t % 2
                    c0 = t * TT
                    if t + 1 < NTILE:
                        ld(t + 1)
                    for pc in range(KP):
                        if pc == KP // 2 and t + 1 < NTILE:
                            pre(t + 1)
                        cnt = cntb[0]
                        j = cnt % 4; cnt += 1; cntb[0] = cnt
                        m = min(128, INW - pc * 128)
                        for k in range(KD):
                            S.op("pe", lambda e, k=k, pc=pc, j=j, m=m: e.matmul(
                                pp[j][0:m, 0:TT], lhsT=WA[:, k * INP + pc * 128: k * INP + pc * 128 + m],
                                rhs=xn[i][:, k, :], start=(k == 0), stop=(k == KD - 1)),
                                reads=[TW["A"][k], Txn[i]], writes=[Tpp[j]])
                        eng = "act" if (cnt % 2 == 0) else "dve"
                        if eng == "act":
                            S.op("act", lambda e, j=j, m=m, pc=pc: e.copy(out=obt[i][0:m, pc, :], in_=pp[j][0:m, 0:TT]),
                                 reads=[Tpp[j]], writes=[Tobt[i]])
                        else:
                            S.op("dve", lambda e, j=j, m=m, pc=pc: e.tensor_copy(out=obt[i][0:m, pc, :], in_=pp[j][0:m, 0:TT]),
                                 reads=[Tpp[j]], writes=[Tobt[i]])
                    S.dma("sp", dstv3[:, :, c0:c0 + TT], obt[i][:], reads=[Tobt[i]], writes=[Tdst])
                ld(0)
                pre(0)
                for t in range(NTILE):
                    body(t)
                S.barrier()
                S.stack = st
            return Tdst

        def phase_epi(src, dst):
            for k in range(KD):
                load_w(WA, TW["A"][k], k * D, w_out[k * 128:(k + 1) * 128, :], D, None)
            for k in range(4):
                load_w(WB, TW["B"][k], k * 512, wglu[k * 128:(k + 1) * 128, :], 512, None)
            with ExitStack() as ps_:
                S.stack = ps_
                hb = [S.sb("hb%d" % i, [128, KD, TT]) for i in range(2)]
                Th = [Trk() for i in range(2)]
                xin = [S.sb("xin%d" % i, [128, 10, TT]) for i in range(2)]
                Tx = [Trk() for i in range(2)]
                mixed = S.sb("mixed", [128, KD, TT], BF16); Tmx = Trk("mixed")
                sq = S.sb("sq", [128, 4, TT], BF16); Tsq = Trk("sq")
                lnv = S.sb("lnv", [128, TT]); Tln = Trk("lnv")
                rstd = S.sb("rstd", [128, TT]); Trs = Trk("rstd")
                t1 = S.sb("t1", [128, 4, TT]); Tt1 = Trk("t1")
                t2 = S.sb("t2", [128, 4, TT]); Tt2 = Trk("t2")
                ge = S.sb("ge", [128, 4, TT]); Tge = Trk("ge")
                geb = S.sb("geb", [128, 4, TT], BF16); Tgeb = Trk("geb")
                vv = S.sb("vv", [128, 4, TT]); Tvv = Trk("vv")
                sg = S.sb("sg", [128, TT]); Tsgm = Trk("sg")
                ps_stat = S.ps("ps_stat", [128, 512]); Tps = Trk("ps_stat")
                pq = [S.ps("pq%d" % i, [128, 512]) for i in range(2)]
                Tpq = [Trk() for i in range(2)]
                srcv = tok_view(src, KD); dstv = tok_view(dst, KD)
                osbv = tok_view(osb, 2); odnv = tok_view(odn, 2); dnzv = tok_view(dnz, 2); ys5v = tok_view(ys5, 4)
                Tdst = Trk("epidst")
                cntb = [0]

                def ld(t):
                    i = t % 2
                    c0 = t * TT
                    S.dma("sp", hb[i][:], srcv[:, :, c0:c0 + TT], writes=[Th[i]])
                    S.dma("sp", xin[i][:, 0:2, :], osbv[:, :, c0:c0 + TT], writes=[Tx[i]])
                    S.dma("sp", xin[i][:, 2:4, :], odnv[:, :, c0:c0 + TT], writes=[Tx[i]])
                    S.dma("sp", xin[i][:, 4:6, :], dnzv[:, :, c0:c0 + TT], writes=[Tx[i]])
                    S.dma("sp", xin[i][:, 6:10, :], ys5v[:, :, c0:c0 + TT], writes=[Tx[i]])

                def body(t):
                    i = t % 2
                    c0 = t * TT
                    if t == 0:
                        ld(0)
                    if t + 1 < NTILE:
                        ld(t + 1)
                    X = xin[i]
                    for c in range(2):
                        S.op("act", lambda e, c=c: e.activation(out=sq[:, 0, :], in_=X[:, c, :], func=AF.Square),
                             reads=[Tx[i]], writes=[Tsq])
                        S.op("pe", lambda e: e.matmul(ps_stat[:, 0:TT], lhsT=blk_bf[:], rhs=sq[:, 0, :], start=True, stop=True),
                             reads=[Tsq, Tconst], writes=[Tps])
                        S.op("act", lambda e: e.activation(out=lnv[:], in_=ps_stat[:, 0:TT], func=AF.Ln, scale=1.0 / 64, bias=eps_t[:, 0:1]),
                             reads=[Tps, Tconst], writes=[Tln])
                        S.op("act", lambda e: e.activation(out=rstd[:], in_=lnv[:], func=AF.Exp, scale=-0.5),
                             reads=[Tln], writes=[Trs])
                        S.op("dve", lambda e, c=c: e.scalar_tensor_tensor(
                            out=mixed[:, c, :], in0=X[:, c, :], scalar=gains[:, 32:33], in1=rstd[:], op0=ALU.mult, op1=ALU.mult),
                            reads=[Tx[i], Trs, Tg], writes=[Tmx])
                    for c in range(2):
                        S.op("act", lambda e, c=c: e.activation(out=sq[:, 0, :], in_=X[:, 2 + c, :], func=AF.Square),
                             reads=[Tx[i]], writes=[Tsq])
                        S.op("pe", lambda e: e.matmul(ps_stat[:, 0:TT], lhsT=blk_bf[:], rhs=sq[:, 0, :], start=True, stop=True),
                             reads=[Tsq, Tconst], writes=[Tps])
                        S.op("act", lambda e: e.activation(out=lnv[:], in_=ps_stat[:, 0:TT], func=AF.Ln, scale=1.0 / 64, bias=eps_t[:, 0:1]),
                             reads=[Tps, Tconst], writes=[Tln])
                        S.op("act", lambda e: e.activation(out=rstd[:], in_=lnv[:], func=AF.Exp, scale=-0.5),
                             reads=[Tln], writes=[Trs])
                        S.op("dve", lambda e, c=c: e.scalar_tensor_tensor(
                            out=t1[:, 0, :], in0=X[:, 2 + c, :], scalar=gains[:, 33:34], in1=rstd[:], op0=ALU.mult, op1=ALU.mult),
                            reads=[Tx[i], Trs, Tg], writes=[Tt1])
                        S.op("act", lambda e, c=c: e.activation(out=t2[:, 0, :], in_=X[:, 4 + c, :], func=AF.Silu),
                             reads=[Tx[i]], writes=[Tt2])
                        S.op("dve", lambda e, c=c: e.tensor_tensor(out=mixed[:, 2 + c, :], in0=t1[:, 0, :], in1=t2[:, 0, :], op=ALU.mult),
                             reads=[Tt1, Tt2], writes=[Tmx])
                    Y = X[:, 6:10, :]
                    S.op("act", lambda e, Y=Y: e.activation(out=t1[:], in_=Y, func=AF.Square), reads=[Tx[i]], writes=[Tt1])
                    S.op("dve", lambda e: e.tensor_scalar(out=t1[:], in0=t1[:], scalar1=0.044715, scalar2=1.0, op0=ALU.mult, op1=ALU.add),
                         reads=[Tt1], writes=[Tt1])
                    S.op("dve", lambda e, Y=Y: e.tensor_tensor(out=t2[:], in0=t1[:], in1=Y, op=ALU.mult),
                         reads=[Tt1, Tx[i]], writes=[Tt2])
                    S.op("act", lambda e: e.activation(out=t1[:], in_=t2[:], func=AF.Tanh, scale=0.7978845608028654),
                         reads=[Tt2], writes=[Tt1])
                    S.op("dve", lambda e, Y=Y: e.scalar_tensor_tensor(out=t2[:], in0=t1[:], scalar=1.0, in1=Y, op0=ALU.add, op1=ALU.mult),
                         reads=[Tt1, Tx[i]], writes=[Tt2])
                    S.op("dve", lambda e: e.tensor_scalar(out=ge[:], in0=t2[:], scalar1=0.5, scalar2=None, op0=ALU.mult),
                         reads=[Tt2], writes=[Tge])
                    S.op("act", lambda e: e.copy(out=geb[:], in_=ge[:]), reads=[Tge], writes=[Tgeb])
                    for co in range(4):
                        j = cntb[0] % 2; cntb[0] += 1
                        for ki in range(4):
                            S.op("pe", lambda e, ki=ki, co=co, j=j: e.matmul(
                                pq[j][:, 0:TT], lhsT=WB[:, ki * 512 + co * 128: ki * 512 + (co + 1) * 128],
                                rhs=geb[:, ki, :], start=(ki == 0), stop=(ki == 3)),
                                reads=[TW["B"][ki], Tgeb], writes=[Tpq[j]])
                        S.op("act", lambda e, co=co, j=j: e.activation(out=sg[:], in_=pq[j][:, 0:TT], func=AF.Sigmoid,
                                                                        bias=gains[:, 34 + co: 35 + co]),
                             reads=[Tpq[j], Tg], writes=[Tsgm])
                        S.op("dve", lambda e, co=co: e.tensor_tensor(out=vv[:, co, :], in0=ge[:, co, :], in1=sg[:], op=ALU.mult),
                             reads=[Tge, Tsgm], writes=[Tvv])
                    norm_stats(None, vv, Tvv, 4, ones_bf, 1.0 / 512, sq, Tsq, ps_stat, Tps, lnv, Tln, rstd, Trs)
                    for c in range(4):
                        S.op("dve", lambda e, c=c: e.scalar_tensor_tensor(
                            out=mixed[:, 4 + c, :], in0=vv[:, c, :], scalar=gains[:, 38 + c: 39 + c], in1=rstd[:],
                            op0=ALU.mult, op1=ALU.mult),
                            reads=[Tvv, Trs, Tg], writes=[Tmx])
                    for dc in range(KD):
                        j = cntb[0] % 2; cntb[0] += 1
                        for k in range(KD):
                            S.op("pe", lambda e, k=k, dc=dc, j=j: e.matmul(
                                pq[j][:, 0:TT], lhsT=WA[:, k * D + dc * 128: k * D + (dc + 1) * 128],
                                rhs=mixed[:, k, :], start=(k == 0), stop=(k == KD - 1)),
                                reads=[TW["A"][k], Tmx], writes=[Tpq[j]])
                        S.op("dve", lambda e, dc=dc, j=j, i=i: e.tensor_tensor(
                            out=hb[i][:, dc, :], in0=hb[i][:, dc, :], in1=pq[j][:, 0:TT], op=ALU.add),
                            reads=[Tpq[j], Th[i]], writes=[Th[i]])
                    S.dma("sp", dstv[:, :, c0:c0 + TT], hb[i][:], reads=[Th[i]], writes=[Tdst])
                for t in range(NTILE):
                    body(t)
                S.barrier()
                S.stack = st
            return Tdst

        cur = h_in
        if do_epi:
            phase_epi(cur, hs_a)
            cur = hs_a
            if do_fin:
                Td = phase_ffn(cur, None, wg2, wu2, wd2, 16, fin_gcol=24, fin_dst=y_out)
                out_trks.append(Td)
            else:
                phase_ffn(cur, hs_b, wg2, wu2, wd2, 16)
                cur = hs_b
        if do_a:
            Td = phase_ffn(cur, h_out, wg1, wu1, wd1, 0)
            out_trks.append(Td)
            Tp = phase_proj(h_out, proj, w_in, 8)
            out_trks.append(Tp)
        S.finish(out_trks)
    return nc

import math
from contextlib import ExitStack

EPS = 1e-6


def sb_phase(S, nc, qT_d, kT_d, v_d, oT_d, NB, PADK):
    import os
    SB_DUMMY = int(os.environ.get("SB_DUMMY", "0"))
    st0 = S.stack
    with ExitStack() as ps_:
        S.stack = ps_
        LP = NB * 128
        Tc = Trk("sbconst")
        qb = S.sb("qb", [64, LP], BF16); Tq = Trk("qb")
        kb = S.sb("kb", [64, LP], BF16); Tk = Trk("kb")
        vb = S.sb("vb", [128, NB * 64], BF16); Tv = Trk("vb")
        stg = [S.sb("sbstg%d" % i, [128, 2048]) for i in range(2)]
        Tstg = [Trk() for i in range(2)]
        si = 0
        for c0 in range(0, LP, 2048):
            n = min(2048, LP - c0)
            for (src, dst, Td, sc) in ((qT_d, qb, Tq, 1.0), (kT_d, kb, Tk, 0.125)):
                i = si % 2; si += 1
                S.dma("sp", stg[i][0:64, 0:n], src[:, c0:c0 + n], writes=[Tstg[i]])
                S.op("pool", lambda e, i=i, n=n, dst=dst, c0=c0, sc=sc: e.tensor_scalar(
                    out=dst[:, c0:c0 + n], in0=stg[i][0:64, 0:n], scalar1=sc, scalar2=1.0, op0=ALU.mult, op1=ALU.mult),
                    reads=[Tstg[i]], writes=[Td])
        vflat = v_d.rearrange("p b d -> p (b d)")
        for c0 in range(0, NB * 64, 2048):
            n = min(2048, NB * 64 - c0)
            i = si % 2; si += 1
            S.dma("sp", stg[i][:, 0:n], vflat[:, c0:c0 + n], writes=[Tstg[i]])
            S.op("pool", lambda e, i=i, n=n, c0=c0: e.tensor_copy(out=vb[:, c0:c0 + n], in_=stg[i][:, 0:n]),
                 reads=[Tstg[i]], writes=[Tv])
        negtri = S.sb("negtri", [128, 128], BF16)
        negones = S.sb("negones", [1, 128], BF16)
        onescol = S.sb("onescol", [128, 1], BF16)
        iot = S.sb("iot", [128, 512])
        masks = [S.sb("mask%d" % m, [128, 512], BF16) for m in range(4)]
        padmask = S.sb("padmask", [128, 512], BF16)
        mask00 = S.sb("mask00", [128, 512], BF16)
        S.op("pool", lambda e: e.iota(iot[:, 0:128], pattern=[[1, 128]], base=0, channel_multiplier=-1,
                                      allow_small_or_imprecise_dtypes=True), writes=[Tc])
        S.op("dve", lambda e: e.tensor_scalar(out=negtri[:], in0=iot[:, 0:128], scalar1=0.0, scalar2=-1.0,
                                              op0=ALU.is_le, op1=ALU.mult), reads=[Tc], writes=[Tc])
        S.op("pool", lambda e: e.memset(negones[:], -1.0), writes=[Tc])
        S.op("pool", lambda e: e.memset(onescol[:], 1.0), writes=[Tc])
        for m in range(4):
            S.op("pool", lambda e, m=m: e.iota(iot[:], pattern=[[1, 512]], base=-128 * m, channel_multiplier=-1,
                                                allow_small_or_imprecise_dtypes=True), reads=[Tc], writes=[Tc])
            S.op("dve", lambda e, m=m: e.tensor_single_scalar(out=masks[m][:], in_=iot[:], scalar=0.0, op=ALU.is_gt),
                 reads=[Tc], writes=[Tc])
        S.op("pool", lambda e: e.memset(padmask[:], 1.0), reads=[Tc], writes=[Tc])
        S.op("pool", lambda e: e.tensor_copy(out=mask00[:], in_=masks[0][:]), reads=[Tc], writes=[Tc])
        if PADK > 0:
            pk = PADK
            S.op("pool", lambda e: e.memset(padmask[0:pk, :], 0.0), reads=[Tc], writes=[Tc])
            S.op("pool", lambda e: e.memset(mask00[0:pk, :], 0.0), reads=[Tc], writes=[Tc])

        eS = [S.sb("eS%d" % i, [128, 512]) for i in range(2)]; TeS = [Trk() for i in range(2)]
        spb = [S.sb("spb%d" % i, [128, 512], BF16) for i in range(2)]; Tsp = [Trk() for i in range(2)]
        wb = [S.sb("wb%d" % i, [128, 512], BF16) for i in range(2)]; Twb = [Trk() for i in range(2)]
        rhi = [S.sb("rhi%d" % i, [1, 512], BF16) for i in range(2)]
        rlo = [S.sb("rlo%d" % i, [1, 512], BF16) for i in range(2)]; Trh = [Trk() for i in range(2)]
        Rf = S.sb("Rf", [1, 512]); TR = Trk("R")
        obuf = [S.sb("obuf%d" % i, [64, 512]) for i in range(2)]; Tob = [Trk() for i in range(2)]
        psA = [S.ps("psA%d" % i, [128, 512]) for i in range(2)]; TpA = [Trk() for i in range(2)]
        psB = [S.ps("psB%d" % i, [128, 512]) for i in range(2)]; TpB = [Trk() for i in range(2)]
        psC = S.ps("psC", [1, 512]); TpC = Trk()
        psO = S.ps("psO", [64, 512]); TpO = Trk()
        Tout = Trk("sbout")

        tiles = []
        b = 0
        while b < NB:
            nb = min(4, NB - b)
            tiles.append((b, nb))
            b += nb
        cnt = [0]
        pend = []

        def mask_for(qb0, j):
            m = j - qb0
            if m >= 0:
                if j == 0:
                    return mask00
                return masks[m]
            if j == 0 and PADK > 0:
                return padmask
            return None

        def stage1(qb0, nq, j, slot):
            N = nq * 128
            q0 = qb0 * 128
            S.op("pe", lambda e: e.matmul(psB[slot][:, 0:N], lhsT=kb[:, j * 128:(j + 1) * 128], rhs=qb[:, q0:q0 + N],
                                          start=True, stop=False, skip_group_check=True), reads=[Tk, Tq], writes=[TpB[slot]])
            S.op("act", lambda e: e.activation(out=eS[slot][:, 0:N], in_=psB[slot][:, 0:N], func=AF.Exp),
                 reads=[TpB[slot]], writes=[TeS[slot]])
            def part_b():
                S.op("act", lambda e: e.activation(out=spb[slot][:, 0:N], in_=eS[slot][:, 0:N], func=AF.Ln, bias=one_t[:, 0:1]),
                     reads=[TeS[slot], Tc], writes=[Tsp[slot]])
                mk = mask_for(qb0, j)
                if mk is not None:
                    S.op("dve", lambda e: e.tensor_tensor(out=spb[slot][:, 0:N], in0=spb[slot][:, 0:N], in1=mk[:, 0:N], op=ALU.mult),
                         reads=[Tsp[slot], Tc], writes=[Tsp[slot]])
            return part_b

        def stage2(qb0, nq, j, slot, rslot, first, last):
            N = nq * 128
            q0 = qb0 * 128
            S.op("pe", lambda e: e.matmul(psB[slot][:, 0:N], lhsT=negtri[:], rhs=spb[slot][:, 0:N],
                                          start=False, stop=False, skip_group_check=True), reads=[Tsp[slot], Tc], writes=[TpB[slot]])
            S.op("pe", lambda e: e.matmul(psB[slot][:, 0:N], lhsT=negones[:], rhs=rhi[rslot][:, 0:N],
                                          start=False, stop=True, skip_group_check=True), reads=[Trh[rslot], Tc], writes=[TpB[slot]])
            if not last:
                S.op("pe", lambda e: e.matmul(psC[:, 0:N], lhsT=onescol[:], rhs=spb[slot][:, 0:N], start=True, stop=True),
                     reads=[Tsp[slot], Tc], writes=[TpC])
            for _d in range(SB_DUMMY):
                S.op("pe", lambda e, _d=_d: e.matmul(psA[_d % 2][:, 0:N], lhsT=negtri[:], rhs=spb[slot][:, 0:N], start=True, stop=True),
                     reads=[Tsp[slot], Tc], writes=[])
            S.op("act", lambda e: e.activation(out=wb[slot][:, 0:N], in_=psB[slot][:, 0:N], func=AF.Exp),
                 reads=[TpB[slot]], writes=[Twb[slot]])
            mk = mask_for(qb0, j)
            if mk is not None:
                S.op("dve", lambda e: e.tensor_tensor(out=wb[slot][:, 0:N], in0=wb[slot][:, 0:N], in1=mk[:, 0:N], op=ALU.mult),
                     reads=[Twb[slot], Tc], writes=[Twb[slot]])
            def emit_o():
                S.op("pe", lambda e: e.matmul(psO[:, 0:N], lhsT=vb[:, j * 64:(j + 1) * 64], rhs=wb[slot][:, 0:N],
                                              start=first, stop=last), reads=[Tv, Twb[slot]], writes=[TpO])
            pend.append(emit_o)
            if not last:
                nr = 1 - rslot
                S.op("dve", lambda e: e.tensor_tensor(out=Rf[:, 0:N], in0=Rf[:, 0:N], in1=psC[:, 0:N], op=ALU.add),
                     reads=[TR, TpC], writes=[TR])
                S.op("dve", lambda e: e.tensor_copy(out=rhi[nr][:, 0:N], in_=Rf[:, 0:N]),
                     reads=[TR], writes=[Trh[nr]])


        one_t = S.sb("one_t", [128, 1])
        S.op("pool", lambda e: e.memset(one_t[:], 1.0), writes=[Tc])

        for ti, (qb0, nq) in enumerate(tiles):
            N = nq * 128
            js = list(range(qb0 + nq - 1, -1, -1))
            S.op("dve", lambda e: e.memset(Rf[:], 0.0), reads=[TR], writes=[TR])
            S.op("dve", lambda e: e.memset(rhi[0][:], 0.0), reads=[Trh[0]], writes=[Trh[0]])
            S.op("dve", lambda e: e.memset(rlo[0][:], 0.0), reads=[Trh[0]], writes=[Trh[0]])
            rslot = 0
            slot0 = cnt[0] % 2
            stage1(qb0, nq, js[0], slot0)()
            for idx, j in enumerate(js):
                slot = cnt[0] % 2; cnt[0] += 1
                pb = None
                if idx + 1 < len(js):
                    pb = stage1(qb0, nq, js[idx + 1], 1 - slot)
                stage2(qb0, nq, j, slot, rslot, idx == 0, idx == len(js) - 1)
                if pb is not None:
                    pb()
                rslot = 1 - rslot
                while len(pend) > 1:
                    pend.pop(0)()
            while pend:
                pend.pop(0)()
            oi = ti % 2
            S.op("dve", lambda e, oi=oi, N=N: e.tensor_copy(out=obuf[oi][:, 0:N], in_=psO[:, 0:N]),
                 reads=[TpO], writes=[Tob[oi]])
            S.dma("sp", oT_d[:, qb0 * 128: qb0 * 128 + N], obuf[oi][:, 0:N], reads=[Tob[oi]], writes=[Tout])
        S.barrier()
        S.stack = st0
    return Tout


def s5_phase(S, nc, u_d, are_d, aim_d, ldt_d, bre_d, bim_d, cre_d, cim_d, dsk_d, ys_d, L, NBATCH=2):
    st0 = S.stack
    NCH = L // 16
    assert NCH * 16 == L
    TWO_PI = 2.0 * math.pi
    MAGIC = 12582912.0
    with ExitStack() as ps_:
        S.stack = ps_
        Tc = Trk("s5c")
        prm = S.sb("prm", [128, 8]); Tp = Trk("prm")
        bre = S.sb("bre", [128, 2, 16]); bim = S.sb("bim", [128, 2, 16])
        cre = S.sb("cre", [128, 2, 16]); cim = S.sb("cim", [128, 2, 16])
        dsk = S.sb("dsk", [64, 1])
        S.dma("sp", prm[:, 0:2], are_d[:, :], writes=[Tp])
        S.dma("sp", prm[:, 2:4], aim_d[:, :], writes=[Tp])
        S.dma("sp", prm[:, 4:6], ldt_d[:, :], writes=[Tp])
        S.dma("sp", bre[:], bre_d[:, :, :], writes=[Tp])
        S.dma("sp", bim[:], bim_d[:, :, :], writes=[Tp])
        S.dma("sp", cre[:], cre_d[:, :, :], writes=[Tp])
        S.dma("sp", cim[:], cim_d[:, :, :], writes=[Tp])
        S.dma("sp", dsk[:], dsk_d[:, :], writes=[Tp])
        sc = S.sb("s5sc", [128, 64]); Ts = Trk("s5sc")
        dve = lambda fn, r=(), w=(): S.op("dve", fn, reads=list(r) + [Tp, Ts, Tc], writes=list(w) if w else [Ts])
        S.op("act", lambda e: e.activation(out=sc[:, 0:2], in_=prm[:, 4:6], func=AF.Exp), reads=[Tp], writes=[Ts])
        dve(lambda e: e.tensor_tensor(out=sc[:, 2:4], in0=prm[:, 0:2], in1=sc[:, 0:2], op=ALU.mult))
        dve(lambda e: e.tensor_tensor(out=sc[:, 4:6], in0=prm[:, 2:4], in1=sc[:, 0:2], op=ALU.mult))
        NP = 17
        mag = S.sb("mag", [128, 2, NP]); ang = S.sb("ang", [128, 2, 2 * NP]); ang2 = S.sb("ang2", [128, 2, 2 * NP])
        trg = S.sb("trg", [128, 2, 2 * NP])
        pwr = S.sb("pwr", [128, 2, NP]); pwi = S.sb("pwi", [128, 2, NP]); pwn = S.sb("pwn", [128, 2, NP])
        for m in range(NP):
            S.op("act", lambda e, m=m: e.activation(out=mag[:, :, m], in_=sc[:, 2:4], func=AF.Exp, scale=float(m)),
                 reads=[Ts], writes=[Ts])
            dve(lambda e, m=m: e.tensor_scalar(out=ang[:, :, m], in0=sc[:, 4:6], scalar1=float(m), scalar2=0.0,
                                               op0=ALU.mult, op1=ALU.add))
            dve(lambda e, m=m: e.tensor_scalar(out=ang[:, :, NP + m], in0=sc[:, 4:6], scalar1=float(m), scalar2=math.pi / 2,
                                               op0=ALU.mult, op1=ALU.add))
        dve(lambda e: e.tensor_scalar(out=ang2[:], in0=ang[:], scalar1=1.0 / TWO_PI, scalar2=MAGIC, op0=ALU.mult, op1=ALU.add))
        dve(lambda e: e.tensor_scalar(out=ang2[:], in0=ang2[:], scalar1=-MAGIC, scalar2=None, op0=ALU.add))
        dve(lambda e: e.scalar_tensor_tensor(out=ang2[:], in0=ang2[:], scalar=-TWO_PI, in1=ang[:], op0=ALU.mult, op1=ALU.add))
        dve(lambda e: e.tensor_scalar(out=ang2[:], in0=ang2[:], scalar1=3.141592, scalar2=-3.141592, op0=ALU.min, op1=ALU.max))
        S.op("act", lambda e: e.activation(out=trg[:], in_=ang2[:], func=AF.Sin), reads=[Ts], writes=[Ts])
        dve(lambda e: e.tensor_tensor(out=pwi[:], in0=mag[:], in1=trg[:, :, 0:NP], op=ALU.mult))
        dve(lambda e: e.tensor_tensor(out=pwr[:], in0=mag[:], in1=trg[:, :, NP:2 * NP], op=ALU.mult))
        dve(lambda e: e.tensor_scalar(out=pwn[:], in0=pwi[:], scalar1=-1.0, scalar2=None, op0=ALU.mult))
        X = sc[:, 6:8]; DEN = sc[:, 8:10]; RDEN = sc[:, 10:12]; CFR = sc[:, 12:14]; CFI = sc[:, 14:16]; T1 = sc[:, 16:18]; T2 = sc[:, 18:20]
        ARE = prm[:, 0:2]; AIM = prm[:, 2:4]
        dve(lambda e: e.tensor_scalar(out=X, in0=pwr[:, :, 1], scalar1=-1.0, scalar2=None, op0=ALU.add))
        dve(lambda e: e.tensor_tensor(out=DEN, in0=ARE, in1=ARE, op=ALU.mult))
        dve(lambda e: e.tensor_tensor(out=T1, in0=AIM, in1=AIM, op=ALU.mult))
        dve(lambda e: e.tensor_tensor(out=DEN, in0=DEN, in1=T1, op=ALU.add))
        dve(lambda e: e.reciprocal(out=RDEN, in_=DEN))
        dve(lambda e: e.tensor_tensor(out=T1, in0=X, in1=ARE, op=ALU.mult))
        dve(lambda e: e.tensor_tensor(out=T2, in0=pwi[:, :, 1], in1=AIM, op=ALU.mult))
        dve(lambda e: e.tensor_tensor(out=T1, in0=T1, in1=T2, op=ALU.add))
        dve(lambda e: e.tensor_tensor(out=CFR, in0=T1, in1=RDEN, op=ALU.mult))
        dve(lambda e: e.tensor_tensor(out=T1, in0=pwi[:, :, 1], in1=ARE, op=ALU.mult))
        dve(lambda e: e.tensor_tensor(out=T2, in0=X, in1=AIM, op=ALU.mult))
        dve(lambda e: e.tensor_tensor(out=T1, in0=T1, in1=T2, op=ALU.subtract))
        dve(lambda e: e.tensor_tensor(out=CFI, in0=T1, in1=RDEN, op=ALU.mult))
        iot = S.sb("s5iot", [128, 128]); ident = S.sb("s5ident", [128, 128])
        S.op("pool", lambda e: e.iota(iot[:], pattern=[[1, 128]], base=0, channel_multiplier=-1,
                                      allow_small_or_imprecise_dtypes=True), writes=[Tc])
        S.op("dve", lambda e: e.tensor_single_scalar(out=ident[:], in_=iot[:], scalar=0.0, op=ALU.is_equal), reads=[Tc], writes=[Tc])
        bblk = [S.sb("bblk%d" % i, [128, 64]) for i in range(2)]
        tmpb = S.sb("tmpb", [128, 16])
        BT = [[S.sb("BT%d%d" % (pr, pt), [64, 128], BF16) for pt in range(2)] for pr in range(2)]
        CB = [[S.sb("CB%d%d" % (pr, pt), [128, 64], BF16) for pt in range(2)] for pr in range(2)]
        psT = S.ps("psT", [128, 512]); TpT = Trk()
        Tbb = Trk("bblk")
        for pr in range(2):
            for i in range(2):
                S.op("pool", lambda e, i=i: e.memset(bblk[i][:], 0.0), reads=[Tbb], writes=[Tbb])
            for g2 in range(2):
                r0, r1 = 64 * g2, 64 * g2 + 64
                c0 = 32 * pr + 16 * g2
                cfr = sc[r0:r1, 12 + pr:13 + pr]; cfi = sc[r0:r1, 14 + pr:15 + pr]
                S.op("dve", lambda e, r0=r0, r1=r1, pr=pr, cfi=cfi: e.tensor_scalar(
                    out=tmpb[r0:r1, :], in0=bim[r0:r1, pr, :], scalar1=cfi, scalar2=None, op0=ALU.mult),
                    reads=[Tp, Ts], writes=[Tbb])
                S.op("dve", lambda e, r0=r0, r1=r1, pr=pr, cfr=cfr, c0=c0: e.scalar_tensor_tensor(
                    out=bblk[0][r0:r1, c0:c0 + 16], in0=bre[r0:r1, pr, :], scalar=cfr, in1=tmpb[r0:r1, :],
                    op0=ALU.mult, op1=ALU.subtract), reads=[Tp, Ts, Tbb], writes=[Tbb])
                S.op("dve", lambda e, r0=r0, r1=r1, pr=pr, cfi=cfi: e.tensor_scalar(
                    out=tmpb[r0:r1, :], in0=bre[r0:r1, pr, :], scalar1=cfi, scalar2=None, op0=ALU.mult),
                    reads=[Tp, Ts, Tbb], writes=[Tbb])
                S.op("dve", lambda e, r0=r0, r1=r1, pr=pr, cfr=cfr, c0=c0: e.scalar_tensor_tensor(
                    out=bblk[1][r0:r1, c0:c0 + 16], in0=bim[r0:r1, pr, :], scalar=cfr, in1=tmpb[r0:r1, :],
                    op0=ALU.mult, op1=ALU.add), reads=[Tp, Ts, Tbb], writes=[Tbb])
            for pt in range(2):
                S.op("pe", lambda e, pt=pt: e.transpose(out=psT[0:64, 0:128], in_=bblk[pt][:], identity=ident[:]),
                     reads=[Tbb, Tc], writes=[TpT])
                S.op("act", lambda e, pr=pr, pt=pt: e.copy(out=BT[pr][pt][:], in_=psT[0:64, 0:128]),
                     reads=[TpT], writes=[Tc])
            for pt in range(2):
                S.op("pool", lambda e, pr=pr, pt=pt: e.memset(CB[pr][pt][:], 0.0), writes=[Tc])
            for g2 in range(2):
                r0, r1 = 64 * g2, 64 * g2 + 64
                c0 = 32 * pr + 16 * g2
                S.op("dve", lambda e, r0=r0, r1=r1, pr=pr, c0=c0: e.tensor_copy(out=CB[pr][0][r0:r1, c0:c0 + 16], in_=cre[r0:r1, pr, :]),
                     reads=[Tp, Tc], writes=[Tc])
                S.op("dve", lambda e, r0=r0, r1=r1, pr=pr, c0=c0: e.tensor_scalar(
                    out=CB[pr][1][r0:r1, c0:c0 + 16], in0=cim[r0:r1, pr, :], scalar1=-1.0, scalar2=None, op0=ALU.mult),
                    reads=[Tp, Tc], writes=[Tc])
        DR = [[S.sb("DR%d_%d" % (pr, m), [128, 128], BF16) for m in range(NP)] for pr in range(2)]
        DI = [[S.sb("DI%d_%d" % (pr, m), [128, 128], BF16) for m in range(NP)] for pr in range(2)]
        DN = [[S.sb("DN%d_%d" % (pr, m), [128, 128], BF16) for m in range(NP)] for pr in range(2)]
        k = 0
        for pr in range(2):
            for m in range(NP):
                for (dst, src) in ((DR, pwr), (DI, pwi), (DN, pwn)):
                    eng = "dve" if k % 2 == 0 else "pool"; k += 1
                    S.op(eng, lambda e, dst=dst, src=src, pr=pr, m=m: e.tensor_scalar(
                        out=dst[pr][m][:], in0=ident[:], scalar1=src[:, pr, m:m + 1], scalar2=1.0, op0=ALU.mult, op1=ALU.mult),
                        reads=[Ts, Tc], writes=[Tc])
        NLV = max(1, int(math.ceil(math.log2(NCH))))
        lvr = S.sb("lvr", [128, 2, NLV]); lvi = S.sb("lvi", [128, 2, NLV]); lvn = S.sb("lvn", [128, 2, NLV])
        dve(lambda e: e.tensor_copy(out=lvr[:, :, 0], in_=pwr[:, :, 16]))
        dve(lambda e: e.tensor_copy(out=lvi[:, :, 0], in_=pwi[:, :, 16]))
        for kk in range(1, NLV):
            dve(lambda e, kk=kk: e.tensor_tensor(out=T1, in0=lvr[:, :, kk - 1], in1=lvr[:, :, kk - 1], op=ALU.mult))
            dve(lambda e, kk=kk: e.tensor_tensor(out=T2, in0=lvi[:, :, kk - 1], in1=lvi[:, :, kk - 1], op=ALU.mult))
            dve(lambda e, kk=kk: e.tensor_tensor(out=lvr[:, :, kk], in0=T1, in1=T2, op=ALU.subtract))
            dve(lambda e, kk=kk: e.tensor_tensor(out=T1, in0=lvr[:, :, kk - 1], in1=lvi[:, :, kk - 1], op=ALU.mult))
            dve(lambda e, kk=kk: e.tensor_scalar(out=lvi[:, :, kk], in0=T1, scalar1=2.0, scalar2=None, op0=ALU.mult))
        dve(lambda e: e.tensor_scalar(out=lvn[:], in0=lvi[:], scalar1=-1.0, scalar2=None, op0=ALU.mult))

        ub = S.sb("ub", [64, L], BF16); Tub = Trk("ub")
        ustg = [S.sb("ustg%d" % i, [64, 1024]) for i in range(2)]; Tus = [Trk() for i in range(2)]
        BU = [S.sb("BU%d" % i, [128, L], BF16) for i in range(2)]; TBU = [Trk() for i in range(2)]
        CW = (NCH + 2) // 3
        coltiles = [(c, min(CW, NCH - c)) for c in range(0, NCH, CW)]
        stt = [S.sb("stt%d" % i, [128, CW * 16], BF16) for i in range(2)]; Tst = [Trk() for i in range(2)]
        Zr = [S.sb("Zr%d" % i, [128, NCH]) for i in range(2)]; Zi = [S.sb("Zi%d" % i, [128, NCH]) for i in range(2)]
        TZ = [Trk() for i in range(2)]
        ztr = S.sb("ztr", [128, NCH]); zti = S.sb("zti", [128, NCH]); Tzt = Trk()
        Spb = [S.sb("Spb%d" % i, [128, NCH], BF16) for i in range(2)]; TSp = Trk()
        usk = [S.sb("usk%d" % i, [64, 512]) for i in range(2)]; Tusk = [Trk() for i in range(2)]
        ysb = [S.sb("ysb%d" % i, [64, 512]) for i in range(2)]; Tys = [Trk() for i in range(2)]
        psR = [S.ps("psR%d" % i, [128, 512]) for i in range(2)]; TpR = [Trk() for i in range(2)]
        psI = [S.ps("psI%d" % i, [128, 512]) for i in range(2)]; TpI = [Trk() for i in range(2)]
        psY = [S.ps("psY%d" % i, [64, 512]) for i in range(2)]; TpY = [Trk() for i in range(2)]
        Tout = Trk("s5out")
        pc = [0]; yc = [0]; ec = [0]

        def evac(dst_ap, src_ap, reads, writes):
            ec[0] += 1
            if ec[0] % 2 == 0:
                S.op("act", lambda e: e.copy(out=dst_ap, in_=src_ap), reads=reads, writes=writes)
            else:
                S.op("dve", lambda e: e.tensor_copy(out=dst_ap, in_=src_ap), reads=reads, writes=writes)

        def unit(pr, b):
            def bu_body(c0):
                n = min(400, L - c0)
                j = pc[0] % 2; pc[0] += 1
                S.op("pe", lambda e: e.matmul(psR[j][:, 0:n], lhsT=BT[pr][0][:], rhs=ub[:, c0:c0 + n], start=True, stop=True),
                     reads=[Tub, Tc], writes=[TpR[j]])
                S.op("pe", lambda e: e.matmul(psI[j][:, 0:n], lhsT=BT[pr][1][:], rhs=ub[:, c0:c0 + n], start=True, stop=True),
                     reads=[Tub, Tc], writes=[TpI[j]])
                assert c0 % 16 == 0 and n % 16 == 0
                n0 = c0 // 16; nn = n // 16
                for (bt, pst, tpt) in ((BU[0], psR[j], TpR[j]), (BU[1], psI[j], TpI[j])):
                    dst = bt[:].rearrange("p (t n) -> p t n", t=16)[:, :, n0:n0 + nn]
                    srcv = pst[:, 0:n].rearrange("p (n t) -> p t n", t=16)
                    evac(dst, srcv, [tpt], [TBU[0] if bt is BU[0] else TBU[1]])
            for c0 in range(0, L, 400):
                bu_body(c0)

            def p1_body(cc, n):
                j = pc[0] % 2; pc[0] += 1
                lo, hi = 16 * cc, 16 * (cc + n)
                for tp in range(16):
                    d = 15 - tp
                    r_re = BU[0][:, tp * NCH + cc:tp * NCH + cc + n]; r_im = BU[1][:, tp * NCH + cc:tp * NCH + cc + n]
                    S.op("pe", lambda e, d=d, r=r_re, tp=tp: e.matmul(psR[j][:, 0:n], lhsT=DR[pr][d][:], rhs=r, start=(tp == 0), stop=False),
                         reads=[TBU[0], Tc], writes=[TpR[j]])
                    S.op("pe", lambda e, d=d, r=r_im, tp=tp: e.matmul(psR[j][:, 0:n], lhsT=DN[pr][d][:], rhs=r, start=False, stop=(tp == 15)),
                         reads=[TBU[1], Tc], writes=[TpR[j]])
                    S.op("pe", lambda e, d=d, r=r_re, tp=tp: e.matmul(psI[j][:, 0:n], lhsT=DI[pr][d][:], rhs=r, start=(tp == 0), stop=False),
                         reads=[TBU[0], Tc], writes=[TpI[j]])
                    S.op("pe", lambda e, d=d, r=r_im, tp=tp: e.matmul(psI[j][:, 0:n], lhsT=DR[pr][d][:], rhs=r, start=False, stop=(tp == 15)),
                         reads=[TBU[1], Tc], writes=[TpI[j]])
                evac(Zr[0][:, cc:cc + n], psR[j][:, 0:n], [TpR[j]], [TZ[0]])
                evac(Zi[0][:, cc:cc + n], psI[j][:, 0:n], [TpI[j]], [TZ[0]])
            for (cc, n) in coltiles:
                p1_body(cc, n)
            cur = 0
            for kk in range(NLV):
                o = 1 << kk
                if o >= NCH:
                    break
                nxt = 1 - cur
                m = NCH - o
                ar = lvr[:, pr, kk:kk + 1]; ai = lvi[:, pr, kk:kk + 1]; na = lvn[:, pr, kk:kk + 1]
                S.op("dve", lambda e, cur=cur, o=o, m=m, na=na: e.scalar_tensor_tensor(
                    out=ztr[:, 0:m], in0=Zi[cur][:, 0:m], scalar=na, in1=Zr[cur][:, o:NCH], op0=ALU.mult, op1=ALU.add),
                    reads=[TZ[cur], Ts], writes=[Tzt])
                S.op("dve", lambda e, cur=cur, nxt=nxt, o=o, m=m, ar=ar: e.scalar_tensor_tensor(
                    out=Zr[nxt][:, o:NCH], in0=Zr[cur][:, 0:m], scalar=ar, in1=ztr[:, 0:m], op0=ALU.mult, op1=ALU.add),
                    reads=[TZ[cur], Tzt, Ts], writes=[TZ[nxt]])
                S.op("dve", lambda e, cur=cur, o=o, m=m, ai=ai: e.scalar_tensor_tensor(
                    out=zti[:, 0:m], in0=Zr[cur][:, 0:m], scalar=ai, in1=Zi[cur][:, o:NCH], op0=ALU.mult, op1=ALU.add),
                    reads=[TZ[cur], Ts], writes=[Tzt])
                S.op("dve", lambda e, cur=cur, nxt=nxt, o=o, m=m, ar=ar: e.scalar_tensor_tensor(
                    out=Zi[nxt][:, o:NCH], in0=Zi[cur][:, 0:m], scalar=ar, in1=zti[:, 0:m], op0=ALU.mult, op1=ALU.add),
                    reads=[TZ[cur], Tzt, Ts], writes=[TZ[nxt]])
                S.op("pool", lambda e, cur=cur, nxt=nxt, o=o: e.tensor_copy(out=Zr[nxt][:, 0:o], in_=Zr[cur][:, 0:o]),
                     reads=[TZ[cur]], writes=[TZ[nxt]])
                S.op("pool", lambda e, cur=cur, nxt=nxt, o=o: e.tensor_copy(out=Zi[nxt][:, 0:o], in_=Zi[cur][:, 0:o]),
                     reads=[TZ[cur]], writes=[TZ[nxt]])
                cur = nxt
            S.op("pool", lambda e: e.memset(Spb[0][:, 0:1], 0.0), reads=[TSp], writes=[TSp])
            S.op("pool", lambda e: e.memset(Spb[1][:, 0:1], 0.0), reads=[TSp], writes=[TSp])
            if NCH > 1:
                S.op("dve", lambda e, cur=cur: e.tensor_copy(out=Spb[0][:, 1:NCH], in_=Zr[cur][:, 0:NCH - 1]), reads=[TZ[cur], TSp], writes=[TSp])
                S.op("dve", lambda e, cur=cur: e.tensor_copy(out=Spb[1][:, 1:NCH], in_=Zi[cur][:, 0:NCH - 1]), reads=[TZ[cur], TSp], writes=[TSp])
            def p2_tau(cc, n, tau):
                    lo, hi = 16 * cc, 16 * (cc + n)
                    j = pc[0] % 2; pc[0] += 1
                    for tp in range(tau + 1):
                        d = tau - tp
                        r_re = BU[0][:, tp * NCH + cc:tp * NCH + cc + n]; r_im = BU[1][:, tp * NCH + cc:tp * NCH + cc + n]
                        S.op("pe", lambda e, d=d, r=r_re, tp=tp: e.matmul(psR[j][:, 0:n], lhsT=DR[pr][d][:], rhs=r, start=(tp == 0), stop=False),
                             reads=[TBU[0], Tc], writes=[TpR[j]])
                        S.op("pe", lambda e, d=d, r=r_im: e.matmul(psR[j][:, 0:n], lhsT=DN[pr][d][:], rhs=r, start=False, stop=False),
                             reads=[TBU[1], Tc], writes=[TpR[j]])
                        S.op("pe", lambda e, d=d, r=r_re, tp=tp: e.matmul(psI[j][:, 0:n], lhsT=DI[pr][d][:], rhs=r, start=(tp == 0), stop=False),
                             reads=[TBU[0], Tc], writes=[TpI[j]])
                        S.op("pe", lambda e, d=d, r=r_im: e.matmul(psI[j][:, 0:n], lhsT=DR[pr][d][:], rhs=r, start=False, stop=False),
                             reads=[TBU[1], Tc], writes=[TpI[j]])
                    d = tau + 1
                    S.op("pe", lambda e, d=d: e.matmul(psR[j][:, 0:n], lhsT=DR[pr][d][:], rhs=Spb[0][:, cc:cc + n], start=False, stop=False),
                         reads=[TSp, Tc], writes=[TpR[j]])
                    S.op("pe", lambda e, d=d: e.matmul(psR[j][:, 0:n], lhsT=DN[pr][d][:], rhs=Spb[1][:, cc:cc + n], start=False, stop=True),
                         reads=[TSp, Tc], writes=[TpR[j]])
                    S.op("pe", lambda e, d=d: e.matmul(psI[j][:, 0:n], lhsT=DI[pr][d][:], rhs=Spb[0][:, cc:cc + n], start=False, stop=False),
                         reads=[TSp, Tc], writes=[TpI[j]])
                    S.op("pe", lambda e, d=d: e.matmul(psI[j][:, 0:n], lhsT=DR[pr][d][:], rhs=Spb[1][:, cc:cc + n], start=False, stop=True),
                         reads=[TSp, Tc], writes=[TpI[j]])
                    evac(stt[0][:, tau:16 * n:16], psR[j][:, 0:n], [TpR[j]], [Tst[0]])
                    evac(stt[1][:, tau:16 * n:16], psI[j][:, 0:n], [TpI[j]], [Tst[1]])

            def p2_y(cc, n, x0):
                    lo = 16 * cc
                    ntok = 16 * n
                    w = min(512, ntok - x0)
                    jy = yc[0] % 2; yc[0] += 1
                    t0 = lo + x0
                    r0, r1 = 32 * pr, 32 * pr + 32
                    S.dma("sp", usk[jy][r0:r1, 0:w], u_d[r0:r1, b, t0:t0 + w], writes=[Tusk[jy]])
                    S.op("pe", lambda e, jy=jy, x0=x0, w=w: e.matmul(psY[jy][:, 0:w], lhsT=CB[pr][0][:], rhs=stt[0][:, x0:x0 + w], start=True, stop=False),
                         reads=[Tst[0], Tc], writes=[TpY[jy]])
                    S.op("pe", lambda e, jy=jy, x0=x0, w=w: e.matmul(psY[jy][:, 0:w], lhsT=CB[pr][1][:], rhs=stt[1][:, x0:x0 + w], start=False, stop=True),
                         reads=[Tst[1], Tc], writes=[TpY[jy]])
                    S.op("dve", lambda e, jy=jy, w=w, r0=r0, r1=r1: e.scalar_tensor_tensor(
                        out=ysb[jy][r0:r1, 0:w], in0=usk[jy][r0:r1, 0:w], scalar=dsk[r0:r1, 0:1], in1=psY[jy][r0:r1, 0:w],
                        op0=ALU.mult, op1=ALU.add), reads=[Tusk[jy], TpY[jy], Tp], writes=[Tys[jy]])
                    S.dma("sp", ys_d[r0:r1, b, t0:t0 + w], ysb[jy][r0:r1, 0:w], reads=[Tys[jy]], writes=[Tout])
            for (cc, n) in coltiles:
                for tau in range(16):
                    p2_tau(cc, n, tau)
                for x0 in range(0, 16 * n, 512):
                    p2_y(cc, n, x0)

        si = 0
        for b in range(NBATCH):
            for c0 in range(0, L, 1024):
                n = min(1024, L - c0)
                i = si % 2; si += 1
                S.dma("sp", ustg[i][:, 0:n], u_d[:, b, c0:c0 + n], writes=[Tus[i]])
                S.op("pool", lambda e, i=i, n=n, c0=c0: e.tensor_copy(out=ub[:, c0:c0 + n], in_=ustg[i][:, 0:n]),
                     reads=[Tus[i]], writes=[Tub])
            for pr in range(2):
                unit(pr, b)
        S.barrier()
        S.stack = st0
    return Tout


def dn_phase(S, nc, xq_d, xk_d, xv_d, cw_d, a_d, b_d, hp_d, o_d, NC, NPAD):
    st0 = S.stack
    LP = NC * 64
    with ExitStack() as ps_:
        S.stack = ps_
        Tc = Trk("dnc")
        cw = S.sb("cw", [64, 12]); hp = S.sb("hp", [64, 2]); acol = S.sb("acol", [64, NC]); bcol = S.sb("bcol", [64, NC])
        Tp = Trk("dnp")
        S.dma("sp", cw[:], cw_d[:, :], writes=[Tp]); S.dma("sp", hp[:], hp_d[:, :], writes=[Tp])
        S.dma("sp", acol[:], a_d[:, :], writes=[Tp]); S.dma("sp", bcol[:], b_d[:, :], writes=[Tp])
        one_t = S.sb("done", [64, 1]); eps_t = S.sb("deps", [64, 1])
        S.op("pool", lambda e: e.memset(one_t[:], 1.0), writes=[Tc])
        S.op("pool", lambda e: e.memset(eps_t[:], EPS), writes=[Tc])
        iot = S.sb("dniot", [64, 64]); ident = S.sb("dnident", [64, 64]); identb = S.sb("dnidentb", [64, 64], BF16)
        triu = S.sb("triu", [64, 64]); slow = S.sb("slow", [64, 64]); uinc = S.sb("uinc", [64, 64])
        ones64 = S.sb("ones64", [64, 64]); nones64 = S.sb("nones64", [64, 64]); ones64b = S.sb("ones64b", [64, 64], BF16)
        S.op("pool", lambda e: e.iota(iot[:], pattern=[[1, 64]], base=0, channel_multiplier=-1, allow_small_or_imprecise_dtypes=True), writes=[Tc])
        S.op("dve", lambda e: e.tensor_single_scalar(out=ident[:], in_=iot[:], scalar=0.0, op=ALU.is_equal), reads=[Tc], writes=[Tc])
        S.op("dve", lambda e: e.tensor_copy(out=identb[:], in_=ident[:]), reads=[Tc], writes=[Tc])
        S.op("dve", lambda e: e.tensor_single_scalar(out=triu[:], in_=iot[:], scalar=0.0, op=ALU.is_ge), reads=[Tc], writes=[Tc])
        S.op("dve", lambda e: e.tensor_copy(out=uinc[:], in_=triu[:]), reads=[Tc], writes=[Tc])
        S.op("dve", lambda e: e.tensor_single_scalar(out=slow[:], in_=iot[:], scalar=0.0, op=ALU.is_lt), reads=[Tc], writes=[Tc])
        S.op("pool", lambda e: e.memset(ones64[:], 1.0), writes=[Tc])
        S.op("pool", lambda e: e.memset(nones64[:], -1.0), writes=[Tc])
        S.op("pool", lambda e: e.memset(ones64b[:], 1.0), writes=[Tc])

        gcol = S.sb("gcol", [64, NC]); gccol = S.sb("gccol", [64, NC]); glast = S.sb("glast", [64, NC])
        egc = S.sb("egc", [64, NC]); eglast = S.sb("eglast", [64, NC]); edec = S.sb("edec", [64, NC])
        beta = S.sb("beta", [64, NC]); nbeta = S.sb("nbeta", [64, NC]); begc = S.sb("begc", [64, NC])
        nexpA = S.sb("nexpA", [64, 1]); tmpc = S.sb("tmpc", [64, NC])
        Tg = Trk("gates")
        psG = S.ps("psG", [64, 512]); TpG = Trk()
        S.op("act", lambda e: e.activation(out=tmpc[:], in_=acol[:], func=AF.Exp, bias=hp[:, 1:2]), reads=[Tp], writes=[Tg])
        S.op("act", lambda e: e.activation(out=tmpc[:], in_=tmpc[:], func=AF.Ln, bias=one_t[:, 0:1]), reads=[Tg, Tc], writes=[Tg])
        S.op("act", lambda e: e.activation(out=nexpA[:], in_=hp[:, 0:1], func=AF.Exp), reads=[Tp], writes=[Tg])
        S.op("dve", lambda e: e.tensor_scalar(out=gcol[:], in0=tmpc[:], scalar1=nexpA[:, 0:1], scalar2=-1.0, op0=ALU.mult, op1=ALU.mult),
             reads=[Tg], writes=[Tg])
        if NPAD > 0:
            S.op("dve", lambda e: e.memset(gcol[0:NPAD, 0:1], 0.0), reads=[Tg], writes=[Tg])
        S.op("pe", lambda e: e.matmul(psG[:, 0:NC], lhsT=triu[:], rhs=gcol[:], start=True, stop=True), reads=[Tg, Tc], writes=[TpG])
        S.op("dve", lambda e: e.tensor_copy(out=gccol[:], in_=psG[:, 0:NC]), reads=[TpG], writes=[Tg])
        S.op("pe", lambda e: e.matmul(psG[:, 0:NC], lhsT=ones64[:], rhs=gcol[:], start=True, stop=True), reads=[Tg, Tc], writes=[TpG])
        S.op("dve", lambda e: e.tensor_copy(out=glast[:], in_=psG[:, 0:NC]), reads=[TpG], writes=[Tg])
        S.op("act", lambda e: e.activation(out=egc[:], in_=gccol[:], func=AF.Exp), reads=[Tg], writes=[Tg])
        S.op("act", lambda e: e.activation(out=eglast[:], in_=glast[:], func=AF.Exp), reads=[Tg], writes=[Tg])
        S.op("dve", lambda e: e.tensor_tensor(out=tmpc[:], in0=glast[:], in1=gccol[:], op=ALU.subtract), reads=[Tg], writes=[Tg])
        S.op("act", lambda e: e.activation(out=edec[:], in_=tmpc[:], func=AF.Exp), reads=[Tg], writes=[Tg])
        S.op("act", lambda e: e.activation(out=beta[:], in_=bcol[:], func=AF.Sigmoid), reads=[Tp], writes=[Tg])
        S.op("dve", lambda e: e.tensor_scalar(out=nbeta[:], in0=beta[:], scalar1=-1.0, scalar2=None, op0=ALU.mult), reads=[Tg], writes=[Tg])
        S.op("dve", lambda e: e.tensor_tensor(out=begc[:], in0=beta[:], in1=egc[:], op=ALU.mult), reads=[Tg], writes=[Tg])

        qb = S.sb("dqb", [64, LP], BF16); kb = S.sb("dkb", [64, LP], BF16); vb = S.sb("dvb", [64, LP], BF16)
        Tqkv = [Trk("dq"), Trk("dk"), Trk("dv")]
        CT = 2048
        xin = [S.sb("dxin%d" % i, [64, CT + 3]) for i in range(2)]; Txin = [Trk() for i in range(2)]
        acc = [S.sb("dacc%d" % i, [64, CT]) for i in range(2)]; Tacc = [Trk() for i in range(2)]
        sqb = S.sb("dsq", [64, CT], BF16); Tsq = Trk()
        rinv = S.sb("drinv", [64, 512]); Tri = Trk()
        psS = [S.ps("psS%d" % i, [64, 512]) for i in range(2)]; TpS = [Trk() for i in range(2)]
        cc = [0]

        def conv_tile(src, which, c0):
            n = min(CT, LP - c0)
            i = cc[0] % 2; cc[0] += 1
            dst = (qb, kb, vb)[which]
            S.dma("sp", xin[i][:, 0:n + 3], src[:, c0:c0 + n + 3], writes=[Txin[i]])
            S.op("dve", lambda e: e.tensor_scalar(out=acc[i][:, 0:n], in0=xin[i][:, 0:n], scalar1=cw[:, 4 * which:4 * which + 1],
                                                  scalar2=None, op0=ALU.mult), reads=[Txin[i], Tp], writes=[Tacc[i]])
            for j in range(1, 4):
                S.op("dve", lambda e, j=j: e.scalar_tensor_tensor(
                    out=acc[i][:, 0:n], in0=xin[i][:, j:j + n], scalar=cw[:, 4 * which + j:4 * which + j + 1], in1=acc[i][:, 0:n],
                    op0=ALU.mult, op1=ALU.add), reads=[Txin[i], Tp, Tacc[i]], writes=[Tacc[i]])
            S.op("act", lambda e: e.activation(out=acc[i][:, 0:n], in_=acc[i][:, 0:n], func=AF.Silu), reads=[Tacc[i]], writes=[Tacc[i]])
            if which == 2:
                S.op("pool", lambda e: e.tensor_copy(out=dst[:, c0:c0 + n], in_=acc[i][:, 0:n]), reads=[Tacc[i]], writes=[Tqkv[2]])
                return
            S.op("act", lambda e: e.activation(out=sqb[:, 0:n], in_=acc[i][:, 0:n], func=AF.Square), reads=[Tacc[i]], writes=[Tsq])
            for s0 in range(0, n, 512):
                w = min(512, n - s0)
                j = cc[0] % 2; cc[0] += 1
                S.op("pe", lambda e, s0=s0, w=w, j=j: e.matmul(psS[j][:, 0:w], lhsT=ones64b[:], rhs=sqb[:, s0:s0 + w], start=True, stop=True),
                     reads=[Tsq, Tc], writes=[TpS[j]])
                S.op("act", lambda e, w=w, j=j: e.activation(out=rinv[:, 0:w], in_=psS[j][:, 0:w], func=AF.Ln, bias=eps_t[:, 0:1]),
                     reads=[TpS[j], Tc], writes=[Tri])
                S.op("act", lambda e, w=w: e.activation(out=rinv[:, 0:w], in_=rinv[:, 0:w], func=AF.Exp, scale=-0.5), reads=[Tri], writes=[Tri])
                sc_ = 0.125 if which == 0 else 1.0
                S.op("dve", lambda e, s0=s0, w=w, sc_=sc_: e.scalar_tensor_tensor(
                    out=dst[:, c0 + s0:c0 + s0 + w], in0=acc[i][:, s0:s0 + w], scalar=sc_, in1=rinv[:, 0:w], op0=ALU.mult, op1=ALU.mult),
                    reads=[Tacc[i], Tri], writes=[Tqkv[which]])

        for c0 in range(0, LP, CT):
            conv_tile(xq_d, 0, c0); conv_tile(xk_d, 1, c0); conv_tile(xv_d, 2, c0)

        GS = 8
        psD = S.ps("psD", [64, 512]); TpD = Trk()
        psA = S.ps("psA", [64, 512]); TpA = Trk()
        psQ = S.ps("psQ", [64, 512]); TpQ = Trk()
        psTr = S.ps("psTr", [64, 512]); TpTr = Trk()
        psN0 = S.ps("psN", [64, 512])
        S.barrier()
        psNl = [psN0, psG]; TpNl = [Trk(), Trk()]
        psU = psS[0]; TpU = Trk()
        psQ2 = psS[1]; TpSeq = Trk()
        gt = S.sb("gt", [64, GS * 64]); Tgt = Trk()
        E = S.sb("Emat", [64, GS * 64]); TE = Trk()
        EL = S.sb("EL", [64, GS * 64]); EU = S.sb("EU", [64, GS * 64]); TEm = Trk()
        NL = 2
        Yl = [[S.sb("Yk%d_%d" % (l, i), [64, 64]) for i in range(2)] for l in range(NL)]
        Xl = [[S.sb("Xk%d_%d" % (l, i), [64, 64]) for i in range(2)] for l in range(NL)]
        TYl = [[Trk() for i in range(2)] for l in range(NL)]; TXl = [[Trk() for i in range(2)] for l in range(NL)]
        Pl = [S.sb("Pm%d" % l, [64, 64]) for l in range(NL)]; TPl = [Trk() for l in range(NL)]
        TTbl = [S.sb("TTb%d" % l, [64, 64], BF16) for l in range(NL)]; TTTl = [Trk() for l in range(NL)]
        vbetal = [S.sb("vbeta%d" % l, [64, 64], BF16) for l in range(NL)]
        kbgl = [S.sb("kbg%d" % l, [64, 64], BF16) for l in range(NL)]; Tvkl = [Trk() for l in range(NL)]
        u_g = [S.sb("u_g%d" % i, [64, GS * 64]) for i in range(2)]
        wT_g = [S.sb("wT_g%d" % i, [64, GS * 64], BF16) for i in range(2)]
        attnT_g = [S.sb("attnT_g%d" % i, [64, GS * 64], BF16) for i in range(2)]
        kdec_g = [S.sb("kdec_g%d" % i, [64, GS * 64], BF16) for i in range(2)]
        Tgrp = [[Trk() for _ in range(GS)] for i in range(2)]
        Sf = S.sb("Sf", [64, 64]); Sb = S.sb("Sb", [64, 64], BF16); TS = Trk()
        vnew = S.sb("vnew", [64, 64], BF16); Tvn = Trk()
        qs_t = S.sb("qs_t", [64, 64]); Tqs = Trk()
        o_g = [S.sb("o_g%d" % i, [64, GS * 64]) for i in range(2)]; Tog = [Trk() for i in range(2)]
        Tout = Trk("dnout")
        S.op("pool", lambda e: e.memset(Sf[:], 0.0), writes=[TS])
        S.op("pool", lambda e: e.memset(Sb[:], 0.0), reads=[TS], writes=[TS])

        def pre_group_head(n0, ng, gp):
            W = ng * 64
            for c in range(ng):
                n = n0 + c
                S.op("dve", lambda e, c=c, n=n: e.tensor_scalar(out=gt[:, c * 64:(c + 1) * 64], in0=triu[:], scalar1=gcol[:, n:n + 1],
                                                                 scalar2=None, op0=ALU.mult), reads=[Tg, Tc, Tgt], writes=[Tgt])
                S.op("pe", lambda e, c=c: e.matmul(psD[:, c * 64:(c + 1) * 64], lhsT=gt[:, c * 64:(c + 1) * 64], rhs=ones64[:], start=True, stop=False),
                     reads=[Tgt, Tc], writes=[TpD])
                S.op("pe", lambda e, c=c: e.matmul(psD[:, c * 64:(c + 1) * 64], lhsT=nones64[:], rhs=gt[:, c * 64:(c + 1) * 64], start=False, stop=True),
                     reads=[Tgt, Tc], writes=[TpD])
            yield
            S.op("dve", lambda e: e.tensor_scalar(out=E[:, 0:W], in0=psD[:, 0:W], scalar1=-1.0, scalar2=None, op0=ALU.mult), reads=[TpD, TE, TEm], writes=[TE])
            S.op("dve", lambda e: e.tensor_tensor(out=E[:, 0:W], in0=E[:, 0:W], in1=psD[:, 0:W], op=ALU.min), reads=[TpD, TE], writes=[TE])
            S.op("act", lambda e: e.activation(out=E[:, 0:W], in_=E[:, 0:W], func=AF.Exp), reads=[TE], writes=[TE])
            yield
            S.op("dve", lambda e: e.tensor_tensor(out=EL[:, 0:W].rearrange("p (c f) -> p c f", f=64), in0=E[:, 0:W].rearrange("p (c f) -> p c f", f=64),
                                                  in1=slow[:].unsqueeze(1).broadcast_to([64, ng, 64]), op=ALU.mult), reads=[TE, Tc, TEm], writes=[TEm])
            S.op("dve", lambda e: e.tensor_tensor(out=EU[:, 0:W].rearrange("p (c f) -> p c f", f=64), in0=E[:, 0:W].rearrange("p (c f) -> p c f", f=64),
                                                  in1=uinc[:].unsqueeze(1).broadcast_to([64, ng, 64]), op=ALU.mult), reads=[TE, Tc, TEm], writes=[TEm])
            for c in range(ng):
                n = n0 + c
                ks = kb[:, n * 64:(n + 1) * 64]; qs = qb[:, n * 64:(n + 1) * 64]
                S.op("pe", lambda e, c=c, ks=ks: e.matmul(psA[:, c * 64:(c + 1) * 64], lhsT=ks, rhs=ks, start=True, stop=True),
                     reads=[Tqkv[1]], writes=[TpA])
                S.op("pe", lambda e, c=c, ks=ks, qs=qs: e.matmul(psQ[:, c * 64:(c + 1) * 64], lhsT=ks, rhs=qs, start=True, stop=True),
                     reads=[Tqkv[0], Tqkv[1]], writes=[TpQ])
            yield
            S.op("dve", lambda e: e.tensor_tensor(out=attnT_g[gp][:, 0:W], in0=EU[:, 0:W], in1=psQ[:, 0:W], op=ALU.mult),
                 reads=[TEm, TpQ] + Tgrp[gp], writes=Tgrp[gp])
            yield

        def pre_chunk(n, c, gp, l):
            cs = slice(c * 64, (c + 1) * 64)
            lo_ = 128 * l
            Y = Yl[l]; Xm = Xl[l]; TY = TYl[l]; TX = TXl[l]; P = Pl[l]; TP = TPl[l]; psN = psNl[l]; TpN = TpNl[l]
            TTb = TTbl[l]; TTT = TTTl[l]; vbeta = vbetal[l]; kbg = kbgl[l]; Tvk = Tvkl[l]
            S.op("dve", lambda e: e.scalar_tensor_tensor(out=Y[0][:], in0=psA[:, cs], scalar=nbeta[:, n:n + 1], in1=EL[:, cs],
                                                         op0=ALU.mult, op1=ALU.mult), reads=[TpA, Tg, TEm, TY[0]], writes=[TY[0]])
            yield
            S.op("pe", lambda e: e.matmul(psN[:, 0:64], lhsT=Y[0][:], rhs=ident[:], start=True, stop=True), reads=[TY[0], Tc], writes=[TpN])
            yield
            S.op("dve", lambda e: e.tensor_copy(out=Xm[0][:], in_=psN[:, 0:64]), reads=[TpN], writes=[TX[0]])
            S.op("dve", lambda e: e.tensor_tensor(out=P[:], in0=Xm[0][:], in1=ident[:], op=ALU.add), reads=[TX[0], Tc, TP], writes=[TP])
            yield
            cur = 0
            for lv in range(5):
                nx = 1 - cur
                S.op("pe", lambda e, cur=cur: e.matmul(psN[:, 64:128], lhsT=Y[cur][:], rhs=Xm[cur][:], start=True, stop=True),
                     reads=[TY[cur], TX[cur]], writes=[TpN])
                S.op("pe", lambda e, cur=cur: e.matmul(psN[:, 128:192], lhsT=Xm[cur][:], rhs=Y[cur][:], start=True, stop=True),
                     reads=[TY[cur], TX[cur]], writes=[TpN])
                yield
                S.op("dve", lambda e, nx=nx: e.tensor_copy(out=Xm[nx][:], in_=psN[:, 64:128]), reads=[TpN], writes=[TX[nx]])
                S.op("dve", lambda e, nx=nx: e.tensor_copy(out=Y[nx][:], in_=psN[:, 128:192]), reads=[TpN], writes=[TY[nx]])
                yield
                S.op("pe", lambda e, nx=nx: e.matmul(psN[:, 192:256], lhsT=Y[nx][:], rhs=P[:], start=True, stop=True),
                     reads=[TY[nx], TP], writes=[TpN])
                yield
                S.op("dve", lambda e: e.tensor_tensor(out=P[:], in0=P[:], in1=psN[:, 192:256], op=ALU.add), reads=[TpN, TP], writes=[TP])
                yield
                cur = nx
            S.op("act", lambda e: e.copy(out=TTb[:], in_=P[:]), reads=[TP], writes=[TTT])
            ks = kb[:, n * 64:(n + 1) * 64]; vs = vb[:, n * 64:(n + 1) * 64]
            S.op("pe", lambda e: e.matmul(psTr[:, lo_ + 0:lo_ + 64], lhsT=ks, rhs=identb[:], start=True, stop=True), reads=[Tqkv[1], Tc], writes=[TpTr])
            S.op("pe", lambda e: e.matmul(psTr[:, lo_ + 64:lo_ + 128], lhsT=vs, rhs=identb[:], start=True, stop=True), reads=[Tqkv[2], Tc], writes=[TpTr])
            yield
            S.op("dve", lambda e: e.tensor_scalar(out=kbg[:], in0=psTr[:, lo_ + 0:lo_ + 64], scalar1=begc[:, n:n + 1], scalar2=None, op0=ALU.mult),
                 reads=[TpTr, Tg, Tvk], writes=[Tvk])
            S.op("dve", lambda e: e.tensor_scalar(out=vbeta[:], in0=psTr[:, lo_ + 64:lo_ + 128], scalar1=beta[:, n:n + 1], scalar2=None, op0=ALU.mult),
                 reads=[TpTr, Tg, Tvk], writes=[Tvk])
            S.op("dve", lambda e: e.tensor_scalar(out=kdec_g[gp][:, cs], in0=psTr[:, lo_ + 0:lo_ + 64], scalar1=edec[:, n:n + 1], scalar2=None, op0=ALU.mult),
                 reads=[TpTr, Tg, Tgrp[gp][c]], writes=[Tgrp[gp][c]])
            yield
            S.op("pe", lambda e: e.matmul(psU[:, lo_ + 0:lo_ + 64], lhsT=TTb[:], rhs=vbeta[:], start=True, stop=True), reads=[TTT, Tvk], writes=[TpU])
            S.op("pe", lambda e: e.matmul(psU[:, lo_ + 64:lo_ + 128], lhsT=kbg[:], rhs=TTb[:], start=True, stop=True), reads=[TTT, Tvk], writes=[TpU])
            yield
            S.op("dve", lambda e: e.tensor_copy(out=u_g[gp][:, cs], in_=psU[:, lo_ + 0:lo_ + 64]), reads=[TpU, Tgrp[gp][c]], writes=[Tgrp[gp][c]])
            S.op("dve", lambda e: e.tensor_copy(out=wT_g[gp][:, cs], in_=psU[:, lo_ + 64:lo_ + 128]), reads=[TpU, Tgrp[gp][c]], writes=[Tgrp[gp][c]])
            yield

        def seq_chunk(n, c, gp, og):
            cs = slice(c * 64, (c + 1) * 64)
            qs = qb[:, n * 64:(n + 1) * 64]
            Tgc = Tgrp[gp][c]
            S.op("pe", lambda e: e.matmul(psQ2[:, 0:64], lhsT=wT_g[gp][:, cs], rhs=Sb[:], start=True, stop=True), reads=[Tgc, TS], writes=[TpSeq])
            S.op("pe", lambda e: e.matmul(psQ2[:, 64:128], lhsT=qs, rhs=Sb[:], start=True, stop=True), reads=[Tqkv[0], TS], writes=[TpSeq])
            yield
            S.op("dve", lambda e: e.tensor_tensor(out=vnew[:], in0=u_g[gp][:, cs], in1=psQ2[:, 0:64], op=ALU.subtract),
                 reads=[Tgc, TpSeq, Tvn], writes=[Tvn])
            S.op("dve", lambda e: e.tensor_scalar(out=qs_t[:], in0=psQ2[:, 64:128], scalar1=egc[:, n:n + 1], scalar2=None, op0=ALU.mult),
                 reads=[TpSeq, Tg, Tqs], writes=[Tqs])
            yield
            S.op("pe", lambda e: e.matmul(psQ2[:, 128:192], lhsT=attnT_g[gp][:, cs], rhs=vnew[:], start=True, stop=True), reads=[Tgc, Tvn], writes=[TpSeq])
            S.op("pe", lambda e: e.matmul(psQ2[:, 192:256], lhsT=kdec_g[gp][:, cs], rhs=vnew[:], start=True, stop=True), reads=[Tgc, Tvn], writes=[TpSeq])
            yield
            S.op("dve", lambda e: e.scalar_tensor_tensor(out=Sf[:], in0=Sf[:], scalar=eglast[:, n:n + 1], in1=psQ2[:, 192:256],
                                                         op0=ALU.mult, op1=ALU.add), reads=[TS, Tg, TpSeq], writes=[TS])
            S.op("act", lambda e: e.copy(out=Sb[:], in_=Sf[:]), reads=[TS], writes=[TS])
            S.op("dve", lambda e: e.tensor_tensor(out=o_g[og][:, cs], in0=qs_t[:], in1=psQ2[:, 128:192], op=ALU.add),
                 reads=[Tqs, TpSeq, Tog[og]], writes=[Tog[og]])
            yield

        def pre_gen(n0, ng, gp):
            yield from pre_group_head(n0, ng, gp)
            for c in range(0, ng, NL):
                gens = [pre_chunk(n0 + c + l, c + l, gp, l) for l in range(NL) if c + l < ng]
                while gens:
                    for g_ in list(gens):
                        try:
                            next(g_)
                        except StopIteration:
                            gens.remove(g_)
                    yield

        def seq_gen(n0, ng, gp, og):
            for c in range(ng):
                yield from seq_chunk(n0 + c, c, gp, og)
            S.dma("sp", o_d[:, n0:n0 + ng, :], o_g[og][:, 0:ng * 64].rearrange("p (c f) -> p c f", f=64), reads=[Tog[og]], writes=[Tout])

        groups = [(n0, min(GS, NC - n0)) for n0 in range(0, NC, GS)]
        for _ in pre_gen(groups[0][0], groups[0][1], 0):
            pass
        for gi, (n0, ng) in enumerate(groups):
            gp = gi % 2
            active = [seq_gen(n0, ng, gp, gi % 2)]
            if gi + 1 < len(groups):
                active.append(pre_gen(groups[gi + 1][0], groups[gi + 1][1], 1 - gp))
            while active:
                for g_ in list(active):
                    try:
                        next(g_)
                    except StopIteration:
                        active.remove(g_)
        S.barrier()
        S.stack = st0
    return Tout


from concourse.bass_utils import run_bass_kernel_spmd

N_META = 16
SEQ = 16384
LTOK = N_META + SEQ
NTOK = 4100
SB_NB = 129
SB_PAD = 112
DN_NC = 257
DN_PAD = 48


def build_mix():
    nc = bass.Bass("TRN2", target_bir_lowering=False)
    di = lambda n, s: nc.dram_tensor(n, list(s), F32, kind="ExternalInput").ap()
    do = lambda n, s: nc.dram_tensor(n, list(s), F32, kind="ExternalOutput").ap()
    LPS = SB_NB * 128
    LPD = DN_NC * 64
    qT = di("sb_q", [64, LPS]); kT = di("sb_k", [64, LPS]); vv = di("sb_v", [128, SB_NB, 64]); oT = do("sb_o", [64, LPS])
    xq = di("dn_xq", [64, LPD + 3]); xk = di("dn_xk", [64, LPD + 3]); xv = di("dn_xv", [64, LPD + 3])
    cw = di("dn_cw", [64, 12]); a_d = di("dn_a", [64, DN_NC]); b_d = di("dn_b", [64, DN_NC]); hp = di("dn_hp", [64, 2])
    dn_o = do("dn_o", [64, DN_NC, 64])
    u_d = di("s5_u", [64, 2, LTOK]); are = di("s5_are", [128, 2]); aim = di("s5_aim", [128, 2]); ldt = di("s5_ldt", [128, 2])
    bre = di("s5_bre", [128, 2, 16]); bim = di("s5_bim", [128, 2, 16]); cre = di("s5_cre", [128, 2, 16]); cim = di("s5_cim", [128, 2, 16])
    dsk = di("s5_dsk", [64, 1]); ys = do("s5_y", [64, 2, LTOK])
    with ExitStack() as st:
        S = Sched(nc, st)
        T1 = sb_phase(S, nc, qT, kT, vv, oT, SB_NB, SB_PAD)
        T2 = dn_phase(S, nc, xq, xk, xv, cw, a_d, b_d, hp, dn_o, DN_NC, DN_PAD)
        T3 = s5_phase(S, nc, u_d, are, aim, ldt, bre, bim, cre, cim, dsk, ys, LTOK, 2)
        S.finish([T1, T2, T3])
    return nc


def _g8(g):
    return np.ascontiguousarray(np.asarray(g, np.float32).reshape(-1, 128).T)


def _pairlay(x):
    x = np.asarray(x, np.float32)
    sh = x.shape[2:]
    return np.ascontiguousarray(x.reshape(2, 2, 64, *sh).transpose(1, 2, 0, *range(3, 3 + len(sh))).reshape(128, 2, *sh))


def _c(x):
    return np.ascontiguousarray(x, dtype=np.float32)


def _run(nc, maps):
    res = run_bass_kernel_spmd(nc, maps, core_ids=list(range(8)))
    return res.results


def kernel(**I):
    I = {k: np.asarray(v) for k, v in I.items()}
    x = I["x"]; meta = I["meta_tokens"]
    h = np.concatenate([np.broadcast_to(meta[None], (2, N_META, D)), x], axis=1)
    hT = [_c(h[c // 4, (c % 4) * NTOK:(c % 4 + 1) * NTOK].T) for c in range(8)]
    progs = {}

    def tok_launch(mode, l, hT, mix=None):
        if mode not in progs:
            progs[mode] = build_tok(mode)
        maps = []
        for c in range(8):
            m = {"h_in": hT[c]}
            if mode in ("CA", "C1"):
                b, q = c // 4, c % 4
                sl = slice(q * NTOK, (q + 1) * NTOK)
                m.update(osb=_c(mix["osb"][b][:, sl]), odn=_c(mix["odn"][b][:, sl]), dnz=_c(mix["dnz"][b][:, sl]), ys5=_c(mix["ys5"][b][:, sl]),
                         sbn=_c(np.tile(I["sb_out_norm"][l], 2).reshape(128, 1)), dnn=_c(np.tile(I["dn_out_norm"][l], 2).reshape(128, 1)),
                         wglu=_c(I["s5_w_glu"][l]), bglu=_g8(I["s5_b_glu"][l]), s5n=_g8(I["s5_out_norm"][l]), w_out=_c(I["w_out"][l]),
                         wg2=_c(I["ffn2_w_gate"][l]), wu2=_c(I["ffn2_w_up"][l]), wd2=_c(I["ffn2_w_down"][l]), n2=_g8(I["ffn2_norm"][l]))
            if mode == "CA":
                l2 = l + 1
            else:
                l2 = l
            if mode in ("A0", "CA"):
                m.update(wg1=_c(I["ffn1_w_gate"][l2]), wu1=_c(I["ffn1_w_up"][l2]), wd1=_c(I["ffn1_w_down"][l2]), n1=_g8(I["ffn1_norm"][l2]),
                         w_in=_c(I["w_in"][l2]), nmix=_g8(I["mix_norm"][l2]))
            if mode == "C1":
                m.update(nfin=_g8(I["final_norm"]))
            maps.append(m)
        return _run(progs[mode], maps)

    def mix_launch(l, projT):
        if "mix" not in progs:
            progs["mix"] = build_mix()
        maps = []
        for c in range(8):
            b, hh = c // 4, c % 4
            P = projT[b]
            m = {}
            padz = lambda r, n: _c(np.concatenate([np.zeros((r.shape[0], n), np.float32), r], axis=1))
            m["sb_q"] = padz(P[hh * 64:(hh + 1) * 64], SB_PAD)
            m["sb_k"] = padz(P[256 + hh * 64:256 + (hh + 1) * 64], SB_PAD)
            vT = padz(P[512 + hh * 64:512 + (hh + 1) * 64], SB_PAD)
            m["sb_v"] = _c(vT.T.reshape(SB_NB, 128, 64).transpose(1, 0, 2))
            o = 768
            m["dn_xq"] = padz(P[o + hh * 64:o + (hh + 1) * 64], DN_PAD + 3)
            m["dn_xk"] = padz(P[o + 256 + hh * 64:o + 256 + (hh + 1) * 64], DN_PAD + 3)
            m["dn_xv"] = padz(P[o + 512 + hh * 64:o + 512 + (hh + 1) * 64], DN_PAD + 3)
            cwl = I["dn_conv_w"][l]
            m["dn_cw"] = _c(np.concatenate([cwl[:, hh * 64:(hh + 1) * 64].T, cwl[:, 256 + hh * 64:256 + (hh + 1) * 64].T,
                                            cwl[:, 512 + hh * 64:512 + (hh + 1) * 64].T], axis=1))
            brow = P[1792 + hh]; arow = P[1796 + hh]
            col = lambda r: _c(np.concatenate([np.zeros(DN_PAD, np.float32), r]).reshape(DN_NC, 64).T)
            m["dn_a"] = col(arow); m["dn_b"] = col(brow)
            m["dn_hp"] = _c(np.tile(np.array([[I["dn_a_log"][l, hh], I["dn_dt_bias"][l, hh]]], np.float32), (64, 1)))
            g0 = 4 * c
            m["s5_u"] = _c(np.stack([projT[0][1800 + 64 * c:1800 + 64 * c + 64], projT[1][1800 + 64 * c:1800 + 64 * c + 64]], axis=1))
            m["s5_are"] = _pairlay(I["s5_a_re"][l, g0:g0 + 4]); m["s5_aim"] = _pairlay(I["s5_a_im"][l, g0:g0 + 4])
            m["s5_ldt"] = _pairlay(np.repeat(I["s5_log_dt"][l, g0:g0 + 4][:, None], 64, 1))
            m["s5_bre"] = _pairlay(I["s5_b_re"][l, g0:g0 + 4]); m["s5_bim"] = _pairlay(I["s5_b_im"][l, g0:g0 + 4])
            m["s5_cre"] = _pairlay(I["s5_c_re"][l, g0:g0 + 4].transpose(0, 2, 1)); m["s5_cim"] = _pairlay(I["s5_c_im"][l, g0:g0 + 4].transpose(0, 2, 1))
            m["s5_dsk"] = _c(I["s5_d"][l, 64 * c:64 * c + 64].reshape(64, 1))
            maps.append(m)
        res = _run(progs["mix"], maps)
        osb = [np.zeros((256, LTOK), np.float32) for _ in range(2)]
        odn = [np.zeros((256, LTOK), np.float32) for _ in range(2)]
        ys5 = [np.zeros((512, LTOK), np.float32) for _ in range(2)]
        for c in range(8):
            b, hh = c // 4, c % 4
            osb[b][hh * 64:(hh + 1) * 64] = res[c]["sb_o"][:, SB_PAD:]
            od = res[c]["dn_o"].transpose(1, 0, 2).reshape(DN_NC * 64, 64)[DN_PAD:]
            odn[b][hh * 64:(hh + 1) * 64] = od.T
            for bb in range(2):
                ys5[bb][64 * c:64 * c + 64] = res[c]["s5_y"][:, bb]
        dnz = [projT[b][1536:1792] for b in range(2)]
        return {"osb": osb, "odn": odn, "dnz": dnz, "ys5": ys5}

    def gather_proj(res):
        projT = [np.zeros((INW, LTOK), np.float32) for _ in range(2)]
        for c in range(8):
            b, q = c // 4, c % 4
            projT[b][:, q * NTOK:(q + 1) * NTOK] = res[c]["proj"][:INW]
        return projT

    r = tok_launch("A0", 0, hT)
    hT = [r[c]["h_out"] for c in range(8)]
    mix = mix_launch(0, gather_proj(r))
    r = tok_launch("CA", 0, hT, mix)
    hT = [r[c]["h_out"] for c in range(8)]
    mix = mix_launch(1, gather_proj(r))
    r = tok_launch("C1", 1, hT, mix)
    out = np.zeros((2, LTOK, D), np.float32)
    for c in range(8):
        b, q = c // 4, c % 4
        out[b, q * NTOK:(q + 1) * NTOK] = r[c]["y_out"].T
    return np.ascontiguousarray(out[:, N_META:])
```

```python
from contextlib import ExitStack
import numpy as np
import concourse.bass as bass
import concourse.mybir as mybir

F32 = mybir.dt.float32
BF16 = mybir.dt.bfloat16
AF = mybir.ActivationFunctionType
ALU = mybir.AluOpType
AX = mybir.AxisListType

N_DMA_SEMS = 24


class Trk:
    __slots__ = ("name", "w", "r")

    def __init__(self, name=""):
        self.name = name
        self.w = None
        self.r = []


class Sched:
    ENG = ("pe", "act", "dve", "pool", "sp")

    def __init__(self, nc, stack):
        self.nc = nc
        self.stack = stack
        self.prog = {e: [] for e in self.ENG}
        self.cnt = {e: 0 for e in ("pe", "act", "dve", "pool")}
        self.sems = {}
        for e in ("pe", "act", "dve", "pool"):
            self.sems[e] = stack.enter_context(nc.semaphore("s_" + e))
        self.dsems = [stack.enter_context(nc.semaphore("d%d" % i)) for i in range(N_DMA_SEMS)]
        self.dcnt = [0] * N_DMA_SEMS
        self.dnext = 0
        self.seen = {e: {} for e in self.ENG}
        self.n_wait = 0

    def sb(self, name, shape, dt=F32):
        self.uid = getattr(self, "uid", 0) + 1
        if not hasattr(self, "names"):
            self.names = {}
        self.names[name] = "%s_%d" % (name, self.uid)
        return self.stack.enter_context(self.nc.sbuf_tensor("%s_%d" % (name, self.uid), list(shape), dt))

    def ps(self, name, shape, dt=F32):
        self.uid = getattr(self, "uid", 0) + 1
        return self.stack.enter_context(self.nc.psum_tensor("%s_%d" % (name, self.uid), list(shape), dt))

    def _semobj(self, key):
        return self.sems[key] if isinstance(key, str) else self.dsems[key]

    def _need(self, eng, ev, waits):
        if ev is None:
            return
        key, val, src = ev
        if eng == "pe" and src == "pe":
            return
        if self.seen[eng].get(key, 0) >= val:
            return
        waits[key] = max(waits.get(key, 0), val)

    def _collect(self, eng, reads, writes):
        waits = {}
        for t in reads:
            self._need(eng, t.w, waits)
        for t in writes:
            self._need(eng, t.w, waits)
            for ev in t.r:
                self._need(eng, ev, waits)
        for key, val in waits.items():
            self.seen[eng][key] = val
        return list(waits.items())

    def _record(self, ev, reads, writes):
        for t in reads:
            t.r.append(ev)
            if len(t.r) > 64:
                best = {}
                for k, v, s in t.r:
                    if k not in best or best[k][1] < v:
                        best[k] = (k, v, s)
                t.r = list(best.values())
        for t in writes:
            t.w = ev
            t.r = []

    def op(self, eng, fn, reads=(), writes=()):
        waits = self._collect(eng, reads, writes)
        self.cnt[eng] += 1
        n = self.cnt[eng]
        sem = self.sems[eng]
        wl = [(self._semobj(k), v) for k, v in waits]
        self.n_wait += len(wl)

        def emit(e, wl=wl, fn=fn, sem=sem):
            for s, v in wl:
                e.wait_ge(s, v)
            fn(e).then_inc(sem, 1)
        self.prog[eng].append(emit)
        self._record((eng, n, eng), reads, writes)

    def dma(self, q, out, in_, reads=(), writes=(), **kw):
        i = self.dnext
        self.dnext = (self.dnext + 1) % N_DMA_SEMS
        waits = dict(self._collect(q, reads, writes))
        if self.dcnt[i] > 0 and self.seen[q].get(i, 0) < self.dcnt[i]:
            waits[i] = max(waits.get(i, 0), self.dcnt[i])
            self.seen[q][i] = self.dcnt[i]
        self.dcnt[i] += 16
        val = self.dcnt[i]
        sem = self.dsems[i]
        wl = [(self._semobj(k), v) for k, v in waits.items()]

        def emit(e, wl=wl, sem=sem, out=out, in_=in_, kw=kw):
            for s, v in wl:
                e.wait_ge(s, v)
            e.dma_start(out=out, in_=in_, **kw).then_inc(sem, 16)
        self.prog[q].append(emit)
        self._record((i, val, "dma"), reads, writes)

    def barrier(self):
        for eng in self.ENG:
            waits = {}
            for e in ("pe", "act", "dve", "pool"):
                if e != eng and self.cnt[e] > 0 and self.seen[eng].get(e, 0) < self.cnt[e]:
                    waits[e] = self.cnt[e]
            for i in range(N_DMA_SEMS):
                if self.dcnt[i] > 0 and self.seen[eng].get(i, 0) < self.dcnt[i]:
                    waits[i] = self.dcnt[i]
            for k, v in waits.items():
                self.seen[eng][k] = v
            wl = [(self._semobj(k), v) for k, v in waits.items()]

            def emit(e, wl=wl):
                for s, v in wl:
                    e.wait_ge(s, v)
            self.prog[eng].append(emit)

    def finish(self, out_trackers):
        nc = self.nc
        waits = {}
        for t in out_trackers:
            self._need("sp", t.w, waits)
        for i in range(N_DMA_SEMS):
            if self.dcnt[i] > 0 and self.seen["sp"].get(i, 0) < self.dcnt[i]:
                waits[i] = max(waits.get(i, 0), self.dcnt[i])
        for e in ("pe", "act", "dve", "pool"):
            if self.cnt[e] > 0:
                waits[e] = max(waits.get(e, 0), self.cnt[e])
        wl = [(self._semobj(k), v) for k, v in waits.items()]

        def emit(e, wl=wl):
            for s, v in wl:
                e.wait_ge(s, v)
        self.prog["sp"].append(emit)

        prog = self.prog
        with nc.Block() as block:
            @block.tensor
            def _(e):
                for f in prog["pe"]:
                    f(e)

            @block.scalar
            def _(e):
                for f in prog["act"]:
                    f(e)

            @block.vector
            def _(e):
                for f in prog["dve"]:
                    f(e)

            @block.gpsimd
            def _(e):
                for f in prog["pool"]:
                    f(e)

            @block.sync
            def _(e):
                for f in prog["sp"]:
                    f(e)

from contextlib import ExitStack

D = 1024
KD = 8
DFF = 2816
KF = 22
INW = 2312
INP = 2432
KP = 19
EPS = 1e-6
WCOLS = 22528
STG = 1408


def build_tok(mode, NT=4100, TT=205):
    assert NT % TT == 0
    NTILE = NT // TT
    nc = bass.Bass("TRN2", target_bir_lowering=False)

    def din(name, shape):
        return nc.dram_tensor(name, list(shape), F32, kind="ExternalInput").ap()

    def dout(name, shape):
        return nc.dram_tensor(name, list(shape), F32, kind="ExternalOutput").ap()

    def dint(name, shape):
        return nc.dram_tensor(name, list(shape), F32, kind="Internal").ap()

    h_in = din("h_in", [D, NT])
    do_epi = mode in ("CA", "C1")
    do_a = mode in ("A0", "CA")
    do_fin = mode == "C1"
    if do_epi:
        osb = din("osb", [256, NT]); odn = din("odn", [256, NT]); dnz = din("dnz", [256, NT])
        ys5 = din("ys5", [512, NT])
        sbn = din("sbn", [128, 1]); dnn = din("dnn", [128, 1])
        wglu = din("wglu", [512, 512]); bglu = din("bglu", [128, 4]); s5n = din("s5n", [128, 4])
        w_out = din("w_out", [D, D])
        wg2 = din("wg2", [D, DFF]); wu2 = din("wu2", [D, DFF]); wd2 = din("wd2", [DFF, D]); n2 = din("n2", [128, KD])
        hs_a = dint("hs_a", [D, NT])
    if do_a:
        wg1 = din("wg1", [D, DFF]); wu1 = din("wu1", [D, DFF]); wd1 = din("wd1", [DFF, D]); n1 = din("n1", [128, KD])
        w_in = din("w_in", [D, INW]); nmix = din("nmix", [128, KD])
        h_out = dout("h_out", [D, NT])
        proj = dout("proj", [INP, NT])
        if do_epi:
            hs_b = dint("hs_b", [D, NT])
    if do_fin:
        nfin = din("nfin", [128, KD])
        y_out = dout("y_out", [D, NT])

    out_trks = []
    with ExitStack() as st:
        S = Sched(nc, st)
        WA = S.sb("WA", [128, WCOLS], BF16)
        WB = S.sb("WB", [128, WCOLS], BF16)
        WC = S.sb("WC", [128, WCOLS], BF16)
        stage = [S.sb("stg%d" % i, [128, STG]) for i in range(2)]
        Tstage = [Trk("stg%d" % i) for i in range(2)]
        sidx = [0]
        ones_bf = S.sb("ones_bf", [128, 128], BF16)
        blk_bf = S.sb("blk_bf", [128, 128], BF16)
        gains = S.sb("gains", [128, 64])
        Tconst = Trk("const")
        Tg = Trk("gains")
        S.op("pool", lambda e: e.memset(ones_bf[:], 1.0), writes=[Tconst])
        S.op("pool", lambda e: e.memset(blk_bf[:], 0.0), writes=[Tconst])
        S.op("pool", lambda e: e.memset(blk_bf[0:64, 0:64], 1.0), writes=[Tconst])
        S.op("pool", lambda e: e.memset(blk_bf[64:128, 64:128], 1.0), writes=[Tconst])
        if do_a:
            S.dma("sp", gains[:, 0:8], n1[:, :], writes=[Tg])
            S.dma("sp", gains[:, 8:16], nmix[:, :], writes=[Tg])
        if do_epi:
            S.dma("sp", gains[:, 16:24], n2[:, :], writes=[Tg])
            S.dma("sp", gains[:, 32:33], sbn[:, :], writes=[Tg])
            S.dma("sp", gains[:, 33:34], dnn[:, :], writes=[Tg])
            S.dma("sp", gains[:, 34:38], bglu[:, :], writes=[Tg])
            S.dma("sp", gains[:, 38:42], s5n[:, :], writes=[Tg])
        if do_fin:
            S.dma("sp", gains[:, 24:32], nfin[:, :], writes=[Tg])

        TW = {"A": [Trk("WA%d" % k) for k in range(KF)], "B": [Trk("WB%d" % k) for k in range(KF)],
              "C": [Trk("WC%d" % k) for k in range(KF)]}

        def load_w(wtile, trk, dst_c0, src_ap, ncols, scale_ap):
            for c0 in range(0, ncols, STG):
                n = min(STG, ncols - c0)
                i = sidx[0]; sidx[0] ^= 1
                stg = stage[i]
                S.dma("sp", stg[:, 0:n], src_ap[:, c0:c0 + n], writes=[Tstage[i]])
                dst = wtile[:, dst_c0 + c0: dst_c0 + c0 + n]
                if scale_ap is None:
                    S.op("pool", lambda e, dst=dst, stg=stg, n=n: e.tensor_copy(out=dst, in_=stg[:, 0:n]),
                         reads=[Tstage[i]], writes=[trk])
                else:
                    S.op("pool", lambda e, dst=dst, stg=stg, n=n, sc=scale_ap: e.tensor_scalar(
                        out=dst, in0=stg[:, 0:n], scalar1=sc, scalar2=1.0, op0=ALU.mult, op1=ALU.mult),
                        reads=[Tstage[i], Tg], writes=[trk])

        def load_ffn_weights(wg, wu, wd, gcol):
            for k in range(KD):
                load_w(WA, TW["A"][k], k * DFF, wg[k * 128:(k + 1) * 128, :], DFF, gains[:, gcol + k: gcol + k + 1])
                load_w(WB, TW["B"][k], k * DFF, wu[k * 128:(k + 1) * 128, :], DFF, gains[:, gcol + k: gcol + k + 1])
            for f in range(KF):
                load_w(WC, TW["C"][f], f * D, wd[f * 128:(f + 1) * 128, :], D, None)

        def tok_view(ap, nch):
            return ap.rearrange("(k p) n -> p k n", p=128)

        def norm_stats(ph, src_tile, Tsrc, nk, lhs_ones, inv_n, sq, Tsq, ps_stat, Tps, lnv, Tln, rstd, Trs):
            S.op("act", lambda e: e.activation(out=sq[:, 0:nk, :], in_=src_tile[:, 0:nk, :], func=AF.Square),
                 reads=[Tsrc], writes=[Tsq])
            for k in range(nk):
                S.op("pe", lambda e, k=k: e.matmul(ps_stat[:, 0:TT], lhsT=lhs_ones[:], rhs=sq[:, k, :],
                                                    start=(k == 0), stop=(k == nk - 1)),
                     reads=[Tsq, Tconst], writes=[Tps])
            S.op("act", lambda e: e.activation(out=lnv[:], in_=ps_stat[:, 0:TT], func=AF.Ln, scale=inv_n, bias=eps_t[:, 0:1]),
                 reads=[Tps, Tconst], writes=[Tln])
            S.op("act", lambda e: e.activation(out=rstd[:], in_=lnv[:], func=AF.Exp, scale=-0.5),
                 reads=[Tln], writes=[Trs])

        eps_t = S.sb("eps_t", [128, 1])
        S.op("pool", lambda e: e.memset(eps_t[:], EPS), writes=[Tconst])

        def phase_ffn(src, dst, wg, wu, wd, gcol, fin_gcol=None, fin_dst=None):
            load_ffn_weights(wg, wu, wd, gcol)
            with ExitStack() as ps_:
                S.stack = ps_
                hb = [S.sb("hb%d" % i, [128, KD, TT]) for i in range(2)]
                Th = [Trk("hb%d" % i) for i in range(2)]
                xn = [S.sb("xn%d" % i, [128, KD, TT], BF16) for i in range(2)]
                Txn = [Trk("xn%d" % i) for i in range(2)]
                sq = S.sb("sq", [128, KD, TT], BF16); Tsq = Trk("sq")
                lnv = S.sb("lnv", [128, TT]); Tln = Trk("lnv")
                rstd = S.sb("rstd", [128, TT]); Trs = Trk("rstd")
                hmid = S.sb("hmid", [128, KF, TT], BF16)
                Thm = [Trk("hm%d" % f) for f in range(KF)]
                sgt = [S.sb("sgt%d" % i, [128, TT]) for i in range(2)]
                Tsg = [Trk("sgt%d" % i) for i in range(2)]
                ps_stat = S.ps("ps_stat", [128, 512]); Tps = Trk("ps_stat")
                psg = [S.ps("psg%d" % i, [128, 512]) for i in range(2)]
                psu = [S.ps("psu%d" % i, [128, 512]) for i in range(2)]
                psd = [S.ps("psd%d" % i, [128, 512]) for i in range(2)]
                Tpg = [Trk() for i in range(2)]; Tpu = [Trk() for i in range(2)]; Tpd = [Trk() for i in range(2)]
                if fin_dst is not None:
                    ob = S.sb("ob", [128, KD, TT]); Tob = Trk("ob")
                srcv = tok_view(src, KD); dstv = tok_view(dst, KD) if dst is not None else None
                finv = tok_view(fin_dst, KD) if fin_dst is not None else None
                Tdst = Trk("dst")

                def pre(t):
                    i = t % 2
                    c0 = t * TT
                    S.dma("sp", hb[i][:], srcv[:, :, c0:c0 + TT], writes=[Th[i]])
                    norm_stats(None, hb[i], Th[i], KD, ones_bf, 1.0 / D, sq, Tsq, ps_stat, Tps, lnv, Tln, rstd, Trs)
                    S.op("dve", lambda e, i=i: e.tensor_tensor(
                        out=xn[i][:], in0=hb[i][:], in1=rstd[:].unsqueeze(1).broadcast_to([128, KD, TT]), op=ALU.mult),
                        reads=[Th[i], Trs], writes=[Txn[i]])

                gcnt = [0]

                def gateup(t):
                    i = t % 2
                    for f in range(KF):
                        j = gcnt[0] % 2; gcnt[0] += 1
                        for k in range(KD):
                            S.op("pe", lambda e, k=k, f=f, j=j: e.matmul(
                                psg[j][:, 0:TT], lhsT=WA[:, k * DFF + f * 128: k * DFF + (f + 1) * 128],
                                rhs=xn[i][:, k, :], start=(k == 0), stop=(k == KD - 1)),
                                reads=[TW["A"][k], Txn[i]], writes=[Tpg[j]])
                        for k in range(KD):
                            S.op("pe", lambda e, k=k, f=f, j=j: e.matmul(
                                psu[j][:, 0:TT], lhsT=WB[:, k * DFF + f * 128: k * DFF + (f + 1) * 128],
                                rhs=xn[i][:, k, :], start=(k == 0), stop=(k == KD - 1)),
                                reads=[TW["B"][k], Txn[i]], writes=[Tpu[j]])
                        S.op("act", lambda e, j=j: e.activation(out=sgt[j][:], in_=psg[j][:, 0:TT], func=AF.Silu),
                             reads=[Tpg[j]], writes=[Tsg[j]])
                        S.op("dve", lambda e, j=j, f=f: e.tensor_tensor(
                            out=hmid[:, f, :], in0=sgt[j][:], in1=psu[j][:, 0:TT], op=ALU.mult),
                            reads=[Tsg[j], Tpu[j]], writes=[Thm[f]])

                dcnt = [0]

                def down(t):
                    i = t % 2
                    c0 = t * TT
                    for dc in range(KD):
                        j = dcnt[0] % 2; dcnt[0] += 1
                        for f in range(KF):
                            S.op("pe", lambda e, f=f, dc=dc, j=j: e.matmul(
                                psd[j][:, 0:TT], lhsT=WC[:, f * D + dc * 128: f * D + (dc + 1) * 128],
                                rhs=hmid[:, f, :], start=(f == 0), stop=(f == KF - 1)),
                                reads=[TW["C"][f], Thm[f]], writes=[Tpd[j]])
                        S.op("dve", lambda e, dc=dc, j=j, i=i: e.scalar_tensor_tensor(
                            out=hb[i][:, dc, :], in0=psd[j][:, 0:TT], scalar=0.5, in1=hb[i][:, dc, :],
                            op0=ALU.mult, op1=ALU.add),
                            reads=[Tpd[j], Th[i]], writes=[Th[i]])
                    if dstv is not None:
                        S.dma("sp", dstv[:, :, c0:c0 + TT], hb[i][:], reads=[Th[i]], writes=[Tdst])
                    if fin_dst is not None:
                        norm_stats(None, hb[i], Th[i], KD, ones_bf, 1.0 / D, sq, Tsq, ps_stat, Tps, lnv, Tln, rstd, Trs)
                        for k in range(KD):
                            S.op("dve", lambda e, k=k, i=i: e.scalar_tensor_tensor(
                                out=ob[:, k, :], in0=hb[i][:, k, :], scalar=gains[:, fin_gcol + k: fin_gcol + k + 1],
                                in1=rstd[:], op0=ALU.mult, op1=ALU.mult),
                                reads=[Th[i], Trs, Tg], writes=[Tob])
                        S.dma("sp", finv[:, :, c0:c0 + TT], ob[:], reads=[Tob], writes=[Tdst])

                pre(0)
                for t in range(NTILE):
                    gateup(t)
                    if t + 1 < NTILE:
                        pre(t + 1)
                    down(t)
                S.barrier()
                S.stack = st
            return Tdst

        def phase_proj(src, dstp, w, gcol):
            for k in range(KD):
                load_w(WA, TW["A"][k], k * INP, w[k * 128:(k + 1) * 128, :], INW, gains[:, gcol + k: gcol + k + 1])
            with ExitStack() as ps_:
                S.stack = ps_
                hb = [S.sb("hb%d" % i, [128, KD, TT]) for i in range(2)]
                Th = [Trk() for i in range(2)]
                xn = [S.sb("xn%d" % i, [128, KD, TT], BF16) for i in range(2)]
                Txn = [Trk() for i in range(2)]
                sq = S.sb("sq", [128, KD, TT], BF16); Tsq = Trk("sq")
                lnv = S.sb("lnv", [128, TT]); Tln = Trk("lnv")
                rstd = S.sb("rstd", [128, TT]); Trs = Trk("rstd")
                obt = [S.sb("obt%d" % i, [128, KP, TT]) for i in range(2)]
                Tobt = [Trk() for i in range(2)]
                for i_ in range(2):
                    S.op("pool", lambda e, i_=i_: e.memset(obt[i_][:, KP - 1, :], 0.0), writes=[Tobt[i_]])
                dstv3 = dstp.rearrange("(c p) n -> p c n", p=128)
                ps_stat = S.ps("ps_stat", [128, 512]); Tps = Trk("ps_stat")
                pp = [S.ps("pp%d" % i, [128, 512]) for i in range(4)]
                Tpp = [Trk() for i in range(4)]
                srcv = tok_view(src, KD)
                Tdst = Trk("projdst")
                cntb = [0]

                def ld(t):
                    i = t % 2
                    c0 = t * TT
                    S.dma("sp", hb[i][:], srcv[:, :, c0:c0 + TT], writes=[Th[i]])

                def pre(t):
                    i = t % 2
                    norm_stats(None, hb[i], Th[i], KD, ones_bf, 1.0 / D, sq, Tsq, ps_stat, Tps, lnv, Tln, rstd, Trs)
                    S.op("dve", lambda e, i=i: e.tensor_tensor(
                        out=xn[i][:], in0=hb[i][:], in1=rstd[:].unsqueeze(1).broadcast_to([128, KD, TT]), op=ALU.mult),
                        reads=[Th[i], Trs], writes=[Txn[i]])

                def body(t):
                    i = t % 2
                    c0 = t * TT
                    if t + 1 < NTILE:
                        ld(t + 1)
                    for pc in range(KP):
                        if pc == KP // 2 and t + 1 < NTILE:
                            pre(t + 1)
                        cnt = cntb[0]
                        j = cnt % 4; cnt += 1; cntb[0] = cnt
                        m = min(128, INW - pc * 128)
                        for k in range(KD):
                            S.op("pe", lambda e, k=k, pc=pc, j=j, m=m: e.matmul(
                                pp[j][0:m, 0:TT], lhsT=WA[:, k * INP + pc * 128: k * INP + pc * 128 + m],
                                rhs=xn[i][:, k, :], start=(k == 0), stop=(k == KD - 1)),
                                reads=[TW["A"][k], Txn[i]], writes=[Tpp[j]])
                        eng = "act" if (cnt % 2 == 0) else "dve"
                        if eng == "act":
                            S.op("act", lambda e, j=j, m=m, pc=pc: e.copy(out=obt[i][0:m, pc, :], in_=pp[j][0:m, 0:TT]),
                                 reads=[Tpp[j]], writes=[Tobt[i]])
                        else:
                            S.op("dve", lambda e, j=j, m=m, pc=pc: e.tensor_copy(out=obt[i][0:m, pc, :], in_=pp[j][0:m, 0:TT]),
                                 reads=[Tpp[j]], writes=[Tobt[i]])
                    S.dma("sp", dstv3[:, :, c0:c0 + TT], obt[i][:], reads=[Tobt[i]], writes=[Tdst])
                ld(0)
                pre(0)
                for t in range(NTILE):
                    body(t)
                S.barrier()
                S.stack = st
            return Tdst

        def phase_epi(src, dst):
            for k in range(KD):
                load_w(WA, TW["A"][k], k * D, w_out[k * 128:(k + 1) * 128, :], D, None)
            for k in range(4):
                load_w(WB, TW["B"][k], k * 512, wglu[k * 128:(k + 1) * 128, :], 512, None)
            with ExitStack() as ps_:
                S.stack = ps_
                hb = [S.sb("hb%d" % i, [128, KD, TT]) for i in range(2)]
                Th = [Trk() for i in range(2)]
                xin = [S.sb("xin%d" % i, [128, 10, TT]) for i in range(2)]
                Tx = [Trk() for i in range(2)]
                mixed = S.sb("mixed", [128, KD, TT], BF16); Tmx = Trk("mixed")
                sq = S.sb("sq", [128, 4, TT], BF16); Tsq = Trk("sq")
                lnv = S.sb("lnv", [128, TT]); Tln = Trk("lnv")
                rstd = S.sb("rstd", [128, TT]); Trs = Trk("rstd")
                t1 = S.sb("t1", [128, 4, TT]); Tt1 = Trk("t1")
                t2 = S.sb("t2", [128, 4, TT]); Tt2 = Trk("t2")
                ge = S.sb("ge", [128, 4, TT]); Tge = Trk("ge")
                geb = S.sb("geb", [128, 4, TT], BF16); Tgeb = Trk("geb")
                vv = S.sb("vv", [128, 4, TT]); Tvv = Trk("vv")
                sg = S.sb("sg", [128, TT]); Tsgm = Trk("sg")
                ps_stat = S.ps("ps_stat", [128, 512]); Tps = Trk("ps_stat")
                pq = [S.ps("pq%d" % i, [128, 512]) for i in range(2)]
                Tpq = [Trk() for i in range(2)]
                srcv = tok_view(src, KD); dstv = tok_view(dst, KD)
                osbv = tok_view(osb, 2); odnv = tok_view(odn, 2); dnzv = tok_view(dnz, 2); ys5v = tok_view(ys5, 4)
                Tdst = Trk("epidst")
                cntb = [0]

                def ld(t):
                    i = t % 2
                    c0 = t * TT
                    S.dma("sp", hb[i][:], srcv[:, :, c0:c0 + TT], writes=[Th[i]])
                    S.dma("sp", xin[i][:, 0:2, :], osbv[:, :, c0:c0 + TT], writes=[Tx[i]])
                    S.dma("sp", xin[i][:, 2:4, :], odnv[:, :, c0:c0 + TT], writes=[Tx[i]])
                    S.dma("sp", xin[i][:, 4:6, :], dnzv[:, :, c0:c0 + TT], writes=[Tx[i]])
                    S.dma("sp", xin[i][:, 6:10, :], ys5v[:, :, c0:c0 + TT], writes=[Tx[i]])

                def body(t):
                    i = t % 2
                    c0 = t * TT
                    if t == 0:
                        ld(0)
                    if t + 1 < NTILE:
                        ld(t + 1)
                    X = xin[i]
                    for c in range(2):
                        S.op("act", lambda e, c=c: e.activation(out=sq[:, 0, :], in_=X[:, c, :], func=AF.Square),
                             reads=[Tx[i]], writes=[Tsq])
                        S.op("pe", lambda e: e.matmul(ps_stat[:, 0:TT], lhsT=blk_bf[:], rhs=sq[:, 0, :], start=True, stop=True),
                             reads=[Tsq, Tconst], writes=[Tps])
                        S.op("act", lambda e: e.activation(out=lnv[:], in_=ps_stat[:, 0:TT], func=AF.Ln, scale=1.0 / 64, bias=eps_t[:, 0:1]),
                             reads=[Tps, Tconst], writes=[Tln])
                        S.op("act", lambda e: e.activation(out=rstd[:], in_=lnv[:], func=AF.Exp, scale=-0.5),
                             reads=[Tln], writes=[Trs])
                        S.op("dve", lambda e, c=c: e.scalar_tensor_tensor(
                            out=mixed[:, c, :], in0=X[:, c, :], scalar=gains[:, 32:33], in1=rstd[:], op0=ALU.mult, op1=ALU.mult),
                            reads=[Tx[i], Trs, Tg], writes=[Tmx])
                    for c in range(2):
                        S.op("act", lambda e, c=c: e.activation(out=sq[:, 0, :], in_=X[:, 2 + c, :], func=AF.Square),
                             reads=[Tx[i]], writes=[Tsq])
                        S.op("pe", lambda e: e.matmul(ps_stat[:, 0:TT], lhsT=blk_bf[:], rhs=sq[:, 0, :], start=True, stop=True),
                             reads=[Tsq, Tconst], writes=[Tps])
                        S.op("act", lambda e: e.activation(out=lnv[:], in_=ps_stat[:, 0:TT], func=AF.Ln, scale=1.0 / 64, bias=eps_t[:, 0:1]),
                             reads=[Tps, Tconst], writes=[Tln])
                        S.op("act", lambda e: e.activation(out=rstd[:], in_=lnv[:], func=AF.Exp, scale=-0.5),
                             reads=[Tln], writes=[Trs])
                        S.op("dve", lambda e, c=c: e.scalar_tensor_tensor(
                            out=t1[:, 0, :], in0=X[:, 2 + c, :], scalar=gains[:, 33:34], in1=rstd[:], op0=ALU.mult, op1=ALU.mult),
                            reads=[Tx[i], Trs, Tg], writes=[Tt1])
                        S.op("act", lambda e, c=c: e.activation(out=t2[:, 0, :], in_=X[:, 4 + c, :], func=AF.Silu),
                             reads=[Tx[i]], writes=[Tt2])
                        S.op("dve", lambda e, c=c: e.tensor_tensor(out=mixed[:, 2 + c, :], in0=t1[:, 0, :], in1=t2[:, 0, :], op=ALU.mult),
                             reads=[Tt1, Tt2], writes=[Tmx])
                    Y = X[:, 6:10, :]
                    S.op("act", lambda e, Y=Y: e.activation(out=t1[:], in_=Y, func=AF.Square), reads=[Tx[i]], writes=[Tt1])
                    S.op("dve", lambda e: e.tensor_scalar(out=t1[:], in0=t1[:], scalar1=0.044715, scalar2=1.0, op0=ALU.mult, op1=ALU.add),
                         reads=[Tt1], writes=[Tt1])
                    S.op("dve", lambda e, Y=Y: e.tensor_tensor(out=t2[:], in0=t1[:], in1=Y, op=ALU.mult),
                         reads=[Tt1, Tx[i]], writes=[Tt2])
                    S.op("act", lambda e: e.activation(out=t1[:], in_=t2[:], func=AF.Tanh, scale=0.7978845608028654),
                         reads=[Tt2], writes=[Tt1])
                    S.op("dve", lambda e, Y=Y: e.scalar_tensor_tensor(out=t2[:], in0=t1[:], scalar=1.0, in1=Y, op0=ALU.add, op1=ALU.mult),
                         reads=[Tt1, Tx[i]], writes=[Tt2])
                    S.op("dve", lambda e: e.tensor_scalar(out=ge[:], in0=t2[:], scalar1=0.5, scalar2=None, op0=ALU.mult),
                         reads=[Tt2], writes=[Tge])
                    S.op("act", lambda e: e.copy(out=geb[:], in_=ge[:]), reads=[Tge], writes=[Tgeb])
                    for co in range(4):
                        j = cntb[0] % 2; cntb[0] += 1
                        for ki in range(4):
                            S.op("pe", lambda e, ki=ki, co=co, j=j: e.matmul(
                                pq[j][:, 0:TT], lhsT=WB[:, ki * 512 + co * 128: ki * 512 + (co + 1) * 128],
                                rhs=geb[:, ki, :], start=(ki == 0), stop=(ki == 3)),
                                reads=[TW["B"][ki], Tgeb], writes=[Tpq[j]])
                        S.op("act", lambda e, co=co, j=j: e.activation(out=sg[:], in_=pq[j][:, 0:TT], func=AF.Sigmoid,
                                                                        bias=gains[:, 34 + co: 35 + co]),
                             reads=[Tpq[j], Tg], writes=[Tsgm])
                        S.op("dve", lambda e, co=co: e.tensor_tensor(out=vv[:, co, :], in0=ge[:, co, :], in1=sg[:], op=ALU.mult),
                             reads=[Tge, Tsgm], writes=[Tvv])
                    norm_stats(None, vv, Tvv, 4, ones_bf, 1.0 / 512, sq, Tsq, ps_stat, Tps, lnv, Tln, rstd, Trs)
                    for c in range(4):
                        S.op("dve", lambda e, c=c: e.scalar_tensor_tensor(
                            out=mixed[:, 4 + c, :], in0=vv[:, c, :], scalar=gains[:, 38 + c: 39 + c], in1=rstd[:],
                            op0=ALU.mult, op1=ALU.mult),
                            reads=[Tvv, Trs, Tg], writes=[Tmx])
                    for dc in range(KD):
                        j = cntb[0] % 2; cntb[0] += 1
                        for k in range(KD):
                            S.op("pe", lambda e, k=k, dc=dc, j=j: e.matmul(
                                pq[j][:, 0:TT], lhsT=WA[:, k * D + dc * 128: k * D + (dc + 1) * 128],
                                rhs=mixed[:, k, :], start=(k == 0), stop=(k == KD - 1)),
                                reads=[TW["A"][k], Tmx], writes=[Tpq[j]])
                        S.op("dve", lambda e, dc=dc, j=j, i=i: e.tensor_tensor(
                            out=hb[i][:, dc, :], in0=hb[i][:, dc, :], in1=pq[j][:, 0:TT], op=ALU.add),
                            reads=[Tpq[j], Th[i]], writes=[Th[i]])
                    S.dma("sp", dstv[:, :, c0:c0 + TT], hb[i][:], reads=[Th[i]], writes=[Tdst])
                for t in range(NTILE):
                    body(t)
                S.barrier()
                S.stack = st
            return Tdst

        cur = h_in
        if do_epi:
            phase_epi(cur, hs_a)
            cur = hs_a
            if do_fin:
                Td = phase_ffn(cur, None, wg2, wu2, wd2, 16, fin_gcol=24, fin_dst=y_out)
                out_trks.append(Td)
            else:
                phase_ffn(cur, hs_b, wg2, wu2, wd2, 16)
                cur = hs_b
        if do_a:
            Td = phase_ffn(cur, h_out, wg1, wu1, wd1, 0)
            out_trks.append(Td)
            Tp = phase_proj(h_out, proj, w_in, 8)
            out_trks.append(Tp)
        S.finish(out_trks)
    return nc

import math
from contextlib import ExitStack

EPS = 1e-6


def sb_phase(S, nc, qT_d, kT_d, v_d, oT_d, NB, PADK):
    import os
    SB_DUMMY = int(os.environ.get("SB_DUMMY", "4"))
    st0 = S.stack
    with ExitStack() as ps_:
        S.stack = ps_
        LP = NB * 128
        Tc = Trk("sbconst")
        qb = S.sb("qb", [64, LP], BF16); Tq = Trk("qb")
        kb = S.sb("kb", [64, LP], BF16); Tk = Trk("kb")
        vb = S.sb("vb", [128, NB * 64], BF16); Tv = Trk("vb")
        stg = [S.sb("sbstg%d" % i, [128, 2048]) for i in range(2)]
        Tstg = [Trk() for i in range(2)]
        si = 0
        for c0 in range(0, LP, 2048):
            n = min(2048, LP - c0)
            for (src, dst, Td, sc) in ((qT_d, qb, Tq, 1.0), (kT_d, kb, Tk, 0.125)):
                i = si % 2; si += 1
                S.dma("sp", stg[i][0:64, 0:n], src[:, c0:c0 + n], writes=[Tstg[i]])
                S.op("pool", lambda e, i=i, n=n, dst=dst, c0=c0, sc=sc: e.tensor_scalar(
                    out=dst[:, c0:c0 + n], in0=stg[i][0:64, 0:n], scalar1=sc, scalar2=1.0, op0=ALU.mult, op1=ALU.mult),
                    reads=[Tstg[i]], writes=[Td])
        vflat = v_d.rearrange("p b d -> p (b d)")
        for c0 in range(0, NB * 64, 2048):
            n = min(2048, NB * 64 - c0)
            i = si % 2; si += 1
            S.dma("sp", stg[i][:, 0:n], vflat[:, c0:c0 + n], writes=[Tstg[i]])
            S.op("pool", lambda e, i=i, n=n, c0=c0: e.tensor_copy(out=vb[:, c0:c0 + n], in_=stg[i][:, 0:n]),
                 reads=[Tstg[i]], writes=[Tv])
        negtri = S.sb("negtri", [128, 128], BF16)
        negones = S.sb("negones", [1, 128], BF16)
        onescol = S.sb("onescol", [128, 1], BF16)
        iot = S.sb("iot", [128, 512])
        masks = [S.sb("mask%d" % m, [128, 512], BF16) for m in range(4)]
        padmask = S.sb("padmask", [128, 512], BF16)
        mask00 = S.sb("mask00", [128, 512], BF16)
        S.op("pool", lambda e: e.iota(iot[:, 0:128], pattern=[[1, 128]], base=0, channel_multiplier=-1,
                                      allow_small_or_imprecise_dtypes=True), writes=[Tc])
        S.op("dve", lambda e: e.tensor_scalar(out=negtri[:], in0=iot[:, 0:128], scalar1=0.0, scalar2=-1.0,
                                              op0=ALU.is_le, op1=ALU.mult), reads=[Tc], writes=[Tc])
        S.op("pool", lambda e: e.memset(negones[:], -1.0), writes=[Tc])
        S.op("pool", lambda e: e.memset(onescol[:], 1.0), writes=[Tc])
        for m in range(4):
            S.op("pool", lambda e, m=m: e.iota(iot[:], pattern=[[1, 512]], base=-128 * m, channel_multiplier=-1,
                                                allow_small_or_imprecise_dtypes=True), reads=[Tc], writes=[Tc])
            S.op("dve", lambda e, m=m: e.tensor_single_scalar(out=masks[m][:], in_=iot[:], scalar=0.0, op=ALU.is_gt),
                 reads=[Tc], writes=[Tc])
        S.op("pool", lambda e: e.memset(padmask[:], 1.0), reads=[Tc], writes=[Tc])
        S.op("pool", lambda e: e.tensor_copy(out=mask00[:], in_=masks[0][:]), reads=[Tc], writes=[Tc])
        if PADK > 0:
            pk = PADK
            S.op("pool", lambda e: e.memset(padmask[0:pk, :], 0.0), reads=[Tc], writes=[Tc])
            S.op("pool", lambda e: e.memset(mask00[0:pk, :], 0.0), reads=[Tc], writes=[Tc])

        eS = [S.sb("eS%d" % i, [128, 512]) for i in range(2)]; TeS = [Trk() for i in range(2)]
        spb = [S.sb("spb%d" % i, [128, 512], BF16) for i in range(2)]; Tsp = [Trk() for i in range(2)]
        wb = [S.sb("wb%d" % i, [128, 512], BF16) for i in range(2)]; Twb = [Trk() for i in range(2)]
        rhi = [S.sb("rhi%d" % i, [1, 512], BF16) for i in range(2)]
        rlo = [S.sb("rlo%d" % i, [1, 512], BF16) for i in range(2)]; Trh = [Trk() for i in range(2)]
        Rf = S.sb("Rf", [1, 512]); TR = Trk("R")
        obuf = [S.sb("obuf%d" % i, [64, 512]) for i in range(2)]; Tob = [Trk() for i in range(2)]
        psA = [S.ps("psA%d" % i, [128, 512]) for i in range(2)]; TpA = [Trk() for i in range(2)]
        psB = [S.ps("psB%d" % i, [128, 512]) for i in range(2)]; TpB = [Trk() for i in range(2)]
        psC = S.ps("psC", [1, 512]); TpC = Trk()
        psO = S.ps("psO", [64, 512]); TpO = Trk()
        Tout = Trk("sbout")

        tiles = []
        b = 0
        while b < NB:
            nb = min(4, NB - b)
            tiles.append((b, nb))
            b += nb
        cnt = [0]
        pend = []

        def mask_for(qb0, j):
            m = j - qb0
            if m >= 0:
                if j == 0:
                    return mask00
                return masks[m]
            if j == 0 and PADK > 0:
                return padmask
            return None

        def stage1(qb0, nq, j, slot):
            N = nq * 128
            q0 = qb0 * 128
            S.op("pe", lambda e: e.matmul(psB[slot][:, 0:N], lhsT=kb[:, j * 128:(j + 1) * 128], rhs=qb[:, q0:q0 + N],
                                          start=True, stop=False, skip_group_check=True), reads=[Tk, Tq], writes=[TpB[slot]])
            S.op("act", lambda e: e.activation(out=eS[slot][:, 0:N], in_=psB[slot][:, 0:N], func=AF.Exp),
                 reads=[TpB[slot]], writes=[TeS[slot]])
            def part_b():
                S.op("act", lambda e: e.activation(out=spb[slot][:, 0:N], in_=eS[slot][:, 0:N], func=AF.Ln, bias=one_t[:, 0:1]),
                     reads=[TeS[slot], Tc], writes=[Tsp[slot]])
                mk = mask_for(qb0, j)
                if mk is not None:
                    S.op("dve", lambda e: e.tensor_tensor(out=spb[slot][:, 0:N], in0=spb[slot][:, 0:N], in1=mk[:, 0:N], op=ALU.mult),
                         reads=[Tsp[slot], Tc], writes=[Tsp[slot]])
            return part_b

        def stage2(qb0, nq, j, slot, rslot, first, last):
            N = nq * 128
            q0 = qb0 * 128
            S.op("pe", lambda e: e.matmul(psB[slot][:, 0:N], lhsT=negtri[:], rhs=spb[slot][:, 0:N],
                                          start=False, stop=False, skip_group_check=True), reads=[Tsp[slot], Tc], writes=[TpB[slot]])
            S.op("pe", lambda e: e.matmul(psB[slot][:, 0:N], lhsT=negones[:], rhs=rhi[rslot][:, 0:N],
                                          start=False, stop=True, skip_group_check=True), reads=[Trh[rslot], Tc], writes=[TpB[slot]])
            if not last:
                S.op("pe", lambda e: e.matmul(psC[:, 0:N], lhsT=onescol[:], rhs=spb[slot][:, 0:N], start=True, stop=True),
                     reads=[Tsp[slot], Tc], writes=[TpC])
            for _d in range(SB_DUMMY):
                S.op("pe", lambda e, _d=_d: e.matmul(psA[_d % 2][:, 0:N], lhsT=negtri[:], rhs=spb[slot][:, 0:N], start=True, stop=True),
                     reads=[Tsp[slot], Tc], writes=[])
            S.op("act", lambda e: e.activation(out=wb[slot][:, 0:N], in_=psB[slot][:, 0:N], func=AF.Exp),
                 reads=[TpB[slot]], writes=[Twb[slot]])
            mk = mask_for(qb0, j)
            if mk is not None:
                S.op("dve", lambda e: e.tensor_tensor(out=wb[slot][:, 0:N], in0=wb[slot][:, 0:N], in1=mk[:, 0:N], op=ALU.mult),
                     reads=[Twb[slot], Tc], writes=[Twb[slot]])
            def emit_o():
                S.op("pe", lambda e: e.matmul(psO[:, 0:N], lhsT=vb[:, j * 64:(j + 1) * 64], rhs=wb[slot][:, 0:N],
                                              start=first, stop=last), reads=[Tv, Twb[slot]], writes=[TpO])
            pend.append(emit_o)
            if not last:
                nr = 1 - rslot
                S.op("dve", lambda e: e.tensor_tensor(out=Rf[:, 0:N], in0=Rf[:, 0:N], in1=psC[:, 0:N], op=ALU.add),
                     reads=[TR, TpC], writes=[TR])
                S.op("dve", lambda e: e.tensor_copy(out=rhi[nr][:, 0:N], in_=Rf[:, 0:N]),
                     reads=[TR], writes=[Trh[nr]])


        one_t = S.sb("one_t", [128, 1])
        S.op("pool", lambda e: e.memset(one_t[:], 1.0), writes=[Tc])

        for ti, (qb0, nq) in enumerate(tiles):
            N = nq * 128
            js = list(range(qb0 + nq - 1, -1, -1))
            S.op("dve", lambda e: e.memset(Rf[:], 0.0), reads=[TR], writes=[TR])
            S.op("dve", lambda e: e.memset(rhi[0][:], 0.0), reads=[Trh[0]], writes=[Trh[0]])
            S.op("dve", lambda e: e.memset(rlo[0][:], 0.0), reads=[Trh[0]], writes=[Trh[0]])
            rslot = 0
            slot0 = cnt[0] % 2
            stage1(qb0, nq, js[0], slot0)()
            for idx, j in enumerate(js):
                slot = cnt[0] % 2; cnt[0] += 1
                pb = None
                if idx + 1 < len(js):
                    pb = stage1(qb0, nq, js[idx + 1], 1 - slot)
                stage2(qb0, nq, j, slot, rslot, idx == 0, idx == len(js) - 1)
                if pb is not None:
                    pb()
                rslot = 1 - rslot
                while len(pend) > 1:
                    pend.pop(0)()
            while pend:
                pend.pop(0)()
            oi = ti % 2
            S.op("dve", lambda e, oi=oi, N=N: e.tensor_copy(out=obuf[oi][:, 0:N], in_=psO[:, 0:N]),
                 reads=[TpO], writes=[Tob[oi]])
            S.dma("sp", oT_d[:, qb0 * 128: qb0 * 128 + N], obuf[oi][:, 0:N], reads=[Tob[oi]], writes=[Tout])
        S.barrier()
        S.stack = st0
    return Tout


def s5_phase(S, nc, u_d, are_d, aim_d, ldt_d, bre_d, bim_d, cre_d, cim_d, dsk_d, ys_d, L, NBATCH=2):
    st0 = S.stack
    NCH = L // 16
    assert NCH * 16 == L
    TWO_PI = 2.0 * math.pi
    MAGIC = 12582912.0
    with ExitStack() as ps_:
        S.stack = ps_
        Tc = Trk("s5c")
        prm = S.sb("prm", [128, 8]); Tp = Trk("prm")
        bre = S.sb("bre", [128, 2, 16]); bim = S.sb("bim", [128, 2, 16])
        cre = S.sb("cre", [128, 2, 16]); cim = S.sb("cim", [128, 2, 16])
        dsk = S.sb("dsk", [64, 1])
        S.dma("sp", prm[:, 0:2], are_d[:, :], writes=[Tp])
        S.dma("sp", prm[:, 2:4], aim_d[:, :], writes=[Tp])
        S.dma("sp", prm[:, 4:6], ldt_d[:, :], writes=[Tp])
        S.dma("sp", bre[:], bre_d[:, :, :], writes=[Tp])
        S.dma("sp", bim[:], bim_d[:, :, :], writes=[Tp])
        S.dma("sp", cre[:], cre_d[:, :, :], writes=[Tp])
        S.dma("sp", cim[:], cim_d[:, :, :], writes=[Tp])
        S.dma("sp", dsk[:], dsk_d[:, :], writes=[Tp])
        sc = S.sb("s5sc", [128, 64]); Ts = Trk("s5sc")
        dve = lambda fn, r=(), w=(): S.op("dve", fn, reads=list(r) + [Tp, Ts, Tc], writes=list(w) if w else [Ts])
        S.op("act", lambda e: e.activation(out=sc[:, 0:2], in_=prm[:, 4:6], func=AF.Exp), reads=[Tp], writes=[Ts])
        dve(lambda e: e.tensor_tensor(out=sc[:, 2:4], in0=prm[:, 0:2], in1=sc[:, 0:2], op=ALU.mult))
        dve(lambda e: e.tensor_tensor(out=sc[:, 4:6], in0=prm[:, 2:4], in1=sc[:, 0:2], op=ALU.mult))
        NP = 17
        mag = S.sb("mag", [128, 2, NP]); ang = S.sb("ang", [128, 2, 2 * NP]); ang2 = S.sb("ang2", [128, 2, 2 * NP])
        trg = S.sb("trg", [128, 2, 2 * NP])
        pwr = S.sb("pwr", [128, 2, NP]); pwi = S.sb("pwi", [128, 2, NP]); pwn = S.sb("pwn", [128, 2, NP])
        for m in range(NP):
            S.op("act", lambda e, m=m: e.activation(out=mag[:, :, m], in_=sc[:, 2:4], func=AF.Exp, scale=float(m)),
                 reads=[Ts], writes=[Ts])
            dve(lambda e, m=m: e.tensor_scalar(out=ang[:, :, m], in0=sc[:, 4:6], scalar1=float(m), scalar2=0.0,
                                               op0=ALU.mult, op1=ALU.add))
            dve(lambda e, m=m: e.tensor_scalar(out=ang[:, :, NP + m], in0=sc[:, 4:6], scalar1=float(m), scalar2=math.pi / 2,
                                               op0=ALU.mult, op1=ALU.add))
        dve(lambda e: e.tensor_scalar(out=ang2[:], in0=ang[:], scalar1=1.0 / TWO_PI, scalar2=MAGIC, op0=ALU.mult, op1=ALU.add))
        dve(lambda e: e.tensor_scalar(out=ang2[:], in0=ang2[:], scalar1=-MAGIC, scalar2=None, op0=ALU.add))
        dve(lambda e: e.scalar_tensor_tensor(out=ang2[:], in0=ang2[:], scalar=-TWO_PI, in1=ang[:], op0=ALU.mult, op1=ALU.add))
        dve(lambda e: e.tensor_scalar(out=ang2[:], in0=ang2[:], scalar1=3.141592, scalar2=-3.141592, op0=ALU.min, op1=ALU.max))
        S.op("act", lambda e: e.activation(out=trg[:], in_=ang2[:], func=AF.Sin), reads=[Ts], writes=[Ts])
        dve(lambda e: e.tensor_tensor(out=pwi[:], in0=mag[:], in1=trg[:, :, 0:NP], op=ALU.mult))
        dve(lambda e: e.tensor_tensor(out=pwr[:], in0=mag[:], in1=trg[:, :, NP:2 * NP], op=ALU.mult))
        dve(lambda e: e.tensor_scalar(out=pwn[:], in0=pwi[:], scalar1=-1.0, scalar2=None, op0=ALU.mult))
        X = sc[:, 6:8]; DEN = sc[:, 8:10]; RDEN = sc[:, 10:12]; CFR = sc[:, 12:14]; CFI = sc[:, 14:16]; T1 = sc[:, 16:18]; T2 = sc[:, 18:20]
        ARE = prm[:, 0:2]; AIM = prm[:, 2:4]
        dve(lambda e: e.tensor_scalar(out=X, in0=pwr[:, :, 1], scalar1=-1.0, scalar2=None, op0=ALU.add))
        dve(lambda e: e.tensor_tensor(out=DEN, in0=ARE, in1=ARE, op=ALU.mult))
        dve(lambda e: e.tensor_tensor(out=T1, in0=AIM, in1=AIM, op=ALU.mult))
        dve(lambda e: e.tensor_tensor(out=DEN, in0=DEN, in1=T1, op=ALU.add))
        dve(lambda e: e.reciprocal(out=RDEN, in_=DEN))
        dve(lambda e: e.tensor_tensor(out=T1, in0=X, in1=ARE, op=ALU.mult))
        dve(lambda e: e.tensor_tensor(out=T2, in0=pwi[:, :, 1], in1=AIM, op=ALU.mult))
        dve(lambda e: e.tensor_tensor(out=T1, in0=T1, in1=T2, op=ALU.add))
        dve(lambda e: e.tensor_tensor(out=CFR, in0=T1, in1=RDEN, op=ALU.mult))
        dve(lambda e: e.tensor_tensor(out=T1, in0=pwi[:, :, 1], in1=ARE, op=ALU.mult))
        dve(lambda e: e.tensor_tensor(out=T2, in0=X, in1=AIM, op=ALU.mult))
        dve(lambda e: e.tensor_tensor(out=T1, in0=T1, in1=T2, op=ALU.subtract))
        dve(lambda e: e.tensor_tensor(out=CFI, in0=T1, in1=RDEN, op=ALU.mult))
        iot = S.sb("s5iot", [128, 128]); ident = S.sb("s5ident", [128, 128])
        S.op("pool", lambda e: e.iota(iot[:], pattern=[[1, 128]], base=0, channel_multiplier=-1,
                                      allow_small_or_imprecise_dtypes=True), writes=[Tc])
        S.op("dve", lambda e: e.tensor_single_scalar(out=ident[:], in_=iot[:], scalar=0.0, op=ALU.is_equal), reads=[Tc], writes=[Tc])
        bblk = [S.sb("bblk%d" % i, [128, 64]) for i in range(2)]
        tmpb = S.sb("tmpb", [128, 16])
        BT = [[S.sb("BT%d%d" % (pr, pt), [64, 128], BF16) for pt in range(2)] for pr in range(2)]
        CB = [[S.sb("CB%d%d" % (pr, pt), [128, 64], BF16) for pt in range(2)] for pr in range(2)]
        psT = S.ps("psT", [128, 512]); TpT = Trk()
        Tbb = Trk("bblk")
        for pr in range(2):
            for i in range(2):
                S.op("pool", lambda e, i=i: e.memset(bblk[i][:], 0.0), reads=[Tbb], writes=[Tbb])
            for g2 in range(2):
                r0, r1 = 64 * g2, 64 * g2 + 64
                c0 = 32 * pr + 16 * g2
                cfr = sc[r0:r1, 12 + pr:13 + pr]; cfi = sc[r0:r1, 14 + pr:15 + pr]
                S.op("dve", lambda e, r0=r0, r1=r1, pr=pr, cfi=cfi: e.tensor_scalar(
                    out=tmpb[r0:r1, :], in0=bim[r0:r1, pr, :], scalar1=cfi, scalar2=None, op0=ALU.mult),
                    reads=[Tp, Ts], writes=[Tbb])
                S.op("dve", lambda e, r0=r0, r1=r1, pr=pr, cfr=cfr, c0=c0: e.scalar_tensor_tensor(
                    out=bblk[0][r0:r1, c0:c0 + 16], in0=bre[r0:r1, pr, :], scalar=cfr, in1=tmpb[r0:r1, :],
                    op0=ALU.mult, op1=ALU.subtract), reads=[Tp, Ts, Tbb], writes=[Tbb])
                S.op("dve", lambda e, r0=r0, r1=r1, pr=pr, cfi=cfi: e.tensor_scalar(
                    out=tmpb[r0:r1, :], in0=bre[r0:r1, pr, :], scalar1=cfi, scalar2=None, op0=ALU.mult),
                    reads=[Tp, Ts, Tbb], writes=[Tbb])
                S.op("dve", lambda e, r0=r0, r1=r1, pr=pr, cfr=cfr, c0=c0: e.scalar_tensor_tensor(
                    out=bblk[1][r0:r1, c0:c0 + 16], in0=bim[r0:r1, pr, :], scalar=cfr, in1=tmpb[r0:r1, :],
                    op0=ALU.mult, op1=ALU.add), reads=[Tp, Ts, Tbb], writes=[Tbb])
            for pt in range(2):
                S.op("pe", lambda e, pt=pt: e.transpose(out=psT[0:64, 0:128], in_=bblk[pt][:], identity=ident[:]),
                     reads=[Tbb, Tc], writes=[TpT])
                S.op("act", lambda e, pr=pr, pt=pt: e.copy(out=BT[pr][pt][:], in_=psT[0:64, 0:128]),
                     reads=[TpT], writes=[Tc])
            for pt in range(2):
                S.op("pool", lambda e, pr=pr, pt=pt: e.memset(CB[pr][pt][:], 0.0), writes=[Tc])
            for g2 in range(2):
                r0, r1 = 64 * g2, 64 * g2 + 64
                c0 = 32 * pr + 16 * g2
                S.op("dve", lambda e, r0=r0, r1=r1, pr=pr, c0=c0: e.tensor_copy(out=CB[pr][0][r0:r1, c0:c0 + 16], in_=cre[r0:r1, pr, :]),
                     reads=[Tp, Tc], writes=[Tc])
                S.op("dve", lambda e, r0=r0, r1=r1, pr=pr, c0=c0: e.tensor_scalar(
                    out=CB[pr][1][r0:r1, c0:c0 + 16], in0=cim[r0:r1, pr, :], scalar1=-1.0, scalar2=None, op0=ALU.mult),
                    reads=[Tp, Tc], writes=[Tc])
        DR = [[S.sb("DR%d_%d" % (pr, m), [128, 128], BF16) for m in range(NP)] for pr in range(2)]
        DI = [[S.sb("DI%d_%d" % (pr, m), [128, 128], BF16) for m in range(NP)] for pr in range(2)]
        DN = [[S.sb("DN%d_%d" % (pr, m), [128, 128], BF16) for m in range(NP)] for pr in range(2)]
        k = 0
        for pr in range(2):
            for m in range(NP):
                for (dst, src) in ((DR, pwr), (DI, pwi), (DN, pwn)):
                    eng = "dve" if k % 2 == 0 else "pool"; k += 1
                    S.op(eng, lambda e, dst=dst, src=src, pr=pr, m=m: e.tensor_scalar(
                        out=dst[pr][m][:], in0=ident[:], scalar1=src[:, pr, m:m + 1], scalar2=1.0, op0=ALU.mult, op1=ALU.mult),
                        reads=[Ts, Tc], writes=[Tc])
        NLV = max(1, int(math.ceil(math.log2(NCH))))
        lvr = S.sb("lvr", [128, 2, NLV]); lvi = S.sb("lvi", [128, 2, NLV]); lvn = S.sb("lvn", [128, 2, NLV])
        dve(lambda e: e.tensor_copy(out=lvr[:, :, 0], in_=pwr[:, :, 16]))
        dve(lambda e: e.tensor_copy(out=lvi[:, :, 0], in_=pwi[:, :, 16]))
        for kk in range(1, NLV):
            dve(lambda e, kk=kk: e.tensor_tensor(out=T1, in0=lvr[:, :, kk - 1], in1=lvr[:, :, kk - 1], op=ALU.mult))
            dve(lambda e, kk=kk: e.tensor_tensor(out=T2, in0=lvi[:, :, kk - 1], in1=lvi[:, :, kk - 1], op=ALU.mult))
            dve(lambda e, kk=kk: e.tensor_tensor(out=lvr[:, :, kk], in0=T1, in1=T2, op=ALU.subtract))
            dve(lambda e, kk=kk: e.tensor_tensor(out=T1, in0=lvr[:, :, kk - 1], in1=lvi[:, :, kk - 1], op=ALU.mult))
            dve(lambda e, kk=kk: e.tensor_scalar(out=lvi[:, :, kk], in0=T1, scalar1=2.0, scalar2=None, op0=ALU.mult))
        dve(lambda e: e.tensor_scalar(out=lvn[:], in0=lvi[:], scalar1=-1.0, scalar2=None, op0=ALU.mult))

        ub = S.sb("ub", [64, L], BF16); Tub = Trk("ub")
        ustg = [S.sb("ustg%d" % i, [64, 1024]) for i in range(2)]; Tus = [Trk() for i in range(2)]
        BU = [S.sb("BU%d" % i, [128, L], BF16) for i in range(2)]; TBU = [Trk() for i in range(2)]
        CW = (NCH + 2) // 3
        coltiles = [(c, min(CW, NCH - c)) for c in range(0, NCH, CW)]
        stt = [S.sb("stt%d" % i, [128, CW * 16], BF16) for i in range(2)]; Tst = [Trk() for i in range(2)]
        Zr = [S.sb("Zr%d" % i, [128, NCH]) for i in range(2)]; Zi = [S.sb("Zi%d" % i, [128, NCH]) for i in range(2)]
        TZ = [Trk() for i in range(2)]
        ztr = S.sb("ztr", [128, NCH]); zti = S.sb("zti", [128, NCH]); Tzt = Trk()
        Spb = [S.sb("Spb%d" % i, [128, NCH], BF16) for i in range(2)]; TSp = Trk()
        usk = [S.sb("usk%d" % i, [64, 512]) for i in range(2)]; Tusk = [Trk() for i in range(2)]
        ysb = [S.sb("ysb%d" % i, [64, 512]) for i in range(2)]; Tys = [Trk() for i in range(2)]
        psR = [S.ps("psR%d" % i, [128, 512]) for i in range(2)]; TpR = [Trk() for i in range(2)]
        psI = [S.ps("psI%d" % i, [128, 512]) for i in range(2)]; TpI = [Trk() for i in range(2)]
        psY = [S.ps("psY%d" % i, [64, 512]) for i in range(2)]; TpY = [Trk() for i in range(2)]
        Tout = Trk("s5out")
        pc = [0]; yc = [0]; ec = [0]

        def evac(dst_ap, src_ap, reads, writes):
            ec[0] += 1
            if ec[0] % 2 == 0:
                S.op("act", lambda e: e.copy(out=dst_ap, in_=src_ap), reads=reads, writes=writes)
            else:
                S.op("dve", lambda e: e.tensor_copy(out=dst_ap, in_=src_ap), reads=reads, writes=writes)

        def unit(pr, b):
            def bu_body(c0):
                n = min(400, L - c0)
                j = pc[0] % 2; pc[0] += 1
                S.op("pe", lambda e: e.matmul(psR[j][:, 0:n], lhsT=BT[pr][0][:], rhs=ub[:, c0:c0 + n], start=True, stop=True),
                     reads=[Tub, Tc], writes=[TpR[j]])
                S.op("pe", lambda e: e.matmul(psI[j][:, 0:n], lhsT=BT[pr][1][:], rhs=ub[:, c0:c0 + n], start=True, stop=True),
                     reads=[Tub, Tc], writes=[TpI[j]])
                assert c0 % 16 == 0 and n % 16 == 0
                n0 = c0 // 16; nn = n // 16
                for (bt, pst, tpt) in ((BU[0], psR[j], TpR[j]), (BU[1], psI[j], TpI[j])):
                    dst = bt[:].rearrange("p (t n) -> p t n", t=16)[:, :, n0:n0 + nn]
                    srcv = pst[:, 0:n].rearrange("p (n t) -> p t n", t=16)
                    evac(dst, srcv, [tpt], [TBU[0] if bt is BU[0] else TBU[1]])
            for c0 in range(0, L, 400):
                bu_body(c0)

            def p1_body(cc, n):
                j = pc[0] % 2; pc[0] += 1
                lo, hi = 16 * cc, 16 * (cc + n)
                for tp in range(16):
                    d = 15 - tp
                    r_re = BU[0][:, tp * NCH + cc:tp * NCH + cc + n]; r_im = BU[1][:, tp * NCH + cc:tp * NCH + cc + n]
                    S.op("pe", lambda e, d=d, r=r_re, tp=tp: e.matmul(psR[j][:, 0:n], lhsT=DR[pr][d][:], rhs=r, start=(tp == 0), stop=False),
                         reads=[TBU[0], Tc], writes=[TpR[j]])
                    S.op("pe", lambda e, d=d, r=r_im, tp=tp: e.matmul(psR[j][:, 0:n], lhsT=DN[pr][d][:], rhs=r, start=False, stop=(tp == 15)),
                         reads=[TBU[1], Tc], writes=[TpR[j]])
                    S.op("pe", lambda e, d=d, r=r_re, tp=tp: e.matmul(psI[j][:, 0:n], lhsT=DI[pr][d][:], rhs=r, start=(tp == 0), stop=False),
                         reads=[TBU[0], Tc], writes=[TpI[j]])
                    S.op("pe", lambda e, d=d, r=r_im, tp=tp: e.matmul(psI[j][:, 0:n], lhsT=DR[pr][d][:], rhs=r, start=False, stop=(tp == 15)),
                         reads=[TBU[1], Tc], writes=[TpI[j]])
                evac(Zr[0][:, cc:cc + n], psR[j][:, 0:n], [TpR[j]], [TZ[0]])
                evac(Zi[0][:, cc:cc + n], psI[j][:, 0:n], [TpI[j]], [TZ[0]])
            for (cc, n) in coltiles:
                p1_body(cc, n)
            cur = 0
            for kk in range(NLV):
                o = 1 << kk
                if o >= NCH:
                    break
                nxt = 1 - cur
                m = NCH - o
                ar = lvr[:, pr, kk:kk + 1]; ai = lvi[:, pr, kk:kk + 1]; na = lvn[:, pr, kk:kk + 1]
                S.op("dve", lambda e, cur=cur, o=o, m=m, na=na: e.scalar_tensor_tensor(
                    out=ztr[:, 0:m], in0=Zi[cur][:, 0:m], scalar=na, in1=Zr[cur][:, o:NCH], op0=ALU.mult, op1=ALU.add),
                    reads=[TZ[cur], Ts], writes=[Tzt])
                S.op("dve", lambda e, cur=cur, nxt=nxt, o=o, m=m, ar=ar: e.scalar_tensor_tensor(
                    out=Zr[nxt][:, o:NCH], in0=Zr[cur][:, 0:m], scalar=ar, in1=ztr[:, 0:m], op0=ALU.mult, op1=ALU.add),
                    reads=[TZ[cur], Tzt, Ts], writes=[TZ[nxt]])
                S.op("dve", lambda e, cur=cur, o=o, m=m, ai=ai: e.scalar_tensor_tensor(
                    out=zti[:, 0:m], in0=Zr[cur][:, 0:m], scalar=ai, in1=Zi[cur][:, o:NCH], op0=ALU.mult, op1=ALU.add),
                    reads=[TZ[cur], Ts], writes=[Tzt])
                S.op("dve", lambda e, cur=cur, nxt=nxt, o=o, m=m, ar=ar: e.scalar_tensor_tensor(
                    out=Zi[nxt][:, o:NCH], in0=Zi[cur][:, 0:m], scalar=ar, in1=zti[:, 0:m], op0=ALU.mult, op1=ALU.add),
                    reads=[TZ[cur], Tzt, Ts], writes=[TZ[nxt]])
                S.op("pool", lambda e, cur=cur, nxt=nxt, o=o: e.tensor_copy(out=Zr[nxt][:, 0:o], in_=Zr[cur][:, 0:o]),
                     reads=[TZ[cur]], writes=[TZ[nxt]])
                S.op("pool", lambda e, cur=cur, nxt=nxt, o=o: e.tensor_copy(out=Zi[nxt][:, 0:o], in_=Zi[cur][:, 0:o]),
                     reads=[TZ[cur]], writes=[TZ[nxt]])
                cur = nxt
            S.op("pool", lambda e: e.memset(Spb[0][:, 0:1], 0.0), reads=[TSp], writes=[TSp])
            S.op("pool", lambda e: e.memset(Spb[1][:, 0:1], 0.0), reads=[TSp], writes=[TSp])
            if NCH > 1:
                S.op("dve", lambda e, cur=cur: e.tensor_copy(out=Spb[0][:, 1:NCH], in_=Zr[cur][:, 0:NCH - 1]), reads=[TZ[cur], TSp], writes=[TSp])
                S.op("dve", lambda e, cur=cur: e.tensor_copy(out=Spb[1][:, 1:NCH], in_=Zi[cur][:, 0:NCH - 1]), reads=[TZ[cur], TSp], writes=[TSp])
            def p2_tau(cc, n, tau):
                    lo, hi = 16 * cc, 16 * (cc + n)
                    j = pc[0] % 2; pc[0] += 1
                    for tp in range(tau + 1):
                        d = tau - tp
                        r_re = BU[0][:, tp * NCH + cc:tp * NCH + cc + n]; r_im = BU[1][:, tp * NCH + cc:tp * NCH + cc + n]
                        S.op("pe", lambda e, d=d, r=r_re, tp=tp: e.matmul(psR[j][:, 0:n], lhsT=DR[pr][d][:], rhs=r, start=(tp == 0), stop=False),
                             reads=[TBU[0], Tc], writes=[TpR[j]])
                        S.op("pe", lambda e, d=d, r=r_im: e.matmul(psR[j][:, 0:n], lhsT=DN[pr][d][:], rhs=r, start=False, stop=False),
                             reads=[TBU[1], Tc], writes=[TpR[j]])
                        S.op("pe", lambda e, d=d, r=r_re, tp=tp: e.matmul(psI[j][:, 0:n], lhsT=DI[pr][d][:], rhs=r, start=(tp == 0), stop=False),
                             reads=[TBU[0], Tc], writes=[TpI[j]])
                        S.op("pe", lambda e, d=d, r=r_im: e.matmul(psI[j][:, 0:n], lhsT=DR[pr][d][:], rhs=r, start=False, stop=False),
                             reads=[TBU[1], Tc], writes=[TpI[j]])
                    d = tau + 1
                    S.op("pe", lambda e, d=d: e.matmul(psR[j][:, 0:n], lhsT=DR[pr][d][:], rhs=Spb[0][:, cc:cc + n], start=False, stop=False),
                         reads=[TSp, Tc], writes=[TpR[j]])
                    S.op("pe", lambda e, d=d: e.matmul(psR[j][:, 0:n], lhsT=DN[pr][d][:], rhs=Spb[1][:, cc:cc + n], start=False, stop=True),
                         reads=[TSp, Tc], writes=[TpR[j]])
                    S.op("pe", lambda e, d=d: e.matmul(psI[j][:, 0:n], lhsT=DI[pr][d][:], rhs=Spb[0][:, cc:cc + n], start=False, stop=False),
                         reads=[TSp, Tc], writes=[TpI[j]])
                    S.op("pe", lambda e, d=d: e.matmul(psI[j][:, 0:n], lhsT=DR[pr][d][:], rhs=Spb[1][:, cc:cc + n], start=False, stop=True),
                         reads=[TSp, Tc], writes=[TpI[j]])
                    evac(stt[0][:, tau:16 * n:16], psR[j][:, 0:n], [TpR[j]], [Tst[0]])
                    evac(stt[1][:, tau:16 * n:16], psI[j][:, 0:n], [TpI[j]], [Tst[1]])

            def p2_y(cc, n, x0):
                    lo = 16 * cc
                    ntok = 16 * n
                    w = min(512, ntok - x0)
                    jy = yc[0] % 2; yc[0] += 1
                    t0 = lo + x0
                    r0, r1 = 32 * pr, 32 * pr + 32
                    S.dma("sp", usk[jy][r0:r1, 0:w], u_d[r0:r1, b, t0:t0 + w], writes=[Tusk[jy]])
                    S.op("pe", lambda e, jy=jy, x0=x0, w=w: e.matmul(psY[jy][:, 0:w], lhsT=CB[pr][0][:], rhs=stt[0][:, x0:x0 + w], start=True, stop=False),
                         reads=[Tst[0], Tc], writes=[TpY[jy]])
                    S.op("pe", lambda e, jy=jy, x0=x0, w=w: e.matmul(psY[jy][:, 0:w], lhsT=CB[pr][1][:], rhs=stt[1][:, x0:x0 + w], start=False, stop=True),
                         reads=[Tst[1], Tc], writes=[TpY[jy]])
                    S.op("dve", lambda e, jy=jy, w=w, r0=r0, r1=r1: e.scalar_tensor_tensor(
                        out=ysb[jy][r0:r1, 0:w], in0=usk[jy][r0:r1, 0:w], scalar=dsk[r0:r1, 0:1], in1=psY[jy][r0:r1, 0:w],
                        op0=ALU.mult, op1=ALU.add), reads=[Tusk[jy], TpY[jy], Tp], writes=[Tys[jy]])
                    S.dma("sp", ys_d[r0:r1, b, t0:t0 + w], ysb[jy][r0:r1, 0:w], reads=[Tys[jy]], writes=[Tout])
            for (cc, n) in coltiles:
                for tau in range(16):
                    p2_tau(cc, n, tau)
                for x0 in range(0, 16 * n, 512):
                    p2_y(cc, n, x0)

        si = 0
        for b in range(NBATCH):
            for c0 in range(0, L, 1024):
                n = min(1024, L - c0)
                i = si % 2; si += 1
                S.dma("sp", ustg[i][:, 0:n], u_d[:, b, c0:c0 + n], writes=[Tus[i]])
                S.op("pool", lambda e, i=i, n=n, c0=c0: e.tensor_copy(out=ub[:, c0:c0 + n], in_=ustg[i][:, 0:n]),
                     reads=[Tus[i]], writes=[Tub])
            for pr in range(2):
                unit(pr, b)
        S.barrier()
        S.stack = st0
    return Tout


def dn_phase(S, nc, xq_d, xk_d, xv_d, cw_d, a_d, b_d, hp_d, o_d, NC, NPAD):
    st0 = S.stack
    LP = NC * 64
    with ExitStack() as ps_:
        S.stack = ps_
        Tc = Trk("dnc")
        cw = S.sb("cw", [64, 12]); hp = S.sb("hp", [64, 2]); acol = S.sb("acol", [64, NC]); bcol = S.sb("bcol", [64, NC])
        Tp = Trk("dnp")
        S.dma("sp", cw[:], cw_d[:, :], writes=[Tp]); S.dma("sp", hp[:], hp_d[:, :], writes=[Tp])
        S.dma("sp", acol[:], a_d[:, :], writes=[Tp]); S.dma("sp", bcol[:], b_d[:, :], writes=[Tp])
        one_t = S.sb("done", [64, 1]); eps_t = S.sb("deps", [64, 1])
        S.op("pool", lambda e: e.memset(one_t[:], 1.0), writes=[Tc])
        S.op("pool", lambda e: e.memset(eps_t[:], EPS), writes=[Tc])
        iot = S.sb("dniot", [64, 64]); ident = S.sb("dnident", [64, 64]); identb = S.sb("dnidentb", [64, 64], BF16)
        triu = S.sb("triu", [64, 64]); slow = S.sb("slow", [64, 64]); uinc = S.sb("uinc", [64, 64])
        ones64 = S.sb("ones64", [64, 64]); nones64 = S.sb("nones64", [64, 64]); ones64b = S.sb("ones64b", [64, 64], BF16)
        S.op("pool", lambda e: e.iota(iot[:], pattern=[[1, 64]], base=0, channel_multiplier=-1, allow_small_or_imprecise_dtypes=True), writes=[Tc])
        S.op("dve", lambda e: e.tensor_single_scalar(out=ident[:], in_=iot[:], scalar=0.0, op=ALU.is_equal), reads=[Tc], writes=[Tc])
        S.op("dve", lambda e: e.tensor_copy(out=identb[:], in_=ident[:]), reads=[Tc], writes=[Tc])
        S.op("dve", lambda e: e.tensor_single_scalar(out=triu[:], in_=iot[:], scalar=0.0, op=ALU.is_ge), reads=[Tc], writes=[Tc])
        S.op("dve", lambda e: e.tensor_copy(out=uinc[:], in_=triu[:]), reads=[Tc], writes=[Tc])
        S.op("dve", lambda e: e.tensor_single_scalar(out=slow[:], in_=iot[:], scalar=0.0, op=ALU.is_lt), reads=[Tc], writes=[Tc])
        S.op("pool", lambda e: e.memset(ones64[:], 1.0), writes=[Tc])
        S.op("pool", lambda e: e.memset(nones64[:], -1.0), writes=[Tc])
        S.op("pool", lambda e: e.memset(ones64b[:], 1.0), writes=[Tc])

        gcol = S.sb("gcol", [64, NC]); gccol = S.sb("gccol", [64, NC]); glast = S.sb("glast", [64, NC])
        egc = S.sb("egc", [64, NC]); eglast = S.sb("eglast", [64, NC]); edec = S.sb("edec", [64, NC])
        beta = S.sb("beta", [64, NC]); nbeta = S.sb("nbeta", [64, NC]); begc = S.sb("begc", [64, NC])
        nexpA = S.sb("nexpA", [64, 1]); tmpc = S.sb("tmpc", [64, NC])
        Tg = Trk("gates")
        psG = S.ps("psG", [64, 512]); TpG = Trk()
        S.op("act", lambda e: e.activation(out=tmpc[:], in_=acol[:], func=AF.Exp, bias=hp[:, 1:2]), reads=[Tp], writes=[Tg])
        S.op("act", lambda e: e.activation(out=tmpc[:], in_=tmpc[:], func=AF.Ln, bias=one_t[:, 0:1]), reads=[Tg, Tc], writes=[Tg])
        S.op("act", lambda e: e.activation(out=nexpA[:], in_=hp[:, 0:1], func=AF.Exp), reads=[Tp], writes=[Tg])
        S.op("dve", lambda e: e.tensor_scalar(out=gcol[:], in0=tmpc[:], scalar1=nexpA[:, 0:1], scalar2=-1.0, op0=ALU.mult, op1=ALU.mult),
             reads=[Tg], writes=[Tg])
        if NPAD > 0:
            S.op("dve", lambda e: e.memset(gcol[0:NPAD, 0:1], 0.0), reads=[Tg], writes=[Tg])
        S.op("pe", lambda e: e.matmul(psG[:, 0:NC], lhsT=triu[:], rhs=gcol[:], start=True, stop=True), reads=[Tg, Tc], writes=[TpG])
        S.op("dve", lambda e: e.tensor_copy(out=gccol[:], in_=psG[:, 0:NC]), reads=[TpG], writes=[Tg])
        S.op("pe", lambda e: e.matmul(psG[:, 0:NC], lhsT=ones64[:], rhs=gcol[:], start=True, stop=True), reads=[Tg, Tc], writes=[TpG])
        S.op("dve", lambda e: e.tensor_copy(out=glast[:], in_=psG[:, 0:NC]), reads=[TpG], writes=[Tg])
        S.op("act", lambda e: e.activation(out=egc[:], in_=gccol[:], func=AF.Exp), reads=[Tg], writes=[Tg])
        S.op("act", lambda e: e.activation(out=eglast[:], in_=glast[:], func=AF.Exp), reads=[Tg], writes=[Tg])
        S.op("dve", lambda e: e.tensor_tensor(out=tmpc[:], in0=glast[:], in1=gccol[:], op=ALU.subtract), reads=[Tg], writes=[Tg])
        S.op("act", lambda e: e.activation(out=edec[:], in_=tmpc[:], func=AF.Exp), reads=[Tg], writes=[Tg])
        S.op("act", lambda e: e.activation(out=beta[:], in_=bcol[:], func=AF.Sigmoid), reads=[Tp], writes=[Tg])
        S.op("dve", lambda e: e.tensor_scalar(out=nbeta[:], in0=beta[:], scalar1=-1.0, scalar2=None, op0=ALU.mult), reads=[Tg], writes=[Tg])
        S.op("dve", lambda e: e.tensor_tensor(out=begc[:], in0=beta[:], in1=egc[:], op=ALU.mult), reads=[Tg], writes=[Tg])

        qb = S.sb("dqb", [64, LP], BF16); kb = S.sb("dkb", [64, LP], BF16); vb = S.sb("dvb", [64, LP], BF16)
        Tqkv = [Trk("dq"), Trk("dk"), Trk("dv")]
        CT = 2048
        xin = [S.sb("dxin%d" % i, [64, CT + 3]) for i in range(2)]; Txin = [Trk() for i in range(2)]
        acc = [S.sb("dacc%d" % i, [64, CT]) for i in range(2)]; Tacc = [Trk() for i in range(2)]
        sqb = S.sb("dsq", [64, CT], BF16); Tsq = Trk()
        rinv = S.sb("drinv", [64, 512]); Tri = Trk()
        psS = [S.ps("psS%d" % i, [64, 512]) for i in range(2)]; TpS = [Trk() for i in range(2)]
        cc = [0]

        def conv_tile(src, which, c0):
            n = min(CT, LP - c0)
            i = cc[0] % 2; cc[0] += 1
            dst = (qb, kb, vb)[which]
            S.dma("sp", xin[i][:, 0:n + 3], src[:, c0:c0 + n + 3], writes=[Txin[i]])
            S.op("dve", lambda e: e.tensor_scalar(out=acc[i][:, 0:n], in0=xin[i][:, 0:n], scalar1=cw[:, 4 * which:4 * which + 1],
                                                  scalar2=None, op0=ALU.mult), reads=[Txin[i], Tp], writes=[Tacc[i]])
            for j in range(1, 4):
                S.op("dve", lambda e, j=j: e.scalar_tensor_tensor(
                    out=acc[i][:, 0:n], in0=xin[i][:, j:j + n], scalar=cw[:, 4 * which + j:4 * which + j + 1], in1=acc[i][:, 0:n],
                    op0=ALU.mult, op1=ALU.add), reads=[Txin[i], Tp, Tacc[i]], writes=[Tacc[i]])
            S.op("act", lambda e: e.activation(out=acc[i][:, 0:n], in_=acc[i][:, 0:n], func=AF.Silu), reads=[Tacc[i]], writes=[Tacc[i]])
            if which == 2:
                S.op("pool", lambda e: e.tensor_copy(out=dst[:, c0:c0 + n], in_=acc[i][:, 0:n]), reads=[Tacc[i]], writes=[Tqkv[2]])
                return
            S.op("act", lambda e: e.activation(out=sqb[:, 0:n], in_=acc[i][:, 0:n], func=AF.Square), reads=[Tacc[i]], writes=[Tsq])
            for s0 in range(0, n, 512):
                w = min(512, n - s0)
                j = cc[0] % 2; cc[0] += 1
                S.op("pe", lambda e, s0=s0, w=w, j=j: e.matmul(psS[j][:, 0:w], lhsT=ones64b[:], rhs=sqb[:, s0:s0 + w], start=True, stop=True),
                     reads=[Tsq, Tc], writes=[TpS[j]])
                S.op("act", lambda e, w=w, j=j: e.activation(out=rinv[:, 0:w], in_=psS[j][:, 0:w], func=AF.Ln, bias=eps_t[:, 0:1]),
                     reads=[TpS[j], Tc], writes=[Tri])
                S.op("act", lambda e, w=w: e.activation(out=rinv[:, 0:w], in_=rinv[:, 0:w], func=AF.Exp, scale=-0.5), reads=[Tri], writes=[Tri])
                sc_ = 0.125 if which == 0 else 1.0
                S.op("dve", lambda e, s0=s0, w=w, sc_=sc_: e.scalar_tensor_tensor(
                    out=dst[:, c0 + s0:c0 + s0 + w], in0=acc[i][:, s0:s0 + w], scalar=sc_, in1=rinv[:, 0:w], op0=ALU.mult, op1=ALU.mult),
                    reads=[Tacc[i], Tri], writes=[Tqkv[which]])

        for c0 in range(0, LP, CT):
            conv_tile(xq_d, 0, c0); conv_tile(xk_d, 1, c0); conv_tile(xv_d, 2, c0)

        GS = 8
        psD = S.ps("psD", [64, 512]); TpD = Trk()
        psA = S.ps("psA", [64, 512]); TpA = Trk()
        psQ = S.ps("psQ", [64, 512]); TpQ = Trk()
        psTr = S.ps("psTr", [64, 512]); TpTr = Trk()
        psN0 = S.ps("psN", [64, 512])
        S.barrier()
        psNl = [psN0, psG]; TpNl = [Trk(), Trk()]
        psU = psS[0]; TpU = Trk()
        psQ2 = psS[1]; TpSeq = Trk()
        gt = S.sb("gt", [64, GS * 64]); Tgt = Trk()
        E = S.sb("Emat", [64, GS * 64]); TE = Trk()
        EL = S.sb("EL", [64, GS * 64]); EU = S.sb("EU", [64, GS * 64]); TEm = Trk()
        NL = 2
        Yl = [[S.sb("Yk%d_%d" % (l, i), [64, 64]) for i in range(2)] for l in range(NL)]
        Xl = [[S.sb("Xk%d_%d" % (l, i), [64, 64]) for i in range(2)] for l in range(NL)]
        TYl = [[Trk() for i in range(2)] for l in range(NL)]; TXl = [[Trk() for i in range(2)] for l in range(NL)]
        Pl = [S.sb("Pm%d" % l, [64, 64]) for l in range(NL)]; TPl = [Trk() for l in range(NL)]
        TTbl = [S.sb("TTb%d" % l, [64, 64], BF16) for l in range(NL)]; TTTl = [Trk() for l in range(NL)]
        vbetal = [S.sb("vbeta%d" % l, [64, 64], BF16) for l in range(NL)]
        kbgl = [S.sb("kbg%d" % l, [64, 64], BF16) for l in range(NL)]; Tvkl = [Trk() for l in range(NL)]
        u_g = [S.sb("u_g%d" % i, [64, GS * 64]) for i in range(2)]
        wT_g = [S.sb("wT_g%d" % i, [64, GS * 64], BF16) for i in range(2)]
        attnT_g = [S.sb("attnT_g%d" % i, [64, GS * 64], BF16) for i in range(2)]
        kdec_g = [S.sb("kdec_g%d" % i, [64, GS * 64], BF16) for i in range(2)]
        Tgrp = [[Trk() for _ in range(GS)] for i in range(2)]
        Sf = S.sb("Sf", [64, 64]); Sb = S.sb("Sb", [64, 64], BF16); TS = Trk()
        vnew = S.sb("vnew", [64, 64], BF16); Tvn = Trk()
        qs_t = S.sb("qs_t", [64, 64]); Tqs = Trk()
        o_g = [S.sb("o_g%d" % i, [64, GS * 64]) for i in range(2)]; Tog = [Trk() for i in range(2)]
        Tout = Trk("dnout")
        S.op("pool", lambda e: e.memset(Sf[:], 0.0), writes=[TS])
        S.op("pool", lambda e: e.memset(Sb[:], 0.0), reads=[TS], writes=[TS])

        def pre_group_head(n0, ng, gp):
            W = ng * 64
            for c in range(ng):
                n = n0 + c
                S.op("dve", lambda e, c=c, n=n: e.tensor_scalar(out=gt[:, c * 64:(c + 1) * 64], in0=triu[:], scalar1=gcol[:, n:n + 1],
                                                                 scalar2=None, op0=ALU.mult), reads=[Tg, Tc, Tgt], writes=[Tgt])
                S.op("pe", lambda e, c=c: e.matmul(psD[:, c * 64:(c + 1) * 64], lhsT=gt[:, c * 64:(c + 1) * 64], rhs=ones64[:], start=True, stop=False),
                     reads=[Tgt, Tc], writes=[TpD])
                S.op("pe", lambda e, c=c: e.matmul(psD[:, c * 64:(c + 1) * 64], lhsT=nones64[:], rhs=gt[:, c * 64:(c + 1) * 64], start=False, stop=True),
                     reads=[Tgt, Tc], writes=[TpD])
            yield
            S.op("dve", lambda e: e.tensor_scalar(out=E[:, 0:W], in0=psD[:, 0:W], scalar1=-1.0, scalar2=None, op0=ALU.mult), reads=[TpD, TE, TEm], writes=[TE])
            S.op("dve", lambda e: e.tensor_tensor(out=E[:, 0:W], in0=E[:, 0:W], in1=psD[:, 0:W], op=ALU.min), reads=[TpD, TE], writes=[TE])
            S.op("act", lambda e: e.activation(out=E[:, 0:W], in_=E[:, 0:W], func=AF.Exp), reads=[TE], writes=[TE])
            yield
            S.op("dve", lambda e: e.tensor_tensor(out=EL[:, 0:W].rearrange("p (c f) -> p c f", f=64), in0=E[:, 0:W].rearrange("p (c f) -> p c f", f=64),
                                                  in1=slow[:].unsqueeze(1).broadcast_to([64, ng, 64]), op=ALU.mult), reads=[TE, Tc, TEm], writes=[TEm])
            S.op("dve", lambda e: e.tensor_tensor(out=EU[:, 0:W].rearrange("p (c f) -> p c f", f=64), in0=E[:, 0:W].rearrange("p (c f) -> p c f", f=64),
                                                  in1=uinc[:].unsqueeze(1).broadcast_to([64, ng, 64]), op=ALU.mult), reads=[TE, Tc, TEm], writes=[TEm])
            for c in range(ng):
                n = n0 + c
                ks = kb[:, n * 64:(n + 1) * 64]; qs = qb[:, n * 64:(n + 1) * 64]
                S.op("pe", lambda e, c=c, ks=ks: e.matmul(psA[:, c * 64:(c + 1) * 64], lhsT=ks, rhs=ks, start=True, stop=True),
                     reads=[Tqkv[1]], writes=[TpA])
                S.op("pe", lambda e, c=c, ks=ks, qs=qs: e.matmul(psQ[:, c * 64:(c + 1) * 64], lhsT=ks, rhs=qs, start=True, stop=True),
                     reads=[Tqkv[0], Tqkv[1]], writes=[TpQ])
            yield
            S.op("dve", lambda e: e.tensor_tensor(out=attnT_g[gp][:, 0:W], in0=EU[:, 0:W], in1=psQ[:, 0:W], op=ALU.mult),
                 reads=[TEm, TpQ] + Tgrp[gp], writes=Tgrp[gp])
            yield

        def pre_chunk(n, c, gp, l):
            cs = slice(c * 64, (c + 1) * 64)
            lo_ = 128 * l
            Y = Yl[l]; Xm = Xl[l]; TY = TYl[l]; TX = TXl[l]; P = Pl[l]; TP = TPl[l]; psN = psNl[l]; TpN = TpNl[l]
            TTb = TTbl[l]; TTT = TTTl[l]; vbeta = vbetal[l]; kbg = kbgl[l]; Tvk = Tvkl[l]
            S.op("dve", lambda e: e.scalar_tensor_tensor(out=Y[0][:], in0=psA[:, cs], scalar=nbeta[:, n:n + 1], in1=EL[:, cs],
                                                         op0=ALU.mult, op1=ALU.mult), reads=[TpA, Tg, TEm, TY[0]], writes=[TY[0]])
            yield
            S.op("pe", lambda e: e.matmul(psN[:, 0:64], lhsT=Y[0][:], rhs=ident[:], start=True, stop=True), reads=[TY[0], Tc], writes=[TpN])
            yield
            S.op("dve", lambda e: e.tensor_copy(out=Xm[0][:], in_=psN[:, 0:64]), reads=[TpN], writes=[TX[0]])
            S.op("dve", lambda e: e.tensor_tensor(out=P[:], in0=Xm[0][:], in1=ident[:], op=ALU.add), reads=[TX[0], Tc, TP], writes=[TP])
            yield
            cur = 0
            for lv in range(5):
                nx = 1 - cur
                S.op("pe", lambda e, cur=cur: e.matmul(psN[:, 64:128], lhsT=Y[cur][:], rhs=Xm[cur][:], start=True, stop=True),
                     reads=[TY[cur], TX[cur]], writes=[TpN])
                S.op("pe", lambda e, cur=cur: e.matmul(psN[:, 128:192], lhsT=Xm[cur][:], rhs=Y[cur][:], start=True, stop=True),
                     reads=[TY[cur], TX[cur]], writes=[TpN])
                yield
                S.op("dve", lambda e, nx=nx: e.tensor_copy(out=Xm[nx][:], in_=psN[:, 64:128]), reads=[TpN], writes=[TX[nx]])
                S.op("dve", lambda e, nx=nx: e.tensor_copy(out=Y[nx][:], in_=psN[:, 128:192]), reads=[TpN], writes=[TY[nx]])
                yield
                S.op("pe", lambda e, nx=nx: e.matmul(psN[:, 192:256], lhsT=Y[nx][:], rhs=P[:], start=True, stop=True),
                     reads=[TY[nx], TP], writes=[TpN])
                yield
                S.op("dve", lambda e: e.tensor_tensor(out=P[:], in0=P[:], in1=psN[:, 192:256], op=ALU.add), reads=[TpN, TP], writes=[TP])
                yield
                cur = nx
            S.op("act", lambda e: e.copy(out=TTb[:], in_=P[:]), reads=[TP], writes=[TTT])
            ks = kb[:, n * 64:(n + 1) * 64]; vs = vb[:, n * 64:(n + 1) * 64]
            S.op("pe", lambda e: e.matmul(psTr[:, lo_ + 0:lo_ + 64], lhsT=ks, rhs=identb[:], start=True, stop=True), reads=[Tqkv[1], Tc], writes=[TpTr])
            S.op("pe", lambda e: e.matmul(psTr[:, lo_ + 64:lo_ + 128], lhsT=vs, rhs=identb[:], start=True, stop=True), reads=[Tqkv[2], Tc], writes=[TpTr])
            yield
            S.op("dve", lambda e: e.tensor_scalar(out=kbg[:], in0=psTr[:, lo_ + 0:lo_ + 64], scalar1=begc[:, n:n + 1], scalar2=None, op0=ALU.mult),
                 reads=[TpTr, Tg, Tvk], writes=[Tvk])
            S.op("dve", lambda e: e.tensor_scalar(out=vbeta[:], in0=psTr[:, lo_ + 64:lo_ + 128], scalar1=beta[:, n:n + 1], scalar2=None, op0=ALU.mult),
                 reads=[TpTr, Tg, Tvk], writes=[Tvk])
            S.op("dve", lambda e: e.tensor_scalar(out=kdec_g[gp][:, cs], in0=psTr[:, lo_ + 0:lo_ + 64], scalar1=edec[:, n:n + 1], scalar2=None, op0=ALU.mult),
                 reads=[TpTr, Tg, Tgrp[gp][c]], writes=[Tgrp[gp][c]])
            yield
            S.op("pe", lambda e: e.matmul(psU[:, lo_ + 0:lo_ + 64], lhsT=TTb[:], rhs=vbeta[:], start=True, stop=True), reads=[TTT, Tvk], writes=[TpU])
            S.op("pe", lambda e: e.matmul(psU[:, lo_ + 64:lo_ + 128], lhsT=kbg[:], rhs=TTb[:], start=True, stop=True), reads=[TTT, Tvk], writes=[TpU])
            yield
            S.op("dve", lambda e: e.tensor_copy(out=u_g[gp][:, cs], in_=psU[:, lo_ + 0:lo_ + 64]), reads=[TpU, Tgrp[gp][c]], writes=[Tgrp[gp][c]])
            S.op("dve", lambda e: e.tensor_copy(out=wT_g[gp][:, cs], in_=psU[:, lo_ + 64:lo_ + 128]), reads=[TpU, Tgrp[gp][c]], writes=[Tgrp[gp][c]])
            yield

        def seq_chunk(n, c, gp, og):
            cs = slice(c * 64, (c + 1) * 64)
            qs = qb[:, n * 64:(n + 1) * 64]
            Tgc = Tgrp[gp][c]
            S.op("pe", lambda e: e.matmul(psQ2[:, 0:64], lhsT=wT_g[gp][:, cs], rhs=Sb[:], start=True, stop=True), reads=[Tgc, TS], writes=[TpSeq])
            S.op("pe", lambda e: e.matmul(psQ2[:, 64:128], lhsT=qs, rhs=Sb[:], start=True, stop=True), reads=[Tqkv[0], TS], writes=[TpSeq])
            yield
            S.op("dve", lambda e: e.tensor_tensor(out=vnew[:], in0=u_g[gp][:, cs], in1=psQ2[:, 0:64], op=ALU.subtract),
                 reads=[Tgc, TpSeq, Tvn], writes=[Tvn])
            S.op("dve", lambda e: e.tensor_scalar(out=qs_t[:], in0=psQ2[:, 64:128], scalar1=egc[:, n:n + 1], scalar2=None, op0=ALU.mult),
                 reads=[TpSeq, Tg, Tqs], writes=[Tqs])
            yield
            S.op("pe", lambda e: e.matmul(psQ2[:, 128:192], lhsT=attnT_g[gp][:, cs], rhs=vnew[:], start=True, stop=True), reads=[Tgc, Tvn], writes=[TpSeq])
            S.op("pe", lambda e: e.matmul(psQ2[:, 192:256], lhsT=kdec_g[gp][:, cs], rhs=vnew[:], start=True, stop=True), reads=[Tgc, Tvn], writes=[TpSeq])
            yield
            S.op("dve", lambda e: e.scalar_tensor_tensor(out=Sf[:], in0=Sf[:], scalar=eglast[:, n:n + 1], in1=psQ2[:, 192:256],
                                                         op0=ALU.mult, op1=ALU.add), reads=[TS, Tg, TpSeq], writes=[TS])
            S.op("act", lambda e: e.copy(out=Sb[:], in_=Sf[:]), reads=[TS], writes=[TS])
            S.op("dve", lambda e: e.tensor_tensor(out=o_g[og][:, cs], in0=qs_t[:], in1=psQ2[:, 128:192], op=ALU.add),
                 reads=[Tqs, TpSeq, Tog[og]], writes=[Tog[og]])
            yield

        def pre_gen(n0, ng, gp):
            yield from pre_group_head(n0, ng, gp)
            for c in range(0, ng, NL):
                gens = [pre_chunk(n0 + c + l, c + l, gp, l) for l in range(NL) if c + l < ng]
                while gens:
                    for g_ in list(gens):
                        try:
                            next(g_)
                        except StopIteration:
                            gens.remove(g_)
                    yield

        def seq_gen(n0, ng, gp, og):
            for c in range(ng):
                yield from seq_chunk(n0 + c, c, gp, og)
            S.dma("sp", o_d[:, n0:n0 + ng, :], o_g[og][:, 0:ng * 64].rearrange("p (c f) -> p c f", f=64), reads=[Tog[og]], writes=[Tout])

        groups = [(n0, min(GS, NC - n0)) for n0 in range(0, NC, GS)]
        for _ in pre_gen(groups[0][0], groups[0][1], 0):
            pass
        for gi, (n0, ng) in enumerate(groups):
            gp = gi % 2
            active = [seq_gen(n0, ng, gp, gi % 2)]
            if gi + 1 < len(groups):
                active.append(pre_gen(groups[gi + 1][0], groups[gi + 1][1], 1 - gp))
            while active:
                for g_ in list(active):
                    try:
                        next(g_)
                    except StopIteration:
                        active.remove(g_)
        S.barrier()
        S.stack = st0
    return Tout


def s5_phase2(S, nc, u_d, are_d, aim_d, ldt_d, bre_d, bim_d, cre_d, cim_d, dsk_d, ys_d, L, NBATCH=2):
    st0 = S.stack
    NCH = L // 16
    assert NCH * 16 == L
    TWO_PI = 2.0 * math.pi
    MAGIC = 12582912.0
    NG = 4
    with ExitStack() as ps_:
        S.stack = ps_
        Tc = Trk("s5c")
        prm = S.sb("prm", [128, 12]); Tp = Trk("prm")
        bre = S.sb("bre", [128, NG, 16]); bim = S.sb("bim", [128, NG, 16])
        cre = S.sb("cre", [128, NG, 16]); cim = S.sb("cim", [128, NG, 16])
        dsk = S.sb("dsk", [64, 1])
        S.dma("sp", prm[:, 0:4], are_d[:, :], writes=[Tp])
        S.dma("sp", prm[:, 4:8], aim_d[:, :], writes=[Tp])
        S.dma("sp", prm[:, 8:12], ldt_d[:, :], writes=[Tp])
        S.dma("sp", bre[:], bre_d[:, :, :], writes=[Tp])
        S.dma("sp", bim[:], bim_d[:, :, :], writes=[Tp])
        S.dma("sp", cre[:], cre_d[:, :, :], writes=[Tp])
        S.dma("sp", cim[:], cim_d[:, :, :], writes=[Tp])
        S.dma("sp", dsk[:], dsk_d[:, :], writes=[Tp])
        sc = S.sb("s5sc", [128, 64]); Ts = Trk("s5sc")
        dve = lambda fn: S.op("dve", fn, reads=[Tp, Ts, Tc], writes=[Ts])
        DT = sc[:, 0:4]; ARD = sc[:, 4:8]; AID = sc[:, 8:12]; X = sc[:, 12:16]; DEN = sc[:, 16:20]; RDEN = sc[:, 20:24]
        CFR = sc[:, 24:28]; CFI = sc[:, 28:32]; T1 = sc[:, 32:36]; T2 = sc[:, 36:40]
        ARE = prm[:, 0:4]; AIM = prm[:, 4:8]
        S.op("act", lambda e: e.activation(out=DT, in_=prm[:, 8:12], func=AF.Exp), reads=[Tp], writes=[Ts])
        dve(lambda e: e.tensor_tensor(out=ARD, in0=ARE, in1=DT, op=ALU.mult))
        dve(lambda e: e.tensor_tensor(out=AID, in0=AIM, in1=DT, op=ALU.mult))
        NP = 17
        mag = S.sb("mag", [128, NG, NP]); ang = S.sb("ang", [128, NG, 2 * NP]); ang2 = S.sb("ang2", [128, NG, 2 * NP])
        trg = S.sb("trg", [128, NG, 2 * NP])
        pwr = S.sb("pwr", [128, NG, NP]); pwi = S.sb("pwi", [128, NG, NP]); pws = S.sb("pws", [128, NG, NP])
        for m in range(NP):
            S.op("act", lambda e, m=m: e.activation(out=mag[:, :, m], in_=ARD, func=AF.Exp, scale=float(m)), reads=[Ts], writes=[Ts])
            dve(lambda e, m=m: e.tensor_scalar(out=ang[:, :, m], in0=AID, scalar1=float(m), scalar2=0.0, op0=ALU.mult, op1=ALU.add))
            dve(lambda e, m=m: e.tensor_scalar(out=ang[:, :, NP + m], in0=AID, scalar1=float(m), scalar2=math.pi / 2, op0=ALU.mult, op1=ALU.add))

        def sincos(dst_r, dst_i, dst_s, magt, angt, ang2t, trgt, n):
            dve(lambda e: e.tensor_scalar(out=ang2t, in0=angt, scalar1=1.0 / TWO_PI, scalar2=MAGIC, op0=ALU.mult, op1=ALU.add))
            dve(lambda e: e.tensor_scalar(out=ang2t, in0=ang2t, scalar1=-MAGIC, scalar2=None, op0=ALU.add))
            dve(lambda e: e.scalar_tensor_tensor(out=ang2t, in0=ang2t, scalar=-TWO_PI, in1=angt, op0=ALU.mult, op1=ALU.add))
            dve(lambda e: e.tensor_scalar(out=ang2t, in0=ang2t, scalar1=3.141592, scalar2=-3.141592, op0=ALU.min, op1=ALU.max))
            S.op("act", lambda e: e.activation(out=trgt, in_=ang2t, func=AF.Sin), reads=[Ts], writes=[Ts])
        sincos(None, None, None, mag[:], ang[:], ang2[:], trg[:], NP)
        dve(lambda e: e.tensor_tensor(out=pwi[:], in0=mag[:], in1=trg[:, :, 0:NP], op=ALU.mult))
        dve(lambda e: e.tensor_tensor(out=pwr[:], in0=mag[:], in1=trg[:, :, NP:2 * NP], op=ALU.mult))
        dve(lambda e: e.tensor_copy(out=pws[0:64], in_=pwi[0:64]))
        dve(lambda e: e.tensor_scalar(out=pws[64:128], in0=pwi[64:128], scalar1=-1.0, scalar2=None, op0=ALU.mult))
        dve(lambda e: e.tensor_scalar(out=X, in0=pwr[:, :, 1], scalar1=-1.0, scalar2=None, op0=ALU.add))
        dve(lambda e: e.tensor_tensor(out=DEN, in0=ARE, in1=ARE, op=ALU.mult))
        dve(lambda e: e.tensor_tensor(out=T1, in0=AIM, in1=AIM, op=ALU.mult))
        dve(lambda e: e.tensor_tensor(out=DEN, in0=DEN, in1=T1, op=ALU.add))
        dve(lambda e: e.reciprocal(out=RDEN, in_=DEN))
        dve(lambda e: e.tensor_tensor(out=T1, in0=X, in1=ARE, op=ALU.mult))
        dve(lambda e: e.tensor_tensor(out=T2, in0=pwi[:, :, 1], in1=AIM, op=ALU.mult))
        dve(lambda e: e.tensor_tensor(out=T1, in0=T1, in1=T2, op=ALU.add))
        dve(lambda e: e.tensor_tensor(out=CFR, in0=T1, in1=RDEN, op=ALU.mult))
        dve(lambda e: e.tensor_tensor(out=T1, in0=pwi[:, :, 1], in1=ARE, op=ALU.mult))
        dve(lambda e: e.tensor_tensor(out=T2, in0=X, in1=AIM, op=ALU.mult))
        dve(lambda e: e.tensor_tensor(out=T1, in0=T1, in1=T2, op=ALU.subtract))
        dve(lambda e: e.tensor_tensor(out=CFI, in0=T1, in1=RDEN, op=ALU.mult))
        import os
        if os.environ.get("S5_STOP") == "1":
            S.barrier(); S.stack = st0
            return Trk()
        iot = S.sb("s5iot", [128, 128]); ident = S.sb("s5ident", [128, 128]); Jm = S.sb("s5J", [128, 128]); jt = S.sb("s5jt", [128, 128])
        S.op("pool", lambda e: e.iota(iot[:], pattern=[[1, 128]], base=0, channel_multiplier=-1, allow_small_or_imprecise_dtypes=True), writes=[Tc])
        S.op("dve", lambda e: e.tensor_single_scalar(out=ident[:], in_=iot[:], scalar=0.0, op=ALU.is_equal), reads=[Tc], writes=[Tc])
        S.op("dve", lambda e: e.tensor_single_scalar(out=Jm[:], in_=iot[:], scalar=64.0, op=ALU.is_equal), reads=[Tc], writes=[Tc])
        S.op("dve", lambda e: e.tensor_single_scalar(out=jt[:], in_=iot[:], scalar=-64.0, op=ALU.is_equal), reads=[Tc], writes=[Tc])
        S.op("dve", lambda e: e.tensor_tensor(out=Jm[:], in0=Jm[:], in1=jt[:], op=ALU.add), reads=[Tc], writes=[Tc])
        RT = [[S.sb("RT%d_%d" % (g, m), [128, 128], BF16) for m in range(NP)] for g in range(NG)]
        kk_ = 0
        for g in range(NG):
            for m in range(NP):
                S.op("dve", lambda e, g=g, m=m: e.tensor_scalar(out=jt[:], in0=Jm[:], scalar1=pws[:, g, m:m + 1], scalar2=None, op0=ALU.mult),
                     reads=[Ts, Tc], writes=[Tc])
                S.op("dve", lambda e, g=g, m=m: e.scalar_tensor_tensor(out=RT[g][m][:], in0=ident[:], scalar=pwr[:, g, m:m + 1], in1=jt[:],
                                                                      op0=ALU.mult, op1=ALU.add), reads=[Ts, Tc], writes=[Tc])
        NLV = max(1, int(math.ceil(math.log2(NCH))))
        lvr = S.sb("lvr", [128, NG, NLV]); lvi = S.sb("lvi", [128, NG, NLV]); lvs = S.sb("lvs", [128, NG, NLV])
        dve(lambda e: e.tensor_copy(out=lvr[:, :, 0], in_=pwr[:, :, 16]))
        dve(lambda e: e.tensor_copy(out=lvi[:, :, 0], in_=pwi[:, :, 16]))
        for kk in range(1, NLV):
            dve(lambda e, kk=kk: e.tensor_tensor(out=T1, in0=lvr[:, :, kk - 1], in1=lvr[:, :, kk - 1], op=ALU.mult))
            dve(lambda e, kk=kk: e.tensor_tensor(out=T2, in0=lvi[:, :, kk - 1], in1=lvi[:, :, kk - 1], op=ALU.mult))
            dve(lambda e, kk=kk: e.tensor_tensor(out=lvr[:, :, kk], in0=T1, in1=T2, op=ALU.subtract))
            dve(lambda e, kk=kk: e.tensor_tensor(out=T1, in0=lvr[:, :, kk - 1], in1=lvi[:, :, kk - 1], op=ALU.mult))
            dve(lambda e, kk=kk: e.tensor_scalar(out=lvi[:, :, kk], in0=T1, scalar1=2.0, scalar2=None, op0=ALU.mult))
        dve(lambda e: e.tensor_copy(out=lvs[0:64], in_=lvi[0:64]))
        dve(lambda e: e.tensor_scalar(out=lvs[64:128], in0=lvi[64:128], scalar1=-1.0, scalar2=None, op0=ALU.mult))
        RL = [[S.sb("RL%d_%d" % (g, k), [128, 128]) for k in range(NLV)] for g in range(NG)]
        for g in range(NG):
            for k in range(NLV):
                S.op("dve", lambda e, g=g, k=k: e.tensor_scalar(out=jt[:], in0=Jm[:], scalar1=lvs[:, g, k:k + 1], scalar2=None, op0=ALU.mult),
                     reads=[Ts, Tc], writes=[Tc])
                S.op("dve", lambda e, g=g, k=k: e.scalar_tensor_tensor(out=RL[g][k][:], in0=ident[:], scalar=lvr[:, g, k:k + 1], in1=jt[:],
                                                                      op0=ALU.mult, op1=ALU.add), reads=[Ts, Tc], writes=[Tc])
        bst = S.sb("bst", [128, 64]); tmpb = S.sb("tmpb", [128, 16]); Tbb = Trk("bst")
        BT = [S.sb("BTs%d" % g, [64, 128], BF16) for g in range(NG)]
        CS = [S.sb("CSs%d" % g, [128, 64], BF16) for g in range(NG)]
        psT = S.ps("psT", [128, 512]); TpT = Trk()
        for g in range(NG):
            c0 = 16 * g
            S.op("pool", lambda e: e.memset(bst[:], 0.0), reads=[Tbb], writes=[Tbb])
            cfr0 = sc[0:64, 24 + g:25 + g]; cfi0 = sc[0:64, 28 + g:29 + g]
            cfr1 = sc[64:128, 24 + g:25 + g]; cfi1 = sc[64:128, 28 + g:29 + g]
            S.op("dve", lambda e, g=g, cfi0=cfi0: e.tensor_scalar(out=tmpb[0:64, :], in0=bim[0:64, g, :], scalar1=cfi0, scalar2=None, op0=ALU.mult),
                 reads=[Tp, Ts, Tbb], writes=[Tbb])
            S.op("dve", lambda e, g=g, cfr0=cfr0, c0=c0: e.scalar_tensor_tensor(out=bst[0:64, c0:c0 + 16], in0=bre[0:64, g, :], scalar=cfr0,
                                                                               in1=tmpb[0:64, :], op0=ALU.mult, op1=ALU.subtract),
                 reads=[Tp, Ts, Tbb], writes=[Tbb])
            S.op("dve", lambda e, g=g, cfi1=cfi1: e.tensor_scalar(out=tmpb[64:128, :], in0=bre[64:128, g, :], scalar1=cfi1, scalar2=None, op0=ALU.mult),
                 reads=[Tp, Ts, Tbb], writes=[Tbb])
            S.op("dve", lambda e, g=g, cfr1=cfr1, c0=c0: e.scalar_tensor_tensor(out=bst[64:128, c0:c0 + 16], in0=bim[64:128, g, :], scalar=cfr1,
                                                                               in1=tmpb[64:128, :], op0=ALU.mult, op1=ALU.add),
                 reads=[Tp, Ts, Tbb], writes=[Tbb])
            S.op("pe", lambda e: e.transpose(out=psT[0:64, 0:128], in_=bst[:], identity=ident[:]), reads=[Tbb, Tc], writes=[TpT])
            S.op("act", lambda e, g=g: e.copy(out=BT[g][:], in_=psT[0:64, 0:128]), reads=[TpT], writes=[Tc])
            S.op("pool", lambda e, g=g: e.memset(CS[g][:], 0.0), writes=[Tc])
            S.op("dve", lambda e, g=g, c0=c0: e.tensor_copy(out=CS[g][0:64, c0:c0 + 16], in_=cre[0:64, g, :]), reads=[Tp, Tc], writes=[Tc])
            S.op("dve", lambda e, g=g, c0=c0: e.tensor_scalar(out=CS[g][64:128, c0:c0 + 16], in0=cim[64:128, g, :], scalar1=-1.0, scalar2=None,
                                                              op0=ALU.mult), reads=[Tp, Tc], writes=[Tc])

        ub = S.sb("ub", [64, L], BF16); Tub = Trk("ub")
        ustg = [S.sb("ustg%d" % i, [64, 1024]) for i in range(2)]; Tus = [Trk() for i in range(2)]
        BU = [S.sb("BU%d" % i, [128, L], BF16) for i in range(2)]; TBU = [Trk() for i in range(2)]
        CW = (NCH + 2) // 3
        coltiles = [(c, min(CW, NCH - c)) for c in range(0, NCH, CW)]
        stt = [S.sb("stt%d" % i, [128, CW * 16], BF16) for i in range(2)]; Tst = [Trk() for i in range(2)]
        Z = [[S.sb("Z%d_%d" % (i, k), [128, NCH]) for k in range(2)] for i in range(2)]; TZ = [[Trk() for k in range(2)] for i in range(2)]
        Spb = [S.sb("Spb%d" % i, [128, NCH], BF16) for i in range(2)]; TSp = [Trk() for i in range(2)]
        usk = [S.sb("usk%d" % i, [64, 512]) for i in range(2)]; Tusk = [Trk() for i in range(2)]
        ysb = [S.sb("ysb%d" % i, [64, 512]) for i in range(2)]; Tys = [Trk() for i in range(2)]
        psR = [S.ps("psR%d" % i, [128, 512]) for i in range(4)]; TpR = [Trk() for i in range(4)]
        psY = [S.ps("psY%d" % i, [64, 512]) for i in range(2)]; TpY = [Trk() for i in range(2)]
        Tout = Trk("s5out")
        pc = [0]; yc = [0]; ec = [0]

        def evac(dst_ap, src_ap, reads, writes):
            ec[0] += 1
            if ec[0] % 2 == 0:
                S.op("act", lambda e: e.copy(out=dst_ap, in_=src_ap), reads=reads, writes=writes)
            else:
                S.op("dve", lambda e: e.tensor_copy(out=dst_ap, in_=src_ap), reads=reads, writes=writes)

        def prep_group(g, gi):
            def bu_body(c0):
                n = min(400, L - c0)
                j = pc[0] % 4; pc[0] += 1
                S.op("pe", lambda e: e.matmul(psR[j][:, 0:n], lhsT=BT[g][:], rhs=ub[:, c0:c0 + n], start=True, stop=True),
                     reads=[Tub, Tc], writes=[TpR[j]])
                n0 = c0 // 16; nn = n // 16
                dst = BU[gi][:].rearrange("p (t n) -> p t n", t=16)[:, :, n0:n0 + nn]
                srcv = psR[j][:, 0:n].rearrange("p (n t) -> p t n", t=16)
                evac(dst, srcv, [TpR[j]], [TBU[gi]])
            for c0 in range(0, L, 400):
                bu_body(c0)

            def p1_body(cc, n):
                j = pc[0] % 4; pc[0] += 1
                for tp in range(16):
                    r = BU[gi][:, tp * NCH + cc:tp * NCH + cc + n]
                    S.op("pe", lambda e, tp=tp, r=r: e.matmul(psR[j][:, 0:n], lhsT=RT[g][15 - tp][:], rhs=r, start=(tp == 0), stop=(tp == 15)),
                         reads=[TBU[gi], Tc], writes=[TpR[j]])
                evac(Z[gi][0][:, cc:cc + n], psR[j][:, 0:n], [TpR[j]], [TZ[gi][0]])
            for (cc, n) in coltiles:
                p1_body(cc, n)
            cur = 0
            for kk in range(NLV):
                o = 1 << kk
                if o >= NCH:
                    break
                nxt = 1 - cur
                m = NCH - o

                def lvl(c0, w, cur=cur, nxt=nxt, o=o, kk=kk):
                    j = pc[0] % 4; pc[0] += 1
                    S.op("pe", lambda e: e.matmul(psR[j][:, 0:w], lhsT=RL[g][kk][:], rhs=Z[gi][cur][:, c0:c0 + w], start=True, stop=True),
                         reads=[TZ[gi][cur], Tc], writes=[TpR[j]])
                    S.op("dve", lambda e: e.tensor_tensor(out=Z[gi][nxt][:, o + c0:o + c0 + w], in0=Z[gi][cur][:, o + c0:o + c0 + w],
                                                          in1=psR[j][:, 0:w], op=ALU.add), reads=[TZ[gi][cur], TpR[j]], writes=[TZ[gi][nxt]])
                for c0 in range(0, m, 512):
                    lvl(c0, min(512, m - c0))
                S.op("pool", lambda e, cur=cur, nxt=nxt, o=o: e.tensor_copy(out=Z[gi][nxt][:, 0:o], in_=Z[gi][cur][:, 0:o]),
                     reads=[TZ[gi][cur]], writes=[TZ[gi][nxt]])
                cur = nxt
            S.op("pool", lambda e: e.memset(Spb[gi][:, 0:1], 0.0), reads=[TSp[gi]], writes=[TSp[gi]])
            if NCH > 1:
                S.op("dve", lambda e, cur=cur: e.tensor_copy(out=Spb[gi][:, 1:NCH], in_=Z[gi][cur][:, 0:NCH - 1]),
                     reads=[TZ[gi][cur], TSp[gi]], writes=[TSp[gi]])

        def p2_tau(g, gi, cc, n, tau):
            j = pc[0] % 4; pc[0] += 1
            for tp in range(tau + 1):
                r = BU[gi][:, tp * NCH + cc:tp * NCH + cc + n]
                S.op("pe", lambda e, tp=tp, r=r: e.matmul(psR[j][:, 0:n], lhsT=RT[g][tau - tp][:], rhs=r, start=(tp == 0), stop=False),
                     reads=[TBU[gi], Tc], writes=[TpR[j]])
            S.op("pe", lambda e: e.matmul(psR[j][:, 0:n], lhsT=RT[g][tau + 1][:], rhs=Spb[gi][:, cc:cc + n], start=False, stop=True),
                 reads=[TSp[gi], Tc], writes=[TpR[j]])
            evac(stt[gi][:, tau:16 * n:16], psR[j][:, 0:n], [TpR[j]], [Tst[gi]])

        def p2_y(pr, b, cc, n, x0):
            lo = 16 * cc
            w = min(512, 16 * n - x0)
            jy = yc[0] % 2; yc[0] += 1
            t0 = lo + x0
            r0, r1 = 32 * pr, 32 * pr + 32
            S.dma("sp", usk[jy][r0:r1, 0:w], u_d[r0:r1, b, t0:t0 + w], writes=[Tusk[jy]])
            for gi in range(2):
                g = 2 * pr + gi
                S.op("pe", lambda e, gi=gi, g=g: e.matmul(psY[jy][:, 0:w], lhsT=CS[g][:], rhs=stt[gi][:, x0:x0 + w], start=(gi == 0), stop=(gi == 1)),
                     reads=[Tst[gi], Tc], writes=[TpY[jy]])
            S.op("dve", lambda e: e.scalar_tensor_tensor(out=ysb[jy][r0:r1, 0:w], in0=usk[jy][r0:r1, 0:w], scalar=dsk[r0:r1, 0:1],
                                                         in1=psY[jy][r0:r1, 0:w], op0=ALU.mult, op1=ALU.add),
                 reads=[Tusk[jy], TpY[jy], Tp], writes=[Tys[jy]])
            S.dma("sp", ys_d[r0:r1, b, t0:t0 + w], ysb[jy][r0:r1, 0:w], reads=[Tys[jy]], writes=[Tout])

        si = 0
        for b in range(NBATCH):
            for c0 in range(0, L, 1024):
                n = min(1024, L - c0)
                i = si % 2; si += 1
                S.dma("sp", ustg[i][:, 0:n], u_d[:, b, c0:c0 + n], writes=[Tus[i]])
                S.op("pool", lambda e, i=i, n=n, c0=c0: e.tensor_copy(out=ub[:, c0:c0 + n], in_=ustg[i][:, 0:n]),
                     reads=[Tus[i]], writes=[Tub])
            for pr in range(2):
                for gi in range(2):
                    prep_group(2 * pr + gi, gi)
                for (cc, n) in coltiles:
                    for gi in range(2):
                        for tau in range(16):
                            p2_tau(2 * pr + gi, gi, cc, n, tau)
                    for x0 in range(0, 16 * n, 512):
                        p2_y(pr, b, cc, n, x0)
        S.barrier()
        S.stack = st0
    return Tout


from concourse.bass_utils import run_bass_kernel_spmd

N_META = 16
SEQ = 16384
LTOK = N_META + SEQ
NTOK = 4100
SB_NB = 129
SB_PAD = 112
DN_NC = 257
DN_PAD = 48


def build_mix():
    nc = bass.Bass("TRN2", target_bir_lowering=False)
    di = lambda n, s: nc.dram_tensor(n, list(s), F32, kind="ExternalInput").ap()
    do = lambda n, s: nc.dram_tensor(n, list(s), F32, kind="ExternalOutput").ap()
    LPS = SB_NB * 128
    LPD = DN_NC * 64
    qT = di("sb_q", [64, LPS]); kT = di("sb_k", [64, LPS]); vv = di("sb_v", [128, SB_NB, 64]); oT = do("sb_o", [64, LPS])
    xq = di("dn_xq", [64, LPD + 3]); xk = di("dn_xk", [64, LPD + 3]); xv = di("dn_xv", [64, LPD + 3])
    cw = di("dn_cw", [64, 12]); a_d = di("dn_a", [64, DN_NC]); b_d = di("dn_b", [64, DN_NC]); hp = di("dn_hp", [64, 2])
    dn_o = do("dn_o", [64, DN_NC, 64])
    u_d = di("s5_u", [64, 2, LTOK]); are = di("s5_are", [128, 4]); aim = di("s5_aim", [128, 4]); ldt = di("s5_ldt", [128, 4])
    bre = di("s5_bre", [128, 4, 16]); bim = di("s5_bim", [128, 4, 16]); cre = di("s5_cre", [128, 4, 16]); cim = di("s5_cim", [128, 4, 16])
    dsk = di("s5_dsk", [64, 1]); ys = do("s5_y", [64, 2, LTOK])
    with ExitStack() as st:
        S = Sched(nc, st)
        T1 = sb_phase(S, nc, qT, kT, vv, oT, SB_NB, SB_PAD)
        T2 = dn_phase(S, nc, xq, xk, xv, cw, a_d, b_d, hp, dn_o, DN_NC, DN_PAD)
        T3 = s5_phase2(S, nc, u_d, are, aim, ldt, bre, bim, cre, cim, dsk, ys, LTOK, 2)
        S.finish([T1, T2, T3])
    return nc


def _g8(g):
    return np.ascontiguousarray(np.asarray(g, np.float32).reshape(-1, 128).T)


def _pairlay(x):
    x = np.asarray(x, np.float32)
    y = np.moveaxis(x, 0, 1)
    return np.ascontiguousarray(np.concatenate([y, y], axis=0))


def _c(x):
    return np.ascontiguousarray(x, dtype=np.float32)


def _run(nc, maps):
    res = run_bass_kernel_spmd(nc, maps, core_ids=list(range(8)))
    return res.results


def kernel(**I):
    I = {k: np.asarray(v) for k, v in I.items()}
    x = I["x"]; meta = I["meta_tokens"]
    h = np.concatenate([np.broadcast_to(meta[None], (2, N_META, D)), x], axis=1)
    hT = [_c(h[c // 4, (c % 4) * NTOK:(c % 4 + 1) * NTOK].T) for c in range(8)]
    progs = {}

    def tok_launch(mode, l, hT, mix=None):
        if mode not in progs:
            progs[mode] = build_tok(mode)
        maps = []
        for c in range(8):
            m = {"h_in": hT[c]}
            if mode in ("CA", "C1"):
                b, q = c // 4, c % 4
                sl = slice(q * NTOK, (q + 1) * NTOK)
                m.update(osb=_c(mix["osb"][b][:, sl]), odn=_c(mix["odn"][b][:, sl]), dnz=_c(mix["dnz"][b][:, sl]), ys5=_c(mix["ys5"][b][:, sl]),
                         sbn=_c(np.tile(I["sb_out_norm"][l], 2).reshape(128, 1)), dnn=_c(np.tile(I["dn_out_norm"][l], 2).reshape(128, 1)),
                         wglu=_c(I["s5_w_glu"][l]), bglu=_g8(I["s5_b_glu"][l]), s5n=_g8(I["s5_out_norm"][l]), w_out=_c(I["w_out"][l]),
                         wg2=_c(I["ffn2_w_gate"][l]), wu2=_c(I["ffn2_w_up"][l]), wd2=_c(I["ffn2_w_down"][l]), n2=_g8(I["ffn2_norm"][l]))
            if mode == "CA":
                l2 = l + 1
            else:
                l2 = l
            if mode in ("A0", "CA"):
                m.update(wg1=_c(I["ffn1_w_gate"][l2]), wu1=_c(I["ffn1_w_up"][l2]), wd1=_c(I["ffn1_w_down"][l2]), n1=_g8(I["ffn1_norm"][l2]),
                         w_in=_c(I["w_in"][l2]), nmix=_g8(I["mix_norm"][l2]))
            if mode == "C1":
                m.update(nfin=_g8(I["final_norm"]))
            maps.append(m)
        return _run(progs[mode], maps)

    def mix_launch(l, projT):
        if "mix" not in progs:
            progs["mix"] = build_mix()
        maps = []
        for c in range(8):
            b, hh = c // 4, c % 4
            P = projT[b]
            m = {}
            padz = lambda r, n: _c(np.concatenate([np.zeros((r.shape[0], n), np.float32), r], axis=1))
            m["sb_q"] = padz(P[hh * 64:(hh + 1) * 64], SB_PAD)
            m["sb_k"] = padz(P[256 + hh * 64:256 + (hh + 1) * 64], SB_PAD)
            vT = padz(P[512 + hh * 64:512 + (hh + 1) * 64], SB_PAD)
            m["sb_v"] = _c(vT.T.reshape(SB_NB, 128, 64).transpose(1, 0, 2))
            o = 768
            m["dn_xq"] = padz(P[o + hh * 64:o + (hh + 1) * 64], DN_PAD + 3)
            m["dn_xk"] = padz(P[o + 256 + hh * 64:o + 256 + (hh + 1) * 64], DN_PAD + 3)
            m["dn_xv"] = padz(P[o + 512 + hh * 64:o + 512 + (hh + 1) * 64], DN_PAD + 3)
            cwl = I["dn_conv_w"][l]
            m["dn_cw"] = _c(np.concatenate([cwl[:, hh * 64:(hh + 1) * 64].T, cwl[:, 256 + hh * 64:256 + (hh + 1) * 64].T,
                                            cwl[:, 512 + hh * 64:512 + (hh + 1) * 64].T], axis=1))
            brow = P[1792 + hh]; arow = P[1796 + hh]
            col = lambda r: _c(np.concatenate([np.zeros(DN_PAD, np.float32), r]).reshape(DN_NC, 64).T)
            m["dn_a"] = col(arow); m["dn_b"] = col(brow)
            m["dn_hp"] = _c(np.tile(np.array([[I["dn_a_log"][l, hh], I["dn_dt_bias"][l, hh]]], np.float32), (64, 1)))
            g0 = 4 * c
            m["s5_u"] = _c(np.stack([projT[0][1800 + 64 * c:1800 + 64 * c + 64], projT[1][1800 + 64 * c:1800 + 64 * c + 64]], axis=1))
            m["s5_are"] = _pairlay(I["s5_a_re"][l, g0:g0 + 4]); m["s5_aim"] = _pairlay(I["s5_a_im"][l, g0:g0 + 4])
            m["s5_ldt"] = _pairlay(np.repeat(I["s5_log_dt"][l, g0:g0 + 4][:, None], 64, 1))
            m["s5_bre"] = _pairlay(I["s5_b_re"][l, g0:g0 + 4]); m["s5_bim"] = _pairlay(I["s5_b_im"][l, g0:g0 + 4])
            m["s5_cre"] = _pairlay(I["s5_c_re"][l, g0:g0 + 4].transpose(0, 2, 1)); m["s5_cim"] = _pairlay(I["s5_c_im"][l, g0:g0 + 4].transpose(0, 2, 1))
            m["s5_dsk"] = _c(I["s5_d"][l, 64 * c:64 * c + 64].reshape(64, 1))
            maps.append(m)
        res = _run(progs["mix"], maps)
        osb = [np.zeros((256, LTOK), np.float32) for _ in range(2)]
        odn = [np.zeros((256, LTOK), np.float32) for _ in range(2)]
        ys5 = [np.zeros((512, LTOK), np.float32) for _ in range(2)]
        for c in range(8):
            b, hh = c // 4, c % 4
            osb[b][hh * 64:(hh + 1) * 64] = res[c]["sb_o"][:, SB_PAD:]
            od = res[c]["dn_o"].transpose(1, 0, 2).reshape(DN_NC * 64, 64)[DN_PAD:]
            odn[b][hh * 64:(hh + 1) * 64] = od.T
            for bb in range(2):
                ys5[bb][64 * c:64 * c + 64] = res[c]["s5_y"][:, bb]
        dnz = [projT[b][1536:1792] for b in range(2)]
        return {"osb": osb, "odn": odn, "dnz": dnz, "ys5": ys5}

    def gather_proj(res):
        projT = [np.zeros((INW, LTOK), np.float32) for _ in range(2)]
        for c in range(8):
            b, q = c // 4, c % 4
            projT[b][:, q * NTOK:(q + 1) * NTOK] = res[c]["proj"][:INW]
        return projT

    r = tok_launch("A0", 0, hT)
    hT = [r[c]["h_out"] for c in range(8)]
    mix = mix_launch(0, gather_proj(r))
    r = tok_launch("CA", 0, hT, mix)
    hT = [r[c]["h_out"] for c in range(8)]
    mix = mix_launch(1, gather_proj(r))
    r = tok_launch("C1", 1, hT, mix)
    out = np.zeros((2, LTOK, D), np.float32)
    for c in range(8):
        b, q = c // 4, c % 4
        out[b, q * NTOK:(q + 1) * NTOK] = r[c]["y_out"].T
    return np.ascontiguousarray(out[:, N_META:])
```

```python
from contextlib import ExitStack
import numpy as np
import concourse.bass as bass
import concourse.mybir as mybir

F32 = mybir.dt.float32
BF16 = mybir.dt.bfloat16
AF = mybir.ActivationFunctionType
ALU = mybir.AluOpType
AX = mybir.AxisListType

N_DMA_SEMS = 24


class Trk:
    __slots__ = ("name", "w", "r")

    def __init__(self, name=""):
        self.name = name
        self.w = None
        self.r = []


class Sched:
    ENG = ("pe", "act", "dve", "pool", "sp")

    def __init__(self, nc, stack):
        self.nc = nc
        self.stack = stack
        self.prog = {e: [] for e in self.ENG}
        self.cnt = {e: 0 for e in ("pe", "act", "dve", "pool")}
        self.sems = {}
        for e in ("pe", "act", "dve", "pool"):
            self.sems[e] = stack.enter_context(nc.semaphore("s_" + e))
        self.dsems = [stack.enter_context(nc.semaphore("d%d" % i)) for i in range(N_DMA_SEMS)]
        self.dcnt = [0] * N_DMA_SEMS
        self.dnext = 0
        self.seen = {e: {} for e in self.ENG}
        self.n_wait = 0

    def sb(self, name, shape, dt=F32):
        self.uid = getattr(self, "uid", 0) + 1
        if not hasattr(self, "names"):
            self.names = {}
        self.names[name] = "%s_%d" % (name, self.uid)
        return self.stack.enter_context(self.nc.sbuf_tensor("%s_%d" % (name, self.uid), list(shape), dt))

    def ps(self, name, shape, dt=F32):
        self.uid = getattr(self, "uid", 0) + 1
        return self.stack.enter_context(self.nc.psum_tensor("%s_%d" % (name, self.uid), list(shape), dt))

    def _semobj(self, key):
        return self.sems[key] if isinstance(key, str) else self.dsems[key]

    def _need(self, eng, ev, waits):
        if ev is None:
            return
        key, val, src = ev
        if eng == "pe" and src == "pe":
            return
        if self.seen[eng].get(key, 0) >= val:
            return
        waits[key] = max(waits.get(key, 0), val)

    def _collect(self, eng, reads, writes):
        waits = {}
        for t in reads:
            self._need(eng, t.w, waits)
        for t in writes:
            self._need(eng, t.w, waits)
            for ev in t.r:
                self._need(eng, ev, waits)
        for key, val in waits.items():
            self.seen[eng][key] = val
        return list(waits.items())

    def _record(self, ev, reads, writes):
        for t in reads:
            t.r.append(ev)
            if len(t.r) > 64:
                best = {}
                for k, v, s in t.r:
                    if k not in best or best[k][1] < v:
                        best[k] = (k, v, s)
                t.r = list(best.values())
        for t in writes:
            t.w = ev
            t.r = []

    def op(self, eng, fn, reads=(), writes=()):
        waits = self._collect(eng, reads, writes)
        self.cnt[eng] += 1
        n = self.cnt[eng]
        sem = self.sems[eng]
        wl = [(self._semobj(k), v) for k, v in waits]
        self.n_wait += len(wl)

        def emit(e, wl=wl, fn=fn, sem=sem):
            for s, v in wl:
                e.wait_ge(s, v)
            fn(e).then_inc(sem, 1)
        self.prog[eng].append(emit)
        self._record((eng, n, eng), reads, writes)

    def dma(self, q, out, in_, reads=(), writes=(), **kw):
        i = self.dnext
        self.dnext = (self.dnext + 1) % N_DMA_SEMS
        waits = dict(self._collect(q, reads, writes))
        if self.dcnt[i] > 0 and self.seen[q].get(i, 0) < self.dcnt[i]:
            waits[i] = max(waits.get(i, 0), self.dcnt[i])
            self.seen[q][i] = self.dcnt[i]
        self.dcnt[i] += 16
        val = self.dcnt[i]
        sem = self.dsems[i]
        wl = [(self._semobj(k), v) for k, v in waits.items()]

        def emit(e, wl=wl, sem=sem, out=out, in_=in_, kw=kw):
            for s, v in wl:
                e.wait_ge(s, v)
            e.dma_start(out=out, in_=in_, **kw).then_inc(sem, 16)
        self.prog[q].append(emit)
        self._record((i, val, "dma"), reads, writes)

    def barrier(self):
        for eng in self.ENG:
            waits = {}
            for e in ("pe", "act", "dve", "pool"):
                if e != eng and self.cnt[e] > 0 and self.seen[eng].get(e, 0) < self.cnt[e]:
                    waits[e] = self.cnt[e]
            for i in range(N_DMA_SEMS):
                if self.dcnt[i] > 0 and self.seen[eng].get(i, 0) < self.dcnt[i]:
                    waits[i] = self.dcnt[i]
            for k, v in waits.items():
                self.seen[eng][k] = v
            wl = [(self._semobj(k), v) for k, v in waits.items()]

            def emit(e, wl=wl):
                for s, v in wl:
                    e.wait_ge(s, v)
            self.prog[eng].append(emit)

    def finish(self, out_trackers):
        nc = self.nc
        waits = {}
        for t in out_trackers:
            self._need("sp", t.w, waits)
        for i in range(N_DMA_SEMS):
            if self.dcnt[i] > 0 and self.seen["sp"].get(i, 0) < self.dcnt[i]:
                waits[i] = max(waits.get(i, 0), self.dcnt[i])
        for e in ("pe", "act", "dve", "pool"):
            if self.cnt[e] > 0:
                waits[e] = max(waits.get(e, 0), self.cnt[e])
        wl = [(self._semobj(k), v) for k, v in waits.items()]

        def emit(e, wl=wl):
            for s, v in wl:
                e.wait_ge(s, v)
        self.prog["sp"].append(emit)

        prog = self.prog
        with nc.Block() as block:
            @block.tensor
            def _(e):
                for f in prog["pe"]:
                    f(e)

            @block.scalar
            def _(e):
                for f in prog["act"]:
                    f(e)

            @block.vector
            def _(e):
                for f in prog["dve"]:
                    f(e)

            @block.gpsimd
            def _(e):
                for f in prog["pool"]:
                    f(e)

            @block.sync
            def _(e):
                for f in prog["sp"]:
                    f(e)

from contextlib import ExitStack

D = 1024
KD = 8
DFF = 2816
KF = 22
INW = 2312
INP = 2432
KP = 19
EPS = 1e-6
WCOLS = 22528
STG = 1408


def build_tok(mode, NT=4100, TT=205):
    assert NT % TT == 0
    NTILE = NT // TT
    nc = bass.Bass("TRN2", target_bir_lowering=False)

    def din(name, shape):
        return nc.dram_tensor(name, list(shape), F32, kind="ExternalInput").ap()

    def dout(name, shape):
        return nc.dram_tensor(name, list(shape), F32, kind="ExternalOutput").ap()

    def dint(name, shape):
        return nc.dram_tensor(name, list(shape), F32, kind="Internal").ap()

    h_in = din("h_in", [D, NT])
    do_epi = mode in ("CA", "C1")
    do_a = mode in ("A0", "CA")
    do_fin = mode == "C1"
    if do_epi:
        osb = din("osb", [256, NT]); odn = din("odn", [256, NT]); dnz = din("dnz", [256, NT])
        ys5 = din("ys5", [512, NT])
        sbn = din("sbn", [128, 1]); dnn = din("dnn", [128, 1])
        wglu = din("wglu", [512, 512]); bglu = din("bglu", [128, 4]); s5n = din("s5n", [128, 4])
        w_out = din("w_out", [D, D])
        wg2 = din("wg2", [D, DFF]); wu2 = din("wu2", [D, DFF]); wd2 = din("wd2", [DFF, D]); n2 = din("n2", [128, KD])
        hs_a = dint("hs_a", [D, NT])
    if do_a:
        wg1 = din("wg1", [D, DFF]); wu1 = din("wu1", [D, DFF]); wd1 = din("wd1", [DFF, D]); n1 = din("n1", [128, KD])
        w_in = din("w_in", [D, INW]); nmix = din("nmix", [128, KD])
        h_out = dout("h_out", [D, NT])
        proj = dout("proj", [INP, NT])
        if do_epi:
            hs_b = dint("hs_b", [D, NT])
    if do_fin:
        nfin = din("nfin", [128, KD])
        y_out = dout("y_out", [D, NT])

    out_trks = []
    with ExitStack() as st:
        S = Sched(nc, st)
        WA = S.sb("WA", [128, WCOLS], BF16)
        WB = S.sb("WB", [128, WCOLS], BF16)
        WC = S.sb("WC", [128, WCOLS], BF16)
        NSTG = 3
        stage = [S.sb("stg%d" % i, [128, STG]) for i in range(NSTG)]
        Tstage = [Trk("stg%d" % i) for i in range(NSTG)]
        sidx = [0]
        ones_bf = S.sb("ones_bf", [128, 128], BF16)
        blk_bf = S.sb("blk_bf", [128, 128], BF16)
        gains = S.sb("gains", [128, 64])
        Tconst = Trk("const")
        Tg = Trk("gains")
        S.op("pool", lambda e: e.memset(ones_bf[:], 1.0), writes=[Tconst])
        S.op("pool", lambda e: e.memset(blk_bf[:], 0.0), writes=[Tconst])
        S.op("pool", lambda e: e.memset(blk_bf[0:64, 0:64], 1.0), writes=[Tconst])
        S.op("pool", lambda e: e.memset(blk_bf[64:128, 64:128], 1.0), writes=[Tconst])
        if do_a:
            S.dma("sp", gains[:, 0:8], n1[:, :], writes=[Tg])
            S.dma("sp", gains[:, 8:16], nmix[:, :], writes=[Tg])
        if do_epi:
            S.dma("sp", gains[:, 16:24], n2[:, :], writes=[Tg])
            S.dma("sp", gains[:, 32:33], sbn[:, :], writes=[Tg])
            S.dma("sp", gains[:, 33:34], dnn[:, :], writes=[Tg])
            S.dma("sp", gains[:, 34:38], bglu[:, :], writes=[Tg])
            S.dma("sp", gains[:, 38:42], s5n[:, :], writes=[Tg])
        if do_fin:
            S.dma("sp", gains[:, 24:32], nfin[:, :], writes=[Tg])

        TW = {"A": [Trk("WA%d" % k) for k in range(KF)], "B": [Trk("WB%d" % k) for k in range(KF)],
              "C": [Trk("WC%d" % k) for k in range(KF)]}

        def load_w(wtile, trk, dst_c0, src_ap, ncols, scale_ap):
            for c0 in range(0, ncols, STG):
                n = min(STG, ncols - c0)
                i = sidx[0] % NSTG; sidx[0] += 1
                stg = stage[i]
                ceng = ("pool", "dve", "act")[i] if scale_ap is None else ("pool", "dve", "dve")[i]
                S.dma("sp", stg[:, 0:n], src_ap[:, c0:c0 + n], writes=[Tstage[i]])
                dst = wtile[:, dst_c0 + c0: dst_c0 + c0 + n]
                if scale_ap is None:
                    if ceng == "act":
                        S.op("act", lambda e, dst=dst, stg=stg, n=n: e.copy(out=dst, in_=stg[:, 0:n]),
                             reads=[Tstage[i]], writes=[trk])
                    else:
                        S.op(ceng, lambda e, dst=dst, stg=stg, n=n: e.tensor_copy(out=dst, in_=stg[:, 0:n]),
                             reads=[Tstage[i]], writes=[trk])
                else:
                    S.op(ceng, lambda e, dst=dst, stg=stg, n=n, sc=scale_ap: e.tensor_scalar(
                        out=dst, in0=stg[:, 0:n], scalar1=sc, scalar2=1.0, op0=ALU.mult, op1=ALU.mult),
                        reads=[Tstage[i], Tg], writes=[trk])

        def load_ffn_weights(wg, wu, wd, gcol):
            for k in range(KD):
                load_w(WA, TW["A"][k], k * DFF, wg[k * 128:(k + 1) * 128, :], DFF, gains[:, gcol + k: gcol + k + 1])
                load_w(WB, TW["B"][k], k * DFF, wu[k * 128:(k + 1) * 128, :], DFF, gains[:, gcol + k: gcol + k + 1])
            for f in range(KF):
                load_w(WC, TW["C"][f], f * D, wd[f * 128:(f + 1) * 128, :], D, None)

        def tok_view(ap, nch):
            return ap.rearrange("(k p) n -> p k n", p=128)

        def norm_stats(ph, src_tile, Tsrc, nk, lhs_ones, inv_n, sq, Tsq, ps_stat, Tps, lnv, Tln, rstd, Trs):
            S.op("act", lambda e: e.activation(out=sq[:, 0:nk, :], in_=src_tile[:, 0:nk, :], func=AF.Square),
                 reads=[Tsrc], writes=[Tsq])
            for k in range(nk):
                S.op("pe", lambda e, k=k: e.matmul(ps_stat[:, 0:TT], lhsT=lhs_ones[:], rhs=sq[:, k, :],
                                                    start=(k == 0), stop=(k == nk - 1)),
                     reads=[Tsq, Tconst], writes=[Tps])
            S.op("act", lambda e: e.activation(out=lnv[:], in_=ps_stat[:, 0:TT], func=AF.Ln, scale=inv_n, bias=eps_t[:, 0:1]),
                 reads=[Tps, Tconst], writes=[Tln])
            S.op("act", lambda e: e.activation(out=rstd[:], in_=lnv[:], func=AF.Exp, scale=-0.5),
                 reads=[Tln], writes=[Trs])

        eps_t = S.sb("eps_t", [128, 1])
        S.op("pool", lambda e: e.memset(eps_t[:], EPS), writes=[Tconst])

        def phase_ffn(src, dst, wg, wu, wd, gcol, fin_gcol=None, fin_dst=None):
            load_ffn_weights(wg, wu, wd, gcol)
            with ExitStack() as ps_:
                S.stack = ps_
                hb = [S.sb("hb%d" % i, [128, KD, TT]) for i in range(2)]
                Th = [Trk("hb%d" % i) for i in range(2)]
                xn = [S.sb("xn%d" % i, [128, KD, TT], BF16) for i in range(2)]
                Txn = [Trk("xn%d" % i) for i in range(2)]
                sq = S.sb("sq", [128, KD, TT], BF16); Tsq = Trk("sq")
                lnv = S.sb("lnv", [128, TT]); Tln = Trk("lnv")
                rstd = S.sb("rstd", [128, TT]); Trs = Trk("rstd")
                hmid = S.sb("hmid", [128, KF, TT], BF16)
                Thm = [Trk("hm%d" % f) for f in range(KF)]
                sgt = [S.sb("sgt%d" % i, [128, TT]) for i in range(2)]
                Tsg = [Trk("sgt%d" % i) for i in range(2)]
                ps_stat = S.ps("ps_stat", [128, 512]); Tps = Trk("ps_stat")
                psg = [S.ps("psg%d" % i, [128, 512]) for i in range(2)]
                psu = [S.ps("psu%d" % i, [128, 512]) for i in range(2)]
                psd = [S.ps("psd%d" % i, [128, 512]) for i in range(2)]
                Tpg = [Trk() for i in range(2)]; Tpu = [Trk() for i in range(2)]; Tpd = [Trk() for i in range(2)]
                if fin_dst is not None:
                    ob = S.sb("ob", [128, KD, TT]); Tob = Trk("ob")
                srcv = tok_view(src, KD); dstv = tok_view(dst, KD) if dst is not None else None
                finv = tok_view(fin_dst, KD) if fin_dst is not None else None
                Tdst = Trk("dst")

                def pre(t):
                    i = t % 2
                    c0 = t * TT
                    S.dma("sp", hb[i][:], srcv[:, :, c0:c0 + TT], writes=[Th[i]])
                    norm_stats(None, hb[i], Th[i], KD, ones_bf, 1.0 / D, sq, Tsq, ps_stat, Tps, lnv, Tln, rstd, Trs)
                    S.op("dve", lambda e, i=i: e.tensor_tensor(
                        out=xn[i][:], in0=hb[i][:], in1=rstd[:].unsqueeze(1).broadcast_to([128, KD, TT]), op=ALU.mult),
                        reads=[Th[i], Trs], writes=[Txn[i]])

                gcnt = [0]

                def gateup(t):
                    i = t % 2
                    for f in range(KF):
                        j = gcnt[0] % 2; gcnt[0] += 1
                        for k in range(KD):
                            S.op("pe", lambda e, k=k, f=f, j=j: e.matmul(
                                psg[j][:, 0:TT], lhsT=WA[:, k * DFF + f * 128: k * DFF + (f + 1) * 128],
                                rhs=xn[i][:, k, :], start=(k == 0), stop=(k == KD - 1)),
                                reads=[TW["A"][k], Txn[i]], writes=[Tpg[j]])
                        for k in range(KD):
                            S.op("pe", lambda e, k=k, f=f, j=j: e.matmul(
                                psu[j][:, 0:TT], lhsT=WB[:, k * DFF + f * 128: k * DFF + (f + 1) * 128],
                                rhs=xn[i][:, k, :], start=(k == 0), stop=(k == KD - 1)),
                                reads=[TW["B"][k], Txn[i]], writes=[Tpu[j]])
                        S.op("act", lambda e, j=j: e.activation(out=sgt[j][:], in_=psg[j][:, 0:TT], func=AF.Silu),
                             reads=[Tpg[j]], writes=[Tsg[j]])
                        S.op("dve", lambda e, j=j, f=f: e.tensor_tensor(
                            out=hmid[:, f, :], in0=sgt[j][:], in1=psu[j][:, 0:TT], op=ALU.mult),
                            reads=[Tsg[j], Tpu[j]], writes=[Thm[f]])

                dcnt = [0]

                def down(t):
                    i = t % 2
                    c0 = t * TT
                    for dc in range(KD):
                        j = dcnt[0] % 2; dcnt[0] += 1
                        for f in range(KF):
                            S.op("pe", lambda e, f=f, dc=dc, j=j: e.matmul(
                                psd[j][:, 0:TT], lhsT=WC[:, f * D + dc * 128: f * D + (dc + 1) * 128],
                                rhs=hmid[:, f, :], start=(f == 0), stop=(f == KF - 1)),
                                reads=[TW["C"][f], Thm[f]], writes=[Tpd[j]])
                        S.op("dve", lambda e, dc=dc, j=j, i=i: e.scalar_tensor_tensor(
                            out=hb[i][:, dc, :], in0=psd[j][:, 0:TT], scalar=0.5, in1=hb[i][:, dc, :],
                            op0=ALU.mult, op1=ALU.add),
                            reads=[Tpd[j], Th[i]], writes=[Th[i]])
                    if dstv is not None:
                        S.dma("sp", dstv[:, :, c0:c0 + TT], hb[i][:], reads=[Th[i]], writes=[Tdst])
                    if fin_dst is not None:
                        norm_stats(None, hb[i], Th[i], KD, ones_bf, 1.0 / D, sq, Tsq, ps_stat, Tps, lnv, Tln, rstd, Trs)
                        for k in range(KD):
                            S.op("dve", lambda e, k=k, i=i: e.scalar_tensor_tensor(
                                out=ob[:, k, :], in0=hb[i][:, k, :], scalar=gains[:, fin_gcol + k: fin_gcol + k + 1],
                                in1=rstd[:], op0=ALU.mult, op1=ALU.mult),
                                reads=[Th[i], Trs, Tg], writes=[Tob])
                        S.dma("sp", finv[:, :, c0:c0 + TT], ob[:], reads=[Tob], writes=[Tdst])

                pre(0)
                for t in range(NTILE):
                    gateup(t)
                    if t + 1 < NTILE:
                        pre(t + 1)
                    down(t)
                S.barrier()
                S.stack = st
            return Tdst

        def phase_proj(src, dstp, w, gcol):
            for k in range(KD):
                load_w(WA, TW["A"][k], k * INP, w[k * 128:(k + 1) * 128, :], INW, gains[:, gcol + k: gcol + k + 1])
            with ExitStack() as ps_:
                S.stack = ps_
                hb = [S.sb("hb%d" % i, [128, KD, TT]) for i in range(2)]
                Th = [Trk() for i in range(2)]
                xn = [S.sb("xn%d" % i, [128, KD, TT], BF16) for i in range(2)]
                Txn = [Trk() for i in range(2)]
                sq = S.sb("sq", [128, KD, TT], BF16); Tsq = Trk("sq")
                lnv = S.sb("lnv", [128, TT]); Tln = Trk("lnv")
                rstd = S.sb("rstd", [128, TT]); Trs = Trk("rstd")
                obt = [S.sb("obt%d" % i, [128, KP, TT]) for i in range(2)]
                Tobt = [Trk() for i in range(2)]
                for i_ in range(2):
                    S.op("pool", lambda e, i_=i_: e.memset(obt[i_][:, KP - 1, :], 0.0), writes=[Tobt[i_]])
                dstv3 = dstp.rearrange("(c p) n -> p c n", p=128)
                ps_stat = S.ps("ps_stat", [128, 512]); Tps = Trk("ps_stat")
                pp = [S.ps("pp%d" % i, [128, 512]) for i in range(4)]
                Tpp = [Trk() for i in range(4)]
                srcv = tok_view(src, KD)
                Tdst = Trk("projdst")
                cntb = [0]

                def ld(t):
                    i = t % 2
                    c0 = t * TT
                    S.dma("sp", hb[i][:], srcv[:, :, c0:c0 + TT], writes=[Th[i]])

                def pre(t):
                    i = t % 2
                    norm_stats(None, hb[i], Th[i], KD, ones_bf, 1.0 / D, sq, Tsq, ps_stat, Tps, lnv, Tln, rstd, Trs)
                    S.op("dve", lambda e, i=i: e.tensor_tensor(
                        out=xn[i][:], in0=hb[i][:], in1=rstd[:].unsqueeze(1).broadcast_to([128, KD, TT]), op=ALU.mult),
                        reads=[Th[i], Trs], writes=[Txn[i]])

                def body(t):
                    i = t % 2
                    c0 = t * TT
                    if t + 1 < NTILE:
                        ld(t + 1)
                    for pc in range(KP):
                        if pc == KP // 2 and t + 1 < NTILE:
                            pre(t + 1)
                        cnt = cntb[0]
                        j = cnt % 4; cnt += 1; cntb[0] = cnt
                        m = min(128, INW - pc * 128)
                        for k in range(KD):
                            S.op("pe", lambda e, k=k, pc=pc, j=j, m=m: e.matmul(
                                pp[j][0:m, 0:TT], lhsT=WA[:, k * INP + pc * 128: k * INP + pc * 128 + m],
                                rhs=xn[i][:, k, :], start=(k == 0), stop=(k == KD - 1)),
                                reads=[TW["A"][k], Txn[i]], writes=[Tpp[j]])
                        eng = "act" if (cnt % 2 == 0) else "dve"
                        if eng == "act":
                            S.op("act", lambda e, j=j, m=m, pc=pc: e.copy(out=obt[i][0:m, pc, :], in_=pp[j][0:m, 0:TT]),
                                 reads=[Tpp[j]], writes=[Tobt[i]])
                        else:
                            S.op("dve", lambda e, j=j, m=m, pc=pc: e.tensor_copy(out=obt[i][0:m, pc, :], in_=pp[j][0:m, 0:TT]),
                                 reads=[Tpp[j]], writes=[Tobt[i]])
                    S.dma("sp", dstv3[:, :, c0:c0 + TT], obt[i][:], reads=[Tobt[i]], writes=[Tdst])
                ld(0)
                pre(0)
                for t in range(NTILE):
                    body(t)
                S.barrier()
                S.stack = st
            return Tdst

        def phase_epi(src, dst):
            for k in range(KD):
                load_w(WA, TW["A"][k], k * D, w_out[k * 128:(k + 1) * 128, :], D, None)
            for k in range(4):
                load_w(WB, TW["B"][k], k * 512, wglu[k * 128:(k + 1) * 128, :], 512, None)
            with ExitStack() as ps_:
                S.stack = ps_
                hb = [S.sb("hb%d" % i, [128, KD, TT]) for i in range(2)]
                Th = [Trk() for i in range(2)]
                xin = [S.sb("xin%d" % i, [128, 10, TT]) for i in range(2)]
                Tx = [Trk() for i in range(2)]
                mixed = S.sb("mixed", [128, KD, TT], BF16); Tmx = Trk("mixed")
                sq = S.sb("sq", [128, 4, TT], BF16); Tsq = Trk("sq")
                lnv = S.sb("lnv", [128, TT]); Tln = Trk("lnv")
                rstd = S.sb("rstd", [128, TT]); Trs = Trk("rstd")
                t1 = S.sb("t1", [128, 4, TT]); Tt1 = Trk("t1")
                t2 = S.sb("t2", [128, 4, TT]); Tt2 = Trk("t2")
                ge = S.sb("ge", [128, 4, TT]); Tge = Trk("ge")
                geb = S.sb("geb", [128, 4, TT], BF16); Tgeb = Trk("geb")
                vv = S.sb("vv", [128, 4, TT]); Tvv = Trk("vv")
                sg = S.sb("sg", [128, TT]); Tsgm = Trk("sg")
                ps_stat = S.ps("ps_stat", [128, 512]); Tps = Trk("ps_stat")
                pq = [S.ps("pq%d" % i, [128, 512]) for i in range(2)]
                Tpq = [Trk() for i in range(2)]
                srcv = tok_view(src, KD); dstv = tok_view(dst, KD)
                osbv = tok_view(osb, 2); odnv = tok_view(odn, 2); dnzv = tok_view(dnz, 2); ys5v = tok_view(ys5, 4)
                Tdst = Trk("epidst")
                cntb = [0]

                def ld(t):
                    i = t % 2
                    c0 = t * TT
                    S.dma("sp", hb[i][:], srcv[:, :, c0:c0 + TT], writes=[Th[i]])
                    S.dma("sp", xin[i][:, 0:2, :], osbv[:, :, c0:c0 + TT], writes=[Tx[i]])
                    S.dma("sp", xin[i][:, 2:4, :], odnv[:, :, c0:c0 + TT], writes=[Tx[i]])
                    S.dma("sp", xin[i][:, 4:6, :], dnzv[:, :, c0:c0 + TT], writes=[Tx[i]])
                    S.dma("sp", xin[i][:, 6:10, :], ys5v[:, :, c0:c0 + TT], writes=[Tx[i]])

                def body(t):
                    i = t % 2
                    c0 = t * TT
                    if t == 0:
                        ld(0)
                    if t + 1 < NTILE:
                        ld(t + 1)
                    X = xin[i]
                    for c in range(2):
                        S.op("act", lambda e, c=c: e.activation(out=sq[:, 0, :], in_=X[:, c, :], func=AF.Square),
                             reads=[Tx[i]], writes=[Tsq])
                        S.op("pe", lambda e: e.matmul(ps_stat[:, 0:TT], lhsT=blk_bf[:], rhs=sq[:, 0, :], start=True, stop=True),
                             reads=[Tsq, Tconst], writes=[Tps])
                        S.op("act", lambda e: e.activation(out=lnv[:], in_=ps_stat[:, 0:TT], func=AF.Ln, scale=1.0 / 64, bias=eps_t[:, 0:1]),
                             reads=[Tps, Tconst], writes=[Tln])
                        S.op("act", lambda e: e.activation(out=rstd[:], in_=lnv[:], func=AF.Exp, scale=-0.5),
                             reads=[Tln], writes=[Trs])
                        S.op("dve", lambda e, c=c: e.scalar_tensor_tensor(
                            out=mixed[:, c, :], in0=X[:, c, :], scalar=gains[:, 32:33], in1=rstd[:], op0=ALU.mult, op1=ALU.mult),
                            reads=[Tx[i], Trs, Tg], writes=[Tmx])
                    for c in range(2):
                        S.op("act", lambda e, c=c: e.activation(out=sq[:, 0, :], in_=X[:, 2 + c, :], func=AF.Square),
                             reads=[Tx[i]], writes=[Tsq])
                        S.op("pe", lambda e: e.matmul(ps_stat[:, 0:TT], lhsT=blk_bf[:], rhs=sq[:, 0, :], start=True, stop=True),
                             reads=[Tsq, Tconst], writes=[Tps])
                        S.op("act", lambda e: e.activation(out=lnv[:], in_=ps_stat[:, 0:TT], func=AF.Ln, scale=1.0 / 64, bias=eps_t[:, 0:1]),
                             reads=[Tps, Tconst], writes=[Tln])
                        S.op("act", lambda e: e.activation(out=rstd[:], in_=lnv[:], func=AF.Exp, scale=-0.5),
                             reads=[Tln], writes=[Trs])
                        S.op("dve", lambda e, c=c: e.scalar_tensor_tensor(
                            out=t1[:, 0, :], in0=X[:, 2 + c, :], scalar=gains[:, 33:34], in1=rstd[:], op0=ALU.mult, op1=ALU.mult),
                            reads=[Tx[i], Trs, Tg], writes=[Tt1])
                        S.op("act", lambda e, c=c: e.activation(out=t2[:, 0, :], in_=X[:, 4 + c, :], func=AF.Silu),
                             reads=[Tx[i]], writes=[Tt2])
                        S.op("dve", lambda e, c=c: e.tensor_tensor(out=mixed[:, 2 + c, :], in0=t1[:, 0, :], in1=t2[:, 0, :], op=ALU.mult),
                             reads=[Tt1, Tt2], writes=[Tmx])
                    Y = X[:, 6:10, :]
                    S.op("act", lambda e, Y=Y: e.activation(out=t1[:], in_=Y, func=AF.Square), reads=[Tx[i]], writes=[Tt1])
                    S.op("dve", lambda e: e.tensor_scalar(out=t1[:], in0=t1[:], scalar1=0.044715, scalar2=1.0, op0=ALU.mult, op1=ALU.add),
                         reads=[Tt1], writes=[Tt1])
                    S.op("dve", lambda e, Y=Y: e.tensor_tensor(out=t2[:], in0=t1[:], in1=Y, op=ALU.mult),
                         reads=[Tt1, Tx[i]], writes=[Tt2])
                    S.op("act", lambda e: e.activation(out=t1[:], in_=t2[:], func=AF.Tanh, scale=0.7978845608028654),
                         reads=[Tt2], writes=[Tt1])
                    S.op("dve", lambda e, Y=Y: e.scalar_tensor_tensor(out=t2[:], in0=t1[:], scalar=1.0, in1=Y, op0=ALU.add, op1=ALU.mult),
                         reads=[Tt1, Tx[i]], writes=[Tt2])
                    S.op("dve", lambda e: e.tensor_scalar(out=ge[:], in0=t2[:], scalar1=0.5, scalar2=None, op0=ALU.mult),
                         reads=[Tt2], writes=[Tge])
                    S.op("act", lambda e: e.copy(out=geb[:], in_=ge[:]), reads=[Tge], writes=[Tgeb])
                    for co in range(4):
                        j = cntb[0] % 2; cntb[0] += 1
                        for ki in range(4):
                            S.op("pe", lambda e, ki=ki, co=co, j=j: e.matmul(
                                pq[j][:, 0:TT], lhsT=WB[:, ki * 512 + co * 128: ki * 512 + (co + 1) * 128],
                                rhs=geb[:, ki, :], start=(ki == 0), stop=(ki == 3)),
                                reads=[TW["B"][ki], Tgeb], writes=[Tpq[j]])
                        S.op("act", lambda e, co=co, j=j: e.activation(out=sg[:], in_=pq[j][:, 0:TT], func=AF.Sigmoid,
                                                                        bias=gains[:, 34 + co: 35 + co]),
                             reads=[Tpq[j], Tg], writes=[Tsgm])
                        S.op("dve", lambda e, co=co: e.tensor_tensor(out=vv[:, co, :], in0=ge[:, co, :], in1=sg[:], op=ALU.mult),
                             reads=[Tge, Tsgm], writes=[Tvv])
                    norm_stats(None, vv, Tvv, 4, ones_bf, 1.0 / 512, sq, Tsq, ps_stat, Tps, lnv, Tln, rstd, Trs)
                    for c in range(4):
                        S.op("dve", lambda e, c=c: e.scalar_tensor_tensor(
                            out=mixed[:, 4 + c, :], in0=vv[:, c, :], scalar=gains[:, 38 + c: 39 + c], in1=rstd[:],
                            op0=ALU.mult, op1=ALU.mult),
                            reads=[Tvv, Trs, Tg], writes=[Tmx])
                    for dc in range(KD):
                        j = cntb[0] % 2; cntb[0] += 1
                        for k in range(KD):
                            S.op("pe", lambda e, k=k, dc=dc, j=j: e.matmul(
                                pq[j][:, 0:TT], lhsT=WA[:, k * D + dc * 128: k * D + (dc + 1) * 128],
                                rhs=mixed[:, k, :], start=(k == 0), stop=(k == KD - 1)),
                                reads=[TW["A"][k], Tmx], writes=[Tpq[j]])
                        S.op("dve", lambda e, dc=dc, j=j, i=i: e.tensor_tensor(
                            out=hb[i][:, dc, :], in0=hb[i][:, dc, :], in1=pq[j][:, 0:TT], op=ALU.add),
                            reads=[Tpq[j], Th[i]], writes=[Th[i]])
                    S.dma("sp", dstv[:, :, c0:c0 + TT], hb[i][:], reads=[Th[i]], writes=[Tdst])
                for t in range(NTILE):
                    body(t)
                S.barrier()
                S.stack = st
            return Tdst

        cur = h_in
        if do_epi:
            phase_epi(cur, hs_a)
            cur = hs_a
            if do_fin:
                Td = phase_ffn(cur, None, wg2, wu2, wd2, 16, fin_gcol=24, fin_dst=y_out)
                out_trks.append(Td)
            else:
                phase_ffn(cur, hs_b, wg2, wu2, wd2, 16)
                cur = hs_b
        if do_a:
            Td = phase_ffn(cur, h_out, wg1, wu1, wd1, 0)
            out_trks.append(Td)
            Tp = phase_proj(h_out, proj, w_in, 8)
            out_trks.append(Tp)
        S.finish(out_trks)
    return nc

import math
from contextlib import ExitStack

EPS = 1e-6


def sb_phase(S, nc, qT_d, kT_d, v_d, oT_d, NB, PADK):
    import os
    SB_DUMMY = int(os.environ.get("SB_DUMMY", "4"))
    SB_DN = int(os.environ.get("SB_DN", "512"))
    st0 = S.stack
    with ExitStack() as ps_:
        S.stack = ps_
        LP = NB * 128
        Tc = Trk("sbconst")
        qb = S.sb("qb", [64, LP], BF16); Tq = Trk("qb")
        kb = S.sb("kb", [64, LP], BF16); Tk = Trk("kb")
        vb = S.sb("vb", [128, NB * 64], BF16); Tv = Trk("vb")
        stg = [S.sb("sbstg%d" % i, [128, 2048]) for i in range(2)]
        Tstg = [Trk() for i in range(2)]
        si = 0
        for c0 in range(0, LP, 2048):
            n = min(2048, LP - c0)
            for (src, dst, Td, sc) in ((qT_d, qb, Tq, 1.0), (kT_d, kb, Tk, 0.125)):
                i = si % 2; si += 1
                S.dma("sp", stg[i][0:64, 0:n], src[:, c0:c0 + n], writes=[Tstg[i]])
                S.op("pool", lambda e, i=i, n=n, dst=dst, c0=c0, sc=sc: e.tensor_scalar(
                    out=dst[:, c0:c0 + n], in0=stg[i][0:64, 0:n], scalar1=sc, scalar2=1.0, op0=ALU.mult, op1=ALU.mult),
                    reads=[Tstg[i]], writes=[Td])
        vflat = v_d.rearrange("p b d -> p (b d)")
        for c0 in range(0, NB * 64, 2048):
            n = min(2048, NB * 64 - c0)
            i = si % 2; si += 1
            S.dma("sp", stg[i][:, 0:n], vflat[:, c0:c0 + n], writes=[Tstg[i]])
            S.op("pool", lambda e, i=i, n=n, c0=c0: e.tensor_copy(out=vb[:, c0:c0 + n], in_=stg[i][:, 0:n]),
                 reads=[Tstg[i]], writes=[Tv])
        negtri = S.sb("negtri", [128, 128], BF16)
        negones = S.sb("negones", [1, 128], BF16)
        onescol = S.sb("onescol", [128, 1], BF16)
        iot = S.sb("iot", [128, 512])
        masks = [S.sb("mask%d" % m, [128, 512], BF16) for m in range(4)]
        padmask = S.sb("padmask", [128, 512], BF16)
        mask00 = S.sb("mask00", [128, 512], BF16)
        S.op("pool", lambda e: e.iota(iot[:, 0:128], pattern=[[1, 128]], base=0, channel_multiplier=-1,
                                      allow_small_or_imprecise_dtypes=True), writes=[Tc])
        S.op("dve", lambda e: e.tensor_scalar(out=negtri[:], in0=iot[:, 0:128], scalar1=0.0, scalar2=-1.0,
                                              op0=ALU.is_le, op1=ALU.mult), reads=[Tc], writes=[Tc])
        S.op("pool", lambda e: e.memset(negones[:], -1.0), writes=[Tc])
        S.op("pool", lambda e: e.memset(onescol[:], 1.0), writes=[Tc])
        for m in range(4):
            S.op("pool", lambda e, m=m: e.iota(iot[:], pattern=[[1, 512]], base=-128 * m, channel_multiplier=-1,
                                                allow_small_or_imprecise_dtypes=True), reads=[Tc], writes=[Tc])
            S.op("dve", lambda e, m=m: e.tensor_single_scalar(out=masks[m][:], in_=iot[:], scalar=0.0, op=ALU.is_gt),
                 reads=[Tc], writes=[Tc])
        S.op("pool", lambda e: e.memset(padmask[:], 1.0), reads=[Tc], writes=[Tc])
        S.op("pool", lambda e: e.tensor_copy(out=mask00[:], in_=masks[0][:]), reads=[Tc], writes=[Tc])
        if PADK > 0:
            pk = PADK
            S.op("pool", lambda e: e.memset(padmask[0:pk, :], 0.0), reads=[Tc], writes=[Tc])
            S.op("pool", lambda e: e.memset(mask00[0:pk, :], 0.0), reads=[Tc], writes=[Tc])

        eS = [S.sb("eS%d" % i, [128, 512]) for i in range(2)]; TeS = [Trk() for i in range(2)]
        spb = [S.sb("spb%d" % i, [128, 512], BF16) for i in range(2)]; Tsp = [Trk() for i in range(2)]
        wb = [S.sb("wb%d" % i, [128, 512], BF16) for i in range(2)]; Twb = [Trk() for i in range(2)]
        rhi = [S.sb("rhi%d" % i, [1, 512], BF16) for i in range(2)]
        rlo = [S.sb("rlo%d" % i, [1, 512], BF16) for i in range(2)]; Trh = [Trk() for i in range(2)]
        Rf = S.sb("Rf", [1, 512]); TR = Trk("R")
        obuf = [S.sb("obuf%d" % i, [64, 512]) for i in range(2)]; Tob = [Trk() for i in range(2)]
        psA = [S.ps("psA%d" % i, [128, 512]) for i in range(2)]; TpA = [Trk() for i in range(2)]
        psB = [S.ps("psB%d" % i, [128, 512]) for i in range(2)]; TpB = [Trk() for i in range(2)]
        psC = S.ps("psC", [1, 512]); TpC = Trk()
        psO = S.ps("psO", [64, 512]); TpO = Trk()
        Tout = Trk("sbout")

        tiles = []
        b = 0
        while b < NB:
            nb = min(4, NB - b)
            tiles.append((b, nb))
            b += nb
        cnt = [0]
        pend = []

        def mask_for(qb0, j):
            m = j - qb0
            if m >= 0:
                if j == 0:
                    return mask00
                return masks[m]
            if j == 0 and PADK > 0:
                return padmask
            return None

        def stage1(qb0, nq, j, slot):
            N = nq * 128
            q0 = qb0 * 128
            S.op("pe", lambda e: e.matmul(psB[slot][:, 0:N], lhsT=kb[:, j * 128:(j + 1) * 128], rhs=qb[:, q0:q0 + N],
                                          start=True, stop=False, skip_group_check=True), reads=[Tk, Tq], writes=[TpB[slot]])
            S.op("act", lambda e: e.activation(out=eS[slot][:, 0:N], in_=psB[slot][:, 0:N], func=AF.Exp),
                 reads=[TpB[slot]], writes=[TeS[slot]])
            def part_b():
                S.op("act", lambda e: e.activation(out=spb[slot][:, 0:N], in_=eS[slot][:, 0:N], func=AF.Ln, bias=one_t[:, 0:1]),
                     reads=[TeS[slot], Tc], writes=[Tsp[slot]])
                mk = mask_for(qb0, j)
                if mk is not None:
                    S.op("dve", lambda e: e.tensor_tensor(out=spb[slot][:, 0:N], in0=spb[slot][:, 0:N], in1=mk[:, 0:N], op=ALU.mult),
                         reads=[Tsp[slot], Tc], writes=[Tsp[slot]])
            return part_b

        def stage2(qb0, nq, j, slot, rslot, first, last):
            N = nq * 128
            q0 = qb0 * 128
            S.op("pe", lambda e: e.matmul(psB[slot][:, 0:N], lhsT=negtri[:], rhs=spb[slot][:, 0:N],
                                          start=False, stop=False, skip_group_check=True), reads=[Tsp[slot], Tc], writes=[TpB[slot]])
            S.op("pe", lambda e: e.matmul(psB[slot][:, 0:N], lhsT=negones[:], rhs=rhi[rslot][:, 0:N],
                                          start=False, stop=True, skip_group_check=True), reads=[Trh[rslot], Tc], writes=[TpB[slot]])
            if not last:
                S.op("pe", lambda e: e.matmul(psC[:, 0:N], lhsT=onescol[:], rhs=spb[slot][:, 0:N], start=True, stop=True),
                     reads=[Tsp[slot], Tc], writes=[TpC])
            for _d in range(SB_DUMMY):
                S.op("pe", lambda e, _d=_d: e.matmul(psA[_d % 2][:, 0:min(N, SB_DN)], lhsT=negtri[:], rhs=spb[slot][:, 0:min(N, SB_DN)], start=True, stop=True),
                     reads=[Tsp[slot], Tc], writes=[])
            S.op("act", lambda e: e.activation(out=wb[slot][:, 0:N], in_=psB[slot][:, 0:N], func=AF.Exp),
                 reads=[TpB[slot]], writes=[Twb[slot]])
            mk = mask_for(qb0, j)
            if mk is not None:
                S.op("dve", lambda e: e.tensor_tensor(out=wb[slot][:, 0:N], in0=wb[slot][:, 0:N], in1=mk[:, 0:N], op=ALU.mult),
                     reads=[Twb[slot], Tc], writes=[Twb[slot]])
            def emit_o():
                S.op("pe", lambda e: e.matmul(psO[:, 0:N], lhsT=vb[:, j * 64:(j + 1) * 64], rhs=wb[slot][:, 0:N],
                                              start=first, stop=last), reads=[Tv, Twb[slot]], writes=[TpO])
            pend.append(emit_o)
            if not last:
                nr = 1 - rslot
                S.op("dve", lambda e: e.tensor_tensor(out=Rf[:, 0:N], in0=Rf[:, 0:N], in1=psC[:, 0:N], op=ALU.add),
                     reads=[TR, TpC], writes=[TR])
                S.op("dve", lambda e: e.tensor_copy(out=rhi[nr][:, 0:N], in_=Rf[:, 0:N]),
                     reads=[TR], writes=[Trh[nr]])


        one_t = S.sb("one_t", [128, 1])
        S.op("pool", lambda e: e.memset(one_t[:], 1.0), writes=[Tc])

        for ti, (qb0, nq) in enumerate(tiles):
            N = nq * 128
            js = list(range(qb0 + nq - 1, -1, -1))
            S.op("dve", lambda e: e.memset(Rf[:], 0.0), reads=[TR], writes=[TR])
            S.op("dve", lambda e: e.memset(rhi[0][:], 0.0), reads=[Trh[0]], writes=[Trh[0]])
            S.op("dve", lambda e: e.memset(rlo[0][:], 0.0), reads=[Trh[0]], writes=[Trh[0]])
            rslot = 0
            slot0 = cnt[0] % 2
            stage1(qb0, nq, js[0], slot0)()
            for idx, j in enumerate(js):
                slot = cnt[0] % 2; cnt[0] += 1
                pb = None
                if idx + 1 < len(js):
                    pb = stage1(qb0, nq, js[idx + 1], 1 - slot)
                stage2(qb0, nq, j, slot, rslot, idx == 0, idx == len(js) - 1)
                if pb is not None:
                    pb()
                rslot = 1 - rslot
                while len(pend) > 1:
                    pend.pop(0)()
            while pend:
                pend.pop(0)()
            oi = ti % 2
            S.op("dve", lambda e, oi=oi, N=N: e.tensor_copy(out=obuf[oi][:, 0:N], in_=psO[:, 0:N]),
                 reads=[TpO], writes=[Tob[oi]])
            S.dma("sp", oT_d[:, qb0 * 128: qb0 * 128 + N], obuf[oi][:, 0:N], reads=[Tob[oi]], writes=[Tout])
        S.barrier()
        S.stack = st0
    return Tout


def s5_phase(S, nc, u_d, are_d, aim_d, ldt_d, bre_d, bim_d, cre_d, cim_d, dsk_d, ys_d, L, NBATCH=2):
    st0 = S.stack
    NCH = L // 16
    assert NCH * 16 == L
    TWO_PI = 2.0 * math.pi
    MAGIC = 12582912.0
    with ExitStack() as ps_:
        S.stack = ps_
        Tc = Trk("s5c")
        prm = S.sb("prm", [128, 8]); Tp = Trk("prm")
        bre = S.sb("bre", [128, 2, 16]); bim = S.sb("bim", [128, 2, 16])
        cre = S.sb("cre", [128, 2, 16]); cim = S.sb("cim", [128, 2, 16])
        dsk = S.sb("dsk", [64, 1])
        S.dma("sp", prm[:, 0:2], are_d[:, :], writes=[Tp])
        S.dma("sp", prm[:, 2:4], aim_d[:, :], writes=[Tp])
        S.dma("sp", prm[:, 4:6], ldt_d[:, :], writes=[Tp])
        S.dma("sp", bre[:], bre_d[:, :, :], writes=[Tp])
        S.dma("sp", bim[:], bim_d[:, :, :], writes=[Tp])
        S.dma("sp", cre[:], cre_d[:, :, :], writes=[Tp])
        S.dma("sp", cim[:], cim_d[:, :, :], writes=[Tp])
        S.dma("sp", dsk[:], dsk_d[:, :], writes=[Tp])
        sc = S.sb("s5sc", [128, 64]); Ts = Trk("s5sc")
        dve = lambda fn, r=(), w=(): S.op("dve", fn, reads=list(r) + [Tp, Ts, Tc], writes=list(w) if w else [Ts])
        S.op("act", lambda e: e.activation(out=sc[:, 0:2], in_=prm[:, 4:6], func=AF.Exp), reads=[Tp], writes=[Ts])
        dve(lambda e: e.tensor_tensor(out=sc[:, 2:4], in0=prm[:, 0:2], in1=sc[:, 0:2], op=ALU.mult))
        dve(lambda e: e.tensor_tensor(out=sc[:, 4:6], in0=prm[:, 2:4], in1=sc[:, 0:2], op=ALU.mult))
        NP = 17
        mag = S.sb("mag", [128, 2, NP]); ang = S.sb("ang", [128, 2, 2 * NP]); ang2 = S.sb("ang2", [128, 2, 2 * NP])
        trg = S.sb("trg", [128, 2, 2 * NP])
        pwr = S.sb("pwr", [128, 2, NP]); pwi = S.sb("pwi", [128, 2, NP]); pwn = S.sb("pwn", [128, 2, NP])
        for m in range(NP):
            S.op("act", lambda e, m=m: e.activation(out=mag[:, :, m], in_=sc[:, 2:4], func=AF.Exp, scale=float(m)),
                 reads=[Ts], writes=[Ts])
            dve(lambda e, m=m: e.tensor_scalar(out=ang[:, :, m], in0=sc[:, 4:6], scalar1=float(m), scalar2=0.0,
                                               op0=ALU.mult, op1=ALU.add))
            dve(lambda e, m=m: e.tensor_scalar(out=ang[:, :, NP + m], in0=sc[:, 4:6], scalar1=float(m), scalar2=math.pi / 2,
                                               op0=ALU.mult, op1=ALU.add))
        dve(lambda e: e.tensor_scalar(out=ang2[:], in0=ang[:], scalar1=1.0 / TWO_PI, scalar2=MAGIC, op0=ALU.mult, op1=ALU.add))
        dve(lambda e: e.tensor_scalar(out=ang2[:], in0=ang2[:], scalar1=-MAGIC, scalar2=None, op0=ALU.add))
        dve(lambda e: e.scalar_tensor_tensor(out=ang2[:], in0=ang2[:], scalar=-TWO_PI, in1=ang[:], op0=ALU.mult, op1=ALU.add))
        dve(lambda e: e.tensor_scalar(out=ang2[:], in0=ang2[:], scalar1=3.141592, scalar2=-3.141592, op0=ALU.min, op1=ALU.max))
        S.op("act", lambda e: e.activation(out=trg[:], in_=ang2[:], func=AF.Sin), reads=[Ts], writes=[Ts])
        dve(lambda e: e.tensor_tensor(out=pwi[:], in0=mag[:], in1=trg[:, :, 0:NP], op=ALU.mult))
        dve(lambda e: e.tensor_tensor(out=pwr[:], in0=mag[:], in1=trg[:, :, NP:2 * NP], op=ALU.mult))
        dve(lambda e: e.tensor_scalar(out=pwn[:], in0=pwi[:], scalar1=-1.0, scalar2=None, op0=ALU.mult))
        X = sc[:, 6:8]; DEN = sc[:, 8:10]; RDEN = sc[:, 10:12]; CFR = sc[:, 12:14]; CFI = sc[:, 14:16]; T1 = sc[:, 16:18]; T2 = sc[:, 18:20]
        ARE = prm[:, 0:2]; AIM = prm[:, 2:4]
        dve(lambda e: e.tensor_scalar(out=X, in0=pwr[:, :, 1], scalar1=-1.0, scalar2=None, op0=ALU.add))
        dve(lambda e: e.tensor_tensor(out=DEN, in0=ARE, in1=ARE, op=ALU.mult))
        dve(lambda e: e.tensor_tensor(out=T1, in0=AIM, in1=AIM, op=ALU.mult))
        dve(lambda e: e.tensor_tensor(out=DEN, in0=DEN, in1=T1, op=ALU.add))
        dve(lambda e: e.reciprocal(out=RDEN, in_=DEN))
        dve(lambda e: e.tensor_tensor(out=T1, in0=X, in1=ARE, op=ALU.mult))
        dve(lambda e: e.tensor_tensor(out=T2, in0=pwi[:, :, 1], in1=AIM, op=ALU.mult))
        dve(lambda e: e.tensor_tensor(out=T1, in0=T1, in1=T2, op=ALU.add))
        dve(lambda e: e.tensor_tensor(out=CFR, in0=T1, in1=RDEN, op=ALU.mult))
        dve(lambda e: e.tensor_tensor(out=T1, in0=pwi[:, :, 1], in1=ARE, op=ALU.mult))
        dve(lambda e: e.tensor_tensor(out=T2, in0=X, in1=AIM, op=ALU.mult))
        dve(lambda e: e.tensor_tensor(out=T1, in0=T1, in1=T2, op=ALU.subtract))
        dve(lambda e: e.tensor_tensor(out=CFI, in0=T1, in1=RDEN, op=ALU.mult))
        iot = S.sb("s5iot", [128, 128]); ident = S.sb("s5ident", [128, 128])
        S.op("pool", lambda e: e.iota(iot[:], pattern=[[1, 128]], base=0, channel_multiplier=-1,
                                      allow_small_or_imprecise_dtypes=True), writes=[Tc])
        S.op("dve", lambda e: e.tensor_single_scalar(out=ident[:], in_=iot[:], scalar=0.0, op=ALU.is_equal), reads=[Tc], writes=[Tc])
        bblk = [S.sb("bblk%d" % i, [128, 64]) for i in range(2)]
        tmpb = S.sb("tmpb", [128, 16])
        BT = [[S.sb("BT%d%d" % (pr, pt), [64, 128], BF16) for pt in range(2)] for pr in range(2)]
        CB = [[S.sb("CB%d%d" % (pr, pt), [128, 64], BF16) for pt in range(2)] for pr in range(2)]
        psT = S.ps("psT", [128, 512]); TpT = Trk()
        Tbb = Trk("bblk")
        for pr in range(2):
            for i in range(2):
                S.op("pool", lambda e, i=i: e.memset(bblk[i][:], 0.0), reads=[Tbb], writes=[Tbb])
            for g2 in range(2):
                r0, r1 = 64 * g2, 64 * g2 + 64
                c0 = 32 * pr + 16 * g2
                cfr = sc[r0:r1, 12 + pr:13 + pr]; cfi = sc[r0:r1, 14 + pr:15 + pr]
                S.op("dve", lambda e, r0=r0, r1=r1, pr=pr, cfi=cfi: e.tensor_scalar(
                    out=tmpb[r0:r1, :], in0=bim[r0:r1, pr, :], scalar1=cfi, scalar2=None, op0=ALU.mult),
                    reads=[Tp, Ts], writes=[Tbb])
                S.op("dve", lambda e, r0=r0, r1=r1, pr=pr, cfr=cfr, c0=c0: e.scalar_tensor_tensor(
                    out=bblk[0][r0:r1, c0:c0 + 16], in0=bre[r0:r1, pr, :], scalar=cfr, in1=tmpb[r0:r1, :],
                    op0=ALU.mult, op1=ALU.subtract), reads=[Tp, Ts, Tbb], writes=[Tbb])
                S.op("dve", lambda e, r0=r0, r1=r1, pr=pr, cfi=cfi: e.tensor_scalar(
                    out=tmpb[r0:r1, :], in0=bre[r0:r1, pr, :], scalar1=cfi, scalar2=None, op0=ALU.mult),
                    reads=[Tp, Ts, Tbb], writes=[Tbb])
                S.op("dve", lambda e, r0=r0, r1=r1, pr=pr, cfr=cfr, c0=c0: e.scalar_tensor_tensor(
                    out=bblk[1][r0:r1, c0:c0 + 16], in0=bim[r0:r1, pr, :], scalar=cfr, in1=tmpb[r0:r1, :],
                    op0=ALU.mult, op1=ALU.add), reads=[Tp, Ts, Tbb], writes=[Tbb])
            for pt in range(2):
                S.op("pe", lambda e, pt=pt: e.transpose(out=psT[0:64, 0:128], in_=bblk[pt][:], identity=ident[:]),
                     reads=[Tbb, Tc], writes=[TpT])
                S.op("act", lambda e, pr=pr, pt=pt: e.copy(out=BT[pr][pt][:], in_=psT[0:64, 0:128]),
                     reads=[TpT], writes=[Tc])
            for pt in range(2):
                S.op("pool", lambda e, pr=pr, pt=pt: e.memset(CB[pr][pt][:], 0.0), writes=[Tc])
            for g2 in range(2):
                r0, r1 = 64 * g2, 64 * g2 + 64
                c0 = 32 * pr + 16 * g2
                S.op("dve", lambda e, r0=r0, r1=r1, pr=pr, c0=c0: e.tensor_copy(out=CB[pr][0][r0:r1, c0:c0 + 16], in_=cre[r0:r1, pr, :]),
                     reads=[Tp, Tc], writes=[Tc])
                S.op("dve", lambda e, r0=r0, r1=r1, pr=pr, c0=c0: e.tensor_scalar(
                    out=CB[pr][1][r0:r1, c0:c0 + 16], in0=cim[r0:r1, pr, :], scalar1=-1.0, scalar2=None, op0=ALU.mult),
                    reads=[Tp, Tc], writes=[Tc])
        DR = [[S.sb("DR%d_%d" % (pr, m), [128, 128], BF16) for m in range(NP)] for pr in range(2)]
        DI = [[S.sb("DI%d_%d" % (pr, m), [128, 128], BF16) for m in range(NP)] for pr in range(2)]
        DN = [[S.sb("DN%d_%d" % (pr, m), [128, 128], BF16) for m in range(NP)] for pr in range(2)]
        k = 0
        for pr in range(2):
            for m in range(NP):
                for (dst, src) in ((DR, pwr), (DI, pwi), (DN, pwn)):
                    eng = "dve" if k % 2 == 0 else "pool"; k += 1
                    S.op(eng, lambda e, dst=dst, src=src, pr=pr, m=m: e.tensor_scalar(
                        out=dst[pr][m][:], in0=ident[:], scalar1=src[:, pr, m:m + 1], scalar2=1.0, op0=ALU.mult, op1=ALU.mult),
                        reads=[Ts, Tc], writes=[Tc])
        NLV = max(1, int(math.ceil(math.log2(NCH))))
        lvr = S.sb("lvr", [128, 2, NLV]); lvi = S.sb("lvi", [128, 2, NLV]); lvn = S.sb("lvn", [128, 2, NLV])
        dve(lambda e: e.tensor_copy(out=lvr[:, :, 0], in_=pwr[:, :, 16]))
        dve(lambda e: e.tensor_copy(out=lvi[:, :, 0], in_=pwi[:, :, 16]))
        for kk in range(1, NLV):
            dve(lambda e, kk=kk: e.tensor_tensor(out=T1, in0=lvr[:, :, kk - 1], in1=lvr[:, :, kk - 1], op=ALU.mult))
            dve(lambda e, kk=kk: e.tensor_tensor(out=T2, in0=lvi[:, :, kk - 1], in1=lvi[:, :, kk - 1], op=ALU.mult))
            dve(lambda e, kk=kk: e.tensor_tensor(out=lvr[:, :, kk], in0=T1, in1=T2, op=ALU.subtract))
            dve(lambda e, kk=kk: e.tensor_tensor(out=T1, in0=lvr[:, :, kk - 1], in1=lvi[:, :, kk - 1], op=ALU.mult))
            dve(lambda e, kk=kk: e.tensor_scalar(out=lvi[:, :, kk], in0=T1, scalar1=2.0, scalar2=None, op0=ALU.mult))
        dve(lambda e: e.tensor_scalar(out=lvn[:], in0=lvi[:], scalar1=-1.0, scalar2=None, op0=ALU.mult))

        ub = S.sb("ub", [64, L], BF16); Tub = Trk("ub")
        ustg = [S.sb("ustg%d" % i, [64, 1024]) for i in range(2)]; Tus = [Trk() for i in range(2)]
        BU = [S.sb("BU%d" % i, [128, L], BF16) for i in range(2)]; TBU = [Trk() for i in range(2)]
        CW = (NCH + 2) // 3
        coltiles = [(c, min(CW, NCH - c)) for c in range(0, NCH, CW)]
        stt = [S.sb("stt%d" % i, [128, CW * 16], BF16) for i in range(2)]; Tst = [Trk() for i in range(2)]
        Zr = [S.sb("Zr%d" % i, [128, NCH]) for i in range(2)]; Zi = [S.sb("Zi%d" % i, [128, NCH]) for i in range(2)]
        TZ = [Trk() for i in range(2)]
        ztr = S.sb("ztr", [128, NCH]); zti = S.sb("zti", [128, NCH]); Tzt = Trk()
        Spb = [S.sb("Spb%d" % i, [128, NCH], BF16) for i in range(2)]; TSp = Trk()
        usk = [S.sb("usk%d" % i, [64, 512]) for i in range(2)]; Tusk = [Trk() for i in range(2)]
        ysb = [S.sb("ysb%d" % i, [64, 512]) for i in range(2)]; Tys = [Trk() for i in range(2)]
        psR = [S.ps("psR%d" % i, [128, 512]) for i in range(2)]; TpR = [Trk() for i in range(2)]
        psI = [S.ps("psI%d" % i, [128, 512]) for i in range(2)]; TpI = [Trk() for i in range(2)]
        psY = [S.ps("psY%d" % i, [64, 512]) for i in range(2)]; TpY = [Trk() for i in range(2)]
        Tout = Trk("s5out")
        pc = [0]; yc = [0]; ec = [0]

        def evac(dst_ap, src_ap, reads, writes):
            ec[0] += 1
            if ec[0] % 2 == 0:
                S.op("act", lambda e: e.copy(out=dst_ap, in_=src_ap), reads=reads, writes=writes)
            else:
                S.op("dve", lambda e: e.tensor_copy(out=dst_ap, in_=src_ap), reads=reads, writes=writes)

        def unit(pr, b):
            def bu_body(c0):
                n = min(400, L - c0)
                j = pc[0] % 2; pc[0] += 1
                S.op("pe", lambda e: e.matmul(psR[j][:, 0:n], lhsT=BT[pr][0][:], rhs=ub[:, c0:c0 + n], start=True, stop=True),
                     reads=[Tub, Tc], writes=[TpR[j]])
                S.op("pe", lambda e: e.matmul(psI[j][:, 0:n], lhsT=BT[pr][1][:], rhs=ub[:, c0:c0 + n], start=True, stop=True),
                     reads=[Tub, Tc], writes=[TpI[j]])
                assert c0 % 16 == 0 and n % 16 == 0
                n0 = c0 // 16; nn = n // 16
                for (bt, pst, tpt) in ((BU[0], psR[j], TpR[j]), (BU[1], psI[j], TpI[j])):
                    dst = bt[:].rearrange("p (t n) -> p t n", t=16)[:, :, n0:n0 + nn]
                    srcv = pst[:, 0:n].rearrange("p (n t) -> p t n", t=16)
                    evac(dst, srcv, [tpt], [TBU[0] if bt is BU[0] else TBU[1]])
            for c0 in range(0, L, 400):
                bu_body(c0)

            def p1_body(cc, n):
                j = pc[0] % 2; pc[0] += 1
                lo, hi = 16 * cc, 16 * (cc + n)
                for tp in range(16):
                    d = 15 - tp
                    r_re = BU[0][:, tp * NCH + cc:tp * NCH + cc + n]; r_im = BU[1][:, tp * NCH + cc:tp * NCH + cc + n]
                    S.op("pe", lambda e, d=d, r=r_re, tp=tp: e.matmul(psR[j][:, 0:n], lhsT=DR[pr][d][:], rhs=r, start=(tp == 0), stop=False),
                         reads=[TBU[0], Tc], writes=[TpR[j]])
                    S.op("pe", lambda e, d=d, r=r_im, tp=tp: e.matmul(psR[j][:, 0:n], lhsT=DN[pr][d][:], rhs=r, start=False, stop=(tp == 15)),
                         reads=[TBU[1], Tc], writes=[TpR[j]])
                    S.op("pe", lambda e, d=d, r=r_re, tp=tp: e.matmul(psI[j][:, 0:n], lhsT=DI[pr][d][:], rhs=r, start=(tp == 0), stop=False),
                         reads=[TBU[0], Tc], writes=[TpI[j]])
                    S.op("pe", lambda e, d=d, r=r_im, tp=tp: e.matmul(psI[j][:, 0:n], lhsT=DR[pr][d][:], rhs=r, start=False, stop=(tp == 15)),
                         reads=[TBU[1], Tc], writes=[TpI[j]])
                evac(Zr[0][:, cc:cc + n], psR[j][:, 0:n], [TpR[j]], [TZ[0]])
                evac(Zi[0][:, cc:cc + n], psI[j][:, 0:n], [TpI[j]], [TZ[0]])
            for (cc, n) in coltiles:
                p1_body(cc, n)
            cur = 0
            for kk in range(NLV):
                o = 1 << kk
                if o >= NCH:
                    break
                nxt = 1 - cur
                m = NCH - o
                ar = lvr[:, pr, kk:kk + 1]; ai = lvi[:, pr, kk:kk + 1]; na = lvn[:, pr, kk:kk + 1]
                S.op("dve", lambda e, cur=cur, o=o, m=m, na=na: e.scalar_tensor_tensor(
                    out=ztr[:, 0:m], in0=Zi[cur][:, 0:m], scalar=na, in1=Zr[cur][:, o:NCH], op0=ALU.mult, op1=ALU.add),
                    reads=[TZ[cur], Ts], writes=[Tzt])
                S.op("dve", lambda e, cur=cur, nxt=nxt, o=o, m=m, ar=ar: e.scalar_tensor_tensor(
                    out=Zr[nxt][:, o:NCH], in0=Zr[cur][:, 0:m], scalar=ar, in1=ztr[:, 0:m], op0=ALU.mult, op1=ALU.add),
                    reads=[TZ[cur], Tzt, Ts], writes=[TZ[nxt]])
                S.op("dve", lambda e, cur=cur, o=o, m=m, ai=ai: e.scalar_tensor_tensor(
                    out=zti[:, 0:m], in0=Zr[cur][:, 0:m], scalar=ai, in1=Zi[cur][:, o:NCH], op0=ALU.mult, op1=ALU.add),
                    reads=[TZ[cur], Ts], writes=[Tzt])
                S.op("dve", lambda e, cur=cur, nxt=nxt, o=o, m=m, ar=ar: e.scalar_tensor_tensor(
                    out=Zi[nxt][:, o:NCH], in0=Zi[cur][:, 0:m], scalar=ar, in1=zti[:, 0:m], op0=ALU.mult, op1=ALU.add),
                    reads=[TZ[cur], Tzt, Ts], writes=[TZ[nxt]])
                S.op("pool", lambda e, cur=cur, nxt=nxt, o=o: e.tensor_copy(out=Zr[nxt][:, 0:o], in_=Zr[cur][:, 0:o]),
                     reads=[TZ[cur]], writes=[TZ[nxt]])
                S.op("pool", lambda e, cur=cur, nxt=nxt, o=o: e.tensor_copy(out=Zi[nxt][:, 0:o], in_=Zi[cur][:, 0:o]),
                     reads=[TZ[cur]], writes=[TZ[nxt]])
                cur = nxt
            S.op("pool", lambda e: e.memset(Spb[0][:, 0:1], 0.0), reads=[TSp], writes=[TSp])
            S.op("pool", lambda e: e.memset(Spb[1][:, 0:1], 0.0), reads=[TSp], writes=[TSp])
            if NCH > 1:
                S.op("dve", lambda e, cur=cur: e.tensor_copy(out=Spb[0][:, 1:NCH], in_=Zr[cur][:, 0:NCH - 1]), reads=[TZ[cur], TSp], writes=[TSp])
                S.op("dve", lambda e, cur=cur: e.tensor_copy(out=Spb[1][:, 1:NCH], in_=Zi[cur][:, 0:NCH - 1]), reads=[TZ[cur], TSp], writes=[TSp])
            def p2_tau(cc, n, tau):
                    lo, hi = 16 * cc, 16 * (cc + n)
                    j = pc[0] % 2; pc[0] += 1
                    for tp in range(tau + 1):
                        d = tau - tp
                        r_re = BU[0][:, tp * NCH + cc:tp * NCH + cc + n]; r_im = BU[1][:, tp * NCH + cc:tp * NCH + cc + n]
                        S.op("pe", lambda e, d=d, r=r_re, tp=tp: e.matmul(psR[j][:, 0:n], lhsT=DR[pr][d][:], rhs=r, start=(tp == 0), stop=False),
                             reads=[TBU[0], Tc], writes=[TpR[j]])
                        S.op("pe", lambda e, d=d, r=r_im: e.matmul(psR[j][:, 0:n], lhsT=DN[pr][d][:], rhs=r, start=False, stop=False),
                             reads=[TBU[1], Tc], writes=[TpR[j]])
                        S.op("pe", lambda e, d=d, r=r_re, tp=tp: e.matmul(psI[j][:, 0:n], lhsT=DI[pr][d][:], rhs=r, start=(tp == 0), stop=False),
                             reads=[TBU[0], Tc], writes=[TpI[j]])
                        S.op("pe", lambda e, d=d, r=r_im: e.matmul(psI[j][:, 0:n], lhsT=DR[pr][d][:], rhs=r, start=False, stop=False),
                             reads=[TBU[1], Tc], writes=[TpI[j]])
                    d = tau + 1
                    S.op("pe", lambda e, d=d: e.matmul(psR[j][:, 0:n], lhsT=DR[pr][d][:], rhs=Spb[0][:, cc:cc + n], start=False, stop=False),
                         reads=[TSp, Tc], writes=[TpR[j]])
                    S.op("pe", lambda e, d=d: e.matmul(psR[j][:, 0:n], lhsT=DN[pr][d][:], rhs=Spb[1][:, cc:cc + n], start=False, stop=True),
                         reads=[TSp, Tc], writes=[TpR[j]])
                    S.op("pe", lambda e, d=d: e.matmul(psI[j][:, 0:n], lhsT=DI[pr][d][:], rhs=Spb[0][:, cc:cc + n], start=False, stop=False),
                         reads=[TSp, Tc], writes=[TpI[j]])
                    S.op("pe", lambda e, d=d: e.matmul(psI[j][:, 0:n], lhsT=DR[pr][d][:], rhs=Spb[1][:, cc:cc + n], start=False, stop=True),
                         reads=[TSp, Tc], writes=[TpI[j]])
                    evac(stt[0][:, tau:16 * n:16], psR[j][:, 0:n], [TpR[j]], [Tst[0]])
                    evac(stt[1][:, tau:16 * n:16], psI[j][:, 0:n], [TpI[j]], [Tst[1]])

            def p2_y(cc, n, x0):
                    lo = 16 * cc
                    ntok = 16 * n
                    w = min(512, ntok - x0)
                    jy = yc[0] % 2; yc[0] += 1
                    t0 = lo + x0
                    r0, r1 = 32 * pr, 32 * pr + 32
                    S.dma("sp", usk[jy][r0:r1, 0:w], u_d[r0:r1, b, t0:t0 + w], writes=[Tusk[jy]])
                    S.op("pe", lambda e, jy=jy, x0=x0, w=w: e.matmul(psY[jy][:, 0:w], lhsT=CB[pr][0][:], rhs=stt[0][:, x0:x0 + w], start=True, stop=False),
                         reads=[Tst[0], Tc], writes=[TpY[jy]])
                    S.op("pe", lambda e, jy=jy, x0=x0, w=w: e.matmul(psY[jy][:, 0:w], lhsT=CB[pr][1][:], rhs=stt[1][:, x0:x0 + w], start=False, stop=True),
                         reads=[Tst[1], Tc], writes=[TpY[jy]])
                    S.op("dve", lambda e, jy=jy, w=w, r0=r0, r1=r1: e.scalar_tensor_tensor(
                        out=ysb[jy][r0:r1, 0:w], in0=usk[jy][r0:r1, 0:w], scalar=dsk[r0:r1, 0:1], in1=psY[jy][r0:r1, 0:w],
                        op0=ALU.mult, op1=ALU.add), reads=[Tusk[jy], TpY[jy], Tp], writes=[Tys[jy]])
                    S.dma("sp", ys_d[r0:r1, b, t0:t0 + w], ysb[jy][r0:r1, 0:w], reads=[Tys[jy]], writes=[Tout])
            for (cc, n) in coltiles:
                for tau in range(16):
                    p2_tau(cc, n, tau)
                for x0 in range(0, 16 * n, 512):
                    p2_y(cc, n, x0)

        si = 0
        for b in range(NBATCH):
            for c0 in range(0, L, 1024):
                n = min(1024, L - c0)
                i = si % 2; si += 1
                S.dma("sp", ustg[i][:, 0:n], u_d[:, b, c0:c0 + n], writes=[Tus[i]])
                S.op("pool", lambda e, i=i, n=n, c0=c0: e.tensor_copy(out=ub[:, c0:c0 + n], in_=ustg[i][:, 0:n]),
                     reads=[Tus[i]], writes=[Tub])
            for pr in range(2):
                unit(pr, b)
        S.barrier()
        S.stack = st0
    return Tout


def dn_phase(S, nc, xq_d, xk_d, xv_d, cw_d, a_d, b_d, hp_d, o_d, NC, NPAD):
    st0 = S.stack
    LP = NC * 64
    with ExitStack() as ps_:
        S.stack = ps_
        Tc = Trk("dnc")
        cw = S.sb("cw", [64, 12]); hp = S.sb("hp", [64, 2]); acol = S.sb("acol", [64, NC]); bcol = S.sb("bcol", [64, NC])
        Tp = Trk("dnp")
        S.dma("sp", cw[:], cw_d[:, :], writes=[Tp]); S.dma("sp", hp[:], hp_d[:, :], writes=[Tp])
        S.dma("sp", acol[:], a_d[:, :], writes=[Tp]); S.dma("sp", bcol[:], b_d[:, :], writes=[Tp])
        one_t = S.sb("done", [64, 1]); eps_t = S.sb("deps", [64, 1])
        S.op("pool", lambda e: e.memset(one_t[:], 1.0), writes=[Tc])
        S.op("pool", lambda e: e.memset(eps_t[:], EPS), writes=[Tc])
        iot = S.sb("dniot", [64, 64]); ident = S.sb("dnident", [64, 64]); identb = S.sb("dnidentb", [64, 64], BF16)
        triu = S.sb("triu", [64, 64]); slow = S.sb("slow", [64, 64]); uinc = S.sb("uinc", [64, 64])
        ones64 = S.sb("ones64", [64, 64]); nones64 = S.sb("nones64", [64, 64]); ones64b = S.sb("ones64b", [64, 64], BF16)
        S.op("pool", lambda e: e.iota(iot[:], pattern=[[1, 64]], base=0, channel_multiplier=-1, allow_small_or_imprecise_dtypes=True), writes=[Tc])
        S.op("dve", lambda e: e.tensor_single_scalar(out=ident[:], in_=iot[:], scalar=0.0, op=ALU.is_equal), reads=[Tc], writes=[Tc])
        S.op("dve", lambda e: e.tensor_copy(out=identb[:], in_=ident[:]), reads=[Tc], writes=[Tc])
        S.op("dve", lambda e: e.tensor_single_scalar(out=triu[:], in_=iot[:], scalar=0.0, op=ALU.is_ge), reads=[Tc], writes=[Tc])
        S.op("dve", lambda e: e.tensor_copy(out=uinc[:], in_=triu[:]), reads=[Tc], writes=[Tc])
        S.op("dve", lambda e: e.tensor_single_scalar(out=slow[:], in_=iot[:], scalar=0.0, op=ALU.is_lt), reads=[Tc], writes=[Tc])
        S.op("pool", lambda e: e.memset(ones64[:], 1.0), writes=[Tc])
        S.op("pool", lambda e: e.memset(nones64[:], -1.0), writes=[Tc])
        S.op("pool", lambda e: e.memset(ones64b[:], 1.0), writes=[Tc])

        gcol = S.sb("gcol", [64, NC]); gccol = S.sb("gccol", [64, NC]); glast = S.sb("glast", [64, NC])
        egc = S.sb("egc", [64, NC]); eglast = S.sb("eglast", [64, NC]); edec = S.sb("edec", [64, NC])
        beta = S.sb("beta", [64, NC]); nbeta = S.sb("nbeta", [64, NC]); begc = S.sb("begc", [64, NC])
        nexpA = S.sb("nexpA", [64, 1]); tmpc = S.sb("tmpc", [64, NC])
        Tg = Trk("gates")
        psG = S.ps("psG", [64, 512]); TpG = Trk()
        S.op("act", lambda e: e.activation(out=tmpc[:], in_=acol[:], func=AF.Exp, bias=hp[:, 1:2]), reads=[Tp], writes=[Tg])
        S.op("act", lambda e: e.activation(out=tmpc[:], in_=tmpc[:], func=AF.Ln, bias=one_t[:, 0:1]), reads=[Tg, Tc], writes=[Tg])
        S.op("act", lambda e: e.activation(out=nexpA[:], in_=hp[:, 0:1], func=AF.Exp), reads=[Tp], writes=[Tg])
        S.op("dve", lambda e: e.tensor_scalar(out=gcol[:], in0=tmpc[:], scalar1=nexpA[:, 0:1], scalar2=-1.0, op0=ALU.mult, op1=ALU.mult),
             reads=[Tg], writes=[Tg])
        if NPAD > 0:
            S.op("dve", lambda e: e.memset(gcol[0:NPAD, 0:1], 0.0), reads=[Tg], writes=[Tg])
        S.op("pe", lambda e: e.matmul(psG[:, 0:NC], lhsT=triu[:], rhs=gcol[:], start=True, stop=True), reads=[Tg, Tc], writes=[TpG])
        S.op("dve", lambda e: e.tensor_copy(out=gccol[:], in_=psG[:, 0:NC]), reads=[TpG], writes=[Tg])
        S.op("pe", lambda e: e.matmul(psG[:, 0:NC], lhsT=ones64[:], rhs=gcol[:], start=True, stop=True), reads=[Tg, Tc], writes=[TpG])
        S.op("dve", lambda e: e.tensor_copy(out=glast[:], in_=psG[:, 0:NC]), reads=[TpG], writes=[Tg])
        S.op("act", lambda e: e.activation(out=egc[:], in_=gccol[:], func=AF.Exp), reads=[Tg], writes=[Tg])
        S.op("act", lambda e: e.activation(out=eglast[:], in_=glast[:], func=AF.Exp), reads=[Tg], writes=[Tg])
        S.op("dve", lambda e: e.tensor_tensor(out=tmpc[:], in0=glast[:], in1=gccol[:], op=ALU.subtract), reads=[Tg], writes=[Tg])
        S.op("act", lambda e: e.activation(out=edec[:], in_=tmpc[:], func=AF.Exp), reads=[Tg], writes=[Tg])
        S.op("act", lambda e: e.activation(out=beta[:], in_=bcol[:], func=AF.Sigmoid), reads=[Tp], writes=[Tg])
        S.op("dve", lambda e: e.tensor_scalar(out=nbeta[:], in0=beta[:], scalar1=-1.0, scalar2=None, op0=ALU.mult), reads=[Tg], writes=[Tg])
        S.op("dve", lambda e: e.tensor_tensor(out=begc[:], in0=beta[:], in1=egc[:], op=ALU.mult), reads=[Tg], writes=[Tg])

        qb = S.sb("dqb", [64, LP], BF16); kb = S.sb("dkb", [64, LP], BF16); vb = S.sb("dvb", [64, LP], BF16)
        Tqkv = [Trk("dq"), Trk("dk"), Trk("dv")]
        CT = 2048
        xin = [S.sb("dxin%d" % i, [64, CT + 3]) for i in range(2)]; Txin = [Trk() for i in range(2)]
        acc = [S.sb("dacc%d" % i, [64, CT]) for i in range(2)]; Tacc = [Trk() for i in range(2)]
        sqb = S.sb("dsq", [64, CT], BF16); Tsq = Trk()
        rinv = S.sb("drinv", [64, 512]); Tri = Trk()
        psS = [S.ps("psS%d" % i, [64, 512]) for i in range(2)]; TpS = [Trk() for i in range(2)]
        cc = [0]

        def conv_tile(src, which, c0):
            n = min(CT, LP - c0)
            i = cc[0] % 2; cc[0] += 1
            dst = (qb, kb, vb)[which]
            S.dma("sp", xin[i][:, 0:n + 3], src[:, c0:c0 + n + 3], writes=[Txin[i]])
            S.op("dve", lambda e: e.tensor_scalar(out=acc[i][:, 0:n], in0=xin[i][:, 0:n], scalar1=cw[:, 4 * which:4 * which + 1],
                                                  scalar2=None, op0=ALU.mult), reads=[Txin[i], Tp], writes=[Tacc[i]])
            for j in range(1, 4):
                S.op("dve", lambda e, j=j: e.scalar_tensor_tensor(
                    out=acc[i][:, 0:n], in0=xin[i][:, j:j + n], scalar=cw[:, 4 * which + j:4 * which + j + 1], in1=acc[i][:, 0:n],
                    op0=ALU.mult, op1=ALU.add), reads=[Txin[i], Tp, Tacc[i]], writes=[Tacc[i]])
            S.op("act", lambda e: e.activation(out=acc[i][:, 0:n], in_=acc[i][:, 0:n], func=AF.Silu), reads=[Tacc[i]], writes=[Tacc[i]])
            if which == 2:
                S.op("pool", lambda e: e.tensor_copy(out=dst[:, c0:c0 + n], in_=acc[i][:, 0:n]), reads=[Tacc[i]], writes=[Tqkv[2]])
                return
            S.op("act", lambda e: e.activation(out=sqb[:, 0:n], in_=acc[i][:, 0:n], func=AF.Square), reads=[Tacc[i]], writes=[Tsq])
            for s0 in range(0, n, 512):
                w = min(512, n - s0)
                j = cc[0] % 2; cc[0] += 1
                S.op("pe", lambda e, s0=s0, w=w, j=j: e.matmul(psS[j][:, 0:w], lhsT=ones64b[:], rhs=sqb[:, s0:s0 + w], start=True, stop=True),
                     reads=[Tsq, Tc], writes=[TpS[j]])
                S.op("act", lambda e, w=w, j=j: e.activation(out=rinv[:, 0:w], in_=psS[j][:, 0:w], func=AF.Ln, bias=eps_t[:, 0:1]),
                     reads=[TpS[j], Tc], writes=[Tri])
                S.op("act", lambda e, w=w: e.activation(out=rinv[:, 0:w], in_=rinv[:, 0:w], func=AF.Exp, scale=-0.5), reads=[Tri], writes=[Tri])
                sc_ = 0.125 if which == 0 else 1.0
                S.op("dve", lambda e, s0=s0, w=w, sc_=sc_: e.scalar_tensor_tensor(
                    out=dst[:, c0 + s0:c0 + s0 + w], in0=acc[i][:, s0:s0 + w], scalar=sc_, in1=rinv[:, 0:w], op0=ALU.mult, op1=ALU.mult),
                    reads=[Tacc[i], Tri], writes=[Tqkv[which]])

        for c0 in range(0, LP, CT):
            conv_tile(xq_d, 0, c0); conv_tile(xk_d, 1, c0); conv_tile(xv_d, 2, c0)

        GS = 8
        psD = S.ps("psD", [64, 512]); TpD = Trk()
        psA = S.ps("psA", [64, 512]); TpA = Trk()
        psQ = S.ps("psQ", [64, 512]); TpQ = Trk()
        psTr = S.ps("psTr", [64, 512]); TpTr = Trk()
        psN0 = S.ps("psN", [64, 512])
        S.barrier()
        psNl = [psN0, psG]; TpNl = [Trk(), Trk()]
        psNbl = [psQ, psS[0]]; TpNbl = [Trk(), Trk()]
        psQ = psD; TpQ = TpD
        psU = psTr; TpU = TpTr
        psQ2 = psS[1]; TpSeq = Trk()
        gt = S.sb("gt", [64, GS * 64]); Tgt = Trk()
        E = S.sb("Emat", [64, GS * 64]); TE = Trk()
        EL = S.sb("EL", [64, GS * 64]); EU = S.sb("EU", [64, GS * 64]); TEm = Trk()
        NL = 2
        Yl = [[S.sb("Yk%d_%d" % (l, i), [64, 64]) for i in range(2)] for l in range(NL)]
        Xl = [[S.sb("Xk%d_%d" % (l, i), [64, 64]) for i in range(2)] for l in range(NL)]
        TYl = [[Trk() for i in range(2)] for l in range(NL)]; TXl = [[Trk() for i in range(2)] for l in range(NL)]
        Pl = [S.sb("Pm%d" % l, [64, 64]) for l in range(NL)]; TPl = [Trk() for l in range(NL)]
        TTbl = [S.sb("TTb%d" % l, [64, 64], BF16) for l in range(NL)]; TTTl = [Trk() for l in range(NL)]
        vbetal = [S.sb("vbeta%d" % l, [64, 64], BF16) for l in range(NL)]
        kbgl = [S.sb("kbg%d" % l, [64, 64], BF16) for l in range(NL)]; Tvkl = [Trk() for l in range(NL)]
        u_g = [S.sb("u_g%d" % i, [64, GS * 64]) for i in range(2)]
        wT_g = [S.sb("wT_g%d" % i, [64, GS * 64], BF16) for i in range(2)]
        attnT_g = [S.sb("attnT_g%d" % i, [64, GS * 64], BF16) for i in range(2)]
        kdec_g = [S.sb("kdec_g%d" % i, [64, GS * 64], BF16) for i in range(2)]
        Tgrp = [[Trk() for _ in range(GS)] for i in range(2)]
        Sf = S.sb("Sf", [64, 64]); Sb = S.sb("Sb", [64, 64], BF16); TS = Trk()
        vnew = S.sb("vnew", [64, 64], BF16); Tvn = Trk()
        qs_t = S.sb("qs_t", [64, 64]); Tqs = Trk()
        o_g = [S.sb("o_g%d" % i, [64, GS * 64]) for i in range(2)]; Tog = [Trk() for i in range(2)]
        Tout = Trk("dnout")
        S.op("pool", lambda e: e.memset(Sf[:], 0.0), writes=[TS])
        S.op("pool", lambda e: e.memset(Sb[:], 0.0), reads=[TS], writes=[TS])

        def pre_group_head(n0, ng, gp):
            W = ng * 64
            for c in range(ng):
                n = n0 + c
                S.op("dve", lambda e, c=c, n=n: e.tensor_scalar(out=gt[:, c * 64:(c + 1) * 64], in0=triu[:], scalar1=gcol[:, n:n + 1],
                                                                 scalar2=None, op0=ALU.mult), reads=[Tg, Tc, Tgt], writes=[Tgt])
                S.op("pe", lambda e, c=c: e.matmul(psD[:, c * 64:(c + 1) * 64], lhsT=gt[:, c * 64:(c + 1) * 64], rhs=ones64[:], start=True, stop=False),
                     reads=[Tgt, Tc], writes=[TpD])
                S.op("pe", lambda e, c=c: e.matmul(psD[:, c * 64:(c + 1) * 64], lhsT=nones64[:], rhs=gt[:, c * 64:(c + 1) * 64], start=False, stop=True),
                     reads=[Tgt, Tc], writes=[TpD])
            yield
            S.op("dve", lambda e: e.tensor_scalar(out=E[:, 0:W], in0=psD[:, 0:W], scalar1=-1.0, scalar2=None, op0=ALU.mult), reads=[TpD, TE, TEm], writes=[TE])
            S.op("dve", lambda e: e.tensor_tensor(out=E[:, 0:W], in0=E[:, 0:W], in1=psD[:, 0:W], op=ALU.min), reads=[TpD, TE], writes=[TE])
            S.op("act", lambda e: e.activation(out=E[:, 0:W], in_=E[:, 0:W], func=AF.Exp), reads=[TE], writes=[TE])
            yield
            S.op("dve", lambda e: e.tensor_tensor(out=EL[:, 0:W].rearrange("p (c f) -> p c f", f=64), in0=E[:, 0:W].rearrange("p (c f) -> p c f", f=64),
                                                  in1=slow[:].unsqueeze(1).broadcast_to([64, ng, 64]), op=ALU.mult), reads=[TE, Tc, TEm], writes=[TEm])
            S.op("dve", lambda e: e.tensor_tensor(out=EU[:, 0:W].rearrange("p (c f) -> p c f", f=64), in0=E[:, 0:W].rearrange("p (c f) -> p c f", f=64),
                                                  in1=uinc[:].unsqueeze(1).broadcast_to([64, ng, 64]), op=ALU.mult), reads=[TE, Tc, TEm], writes=[TEm])
            for c in range(ng):
                n = n0 + c
                ks = kb[:, n * 64:(n + 1) * 64]; qs = qb[:, n * 64:(n + 1) * 64]
                S.op("pe", lambda e, c=c, ks=ks: e.matmul(psA[:, c * 64:(c + 1) * 64], lhsT=ks, rhs=ks, start=True, stop=True),
                     reads=[Tqkv[1]], writes=[TpA])
                S.op("pe", lambda e, c=c, ks=ks, qs=qs: e.matmul(psQ[:, c * 64:(c + 1) * 64], lhsT=ks, rhs=qs, start=True, stop=True),
                     reads=[Tqkv[0], Tqkv[1]], writes=[TpQ])
            yield
            S.op("dve", lambda e: e.tensor_tensor(out=attnT_g[gp][:, 0:W], in0=EU[:, 0:W], in1=psQ[:, 0:W], op=ALU.mult),
                 reads=[TEm, TpQ] + Tgrp[gp], writes=Tgrp[gp])
            yield

        def pre_chunk(n, c, gp, l):
            cs = slice(c * 64, (c + 1) * 64)
            lo_ = 128 * l
            Y = Yl[l]; Xm = Xl[l]; TY = TYl[l]; TX = TXl[l]; P = Pl[l]; TP = TPl[l]; psN = psNl[l]; TpN = TpNl[l]
            TTb = TTbl[l]; TTT = TTTl[l]; vbeta = vbetal[l]; kbg = kbgl[l]; Tvk = Tvkl[l]
            psNb = psNbl[l]; TpNb = TpNbl[l]
            S.op("dve", lambda e: e.scalar_tensor_tensor(out=Y[0][:], in0=psA[:, cs], scalar=nbeta[:, n:n + 1], in1=EL[:, cs],
                                                         op0=ALU.mult, op1=ALU.mult), reads=[TpA, Tg, TEm, TY[0]], writes=[TY[0]])
            yield
            S.op("pe", lambda e: e.matmul(psN[:, 0:64], lhsT=Y[0][:], rhs=ident[:], start=True, stop=True), reads=[TY[0], Tc], writes=[TpN])
            yield
            S.op("dve", lambda e: e.tensor_copy(out=Xm[0][:], in_=psN[:, 0:64]), reads=[TpN], writes=[TX[0]])
            S.op("dve", lambda e: e.tensor_tensor(out=P[:], in0=Xm[0][:], in1=ident[:], op=ALU.add), reads=[TX[0], Tc, TP], writes=[TP])
            yield
            cur = 0
            for lv in range(5):
                nx = 1 - cur
                S.op("pe", lambda e, cur=cur: e.matmul(psN[:, 64:128], lhsT=Y[cur][:], rhs=Xm[cur][:], start=True, stop=True),
                     reads=[TY[cur], TX[cur]], writes=[TpN])
                S.op("pe", lambda e, cur=cur: e.matmul(psNb[:, 0:64], lhsT=Xm[cur][:], rhs=Y[cur][:], start=True, stop=True),
                     reads=[TY[cur], TX[cur]], writes=[TpNb])
                yield
                S.op("act", lambda e, nx=nx: e.copy(out=Xm[nx][:], in_=psN[:, 64:128]), reads=[TpN], writes=[TX[nx]])
                S.op("dve", lambda e, nx=nx: e.tensor_copy(out=Y[nx][:], in_=psNb[:, 0:64]), reads=[TpNb], writes=[TY[nx]])
                yield
                S.op("pe", lambda e, nx=nx: e.matmul(psNb[:, 64:128], lhsT=Y[nx][:], rhs=P[:], start=True, stop=True),
                     reads=[TY[nx], TP], writes=[TpNb])
                yield
                S.op("dve", lambda e: e.tensor_tensor(out=P[:], in0=P[:], in1=psNb[:, 64:128], op=ALU.add), reads=[TpNb, TP], writes=[TP])
                yield
                cur = nx
            S.op("act", lambda e: e.copy(out=TTb[:], in_=P[:]), reads=[TP], writes=[TTT])
            ks = kb[:, n * 64:(n + 1) * 64]; vs = vb[:, n * 64:(n + 1) * 64]
            S.op("pe", lambda e: e.matmul(psTr[:, lo_ + 0:lo_ + 64], lhsT=ks, rhs=identb[:], start=True, stop=True), reads=[Tqkv[1], Tc], writes=[TpTr])
            S.op("pe", lambda e: e.matmul(psTr[:, lo_ + 64:lo_ + 128], lhsT=vs, rhs=identb[:], start=True, stop=True), reads=[Tqkv[2], Tc], writes=[TpTr])
            yield
            S.op("dve", lambda e: e.tensor_scalar(out=kbg[:], in0=psTr[:, lo_ + 0:lo_ + 64], scalar1=begc[:, n:n + 1], scalar2=None, op0=ALU.mult),
                 reads=[TpTr, Tg, Tvk], writes=[Tvk])
            S.op("dve", lambda e: e.tensor_scalar(out=vbeta[:], in0=psTr[:, lo_ + 64:lo_ + 128], scalar1=beta[:, n:n + 1], scalar2=None, op0=ALU.mult),
                 reads=[TpTr, Tg, Tvk], writes=[Tvk])
            S.op("dve", lambda e: e.tensor_scalar(out=kdec_g[gp][:, cs], in0=psTr[:, lo_ + 0:lo_ + 64], scalar1=edec[:, n:n + 1], scalar2=None, op0=ALU.mult),
                 reads=[TpTr, Tg, Tgrp[gp][c]], writes=[Tgrp[gp][c]])
            yield
            S.op("pe", lambda e: e.matmul(psU[:, 256 + lo_ + 0:256 + lo_ + 64], lhsT=TTb[:], rhs=vbeta[:], start=True, stop=True), reads=[TTT, Tvk], writes=[TpU])
            S.op("pe", lambda e: e.matmul(psU[:, 256 + lo_ + 64:256 + lo_ + 128], lhsT=kbg[:], rhs=TTb[:], start=True, stop=True), reads=[TTT, Tvk], writes=[TpU])
            yield
            S.op("dve", lambda e: e.tensor_copy(out=u_g[gp][:, cs], in_=psU[:, 256 + lo_ + 0:256 + lo_ + 64]), reads=[TpU, Tgrp[gp][c]], writes=[Tgrp[gp][c]])
            S.op("dve", lambda e: e.tensor_copy(out=wT_g[gp][:, cs], in_=psU[:, 256 + lo_ + 64:256 + lo_ + 128]), reads=[TpU, Tgrp[gp][c]], writes=[Tgrp[gp][c]])
            yield

        def seq_chunk(n, c, gp, og):
            cs = slice(c * 64, (c + 1) * 64)
            qs = qb[:, n * 64:(n + 1) * 64]
            Tgc = Tgrp[gp][c]
            S.op("pe", lambda e: e.matmul(psQ2[:, 0:64], lhsT=wT_g[gp][:, cs], rhs=Sb[:], start=True, stop=True), reads=[Tgc, TS], writes=[TpSeq])
            S.op("pe", lambda e: e.matmul(psQ2[:, 64:128], lhsT=qs, rhs=Sb[:], start=True, stop=True), reads=[Tqkv[0], TS], writes=[TpSeq])
            yield
            S.op("dve", lambda e: e.tensor_tensor(out=vnew[:], in0=u_g[gp][:, cs], in1=psQ2[:, 0:64], op=ALU.subtract),
                 reads=[Tgc, TpSeq, Tvn], writes=[Tvn])
            S.op("dve", lambda e: e.tensor_scalar(out=qs_t[:], in0=psQ2[:, 64:128], scalar1=egc[:, n:n + 1], scalar2=None, op0=ALU.mult),
                 reads=[TpSeq, Tg, Tqs], writes=[Tqs])
            yield
            S.op("pe", lambda e: e.matmul(psQ2[:, 128:192], lhsT=attnT_g[gp][:, cs], rhs=vnew[:], start=True, stop=True), reads=[Tgc, Tvn], writes=[TpSeq])
            S.op("pe", lambda e: e.matmul(psQ2[:, 192:256], lhsT=kdec_g[gp][:, cs], rhs=vnew[:], start=True, stop=True), reads=[Tgc, Tvn], writes=[TpSeq])
            yield
            S.op("dve", lambda e: e.scalar_tensor_tensor(out=Sf[:], in0=Sf[:], scalar=eglast[:, n:n + 1], in1=psQ2[:, 192:256],
                                                         op0=ALU.mult, op1=ALU.add), reads=[TS, Tg, TpSeq], writes=[TS])
            S.op("act", lambda e: e.copy(out=Sb[:], in_=Sf[:]), reads=[TS], writes=[TS])
            S.op("dve", lambda e: e.tensor_tensor(out=o_g[og][:, cs], in0=qs_t[:], in1=psQ2[:, 128:192], op=ALU.add),
                 reads=[Tqs, TpSeq, Tog[og]], writes=[Tog[og]])
            yield

        def pre_gen(n0, ng, gp):
            yield from pre_group_head(n0, ng, gp)
            for c in range(0, ng, NL):
                gens = [pre_chunk(n0 + c + l, c + l, gp, l) for l in range(NL) if c + l < ng]
                while gens:
                    for g_ in list(gens):
                        try:
                            next(g_)
                        except StopIteration:
                            gens.remove(g_)
                    yield

        def seq_gen(n0, ng, gp, og):
            for c in range(ng):
                yield from seq_chunk(n0 + c, c, gp, og)
            S.dma("sp", o_d[:, n0:n0 + ng, :], o_g[og][:, 0:ng * 64].rearrange("p (c f) -> p c f", f=64), reads=[Tog[og]], writes=[Tout])

        groups = [(n0, min(GS, NC - n0)) for n0 in range(0, NC, GS)]
        for _ in pre_gen(groups[0][0], groups[0][1], 0):
            pass
        for gi, (n0, ng) in enumerate(groups):
            gp = gi % 2
            active = [seq_gen(n0, ng, gp, gi % 2)]
            if gi + 1 < len(groups):
                active.append(pre_gen(groups[gi + 1][0], groups[gi + 1][1], 1 - gp))
            while active:
                for g_ in list(active):
                    try:
                        next(g_)
                    except StopIteration:
                        active.remove(g_)
        S.barrier()
        S.stack = st0
    return Tout


def s5_phase2(S, nc, u_d, are_d, aim_d, ldt_d, bre_d, bim_d, cre_d, cim_d, dsk_d, ys_d, L, NBATCH=2):
    st0 = S.stack
    NCH = L // 16
    assert NCH * 16 == L
    TWO_PI = 2.0 * math.pi
    MAGIC = 12582912.0
    NG = 4
    with ExitStack() as ps_:
        S.stack = ps_
        Tc = Trk("s5c")
        prm = S.sb("prm", [128, 12]); Tp = Trk("prm")
        bre = S.sb("bre", [128, NG, 16]); bim = S.sb("bim", [128, NG, 16])
        cre = S.sb("cre", [128, NG, 16]); cim = S.sb("cim", [128, NG, 16])
        dsk = S.sb("dsk", [64, 1])
        S.dma("sp", prm[:, 0:4], are_d[:, :], writes=[Tp])
        S.dma("sp", prm[:, 4:8], aim_d[:, :], writes=[Tp])
        S.dma("sp", prm[:, 8:12], ldt_d[:, :], writes=[Tp])
        S.dma("sp", bre[:], bre_d[:, :, :], writes=[Tp])
        S.dma("sp", bim[:], bim_d[:, :, :], writes=[Tp])
        S.dma("sp", cre[:], cre_d[:, :, :], writes=[Tp])
        S.dma("sp", cim[:], cim_d[:, :, :], writes=[Tp])
        S.dma("sp", dsk[:], dsk_d[:, :], writes=[Tp])
        sc = S.sb("s5sc", [128, 64]); Ts = Trk("s5sc")
        dve = lambda fn: S.op("dve", fn, reads=[Tp, Ts, Tc], writes=[Ts])
        DT = sc[:, 0:4]; ARD = sc[:, 4:8]; AID = sc[:, 8:12]; X = sc[:, 12:16]; DEN = sc[:, 16:20]; RDEN = sc[:, 20:24]
        CFR = sc[:, 24:28]; CFI = sc[:, 28:32]; T1 = sc[:, 32:36]; T2 = sc[:, 36:40]
        ARE = prm[:, 0:4]; AIM = prm[:, 4:8]
        S.op("act", lambda e: e.activation(out=DT, in_=prm[:, 8:12], func=AF.Exp), reads=[Tp], writes=[Ts])
        dve(lambda e: e.tensor_tensor(out=ARD, in0=ARE, in1=DT, op=ALU.mult))
        dve(lambda e: e.tensor_tensor(out=AID, in0=AIM, in1=DT, op=ALU.mult))
        NP = 17
        mag = S.sb("mag", [128, NG, NP]); ang = S.sb("ang", [128, NG, 2 * NP]); ang2 = S.sb("ang2", [128, NG, 2 * NP])
        trg = S.sb("trg", [128, NG, 2 * NP])
        pwr = S.sb("pwr", [128, NG, NP]); pwi = S.sb("pwi", [128, NG, NP]); pws = S.sb("pws", [128, NG, NP])
        for m in range(NP):
            S.op("act", lambda e, m=m: e.activation(out=mag[:, :, m], in_=ARD, func=AF.Exp, scale=float(m)), reads=[Ts], writes=[Ts])
            dve(lambda e, m=m: e.tensor_scalar(out=ang[:, :, m], in0=AID, scalar1=float(m), scalar2=0.0, op0=ALU.mult, op1=ALU.add))
            dve(lambda e, m=m: e.tensor_scalar(out=ang[:, :, NP + m], in0=AID, scalar1=float(m), scalar2=math.pi / 2, op0=ALU.mult, op1=ALU.add))

        def sincos(dst_r, dst_i, dst_s, magt, angt, ang2t, trgt, n):
            dve(lambda e: e.tensor_scalar(out=ang2t, in0=angt, scalar1=1.0 / TWO_PI, scalar2=MAGIC, op0=ALU.mult, op1=ALU.add))
            dve(lambda e: e.tensor_scalar(out=ang2t, in0=ang2t, scalar1=-MAGIC, scalar2=None, op0=ALU.add))
            dve(lambda e: e.scalar_tensor_tensor(out=ang2t, in0=ang2t, scalar=-TWO_PI, in1=angt, op0=ALU.mult, op1=ALU.add))
            dve(lambda e: e.tensor_scalar(out=ang2t, in0=ang2t, scalar1=3.141592, scalar2=-3.141592, op0=ALU.min, op1=ALU.max))
            S.op("act", lambda e: e.activation(out=trgt, in_=ang2t, func=AF.Sin), reads=[Ts], writes=[Ts])
        sincos(None, None, None, mag[:], ang[:], ang2[:], trg[:], NP)
        dve(lambda e: e.tensor_tensor(out=pwi[:], in0=mag[:], in1=trg[:, :, 0:NP], op=ALU.mult))
        dve(lambda e: e.tensor_tensor(out=pwr[:], in0=mag[:], in1=trg[:, :, NP:2 * NP], op=ALU.mult))
        dve(lambda e: e.tensor_copy(out=pws[0:64], in_=pwi[0:64]))
        dve(lambda e: e.tensor_scalar(out=pws[64:128], in0=pwi[64:128], scalar1=-1.0, scalar2=None, op0=ALU.mult))
        dve(lambda e: e.tensor_scalar(out=X, in0=pwr[:, :, 1], scalar1=-1.0, scalar2=None, op0=ALU.add))
        dve(lambda e: e.tensor_tensor(out=DEN, in0=ARE, in1=ARE, op=ALU.mult))
        dve(lambda e: e.tensor_tensor(out=T1, in0=AIM, in1=AIM, op=ALU.mult))
        dve(lambda e: e.tensor_tensor(out=DEN, in0=DEN, in1=T1, op=ALU.add))
        dve(lambda e: e.reciprocal(out=RDEN, in_=DEN))
        dve(lambda e: e.tensor_tensor(out=T1, in0=X, in1=ARE, op=ALU.mult))
        dve(lambda e: e.tensor_tensor(out=T2, in0=pwi[:, :, 1], in1=AIM, op=ALU.mult))
        dve(lambda e: e.tensor_tensor(out=T1, in0=T1, in1=T2, op=ALU.add))
        dve(lambda e: e.tensor_tensor(out=CFR, in0=T1, in1=RDEN, op=ALU.mult))
        dve(lambda e: e.tensor_tensor(out=T1, in0=pwi[:, :, 1], in1=ARE, op=ALU.mult))
        dve(lambda e: e.tensor_tensor(out=T2, in0=X, in1=AIM, op=ALU.mult))
        dve(lambda e: e.tensor_tensor(out=T1, in0=T1, in1=T2, op=ALU.subtract))
        dve(lambda e: e.tensor_tensor(out=CFI, in0=T1, in1=RDEN, op=ALU.mult))
        import os
        if os.environ.get("S5_STOP") == "1":
            S.barrier(); S.stack = st0
            return Trk()
        iot = S.sb("s5iot", [128, 128]); ident = S.sb("s5ident", [128, 128]); Jm = S.sb("s5J", [128, 128]); jt = S.sb("s5jt", [128, 128])
        S.op("pool", lambda e: e.iota(iot[:], pattern=[[1, 128]], base=0, channel_multiplier=-1, allow_small_or_imprecise_dtypes=True), writes=[Tc])
        S.op("dve", lambda e: e.tensor_single_scalar(out=ident[:], in_=iot[:], scalar=0.0, op=ALU.is_equal), reads=[Tc], writes=[Tc])
        S.op("dve", lambda e: e.tensor_single_scalar(out=Jm[:], in_=iot[:], scalar=64.0, op=ALU.is_equal), reads=[Tc], writes=[Tc])
        S.op("dve", lambda e: e.tensor_single_scalar(out=jt[:], in_=iot[:], scalar=-64.0, op=ALU.is_equal), reads=[Tc], writes=[Tc])
        S.op("dve", lambda e: e.tensor_tensor(out=Jm[:], in0=Jm[:], in1=jt[:], op=ALU.add), reads=[Tc], writes=[Tc])
        RT = [[S.sb("RT%d_%d" % (g, m), [128, 128], BF16) for m in range(NP)] for g in range(NG)]
        kk_ = 0
        for g in range(NG):
            for m in range(NP):
                S.op("dve", lambda e, g=g, m=m: e.tensor_scalar(out=jt[:], in0=Jm[:], scalar1=pws[:, g, m:m + 1], scalar2=None, op0=ALU.mult),
                     reads=[Ts, Tc], writes=[Tc])
                S.op("dve", lambda e, g=g, m=m: e.scalar_tensor_tensor(out=RT[g][m][:], in0=ident[:], scalar=pwr[:, g, m:m + 1], in1=jt[:],
                                                                      op0=ALU.mult, op1=ALU.add), reads=[Ts, Tc], writes=[Tc])
        NLV = max(1, int(math.ceil(math.log2(NCH))))
        lvr = S.sb("lvr", [128, NG, NLV]); lvi = S.sb("lvi", [128, NG, NLV]); lvs = S.sb("lvs", [128, NG, NLV])
        dve(lambda e: e.tensor_copy(out=lvr[:, :, 0], in_=pwr[:, :, 16]))
        dve(lambda e: e.tensor_copy(out=lvi[:, :, 0], in_=pwi[:, :, 16]))
        for kk in range(1, NLV):
            dve(lambda e, kk=kk: e.tensor_tensor(out=T1, in0=lvr[:, :, kk - 1], in1=lvr[:, :, kk - 1], op=ALU.mult))
            dve(lambda e, kk=kk: e.tensor_tensor(out=T2, in0=lvi[:, :, kk - 1], in1=lvi[:, :, kk - 1], op=ALU.mult))
            dve(lambda e, kk=kk: e.tensor_tensor(out=lvr[:, :, kk], in0=T1, in1=T2, op=ALU.subtract))
            dve(lambda e, kk=kk: e.tensor_tensor(out=T1, in0=lvr[:, :, kk - 1], in1=lvi[:, :, kk - 1], op=ALU.mult))
            dve(lambda e, kk=kk: e.tensor_scalar(out=lvi[:, :, kk], in0=T1, scalar1=2.0, scalar2=None, op0=ALU.mult))
        dve(lambda e: e.tensor_copy(out=lvs[0:64], in_=lvi[0:64]))
        dve(lambda e: e.tensor_scalar(out=lvs[64:128], in0=lvi[64:128], scalar1=-1.0, scalar2=None, op0=ALU.mult))
        RL = [[S.sb("RL%d_%d" % (g, k), [128, 128]) for k in range(NLV)] for g in range(NG)]
        for g in range(NG):
            for k in range(NLV):
                S.op("dve", lambda e, g=g, k=k: e.tensor_scalar(out=jt[:], in0=Jm[:], scalar1=lvs[:, g, k:k + 1], scalar2=None, op0=ALU.mult),
                     reads=[Ts, Tc], writes=[Tc])
                S.op("dve", lambda e, g=g, k=k: e.scalar_tensor_tensor(out=RL[g][k][:], in0=ident[:], scalar=lvr[:, g, k:k + 1], in1=jt[:],
                                                                      op0=ALU.mult, op1=ALU.add), reads=[Ts, Tc], writes=[Tc])
        bst = S.sb("bst", [128, 64]); tmpb = S.sb("tmpb", [128, 16]); Tbb = Trk("bst")
        BT = [S.sb("BTs%d" % g, [64, 128], BF16) for g in range(NG)]
        CS = [S.sb("CSs%d" % g, [128, 64], BF16) for g in range(NG)]
        psT = S.ps("psT", [128, 512]); TpT = Trk()
        for g in range(NG):
            c0 = 16 * g
            S.op("pool", lambda e: e.memset(bst[:], 0.0), reads=[Tbb], writes=[Tbb])
            cfr0 = sc[0:64, 24 + g:25 + g]; cfi0 = sc[0:64, 28 + g:29 + g]
            cfr1 = sc[64:128, 24 + g:25 + g]; cfi1 = sc[64:128, 28 + g:29 + g]
            S.op("dve", lambda e, g=g, cfi0=cfi0: e.tensor_scalar(out=tmpb[0:64, :], in0=bim[0:64, g, :], scalar1=cfi0, scalar2=None, op0=ALU.mult),
                 reads=[Tp, Ts, Tbb], writes=[Tbb])
            S.op("dve", lambda e, g=g, cfr0=cfr0, c0=c0: e.scalar_tensor_tensor(out=bst[0:64, c0:c0 + 16], in0=bre[0:64, g, :], scalar=cfr0,
                                                                               in1=tmpb[0:64, :], op0=ALU.mult, op1=ALU.subtract),
                 reads=[Tp, Ts, Tbb], writes=[Tbb])
            S.op("dve", lambda e, g=g, cfi1=cfi1: e.tensor_scalar(out=tmpb[64:128, :], in0=bre[64:128, g, :], scalar1=cfi1, scalar2=None, op0=ALU.mult),
                 reads=[Tp, Ts, Tbb], writes=[Tbb])
            S.op("dve", lambda e, g=g, cfr1=cfr1, c0=c0: e.scalar_tensor_tensor(out=bst[64:128, c0:c0 + 16], in0=bim[64:128, g, :], scalar=cfr1,
                                                                               in1=tmpb[64:128, :], op0=ALU.mult, op1=ALU.add),
                 reads=[Tp, Ts, Tbb], writes=[Tbb])
            S.op("pe", lambda e: e.transpose(out=psT[0:64, 0:128], in_=bst[:], identity=ident[:]), reads=[Tbb, Tc], writes=[TpT])
            S.op("act", lambda e, g=g: e.copy(out=BT[g][:], in_=psT[0:64, 0:128]), reads=[TpT], writes=[Tc])
            S.op("pool", lambda e, g=g: e.memset(CS[g][:], 0.0), writes=[Tc])
            S.op("dve", lambda e, g=g, c0=c0: e.tensor_copy(out=CS[g][0:64, c0:c0 + 16], in_=cre[0:64, g, :]), reads=[Tp, Tc], writes=[Tc])
            S.op("dve", lambda e, g=g, c0=c0: e.tensor_scalar(out=CS[g][64:128, c0:c0 + 16], in0=cim[64:128, g, :], scalar1=-1.0, scalar2=None,
                                                              op0=ALU.mult), reads=[Tp, Tc], writes=[Tc])

        ub = S.sb("ub", [64, L], BF16); Tub = Trk("ub")
        ustg = [S.sb("ustg%d" % i, [64, 1024]) for i in range(2)]; Tus = [Trk() for i in range(2)]
        BU = [S.sb("BU%d" % i, [128, L], BF16) for i in range(2)]; TBU = [Trk() for i in range(2)]
        CW = (NCH + 2) // 3
        coltiles = [(c, min(CW, NCH - c)) for c in range(0, NCH, CW)]
        stt = [S.sb("stt%d" % i, [128, CW * 16], BF16) for i in range(2)]; Tst = [Trk() for i in range(2)]
        Z = [[S.sb("Z%d_%d" % (i, k), [128, NCH]) for k in range(2)] for i in range(2)]; TZ = [[Trk() for k in range(2)] for i in range(2)]
        Spb = [S.sb("Spb%d" % i, [128, NCH], BF16) for i in range(2)]; TSp = [Trk() for i in range(2)]
        usk = [S.sb("usk%d" % i, [64, 512]) for i in range(2)]; Tusk = [Trk() for i in range(2)]
        ysb = [S.sb("ysb%d" % i, [64, 512]) for i in range(2)]; Tys = [Trk() for i in range(2)]
        psR = [S.ps("psR%d" % i, [128, 512]) for i in range(4)]; TpR = [Trk() for i in range(4)]
        psY = [S.ps("psY%d" % i, [64, 512]) for i in range(2)]; TpY = [Trk() for i in range(2)]
        Tout = Trk("s5out")
        pc = [0]; yc = [0]; ec = [0]

        def evac(dst_ap, src_ap, reads, writes):
            ec[0] += 1
            if ec[0] % 2 == 0:
                S.op("act", lambda e: e.copy(out=dst_ap, in_=src_ap), reads=reads, writes=writes)
            else:
                S.op("dve", lambda e: e.tensor_copy(out=dst_ap, in_=src_ap), reads=reads, writes=writes)

        def prep_group(g, gi):
            def bu_body(c0):
                n = min(400, L - c0)
                j = pc[0] % 4; pc[0] += 1
                S.op("pe", lambda e: e.matmul(psR[j][:, 0:n], lhsT=BT[g][:], rhs=ub[:, c0:c0 + n], start=True, stop=True),
                     reads=[Tub, Tc], writes=[TpR[j]])
                n0 = c0 // 16; nn = n // 16
                dst = BU[gi][:].rearrange("p (t n) -> p t n", t=16)[:, :, n0:n0 + nn]
                srcv = psR[j][:, 0:n].rearrange("p (n t) -> p t n", t=16)
                evac(dst, srcv, [TpR[j]], [TBU[gi]])
            for c0 in range(0, L, 400):
                bu_body(c0)

            def p1_body(cc, n):
                j = pc[0] % 4; pc[0] += 1
                for tp in range(16):
                    r = BU[gi][:, tp * NCH + cc:tp * NCH + cc + n]
                    S.op("pe", lambda e, tp=tp, r=r: e.matmul(psR[j][:, 0:n], lhsT=RT[g][15 - tp][:], rhs=r, start=(tp == 0), stop=(tp == 15)),
                         reads=[TBU[gi], Tc], writes=[TpR[j]])
                evac(Z[gi][0][:, cc:cc + n], psR[j][:, 0:n], [TpR[j]], [TZ[gi][0]])
            for (cc, n) in coltiles:
                p1_body(cc, n)
            cur = 0
            for kk in range(NLV):
                o = 1 << kk
                if o >= NCH:
                    break
                nxt = 1 - cur
                m = NCH - o

                def lvl(c0, w, cur=cur, nxt=nxt, o=o, kk=kk):
                    j = pc[0] % 4; pc[0] += 1
                    S.op("pe", lambda e: e.matmul(psR[j][:, 0:w], lhsT=RL[g][kk][:], rhs=Z[gi][cur][:, c0:c0 + w], start=True, stop=True),
                         reads=[TZ[gi][cur], Tc], writes=[TpR[j]])
                    S.op("dve", lambda e: e.tensor_tensor(out=Z[gi][nxt][:, o + c0:o + c0 + w], in0=Z[gi][cur][:, o + c0:o + c0 + w],
                                                          in1=psR[j][:, 0:w], op=ALU.add), reads=[TZ[gi][cur], TpR[j]], writes=[TZ[gi][nxt]])
                for c0 in range(0, m, 512):
                    lvl(c0, min(512, m - c0))
                S.op("pool", lambda e, cur=cur, nxt=nxt, o=o: e.tensor_copy(out=Z[gi][nxt][:, 0:o], in_=Z[gi][cur][:, 0:o]),
                     reads=[TZ[gi][cur]], writes=[TZ[gi][nxt]])
                cur = nxt
            S.op("pool", lambda e: e.memset(Spb[gi][:, 0:1], 0.0), reads=[TSp[gi]], writes=[TSp[gi]])
            if NCH > 1:
                S.op("dve", lambda e, cur=cur: e.tensor_copy(out=Spb[gi][:, 1:NCH], in_=Z[gi][cur][:, 0:NCH - 1]),
                     reads=[TZ[gi][cur], TSp[gi]], writes=[TSp[gi]])

        def p2_tau(g, gi, cc, n, tau):
            j = pc[0] % 4; pc[0] += 1
            for tp in range(tau + 1):
                r = BU[gi][:, tp * NCH + cc:tp * NCH + cc + n]
                S.op("pe", lambda e, tp=tp, r=r: e.matmul(psR[j][:, 0:n], lhsT=RT[g][tau - tp][:], rhs=r, start=(tp == 0), stop=False),
                     reads=[TBU[gi], Tc], writes=[TpR[j]])
            S.op("pe", lambda e: e.matmul(psR[j][:, 0:n], lhsT=RT[g][tau + 1][:], rhs=Spb[gi][:, cc:cc + n], start=False, stop=True),
                 reads=[TSp[gi], Tc], writes=[TpR[j]])
            evac(stt[gi][:, tau:16 * n:16], psR[j][:, 0:n], [TpR[j]], [Tst[gi]])

        def p2_y(pr, b, cc, n, x0):
            lo = 16 * cc
            w = min(512, 16 * n - x0)
            jy = yc[0] % 2; yc[0] += 1
            t0 = lo + x0
            r0, r1 = 32 * pr, 32 * pr + 32
            S.dma("sp", usk[jy][r0:r1, 0:w], u_d[r0:r1, b, t0:t0 + w], writes=[Tusk[jy]])
            for gi in range(2):
                g = 2 * pr + gi
                S.op("pe", lambda e, gi=gi, g=g: e.matmul(psY[jy][:, 0:w], lhsT=CS[g][:], rhs=stt[gi][:, x0:x0 + w], start=(gi == 0), stop=(gi == 1)),
                     reads=[Tst[gi], Tc], writes=[TpY[jy]])
            S.op("dve", lambda e: e.scalar_tensor_tensor(out=ysb[jy][r0:r1, 0:w], in0=usk[jy][r0:r1, 0:w], scalar=dsk[r0:r1, 0:1],
                                                         in1=psY[jy][r0:r1, 0:w], op0=ALU.mult, op1=ALU.add),
                 reads=[Tusk[jy], TpY[jy], Tp], writes=[Tys[jy]])
            S.dma("sp", ys_d[r0:r1, b, t0:t0 + w], ysb[jy][r0:r1, 0:w], reads=[Tys[jy]], writes=[Tout])

        si = 0
        for b in range(NBATCH):
            for c0 in range(0, L, 1024):
                n = min(1024, L - c0)
                i = si % 2; si += 1
                S.dma("sp", ustg[i][:, 0:n], u_d[:, b, c0:c0 + n], writes=[Tus[i]])
                S.op("pool", lambda e, i=i, n=n, c0=c0: e.tensor_copy(out=ub[:, c0:c0 + n], in_=ustg[i][:, 0:n]),
                     reads=[Tus[i]], writes=[Tub])
            for pr in range(2):
                for gi in range(2):
                    prep_group(2 * pr + gi, gi)
                for (cc, n) in coltiles:
                    for gi in range(2):
                        for tau in range(16):
                            p2_tau(2 * pr + gi, gi, cc, n, tau)
                    for x0 in range(0, 16 * n, 512):
                        p2_y(pr, b, cc, n, x0)
        S.barrier()
        S.stack = st0
    return Tout


from concourse.bass_utils import run_bass_kernel_spmd

N_META = 16
SEQ = 16384
LTOK = N_META + SEQ
NTOK = 4100
SB_NB = 129
SB_PAD = 112
DN_NC = 257
DN_PAD = 48


def build_mix():
    nc = bass.Bass("TRN2", target_bir_lowering=False)
    di = lambda n, s: nc.dram_tensor(n, list(s), F32, kind="ExternalInput").ap()
    do = lambda n, s: nc.dram_tensor(n, list(s), F32, kind="ExternalOutput").ap()
    LPS = SB_NB * 128
    LPD = DN_NC * 64
    qT = di("sb_q", [64, LPS]); kT = di("sb_k", [64, LPS]); vv = di("sb_v", [128, SB_NB, 64]); oT = do("sb_o", [64, LPS])
    xq = di("dn_xq", [64, LPD + 3]); xk = di("dn_xk", [64, LPD + 3]); xv = di("dn_xv", [64, LPD + 3])
    cw = di("dn_cw", [64, 12]); a_d = di("dn_a", [64, DN_NC]); b_d = di("dn_b", [64, DN_NC]); hp = di("dn_hp", [64, 2])
    dn_o = do("dn_o", [64, DN_NC, 64])
    u_d = di("s5_u", [64, 2, LTOK]); are = di("s5_are", [128, 4]); aim = di("s5_aim", [128, 4]); ldt = di("s5_ldt", [128, 4])
    bre = di("s5_bre", [128, 4, 16]); bim = di("s5_bim", [128, 4, 16]); cre = di("s5_cre", [128, 4, 16]); cim = di("s5_cim", [128, 4, 16])
    dsk = di("s5_dsk", [64, 1]); ys = do("s5_y", [64, 2, LTOK])
    with ExitStack() as st:
        S = Sched(nc, st)
        T1 = sb_phase(S, nc, qT, kT, vv, oT, SB_NB, SB_PAD)
        T2 = dn_phase(S, nc, xq, xk, xv, cw, a_d, b_d, hp, dn_o, DN_NC, DN_PAD)
        T3 = s5_phase2(S, nc, u_d, are, aim, ldt, bre, bim, cre, cim, dsk, ys, LTOK, 2)
        S.finish([T1, T2, T3])
    return nc


def _g8(g):
    return np.ascontiguousarray(np.asarray(g, np.float32).reshape(-1, 128).T)


def _pairlay(x):
    x = np.asarray(x, np.float32)
    y = np.moveaxis(x, 0, 1)
    return np.ascontiguousarray(np.concatenate([y, y], axis=0))


def _c(x):
    return np.ascontiguousarray(x, dtype=np.float32)


def _run(nc, maps):
    res = run_bass_kernel_spmd(nc, maps, core_ids=list(range(8)))
    return res.results


def kernel(**I):
    I = {k: np.asarray(v) for k, v in I.items()}
    x = I["x"]; meta = I["meta_tokens"]
    h = np.concatenate([np.broadcast_to(meta[None], (2, N_META, D)), x], axis=1)
    hT = [_c(h[c // 4, (c % 4) * NTOK:(c % 4 + 1) * NTOK].T) for c in range(8)]
    progs = {}

    def tok_launch(mode, l, hT, mix=None):
        if mode not in progs:
            progs[mode] = build_tok(mode)
        maps = []
        for c in range(8):
            m = {"h_in": hT[c]}
            if mode in ("CA", "C1"):
                b, q = c // 4, c % 4
                sl = slice(q * NTOK, (q + 1) * NTOK)
                m.update(osb=_c(mix["osb"][b][:, sl]), odn=_c(mix["odn"][b][:, sl]), dnz=_c(mix["dnz"][b][:, sl]), ys5=_c(mix["ys5"][b][:, sl]),
                         sbn=_c(np.tile(I["sb_out_norm"][l], 2).reshape(128, 1)), dnn=_c(np.tile(I["dn_out_norm"][l], 2).reshape(128, 1)),
                         wglu=_c(I["s5_w_glu"][l]), bglu=_g8(I["s5_b_glu"][l]), s5n=_g8(I["s5_out_norm"][l]), w_out=_c(I["w_out"][l]),
                         wg2=_c(I["ffn2_w_gate"][l]), wu2=_c(I["ffn2_w_up"][l]), wd2=_c(I["ffn2_w_down"][l]), n2=_g8(I["ffn2_norm"][l]))
            if mode == "CA":
                l2 = l + 1
            else:
                l2 = l
            if mode in ("A0", "CA"):
                m.update(wg1=_c(I["ffn1_w_gate"][l2]), wu1=_c(I["ffn1_w_up"][l2]), wd1=_c(I["ffn1_w_down"][l2]), n1=_g8(I["ffn1_norm"][l2]),
                         w_in=_c(I["w_in"][l2]), nmix=_g8(I["mix_norm"][l2]))
            if mode == "C1":
                m.update(nfin=_g8(I["final_norm"]))
            maps.append(m)
        return _run(progs[mode], maps)

    def mix_launch(l, projT):
        if "mix" not in progs:
            progs["mix"] = build_mix()
        maps = []
        for c in range(8):
            b, hh = c // 4, c % 4
            P = projT[b]
            m = {}
            padz = lambda r, n: _c(np.concatenate([np.zeros((r.shape[0], n), np.float32), r], axis=1))
            m["sb_q"] = padz(P[hh * 64:(hh + 1) * 64], SB_PAD)
            m["sb_k"] = padz(P[256 + hh * 64:256 + (hh + 1) * 64], SB_PAD)
            vT = padz(P[512 + hh * 64:512 + (hh + 1) * 64], SB_PAD)
            m["sb_v"] = _c(vT.T.reshape(SB_NB, 128, 64).transpose(1, 0, 2))
            o = 768
            m["dn_xq"] = padz(P[o + hh * 64:o + (hh + 1) * 64], DN_PAD + 3)
            m["dn_xk"] = padz(P[o + 256 + hh * 64:o + 256 + (hh + 1) * 64], DN_PAD + 3)
            m["dn_xv"] = padz(P[o + 512 + hh * 64:o + 512 + (hh + 1) * 64], DN_PAD + 3)
            cwl = I["dn_conv_w"][l]
            m["dn_cw"] = _c(np.concatenate([cwl[:, hh * 64:(hh + 1) * 64].T, cwl[:, 256 + hh * 64:256 + (hh + 1) * 64].T,
                                            cwl[:, 512 + hh * 64:512 + (hh + 1) * 64].T], axis=1))
            brow = P[1792 + hh]; arow = P[1796 + hh]
            col = lambda r: _c(np.concatenate([np.zeros(DN_PAD, np.float32), r]).reshape(DN_NC, 64).T)
            m["dn_a"] = col(arow); m["dn_b"] = col(brow)
            m["dn_hp"] = _c(np.tile(np.array([[I["dn_a_log"][l, hh], I["dn_dt_bias"][l, hh]]], np.float32), (64, 1)))
            g0 = 4 * c
            m["s5_u"] = _c(np.stack([projT[0][1800 + 64 * c:1800 + 64 * c + 64], projT[1][1800 + 64 * c:1800 + 64 * c + 64]], axis=1))
            m["s5_are"] = _pairlay(I["s5_a_re"][l, g0:g0 + 4]); m["s5_aim"] = _pairlay(I["s5_a_im"][l, g0:g0 + 4])
            m["s5_ldt"] = _pairlay(np.repeat(I["s5_log_dt"][l, g0:g0 + 4][:, None], 64, 1))
            m["s5_bre"] = _pairlay(I["s5_b_re"][l, g0:g0 + 4]); m["s5_bim"] = _pairlay(I["s5_b_im"][l, g0:g0 + 4])
            m["s5_cre"] = _pairlay(I["s5_c_re"][l, g0:g0 + 4].transpose(0, 2, 1)); m["s5_cim"] = _pairlay(I["s5_c_im"][l, g0:g0 + 4].transpose(0, 2, 1))
            m["s5_dsk"] = _c(I["s5_d"][l, 64 * c:64 * c + 64].reshape(64, 1))
            maps.append(m)
        res = _run(progs["mix"], maps)
        osb = [np.zeros((256, LTOK), np.float32) for _ in range(2)]
        odn = [np.zeros((256, LTOK), np.float32) for _ in range(2)]
        ys5 = [np.zeros((512, LTOK), np.float32) for _ in range(2)]
        for c in range(8):
            b, hh = c // 4, c % 4
            osb[b][hh * 64:(hh + 1) * 64] = res[c]["sb_o"][:, SB_PAD:]
            od = res[c]["dn_o"].transpose(1, 0, 2).reshape(DN_NC * 64, 64)[DN_PAD:]
            odn[b][hh * 64:(hh + 1) * 64] = od.T
            for bb in range(2):
                ys5[bb][64 * c:64 * c + 64] = res[c]["s5_y"][:, bb]
        dnz = [projT[b][1536:1792] for b in range(2)]
        return {"osb": osb, "odn": odn, "dnz": dnz, "ys5": ys5}

    def gather_proj(res):
        projT = [np.zeros((INW, LTOK), np.float32) for _ in range(2)]
        for c in range(8):
            b, q = c // 4, c % 4
            projT[b][:, q * NTOK:(q + 1) * NTOK] = res[c]["proj"][:INW]
        return projT

    r = tok_launch("A0", 0, hT)
    hT = [r[c]["h_out"] for c in range(8)]
    mix = mix_launch(0, gather_proj(r))
    r = tok_launch("CA", 0, hT, mix)
    hT = [r[c]["h_out"] for c in range(8)]
    mix = mix_launch(1, gather_proj(r))
    r = tok_launch("C1", 1, hT, mix)
    out = np.zeros((2, LTOK, D), np.float32)
    for c in range(8):
        b, q = c // 4, c % 4
        out[b, q * NTOK:(q + 1) * NTOK] = r[c]["y_out"].T
    return np.ascontiguousarray(out[:, N_META:])
```

```python
from contextlib import ExitStack
import numpy as np
import concourse.bass as bass
import concourse.mybir as mybir

F32 = mybir.dt.float32
BF16 = mybir.dt.bfloat16
AF = mybir.ActivationFunctionType
ALU = mybir.AluOpType
AX = mybir.AxisListType

N_DMA_SEMS = 24


class Trk:
    __slots__ = ("name", "w", "r")

    def __init__(self, name=""):
        self.name = name
        self.w = None
        self.r = []


class Sched:
    ENG = ("pe", "act", "dve", "pool", "sp")

    def __init__(self, nc, stack):
        self.nc = nc
        self.stack = stack
        self.prog = {e: [] for e in self.ENG}
        self.cnt = {e: 0 for e in ("pe", "act", "dve", "pool")}
        self.sems = {}
        for e in ("pe", "act", "dve", "pool"):
            self.sems[e] = stack.enter_context(nc.semaphore("s_" + e))
        self.dsems = [stack.enter_context(nc.semaphore("d%d" % i)) for i in range(N_DMA_SEMS)]
        self.dcnt = [0] * N_DMA_SEMS
        self.dnext = 0
        self.seen = {e: {} for e in self.ENG}
        self.n_wait = 0

    def sb(self, name, shape, dt=F32):
        self.uid = getattr(self, "uid", 0) + 1
        if not hasattr(self, "names"):
            self.names = {}
        self.names[name] = "%s_%d" % (name, self.uid)
        return self.stack.enter_context(self.nc.sbuf_tensor("%s_%d" % (name, self.uid), list(shape), dt))

    def ps(self, name, shape, dt=F32):
        self.uid = getattr(self, "uid", 0) + 1
        return self.stack.enter_context(self.nc.psum_tensor("%s_%d" % (name, self.uid), list(shape), dt))

    def _semobj(self, key):
        return self.sems[key] if isinstance(key, str) else self.dsems[key]

    def _need(self, eng, ev, waits):
        if ev is None:
            return
        key, val, src = ev
        if eng == "pe" and src == "pe":
            return
        if self.seen[eng].get(key, 0) >= val:
            return
        waits[key] = max(waits.get(key, 0), val)

    def _collect(self, eng, reads, writes):
        waits = {}
        for t in reads:
            self._need(eng, t.w, waits)
        for t in writes:
            self._need(eng, t.w, waits)
            for ev in t.r:
                self._need(eng, ev, waits)
        for key, val in waits.items():
            self.seen[eng][key] = val
        return list(waits.items())

    def _record(self, ev, reads, writes):
        for t in reads:
            t.r.append(ev)
            if len(t.r) > 64:
                best = {}
                for k, v, s in t.r:
                    if k not in best or best[k][1] < v:
                        best[k] = (k, v, s)
                t.r = list(best.values())
        for t in writes:
            t.w = ev
            t.r = []

    def op(self, eng, fn, reads=(), writes=()):
        waits = self._collect(eng, reads, writes)
        self.cnt[eng] += 1
        n = self.cnt[eng]
        sem = self.sems[eng]
        wl = [(self._semobj(k), v) for k, v in waits]
        self.n_wait += len(wl)

        def emit(e, wl=wl, fn=fn, sem=sem):
            for s, v in wl:
                e.wait_ge(s, v)
            fn(e).then_inc(sem, 1)
        self.prog[eng].append(emit)
        self._record((eng, n, eng), reads, writes)

    def dma(self, q, out, in_, reads=(), writes=(), **kw):
        i = self.dnext
        self.dnext = (self.dnext + 1) % N_DMA_SEMS
        waits = dict(self._collect(q, reads, writes))
        if self.dcnt[i] > 0 and self.seen[q].get(i, 0) < self.dcnt[i]:
            waits[i] = max(waits.get(i, 0), self.dcnt[i])
            self.seen[q][i] = self.dcnt[i]
        self.dcnt[i] += 16
        val = self.dcnt[i]
        sem = self.dsems[i]
        wl = [(self._semobj(k), v) for k, v in waits.items()]

        def emit(e, wl=wl, sem=sem, out=out, in_=in_, kw=kw):
            for s, v in wl:
                e.wait_ge(s, v)
            e.dma_start(out=out, in_=in_, **kw).then_inc(sem, 16)
        self.prog[q].append(emit)
        self._record((i, val, "dma"), reads, writes)

    def barrier(self):
        for eng in self.ENG:
            waits = {}
            for e in ("pe", "act", "dve", "pool"):
                if e != eng and self.cnt[e] > 0 and self.seen[eng].get(e, 0) < self.cnt[e]:
                    waits[e] = self.cnt[e]
            for i in range(N_DMA_SEMS):
                if self.dcnt[i] > 0 and self.seen[eng].get(i, 0) < self.dcnt[i]:
                    waits[i] = self.dcnt[i]
            for k, v in waits.items():
                self.seen[eng][k] = v
            wl = [(self._semobj(k), v) for k, v in waits.items()]

            def emit(e, wl=wl):
                for s, v in wl:
                    e.wait_ge(s, v)
            self.prog[eng].append(emit)

    def finish(self, out_trackers):
        nc = self.nc
        waits = {}
        for t in out_trackers:
            self._need("sp", t.w, waits)
        for i in range(N_DMA_SEMS):
            if self.dcnt[i] > 0 and self.seen["sp"].get(i, 0) < self.dcnt[i]:
                waits[i] = max(waits.get(i, 0), self.dcnt[i])
        for e in ("pe", "act", "dve", "pool"):
            if self.cnt[e] > 0:
                waits[e] = max(waits.get(e, 0), self.cnt[e])
        wl = [(self._semobj(k), v) for k, v in waits.items()]

        def emit(e, wl=wl):
            for s, v in wl:
                e.wait_ge(s, v)
        self.prog["sp"].append(emit)

        prog = self.prog
        with nc.Block() as block:
            @block.tensor
            def _(e):
                for f in prog["pe"]:
                    f(e)

            @block.scalar
            def _(e):
                for f in prog["act"]:
                    f(e)

            @block.vector
            def _(e):
                for f in prog["dve"]:
                    f(e)

            @block.gpsimd
            def _(e):
                for f in prog["pool"]:
                    f(e)

            @block.sync
            def _(e):
                for f in prog["sp"]:
                    f(e)

from contextlib import ExitStack

D = 1024
KD = 8
DFF = 2816
KF = 22
INW = 2312
INP = 2432
KP = 19
EPS = 1e-6
WCOLS = 22528
STG = 1408


def build_tok(mode, NT=4100, TT=205):
    assert NT % TT == 0
    NTILE = NT // TT
    nc = bass.Bass("TRN2", target_bir_lowering=False)

    def din(name, shape):
        return nc.dram_tensor(name, list(shape), F32, kind="ExternalInput").ap()

    def dout(name, shape):
        return nc.dram_tensor(name, list(shape), F32, kind="ExternalOutput").ap()

    def dint(name, shape):
        return nc.dram_tensor(name, list(shape), F32, kind="Internal").ap()

    h_in = din("h_in", [D, NT])
    do_epi = mode in ("CA", "C1")
    do_a = mode in ("A0", "CA")
    do_fin = mode == "C1"
    if do_epi:
        osb = din("osb", [256, NT]); odn = din("odn", [256, NT]); dnz = din("dnz", [256, NT])
        ys5 = din("ys5", [512, NT])
        sbn = din("sbn", [128, 1]); dnn = din("dnn", [128, 1])
        wglu = din("wglu", [512, 512]); bglu = din("bglu", [128, 4]); s5n = din("s5n", [128, 4])
        w_out = din("w_out", [D, D])
        wg2 = din("wg2", [D, DFF]); wu2 = din("wu2", [D, DFF]); wd2 = din("wd2", [DFF, D]); n2 = din("n2", [128, KD])
        hs_a = dint("hs_a", [D, NT])
    if do_a:
        wg1 = din("wg1", [D, DFF]); wu1 = din("wu1", [D, DFF]); wd1 = din("wd1", [DFF, D]); n1 = din("n1", [128, KD])
        w_in = din("w_in", [D, INW]); nmix = din("nmix", [128, KD])
        h_out = dout("h_out", [D, NT])
        proj = dout("proj", [INP, NT])
        if do_epi:
            hs_b = dint("hs_b", [D, NT])
    if do_fin:
        nfin = din("nfin", [128, KD])
        y_out = dout("y_out", [D, NT])

    out_trks = []
    with ExitStack() as st:
        S = Sched(nc, st)
        WA = S.sb("WA", [128, WCOLS], BF16)
        WB = S.sb("WB", [128, WCOLS], BF16)
        WC = S.sb("WC", [128, WCOLS], BF16)
        NSTG = 3
        stage = [S.sb("stg%d" % i, [128, STG]) for i in range(NSTG)]
        Tstage = [Trk("stg%d" % i) for i in range(NSTG)]
        sidx = [0]
        ones_bf = S.sb("ones_bf", [128, 128], BF16)
        blk_bf = S.sb("blk_bf", [128, 128], BF16)
        gains = S.sb("gains", [128, 64])
        Tconst = Trk("const")
        Tg = Trk("gains")
        S.op("pool", lambda e: e.memset(ones_bf[:], 1.0), writes=[Tconst])
        S.op("pool", lambda e: e.memset(blk_bf[:], 0.0), writes=[Tconst])
        S.op("pool", lambda e: e.memset(blk_bf[0:64, 0:64], 1.0), writes=[Tconst])
        S.op("pool", lambda e: e.memset(blk_bf[64:128, 64:128], 1.0), writes=[Tconst])
        if do_a:
            S.dma("sp", gains[:, 0:8], n1[:, :], writes=[Tg])
            S.dma("sp", gains[:, 8:16], nmix[:, :], writes=[Tg])
        if do_epi:
            S.dma("sp", gains[:, 16:24], n2[:, :], writes=[Tg])
            S.dma("sp", gains[:, 32:33], sbn[:, :], writes=[Tg])
            S.dma("sp", gains[:, 33:34], dnn[:, :], writes=[Tg])
            S.dma("sp", gains[:, 34:38], bglu[:, :], writes=[Tg])
            S.dma("sp", gains[:, 38:42], s5n[:, :], writes=[Tg])
        if do_fin:
            S.dma("sp", gains[:, 24:32], nfin[:, :], writes=[Tg])

        TW = {"A": [Trk("WA%d" % k) for k in range(KF)], "B": [Trk("WB%d" % k) for k in range(KF)],
              "C": [Trk("WC%d" % k) for k in range(KF)]}

        def load_w(wtile, trk, dst_c0, src_ap, ncols, scale_ap):
            for c0 in range(0, ncols, STG):
                n = min(STG, ncols - c0)
                i = sidx[0] % NSTG; sidx[0] += 1
                stg = stage[i]
                ceng = ("pool", "dve", "act")[i] if scale_ap is None else ("pool", "dve", "dve")[i]
                S.dma("sp", stg[:, 0:n], src_ap[:, c0:c0 + n], writes=[Tstage[i]])
                dst = wtile[:, dst_c0 + c0: dst_c0 + c0 + n]
                if scale_ap is None:
                    if ceng == "act":
                        S.op("act", lambda e, dst=dst, stg=stg, n=n: e.copy(out=dst, in_=stg[:, 0:n]),
                             reads=[Tstage[i]], writes=[trk])
                    else:
                        S.op(ceng, lambda e, dst=dst, stg=stg, n=n: e.tensor_copy(out=dst, in_=stg[:, 0:n]),
                             reads=[Tstage[i]], writes=[trk])
                else:
                    S.op(ceng, lambda e, dst=dst, stg=stg, n=n, sc=scale_ap: e.tensor_scalar(
                        out=dst, in0=stg[:, 0:n], scalar1=sc, scalar2=1.0, op0=ALU.mult, op1=ALU.mult),
                        reads=[Tstage[i], Tg], writes=[trk])

        def load_ffn_weights(wg, wu, wd, gcol):
            for k in range(KD):
                load_w(WA, TW["A"][k], k * DFF, wg[k * 128:(k + 1) * 128, :], DFF, gains[:, gcol + k: gcol + k + 1])
                load_w(WB, TW["B"][k], k * DFF, wu[k * 128:(k + 1) * 128, :], DFF, gains[:, gcol + k: gcol + k + 1])
            for f in range(KF):
                load_w(WC, TW["C"][f], f * D, wd[f * 128:(f + 1) * 128, :], D, None)

        def tok_view(ap, nch):
            return ap.rearrange("(k p) n -> p k n", p=128)

        def norm_stats(ph, src_tile, Tsrc, nk, lhs_ones, inv_n, sq, Tsq, ps_stat, Tps, lnv, Tln, rstd, Trs):
            S.op("act", lambda e: e.activation(out=sq[:, 0:nk, :], in_=src_tile[:, 0:nk, :], func=AF.Square),
                 reads=[Tsrc], writes=[Tsq])
            for k in range(nk):
                S.op("pe", lambda e, k=k: e.matmul(ps_stat[:, 0:TT], lhsT=lhs_ones[:], rhs=sq[:, k, :],
                                                    start=(k == 0), stop=(k == nk - 1)),
                     reads=[Tsq, Tconst], writes=[Tps])
            S.op("act", lambda e: e.activation(out=lnv[:], in_=ps_stat[:, 0:TT], func=AF.Ln, scale=inv_n, bias=eps_t[:, 0:1]),
                 reads=[Tps, Tconst], writes=[Tln])
            S.op("act", lambda e: e.activation(out=rstd[:], in_=lnv[:], func=AF.Exp, scale=-0.5),
                 reads=[Tln], writes=[Trs])

        eps_t = S.sb("eps_t", [128, 1])
        S.op("pool", lambda e: e.memset(eps_t[:], EPS), writes=[Tconst])

        def phase_ffn(src, dst, wg, wu, wd, gcol, fin_gcol=None, fin_dst=None):
            load_ffn_weights(wg, wu, wd, gcol)
            with ExitStack() as ps_:
                S.stack = ps_
                hb = [S.sb("hb%d" % i, [128, KD, TT]) for i in range(2)]
                Th = [Trk("hb%d" % i) for i in range(2)]
                xn = [S.sb("xn%d" % i, [128, KD, TT], BF16) for i in range(2)]
                Txn = [Trk("xn%d" % i) for i in range(2)]
                sq = S.sb("sq", [128, KD, TT], BF16); Tsq = Trk("sq")
                lnv = S.sb("lnv", [128, TT]); Tln = Trk("lnv")
                rstd = S.sb("rstd", [128, TT]); Trs = Trk("rstd")
                hmid = S.sb("hmid", [128, KF, TT], BF16)
                Thm = [Trk("hm%d" % f) for f in range(KF)]
                sgt = [S.sb("sgt%d" % i, [128, TT]) for i in range(2)]
                Tsg = [Trk("sgt%d" % i) for i in range(2)]
                ps_stat = S.ps("ps_stat", [128, 512]); Tps = Trk("ps_stat")
                psg = [S.ps("psg%d" % i, [128, 512]) for i in range(2)]
                psu = [S.ps("psu%d" % i, [128, 512]) for i in range(2)]
                psd = [S.ps("psd%d" % i, [128, 512]) for i in range(2)]
                Tpg = [Trk() for i in range(2)]; Tpu = [Trk() for i in range(2)]; Tpd = [Trk() for i in range(2)]
                if fin_dst is not None:
                    ob = S.sb("ob", [128, KD, TT]); Tob = Trk("ob")
                srcv = tok_view(src, KD); dstv = tok_view(dst, KD) if dst is not None else None
                finv = tok_view(fin_dst, KD) if fin_dst is not None else None
                Tdst = Trk("dst")

                def pre(t):
                    i = t % 2
                    c0 = t * TT
                    S.dma("sp", hb[i][:], srcv[:, :, c0:c0 + TT], writes=[Th[i]])
                    norm_stats(None, hb[i], Th[i], KD, ones_bf, 1.0 / D, sq, Tsq, ps_stat, Tps, lnv, Tln, rstd, Trs)
                    S.op("dve", lambda e, i=i: e.tensor_tensor(
                        out=xn[i][:], in0=hb[i][:], in1=rstd[:].unsqueeze(1).broadcast_to([128, KD, TT]), op=ALU.mult),
                        reads=[Th[i], Trs], writes=[Txn[i]])

                gcnt = [0]

                def gateup(t):
                    i = t % 2
                    for f in range(KF):
                        j = gcnt[0] % 2; gcnt[0] += 1
                        for k in range(KD):
                            S.op("pe", lambda e, k=k, f=f, j=j: e.matmul(
                                psg[j][:, 0:TT], lhsT=WA[:, k * DFF + f * 128: k * DFF + (f + 1) * 128],
                                rhs=xn[i][:, k, :], start=(k == 0), stop=(k == KD - 1)),
                                reads=[TW["A"][k], Txn[i]], writes=[Tpg[j]])
                        for k in range(KD):
                            S.op("pe", lambda e, k=k, f=f, j=j: e.matmul(
                                psu[j][:, 0:TT], lhsT=WB[:, k * DFF + f * 128: k * DFF + (f + 1) * 128],
                                rhs=xn[i][:, k, :], start=(k == 0), stop=(k == KD - 1)),
                                reads=[TW["B"][k], Txn[i]], writes=[Tpu[j]])
                        S.op("act", lambda e, j=j: e.activation(out=sgt[j][:], in_=psg[j][:, 0:TT], func=AF.Silu),
                             reads=[Tpg[j]], writes=[Tsg[j]])
                        S.op("dve", lambda e, j=j, f=f: e.tensor_tensor(
                            out=hmid[:, f, :], in0=sgt[j][:], in1=psu[j][:, 0:TT], op=ALU.mult),
                            reads=[Tsg[j], Tpu[j]], writes=[Thm[f]])

                dcnt = [0]

                def down(t):
                    i = t % 2
                    c0 = t * TT
                    for dc in range(KD):
                        j = dcnt[0] % 2; dcnt[0] += 1
                        for f in range(KF):
                            S.op("pe", lambda e, f=f, dc=dc, j=j: e.matmul(
                                psd[j][:, 0:TT], lhsT=WC[:, f * D + dc * 128: f * D + (dc + 1) * 128],
                                rhs=hmid[:, f, :], start=(f == 0), stop=(f == KF - 1)),
                                reads=[TW["C"][f], Thm[f]], writes=[Tpd[j]])
                        S.op("dve", lambda e, dc=dc, j=j, i=i: e.scalar_tensor_tensor(
                            out=hb[i][:, dc, :], in0=psd[j][:, 0:TT], scalar=0.5, in1=hb[i][:, dc, :],
                            op0=ALU.mult, op1=ALU.add),
                            reads=[Tpd[j], Th[i]], writes=[Th[i]])
                    if dstv is not None:
                        S.dma("sp", dstv[:, :, c0:c0 + TT], hb[i][:], reads=[Th[i]], writes=[Tdst])
                    if fin_dst is not None:
                        norm_stats(None, hb[i], Th[i], KD, ones_bf, 1.0 / D, sq, Tsq, ps_stat, Tps, lnv, Tln, rstd, Trs)
                        for k in range(KD):
                            S.op("dve", lambda e, k=k, i=i: e.scalar_tensor_tensor(
                                out=ob[:, k, :], in0=hb[i][:, k, :], scalar=gains[:, fin_gcol + k: fin_gcol + k + 1],
                                in1=rstd[:], op0=ALU.mult, op1=ALU.mult),
                                reads=[Th[i], Trs, Tg], writes=[Tob])
                        S.dma("sp", finv[:, :, c0:c0 + TT], ob[:], reads=[Tob], writes=[Tdst])

                pre(0)
                for t in range(NTILE):
                    gateup(t)
                    if t + 1 < NTILE:
                        pre(t + 1)
                    down(t)
                S.barrier()
                S.stack = st
            return Tdst

        def phase_proj(src, dstp, w, gcol):
            for k in range(KD):
                load_w(WA, TW["A"][k], k * INP, w[k * 128:(k + 1) * 128, :], INW, gains[:, gcol + k: gcol + k + 1])
            with ExitStack() as ps_:
                S.stack = ps_
                hb = [S.sb("hb%d" % i, [128, KD, TT]) for i in range(2)]
                Th = [Trk() for i in range(2)]
                xn = [S.sb("xn%d" % i, [128, KD, TT], BF16) for i in range(2)]
                Txn = [Trk() for i in range(2)]
                sq = S.sb("sq", [128, KD, TT], BF16); Tsq = Trk("sq")
                lnv = S.sb("lnv", [128, TT]); Tln = Trk("lnv")
                rstd = S.sb("rstd", [128, TT]); Trs = Trk("rstd")
                obt = [S.sb("obt%d" % i, [128, KP, TT]) for i in range(2)]
                Tobt = [Trk() for i in range(2)]
                for i_ in range(2):
                    S.op("pool", lambda e, i_=i_: e.memset(obt[i_][:, KP - 1, :], 0.0), writes=[Tobt[i_]])
                dstv3 = dstp.rearrange("(c p) n -> p c n", p=128)
                ps_stat = S.ps("ps_stat", [128, 512]); Tps = Trk("ps_stat")
                pp = [S.ps("pp%d" % i, [128, 512]) for i in range(4)]
                Tpp = [Trk() for i in range(4)]
                srcv = tok_view(src, KD)
                Tdst = Trk("projdst")
                cntb = [0]

                def ld(t):
                    i = t % 2
                    c0 = t * TT
                    S.dma("sp", hb[i][:], srcv[:, :, c0:c0 + TT], writes=[Th[i]])

                def pre(t):
                    i = t % 2
                    norm_stats(None, hb[i], Th[i], KD, ones_bf, 1.0 / D, sq, Tsq, ps_stat, Tps, lnv, Tln, rstd, Trs)
                    S.op("dve", lambda e, i=i: e.tensor_tensor(
                        out=xn[i][:], in0=hb[i][:], in1=rstd[:].unsqueeze(1).broadcast_to([128, KD, TT]), op=ALU.mult),
                        reads=[Th[i], Trs], writes=[Txn[i]])

                def body(t):
                    i = t % 2
                    c0 = t * TT
                    if t + 1 < NTILE:
                        ld(t + 1)
                    for pc in range(KP):
                        if pc == KP // 2 and t + 1 < NTILE:
                            pre(t + 1)
                        cnt = cntb[0]
                        j = cnt % 4; cnt += 1; cntb[0] = cnt
                        m = min(128, INW - pc * 128)
                        for k in range(KD):
                            S.op("pe", lambda e, k=k, pc=pc, j=j, m=m: e.matmul(
                                pp[j][0:m, 0:TT], lhsT=WA[:, k * INP + pc * 128: k * INP + pc * 128 + m],
                                rhs=xn[i][:, k, :], start=(k == 0), stop=(k == KD - 1)),
                                reads=[TW["A"][k], Txn[i]], writes=[Tpp[j]])
                        eng = "act" if (cnt % 2 == 0) else "dve"
                        if eng == "act":
                            S.op("act", lambda e, j=j, m=m, pc=pc: e.copy(out=obt[i][0:m, pc, :], in_=pp[j][0:m, 0:TT]),
                                 reads=[Tpp[j]], writes=[Tobt[i]])
                        else:
                            S.op("dve", lambda e, j=j, m=m, pc=pc: e.tensor_copy(out=obt[i][0:m, pc, :], in_=pp[j][0:m, 0:TT]),
                                 reads=[Tpp[j]], writes=[Tobt[i]])
                    S.dma("sp", dstv3[:, :, c0:c0 + TT], obt[i][:], reads=[Tobt[i]], writes=[Tdst])
                ld(0)
                pre(0)
                for t in range(NTILE):
                    body(t)
                S.barrier()
                S.stack = st
            return Tdst

        def phase_epi(src, dst):
            for k in range(KD):
                load_w(WA, TW["A"][k], k * D, w_out[k * 128:(k + 1) * 128, :], D, None)
            for k in range(4):
                load_w(WB, TW["B"][k], k * 512, wglu[k * 128:(k + 1) * 128, :], 512, None)
            with ExitStack() as ps_:
                S.stack = ps_
                hb = [S.sb("hb%d" % i, [128, KD, TT]) for i in range(2)]
                Th = [Trk() for i in range(2)]
                xin = [S.sb("xin%d" % i, [128, 10, TT]) for i in range(2)]
                Tx = [Trk() for i in range(2)]
                mixed = S.sb("mixed", [128, KD, TT], BF16); Tmx = Trk("mixed")
                sq = S.sb("sq", [128, 4, TT], BF16); Tsq = Trk("sq")
                lnv = S.sb("lnv", [128, TT]); Tln = Trk("lnv")
                rstd = S.sb("rstd", [128, TT]); Trs = Trk("rstd")
                t1 = S.sb("t1", [128, 4, TT]); Tt1 = Trk("t1")
                t2 = S.sb("t2", [128, 4, TT]); Tt2 = Trk("t2")
                ge = S.sb("ge", [128, 4, TT]); Tge = Trk("ge")
                geb = S.sb("geb", [128, 4, TT], BF16); Tgeb = Trk("geb")
                vv = S.sb("vv", [128, 4, TT]); Tvv = Trk("vv")
                sg = S.sb("sg", [128, TT]); Tsgm = Trk("sg")
                ps_stat = S.ps("ps_stat", [128, 512]); Tps = Trk("ps_stat")
                pq = [S.ps("pq%d" % i, [128, 512]) for i in range(2)]
                Tpq = [Trk() for i in range(2)]
                srcv = tok_view(src, KD); dstv = tok_view(dst, KD)
                osbv = tok_view(osb, 2); odnv = tok_view(odn, 2); dnzv = tok_view(dnz, 2); ys5v = tok_view(ys5, 4)
                Tdst = Trk("epidst")
                cntb = [0]

                def ld(t):
                    i = t % 2
                    c0 = t * TT
                    S.dma("sp", hb[i][:], srcv[:, :, c0:c0 + TT], writes=[Th[i]])
                    S.dma("sp", xin[i][:, 0:2, :], osbv[:, :, c0:c0 + TT], writes=[Tx[i]])
                    S.dma("sp", xin[i][:, 2:4, :], odnv[:, :, c0:c0 + TT], writes=[Tx[i]])
                    S.dma("sp", xin[i][:, 4:6, :], dnzv[:, :, c0:c0 + TT], writes=[Tx[i]])
                    S.dma("sp", xin[i][:, 6:10, :], ys5v[:, :, c0:c0 + TT], writes=[Tx[i]])

                def body(t):
                    i = t % 2
                    c0 = t * TT
                    if t == 0:
                        ld(0)
                    if t + 1 < NTILE:
                        ld(t + 1)
                    X = xin[i]
                    for c in range(2):
                        S.op("act", lambda e, c=c: e.activation(out=sq[:, 0, :], in_=X[:, c, :], func=AF.Square),
                             reads=[Tx[i]], writes=[Tsq])
                        S.op("pe", lambda e: e.matmul(ps_stat[:, 0:TT], lhsT=blk_bf[:], rhs=sq[:, 0, :], start=True, stop=True),
                             reads=[Tsq, Tconst], writes=[Tps])
                        S.op("act", lambda e: e.activation(out=lnv[:], in_=ps_stat[:, 0:TT], func=AF.Ln, scale=1.0 / 64, bias=eps_t[:, 0:1]),
                             reads=[Tps, Tconst], writes=[Tln])
                        S.op("act", lambda e: e.activation(out=rstd[:], in_=lnv[:], func=AF.Exp, scale=-0.5),
                             reads=[Tln], writes=[Trs])
                        S.op("dve", lambda e, c=c: e.scalar_tensor_tensor(
                            out=mixed[:, c, :], in0=X[:, c, :], scalar=gains[:, 32:33], in1=rstd[:], op0=ALU.mult, op1=ALU.mult),
                            reads=[Tx[i], Trs, Tg], writes=[Tmx])
                    for c in range(2):
                        S.op("act", lambda e, c=c: e.activation(out=sq[:, 0, :], in_=X[:, 2 + c, :], func=AF.Square),
                             reads=[Tx[i]], writes=[Tsq])
                        S.op("pe", lambda e: e.matmul(ps_stat[:, 0:TT], lhsT=blk_bf[:], rhs=sq[:, 0, :], start=True, stop=True),
                             reads=[Tsq, Tconst], writes=[Tps])
                        S.op("act", lambda e: e.activation(out=lnv[:], in_=ps_stat[:, 0:TT], func=AF.Ln, scale=1.0 / 64, bias=eps_t[:, 0:1]),
                             reads=[Tps, Tconst], writes=[Tln])
                        S.op("act", lambda e: e.activation(out=rstd[:], in_=lnv[:], func=AF.Exp, scale=-0.5),
                             reads=[Tln], writes=[Trs])
                        S.op("dve", lambda e, c=c: e.scalar_tensor_tensor(
                            out=t1[:, 0, :], in0=X[:, 2 + c, :], scalar=gains[:, 33:34], in1=rstd[:], op0=ALU.mult, op1=ALU.mult),
                            reads=[Tx[i], Trs, Tg], writes=[Tt1])
                        S.op("act", lambda e, c=c: e.activation(out=t2[:, 0, :], in_=X[:, 4 + c, :], func=AF.Silu),
                             reads=[Tx[i]], writes=[Tt2])
                        S.op("dve", lambda e, c=c: e.tensor_tensor(out=mixed[:, 2 + c, :], in0=t1[:, 0, :], in1=t2[:, 0, :], op=ALU.mult),
                             reads=[Tt1, Tt2], writes=[Tmx])
                    Y = X[:, 6:10, :]
                    S.op("act", lambda e, Y=Y: e.activation(out=t1[:], in_=Y, func=AF.Square), reads=[Tx[i]], writes=[Tt1])
                    S.op("dve", lambda e: e.tensor_scalar(out=t1[:], in0=t1[:], scalar1=0.044715, scalar2=1.0, op0=ALU.mult, op1=ALU.add),
                         reads=[Tt1], writes=[Tt1])
                    S.op("dve", lambda e, Y=Y: e.tensor_tensor(out=t2[:], in0=t1[:], in1=Y, op=ALU.mult),
                         reads=[Tt1, Tx[i]], writes=[Tt2])
                    S.op("act", lambda e: e.activation(out=t1[:], in_=t2[:], func=AF.Tanh, scale=0.7978845608028654),
                         reads=[Tt2], writes=[Tt1])
                    S.op("dve", lambda e, Y=Y: e.scalar_tensor_tensor(out=t2[:], in0=t1[:], scalar=1.0, in1=Y, op0=ALU.add, op1=ALU.mult),
                         reads=[Tt1, Tx[i]], writes=[Tt2])
                    S.op("dve", lambda e: e.tensor_scalar(out=ge[:], in0=t2[:], scalar1=0.5, scalar2=None, op0=ALU.mult),
                         reads=[Tt2], writes=[Tge])
                    S.op("act", lambda e: e.copy(out=geb[:], in_=ge[:]), reads=[Tge], writes=[Tgeb])
                    for co in range(4):
                        j = cntb[0] % 2; cntb[0] += 1
                        for ki in range(4):
                            S.op("pe", lambda e, ki=ki, co=co, j=j: e.matmul(
                                pq[j][:, 0:TT], lhsT=WB[:, ki * 512 + co * 128: ki * 512 + (co + 1) * 128],
                                rhs=geb[:, ki, :], start=(ki == 0), stop=(ki == 3)),
                                reads=[TW["B"][ki], Tgeb], writes=[Tpq[j]])
                        S.op("act", lambda e, co=co, j=j: e.activation(out=sg[:], in_=pq[j][:, 0:TT], func=AF.Sigmoid,
                                                                        bias=gains[:, 34 + co: 35 + co]),
                             reads=[Tpq[j], Tg], writes=[Tsgm])
                        S.op("dve", lambda e, co=co: e.tensor_tensor(out=vv[:, co, :], in0=ge[:, co, :], in1=sg[:], op=ALU.mult),
                             reads=[Tge, Tsgm], writes=[Tvv])
                    norm_stats(None, vv, Tvv, 4, ones_bf, 1.0 / 512, sq, Tsq, ps_stat, Tps, lnv, Tln, rstd, Trs)
                    for c in range(4):
                        S.op("dve", lambda e, c=c: e.scalar_tensor_tensor(
                            out=mixed[:, 4 + c, :], in0=vv[:, c, :], scalar=gains[:, 38 + c: 39 + c], in1=rstd[:],
                            op0=ALU.mult, op1=ALU.mult),
                            reads=[Tvv, Trs, Tg], writes=[Tmx])
                    for dc in range(KD):
                        j = cntb[0] % 2; cntb[0] += 1
                        for k in range(KD):
                            S.op("pe", lambda e, k=k, dc=dc, j=j: e.matmul(
                                pq[j][:, 0:TT], lhsT=WA[:, k * D + dc * 128: k * D + (dc + 1) * 128],
                                rhs=mixed[:, k, :], start=(k == 0), stop=(k == KD - 1)),
                                reads=[TW["A"][k], Tmx], writes=[Tpq[j]])
                        S.op("dve", lambda e, dc=dc, j=j, i=i: e.tensor_tensor(
                            out=hb[i][:, dc, :], in0=hb[i][:, dc, :], in1=pq[j][:, 0:TT], op=ALU.add),
                            reads=[Tpq[j], Th[i]], writes=[Th[i]])
                    S.dma("sp", dstv[:, :, c0:c0 + TT], hb[i][:], reads=[Th[i]], writes=[Tdst])
                for t in range(NTILE):
                    body(t)
                S.barrier()
                S.stack = st
            return Tdst

        cur = h_in
        if do_epi:
            phase_epi(cur, hs_a)
            cur = hs_a
            if do_fin:
                Td = phase_ffn(cur, None, wg2, wu2, wd2, 16, fin_gcol=24, fin_dst=y_out)
                out_trks.append(Td)
            else:
                phase_ffn(cur, hs_b, wg2, wu2, wd2, 16)
                cur = hs_b
        if do_a:
            Td = phase_ffn(cur, h_out, wg1, wu1, wd1, 0)
            out_trks.append(Td)
            Tp = phase_proj(h_out, proj, w_in, 8)
            out_trks.append(Tp)
        S.finish(out_trks)
    return nc

import math
from contextlib import ExitStack

EPS = 1e-6


def sb_phase(S, nc, qT_d, kT_d, v_d, oT_d, NB, PADK):
    import os
    SB_DUMMY = int(os.environ.get("SB_DUMMY", "4"))
    SB_DN = int(os.environ.get("SB_DN", "512"))
    st0 = S.stack
    with ExitStack() as ps_:
        S.stack = ps_
        LP = NB * 128
        Tc = Trk("sbconst")
        qb = S.sb("qb", [64, LP], BF16); Tq = Trk("qb")
        kb = S.sb("kb", [64, LP], BF16); Tk = Trk("kb")
        vb = S.sb("vb", [128, NB * 64], BF16); Tv = Trk("vb")
        stg = [S.sb("sbstg%d" % i, [128, 2048]) for i in range(2)]
        Tstg = [Trk() for i in range(2)]
        si = 0
        for c0 in range(0, LP, 2048):
            n = min(2048, LP - c0)
            for (src, dst, Td, sc) in ((qT_d, qb, Tq, 1.0), (kT_d, kb, Tk, 0.125)):
                i = si % 2; si += 1
                S.dma("sp", stg[i][0:64, 0:n], src[:, c0:c0 + n], writes=[Tstg[i]])
                S.op("pool", lambda e, i=i, n=n, dst=dst, c0=c0, sc=sc: e.tensor_scalar(
                    out=dst[:, c0:c0 + n], in0=stg[i][0:64, 0:n], scalar1=sc, scalar2=1.0, op0=ALU.mult, op1=ALU.mult),
                    reads=[Tstg[i]], writes=[Td])
        vflat = v_d.rearrange("p b d -> p (b d)")
        for c0 in range(0, NB * 64, 2048):
            n = min(2048, NB * 64 - c0)
            i = si % 2; si += 1
            S.dma("sp", stg[i][:, 0:n], vflat[:, c0:c0 + n], writes=[Tstg[i]])
            S.op("pool", lambda e, i=i, n=n, c0=c0: e.tensor_copy(out=vb[:, c0:c0 + n], in_=stg[i][:, 0:n]),
                 reads=[Tstg[i]], writes=[Tv])
        negtri = S.sb("negtri", [128, 128], BF16)
        negones = S.sb("negones", [1, 128], BF16)
        onescol = S.sb("onescol", [128, 1], BF16)
        iot = S.sb("iot", [128, 512])
        masks = [S.sb("mask%d" % m, [128, 512], BF16) for m in range(4)]
        padmask = S.sb("padmask", [128, 512], BF16)
        mask00 = S.sb("mask00", [128, 512], BF16)
        S.op("pool", lambda e: e.iota(iot[:, 0:128], pattern=[[1, 128]], base=0, channel_multiplier=-1,
                                      allow_small_or_imprecise_dtypes=True), writes=[Tc])
        S.op("dve", lambda e: e.tensor_scalar(out=negtri[:], in0=iot[:, 0:128], scalar1=0.0, scalar2=-1.0,
                                              op0=ALU.is_le, op1=ALU.mult), reads=[Tc], writes=[Tc])
        S.op("pool", lambda e: e.memset(negones[:], -1.0), writes=[Tc])
        S.op("pool", lambda e: e.memset(onescol[:], 1.0), writes=[Tc])
        for m in range(4):
            S.op("pool", lambda e, m=m: e.iota(iot[:], pattern=[[1, 512]], base=-128 * m, channel_multiplier=-1,
                                                allow_small_or_imprecise_dtypes=True), reads=[Tc], writes=[Tc])
            S.op("dve", lambda e, m=m: e.tensor_single_scalar(out=masks[m][:], in_=iot[:], scalar=0.0, op=ALU.is_gt),
                 reads=[Tc], writes=[Tc])
        S.op("pool", lambda e: e.memset(padmask[:], 1.0), reads=[Tc], writes=[Tc])
        S.op("pool", lambda e: e.tensor_copy(out=mask00[:], in_=masks[0][:]), reads=[Tc], writes=[Tc])
        if PADK > 0:
            pk = PADK
            S.op("pool", lambda e: e.memset(padmask[0:pk, :], 0.0), reads=[Tc], writes=[Tc])
            S.op("pool", lambda e: e.memset(mask00[0:pk, :], 0.0), reads=[Tc], writes=[Tc])

        tiles = []
        b = 0
        while b < NB:
            nb = min(4, NB - b)
            tiles.append((b, nb))
            b += nb
        Tout = Trk("sbout")

        def make_lane(ln):
            eS = [S.sb("eS%d" % i, [128, 512]) for i in range(2)]; TeS = [Trk() for i in range(2)]
            spb = [S.sb("spb%d" % i, [128, 512], BF16) for i in range(2)]; Tsp = [Trk() for i in range(2)]
            wb = [S.sb("wb%d" % i, [128, 512], BF16) for i in range(2)]; Twb = [Trk() for i in range(2)]
            rhi = [S.sb("rhi%d" % i, [1, 512], BF16) for i in range(2)]
            rlo = [S.sb("rlo%d" % i, [1, 512], BF16) for i in range(2)]; Trh = [Trk() for i in range(2)]
            Rf = S.sb("Rf", [1, 512]); TR = Trk("R")
            obuf = [S.sb("obuf%d" % i, [64, 512]) for i in range(2)]; Tob = [Trk() for i in range(2)]
            psB = [S.ps("psB%d" % i, [128, 512]) for i in range(2)]; TpB = [Trk() for i in range(2)]
            psC = S.ps("psC", [128, 512]); TpC = Trk()
            psO = S.ps("psO", [64, 512]); TpO = Trk()

            cnt = [0]
            pend = []

            def mask_for(qb0, j):
                m = j - qb0
                if m >= 0:
                    if j == 0:
                        return mask00
                    return masks[m]
                if j == 0 and PADK > 0:
                    return padmask
                return None

            def stage1(qb0, nq, j, slot):
                N = nq * 128
                q0 = qb0 * 128
                S.op("pe", lambda e: e.matmul(psB[slot][:, 0:N], lhsT=kb[:, j * 128:(j + 1) * 128], rhs=qb[:, q0:q0 + N],
                                              start=True, stop=False, skip_group_check=True), reads=[Tk, Tq], writes=[TpB[slot]])
                S.op("act", lambda e: e.activation(out=eS[slot][:, 0:N], in_=psB[slot][:, 0:N], func=AF.Exp),
                     reads=[TpB[slot]], writes=[TeS[slot]])
                def part_b():
                    S.op("act", lambda e: e.activation(out=spb[slot][:, 0:N], in_=eS[slot][:, 0:N], func=AF.Ln, bias=one_t[:, 0:1]),
                         reads=[TeS[slot], Tc], writes=[Tsp[slot]])
                    mk = mask_for(qb0, j)
                    if mk is not None:
                        S.op("dve", lambda e: e.tensor_tensor(out=spb[slot][:, 0:N], in0=spb[slot][:, 0:N], in1=mk[:, 0:N], op=ALU.mult),
                             reads=[Tsp[slot], Tc], writes=[Tsp[slot]])
                return part_b

            def stage2(qb0, nq, j, slot, rslot, first, last):
                N = nq * 128
                q0 = qb0 * 128
                S.op("pe", lambda e: e.matmul(psB[slot][:, 0:N], lhsT=negtri[:], rhs=spb[slot][:, 0:N],
                                              start=False, stop=False, skip_group_check=True), reads=[Tsp[slot], Tc], writes=[TpB[slot]])
                S.op("pe", lambda e: e.matmul(psB[slot][:, 0:N], lhsT=negones[:], rhs=rhi[rslot][:, 0:N],
                                              start=False, stop=True, skip_group_check=True), reads=[Trh[rslot], Tc], writes=[TpB[slot]])
                if not last:
                    S.op("pe", lambda e: e.matmul(psC[0:1, 0:N], lhsT=onescol[:], rhs=spb[slot][:, 0:N], start=True, stop=True),
                         reads=[Tsp[slot], Tc], writes=[TpC])
                for _d in range(SB_DUMMY):
                    S.op("pe", lambda e, _d=_d: e.matmul(psC[64:128, 0:N], lhsT=negtri[:, 64:128], rhs=spb[slot][:, 0:N], start=True, stop=True),
                         reads=[Tsp[slot], Tc], writes=[])
                S.op("act", lambda e: e.activation(out=wb[slot][:, 0:N], in_=psB[slot][:, 0:N], func=AF.Exp),
                     reads=[TpB[slot]], writes=[Twb[slot]])
                mk = mask_for(qb0, j)
                if mk is not None:
                    S.op("dve", lambda e: e.tensor_tensor(out=wb[slot][:, 0:N], in0=wb[slot][:, 0:N], in1=mk[:, 0:N], op=ALU.mult),
                         reads=[Twb[slot], Tc], writes=[Twb[slot]])
                def emit_o():
                    S.op("pe", lambda e: e.matmul(psO[:, 0:N], lhsT=vb[:, j * 64:(j + 1) * 64], rhs=wb[slot][:, 0:N],
                                                  start=first, stop=last), reads=[Tv, Twb[slot]], writes=[TpO])
                pend.append(emit_o)
                if not last:
                    nr = 1 - rslot
                    S.op("dve", lambda e: e.tensor_tensor(out=Rf[:, 0:N], in0=Rf[:, 0:N], in1=psC[0:1, 0:N], op=ALU.add),
                         reads=[TR, TpC], writes=[TR])
                    S.op("dve", lambda e: e.tensor_copy(out=rhi[nr][:, 0:N], in_=Rf[:, 0:N]),
                         reads=[TR], writes=[Trh[nr]])


            one_t = S.sb("one_t", [128, 1])
            S.op("pool", lambda e: e.memset(one_t[:], 1.0), writes=[Tc])

            def tile_gen(ti, qb0, nq):
                N = nq * 128
                js = list(range(qb0 + nq - 1, -1, -1))
                S.op("dve", lambda e: e.memset(Rf[:], 0.0), reads=[TR], writes=[TR])
                S.op("dve", lambda e: e.memset(rhi[0][:], 0.0), reads=[Trh[0]], writes=[Trh[0]])
                S.op("dve", lambda e: e.memset(rlo[0][:], 0.0), reads=[Trh[0]], writes=[Trh[0]])
                rslot = 0
                slot0 = cnt[0] % 2
                stage1(qb0, nq, js[0], slot0)()
                for idx, j in enumerate(js):
                    slot = cnt[0] % 2; cnt[0] += 1
                    pb = None
                    if idx + 1 < len(js):
                        pb = stage1(qb0, nq, js[idx + 1], 1 - slot)
                    stage2(qb0, nq, j, slot, rslot, idx == 0, idx == len(js) - 1)
                    if pb is not None:
                        pb()
                    rslot = 1 - rslot
                    while len(pend) > 1:
                        pend.pop(0)()
                    yield
                while pend:
                    pend.pop(0)()
                oi = ti % 2
                S.op("dve", lambda e, oi=oi, N=N: e.tensor_copy(out=obuf[oi][:, 0:N], in_=psO[:, 0:N]),
                     reads=[TpO], writes=[Tob[oi]])
                S.dma("sp", oT_d[:, qb0 * 128: qb0 * 128 + N], obuf[oi][:, 0:N], reads=[Tob[oi]], writes=[Tout])

            return tile_gen

        lanes = [make_lane(0), make_lane(1)]
        queue = list(enumerate(tiles))[::-1]
        active = [None, None]
        while queue or any(g is not None for g in active):
            for ln in range(2):
                if active[ln] is None and queue:
                    ti, (qb0, nq) = queue.pop(0)
                    active[ln] = lanes[ln](ti, qb0, nq)
                if active[ln] is not None:
                    try:
                        next(active[ln])
                    except StopIteration:
                        active[ln] = None
        S.barrier()
        S.stack = st0
    return Tout


def s5_phase(S, nc, u_d, are_d, aim_d, ldt_d, bre_d, bim_d, cre_d, cim_d, dsk_d, ys_d, L, NBATCH=2):
    st0 = S.stack
    NCH = L // 16
    assert NCH * 16 == L
    TWO_PI = 2.0 * math.pi
    MAGIC = 12582912.0
    with ExitStack() as ps_:
        S.stack = ps_
        Tc = Trk("s5c")
        prm = S.sb("prm", [128, 8]); Tp = Trk("prm")
        bre = S.sb("bre", [128, 2, 16]); bim = S.sb("bim", [128, 2, 16])
        cre = S.sb("cre", [128, 2, 16]); cim = S.sb("cim", [128, 2, 16])
        dsk = S.sb("dsk", [64, 1])
        S.dma("sp", prm[:, 0:2], are_d[:, :], writes=[Tp])
        S.dma("sp", prm[:, 2:4], aim_d[:, :], writes=[Tp])
        S.dma("sp", prm[:, 4:6], ldt_d[:, :], writes=[Tp])
        S.dma("sp", bre[:], bre_d[:, :, :], writes=[Tp])
        S.dma("sp", bim[:], bim_d[:, :, :], writes=[Tp])
        S.dma("sp", cre[:], cre_d[:, :, :], writes=[Tp])
        S.dma("sp", cim[:], cim_d[:, :, :], writes=[Tp])
        S.dma("sp", dsk[:], dsk_d[:, :], writes=[Tp])
        sc = S.sb("s5sc", [128, 64]); Ts = Trk("s5sc")
        dve = lambda fn, r=(), w=(): S.op("dve", fn, reads=list(r) + [Tp, Ts, Tc], writes=list(w) if w else [Ts])
        S.op("act", lambda e: e.activation(out=sc[:, 0:2], in_=prm[:, 4:6], func=AF.Exp), reads=[Tp], writes=[Ts])
        dve(lambda e: e.tensor_tensor(out=sc[:, 2:4], in0=prm[:, 0:2], in1=sc[:, 0:2], op=ALU.mult))
        dve(lambda e: e.tensor_tensor(out=sc[:, 4:6], in0=prm[:, 2:4], in1=sc[:, 0:2], op=ALU.mult))
        NP = 17
        mag = S.sb("mag", [128, 2, NP]); ang = S.sb("ang", [128, 2, 2 * NP]); ang2 = S.sb("ang2", [128, 2, 2 * NP])
        trg = S.sb("trg", [128, 2, 2 * NP])
        pwr = S.sb("pwr", [128, 2, NP]); pwi = S.sb("pwi", [128, 2, NP]); pwn = S.sb("pwn", [128, 2, NP])
        for m in range(NP):
            S.op("act", lambda e, m=m: e.activation(out=mag[:, :, m], in_=sc[:, 2:4], func=AF.Exp, scale=float(m)),
                 reads=[Ts], writes=[Ts])
            dve(lambda e, m=m: e.tensor_scalar(out=ang[:, :, m], in0=sc[:, 4:6], scalar1=float(m), scalar2=0.0,
                                               op0=ALU.mult, op1=ALU.add))
            dve(lambda e, m=m: e.tensor_scalar(out=ang[:, :, NP + m], in0=sc[:, 4:6], scalar1=float(m), scalar2=math.pi / 2,
                                               op0=ALU.mult, op1=ALU.add))
        dve(lambda e: e.tensor_scalar(out=ang2[:], in0=ang[:], scalar1=1.0 / TWO_PI, scalar2=MAGIC, op0=ALU.mult, op1=ALU.add))
        dve(lambda e: e.tensor_scalar(out=ang2[:], in0=ang2[:], scalar1=-MAGIC, scalar2=None, op0=ALU.add))
        dve(lambda e: e.scalar_tensor_tensor(out=ang2[:], in0=ang2[:], scalar=-TWO_PI, in1=ang[:], op0=ALU.mult, op1=ALU.add))
        dve(lambda e: e.tensor_scalar(out=ang2[:], in0=ang2[:], scalar1=3.141592, scalar2=-3.141592, op0=ALU.min, op1=ALU.max))
        S.op("act", lambda e: e.activation(out=trg[:], in_=ang2[:], func=AF.Sin), reads=[Ts], writes=[Ts])
        dve(lambda e: e.tensor_tensor(out=pwi[:], in0=mag[:], in1=trg[:, :, 0:NP], op=ALU.mult))
        dve(lambda e: e.tensor_tensor(out=pwr[:], in0=mag[:], in1=trg[:, :, NP:2 * NP], op=ALU.mult))
        dve(lambda e: e.tensor_scalar(out=pwn[:], in0=pwi[:], scalar1=-1.0, scalar2=None, op0=ALU.mult))
        X = sc[:, 6:8]; DEN = sc[:, 8:10]; RDEN = sc[:, 10:12]; CFR = sc[:, 12:14]; CFI = sc[:, 14:16]; T1 = sc[:, 16:18]; T2 = sc[:, 18:20]
        ARE = prm[:, 0:2]; AIM = prm[:, 2:4]
        dve(lambda e: e.tensor_scalar(out=X, in0=pwr[:, :, 1], scalar1=-1.0, scalar2=None, op0=ALU.add))
        dve(lambda e: e.tensor_tensor(out=DEN, in0=ARE, in1=ARE, op=ALU.mult))
        dve(lambda e: e.tensor_tensor(out=T1, in0=AIM, in1=AIM, op=ALU.mult))
        dve(lambda e: e.tensor_tensor(out=DEN, in0=DEN, in1=T1, op=ALU.add))
        dve(lambda e: e.reciprocal(out=RDEN, in_=DEN))
        dve(lambda e: e.tensor_tensor(out=T1, in0=X, in1=ARE, op=ALU.mult))
        dve(lambda e: e.tensor_tensor(out=T2, in0=pwi[:, :, 1], in1=AIM, op=ALU.mult))
        dve(lambda e: e.tensor_tensor(out=T1, in0=T1, in1=T2, op=ALU.add))
        dve(lambda e: e.tensor_tensor(out=CFR, in0=T1, in1=RDEN, op=ALU.mult))
        dve(lambda e: e.tensor_tensor(out=T1, in0=pwi[:, :, 1], in1=ARE, op=ALU.mult))
        dve(lambda e: e.tensor_tensor(out=T2, in0=X, in1=AIM, op=ALU.mult))
        dve(lambda e: e.tensor_tensor(out=T1, in0=T1, in1=T2, op=ALU.subtract))
        dve(lambda e: e.tensor_tensor(out=CFI, in0=T1, in1=RDEN, op=ALU.mult))
        iot = S.sb("s5iot", [128, 128]); ident = S.sb("s5ident", [128, 128])
        S.op("pool", lambda e: e.iota(iot[:], pattern=[[1, 128]], base=0, channel_multiplier=-1,
                                      allow_small_or_imprecise_dtypes=True), writes=[Tc])
        S.op("dve", lambda e: e.tensor_single_scalar(out=ident[:], in_=iot[:], scalar=0.0, op=ALU.is_equal), reads=[Tc], writes=[Tc])
        bblk = [S.sb("bblk%d" % i, [128, 64]) for i in range(2)]
        tmpb = S.sb("tmpb", [128, 16])
        BT = [[S.sb("BT%d%d" % (pr, pt), [64, 128], BF16) for pt in range(2)] for pr in range(2)]
        CB = [[S.sb("CB%d%d" % (pr, pt), [128, 64], BF16) for pt in range(2)] for pr in range(2)]
        psT = S.ps("psT", [128, 512]); TpT = Trk()
        Tbb = Trk("bblk")
        for pr in range(2):
            for i in range(2):
                S.op("pool", lambda e, i=i: e.memset(bblk[i][:], 0.0), reads=[Tbb], writes=[Tbb])
            for g2 in range(2):
                r0, r1 = 64 * g2, 64 * g2 + 64
                c0 = 32 * pr + 16 * g2
                cfr = sc[r0:r1, 12 + pr:13 + pr]; cfi = sc[r0:r1, 14 + pr:15 + pr]
                S.op("dve", lambda e, r0=r0, r1=r1, pr=pr, cfi=cfi: e.tensor_scalar(
                    out=tmpb[r0:r1, :], in0=bim[r0:r1, pr, :], scalar1=cfi, scalar2=None, op0=ALU.mult),
                    reads=[Tp, Ts], writes=[Tbb])
                S.op("dve", lambda e, r0=r0, r1=r1, pr=pr, cfr=cfr, c0=c0: e.scalar_tensor_tensor(
                    out=bblk[0][r0:r1, c0:c0 + 16], in0=bre[r0:r1, pr, :], scalar=cfr, in1=tmpb[r0:r1, :],
                    op0=ALU.mult, op1=ALU.subtract), reads=[Tp, Ts, Tbb], writes=[Tbb])
                S.op("dve", lambda e, r0=r0, r1=r1, pr=pr, cfi=cfi: e.tensor_scalar(
                    out=tmpb[r0:r1, :], in0=bre[r0:r1, pr, :], scalar1=cfi, scalar2=None, op0=ALU.mult),
                    reads=[Tp, Ts, Tbb], writes=[Tbb])
                S.op("dve", lambda e, r0=r0, r1=r1, pr=pr, cfr=cfr, c0=c0: e.scalar_tensor_tensor(
                    out=bblk[1][r0:r1, c0:c0 + 16], in0=bim[r0:r1, pr, :], scalar=cfr, in1=tmpb[r0:r1, :],
                    op0=ALU.mult, op1=ALU.add), reads=[Tp, Ts, Tbb], writes=[Tbb])
            for pt in range(2):
                S.op("pe", lambda e, pt=pt: e.transpose(out=psT[0:64, 0:128], in_=bblk[pt][:], identity=ident[:]),
                     reads=[Tbb, Tc], writes=[TpT])
                S.op("act", lambda e, pr=pr, pt=pt: e.copy(out=BT[pr][pt][:], in_=psT[0:64, 0:128]),
                     reads=[TpT], writes=[Tc])
            for pt in range(2):
                S.op("pool", lambda e, pr=pr, pt=pt: e.memset(CB[pr][pt][:], 0.0), writes=[Tc])
            for g2 in range(2):
                r0, r1 = 64 * g2, 64 * g2 + 64
                c0 = 32 * pr + 16 * g2
                S.op("dve", lambda e, r0=r0, r1=r1, pr=pr, c0=c0: e.tensor_copy(out=CB[pr][0][r0:r1, c0:c0 + 16], in_=cre[r0:r1, pr, :]),
                     reads=[Tp, Tc], writes=[Tc])
                S.op("dve", lambda e, r0=r0, r1=r1, pr=pr, c0=c0: e.tensor_scalar(
                    out=CB[pr][1][r0:r1, c0:c0 + 16], in0=cim[r0:r1, pr, :], scalar1=-1.0, scalar2=None, op0=ALU.mult),
                    reads=[Tp, Tc], writes=[Tc])
        DR = [[S.sb("DR%d_%d" % (pr, m), [128, 128], BF16) for m in range(NP)] for pr in range(2)]
        DI = [[S.sb("DI%d_%d" % (pr, m), [128, 128], BF16) for m in range(NP)] for pr in range(2)]
        DN = [[S.sb("DN%d_%d" % (pr, m), [128, 128], BF16) for m in range(NP)] for pr in range(2)]
        k = 0
        for pr in range(2):
            for m in range(NP):
                for (dst, src) in ((DR, pwr), (DI, pwi), (DN, pwn)):
                    eng = "dve" if k % 2 == 0 else "pool"; k += 1
                    S.op(eng, lambda e, dst=dst, src=src, pr=pr, m=m: e.tensor_scalar(
                        out=dst[pr][m][:], in0=ident[:], scalar1=src[:, pr, m:m + 1], scalar2=1.0, op0=ALU.mult, op1=ALU.mult),
                        reads=[Ts, Tc], writes=[Tc])
        NLV = max(1, int(math.ceil(math.log2(NCH))))
        lvr = S.sb("lvr", [128, 2, NLV]); lvi = S.sb("lvi", [128, 2, NLV]); lvn = S.sb("lvn", [128, 2, NLV])
        dve(lambda e: e.tensor_copy(out=lvr[:, :, 0], in_=pwr[:, :, 16]))
        dve(lambda e: e.tensor_copy(out=lvi[:, :, 0], in_=pwi[:, :, 16]))
        for kk in range(1, NLV):
            dve(lambda e, kk=kk: e.tensor_tensor(out=T1, in0=lvr[:, :, kk - 1], in1=lvr[:, :, kk - 1], op=ALU.mult))
            dve(lambda e, kk=kk: e.tensor_tensor(out=T2, in0=lvi[:, :, kk - 1], in1=lvi[:, :, kk - 1], op=ALU.mult))
            dve(lambda e, kk=kk: e.tensor_tensor(out=lvr[:, :, kk], in0=T1, in1=T2, op=ALU.subtract))
            dve(lambda e, kk=kk: e.tensor_tensor(out=T1, in0=lvr[:, :, kk - 1], in1=lvi[:, :, kk - 1], op=ALU.mult))
            dve(lambda e, kk=kk: e.tensor_scalar(out=lvi[:, :, kk], in0=T1, scalar1=2.0, scalar2=None, op0=ALU.mult))
        dve(lambda e: e.tensor_scalar(out=lvn[:], in0=lvi[:], scalar1=-1.0, scalar2=None, op0=ALU.mult))

        ub = S.sb("ub", [64, L], BF16); Tub = Trk("ub")
        ustg = [S.sb("ustg%d" % i, [64, 1024]) for i in range(2)]; Tus = [Trk() for i in range(2)]
        BU = [S.sb("BU%d" % i, [128, L], BF16) for i in range(2)]; TBU = [Trk() for i in range(2)]
        CW = (NCH + 2) // 3
        coltiles = [(c, min(CW, NCH - c)) for c in range(0, NCH, CW)]
        stt = [S.sb("stt%d" % i, [128, CW * 16], BF16) for i in range(2)]; Tst = [Trk() for i in range(2)]
        Zr = [S.sb("Zr%d" % i, [128, NCH]) for i in range(2)]; Zi = [S.sb("Zi%d" % i, [128, NCH]) for i in range(2)]
        TZ = [Trk() for i in range(2)]
        ztr = S.sb("ztr", [128, NCH]); zti = S.sb("zti", [128, NCH]); Tzt = Trk()
        Spb = [S.sb("Spb%d" % i, [128, NCH], BF16) for i in range(2)]; TSp = Trk()
        usk = [S.sb("usk%d" % i, [64, 512]) for i in range(2)]; Tusk = [Trk() for i in range(2)]
        ysb = [S.sb("ysb%d" % i, [64, 512]) for i in range(2)]; Tys = [Trk() for i in range(2)]
        psR = [S.ps("psR%d" % i, [128, 512]) for i in range(2)]; TpR = [Trk() for i in range(2)]
        psI = [S.ps("psI%d" % i, [128, 512]) for i in range(2)]; TpI = [Trk() for i in range(2)]
        psY = [S.ps("psY%d" % i, [64, 512]) for i in range(2)]; TpY = [Trk() for i in range(2)]
        Tout = Trk("s5out")
        pc = [0]; yc = [0]; ec = [0]

        def evac(dst_ap, src_ap, reads, writes):
            ec[0] += 1
            if ec[0] % 2 == 0:
                S.op("act", lambda e: e.copy(out=dst_ap, in_=src_ap), reads=reads, writes=writes)
            else:
                S.op("dve", lambda e: e.tensor_copy(out=dst_ap, in_=src_ap), reads=reads, writes=writes)

        def unit(pr, b):
            def bu_body(c0):
                n = min(400, L - c0)
                j = pc[0] % 2; pc[0] += 1
                S.op("pe", lambda e: e.matmul(psR[j][:, 0:n], lhsT=BT[pr][0][:], rhs=ub[:, c0:c0 + n], start=True, stop=True),
                     reads=[Tub, Tc], writes=[TpR[j]])
                S.op("pe", lambda e: e.matmul(psI[j][:, 0:n], lhsT=BT[pr][1][:], rhs=ub[:, c0:c0 + n], start=True, stop=True),
                     reads=[Tub, Tc], writes=[TpI[j]])
                assert c0 % 16 == 0 and n % 16 == 0
                n0 = c0 // 16; nn = n // 16
                for (bt, pst, tpt) in ((BU[0], psR[j], TpR[j]), (BU[1], psI[j], TpI[j])):
                    dst = bt[:].rearrange("p (t n) -> p t n", t=16)[:, :, n0:n0 + nn]
                    srcv = pst[:, 0:n].rearrange("p (n t) -> p t n", t=16)
                    evac(dst, srcv, [tpt], [TBU[0] if bt is BU[0] else TBU[1]])
            for c0 in range(0, L, 400):
                bu_body(c0)

            def p1_body(cc, n):
                j = pc[0] % 2; pc[0] += 1
                lo, hi = 16 * cc, 16 * (cc + n)
                for tp in range(16):
                    d = 15 - tp
                    r_re = BU[0][:, tp * NCH + cc:tp * NCH + cc + n]; r_im = BU[1][:, tp * NCH + cc:tp * NCH + cc + n]
                    S.op("pe", lambda e, d=d, r=r_re, tp=tp: e.matmul(psR[j][:, 0:n], lhsT=DR[pr][d][:], rhs=r, start=(tp == 0), stop=False),
                         reads=[TBU[0], Tc], writes=[TpR[j]])
                    S.op("pe", lambda e, d=d, r=r_im, tp=tp: e.matmul(psR[j][:, 0:n], lhsT=DN[pr][d][:], rhs=r, start=False, stop=(tp == 15)),
                         reads=[TBU[1], Tc], writes=[TpR[j]])
                    S.op("pe", lambda e, d=d, r=r_re, tp=tp: e.matmul(psI[j][:, 0:n], lhsT=DI[pr][d][:], rhs=r, start=(tp == 0), stop=False),
                         reads=[TBU[0], Tc], writes=[TpI[j]])
                    S.op("pe", lambda e, d=d, r=r_im, tp=tp: e.matmul(psI[j][:, 0:n], lhsT=DR[pr][d][:], rhs=r, start=False, stop=(tp == 15)),
                         reads=[TBU[1], Tc], writes=[TpI[j]])
                evac(Zr[0][:, cc:cc + n], psR[j][:, 0:n], [TpR[j]], [TZ[0]])
                evac(Zi[0][:, cc:cc + n], psI[j][:, 0:n], [TpI[j]], [TZ[0]])
            for (cc, n) in coltiles:
                p1_body(cc, n)
            cur = 0
            for kk in range(NLV):
                o = 1 << kk
                if o >= NCH:
                    break
                nxt = 1 - cur
                m = NCH - o
                ar = lvr[:, pr, kk:kk + 1]; ai = lvi[:, pr, kk:kk + 1]; na = lvn[:, pr, kk:kk + 1]
                S.op("dve", lambda e, cur=cur, o=o, m=m, na=na: e.scalar_tensor_tensor(
                    out=ztr[:, 0:m], in0=Zi[cur][:, 0:m], scalar=na, in1=Zr[cur][:, o:NCH], op0=ALU.mult, op1=ALU.add),
                    reads=[TZ[cur], Ts], writes=[Tzt])
                S.op("dve", lambda e, cur=cur, nxt=nxt, o=o, m=m, ar=ar: e.scalar_tensor_tensor(
                    out=Zr[nxt][:, o:NCH], in0=Zr[cur][:, 0:m], scalar=ar, in1=ztr[:, 0:m], op0=ALU.mult, op1=ALU.add),
                    reads=[TZ[cur], Tzt, Ts], writes=[TZ[nxt]])
                S.op("dve", lambda e, cur=cur, o=o, m=m, ai=ai: e.scalar_tensor_tensor(
                    out=zti[:, 0:m], in0=Zr[cur][:, 0:m], scalar=ai, in1=Zi[cur][:, o:NCH], op0=ALU.mult, op1=ALU.add),
                    reads=[TZ[cur], Ts], writes=[Tzt])
                S.op("dve", lambda e, cur=cur, nxt=nxt, o=o, m=m, ar=ar: e.scalar_tensor_tensor(
                    out=Zi[nxt][:, o:NCH], in0=Zi[cur][:, 0:m], scalar=ar, in1=zti[:, 0:m], op0=ALU.mult, op1=ALU.add),
                    reads=[TZ[cur], Tzt, Ts], writes=[TZ[nxt]])
                S.op("pool", lambda e, cur=cur, nxt=nxt, o=o: e.tensor_copy(out=Zr[nxt][:, 0:o], in_=Zr[cur][:, 0:o]),
                     reads=[TZ[cur]], writes=[TZ[nxt]])
                S.op("pool", lambda e, cur=cur, nxt=nxt, o=o: e.tensor_copy(out=Zi[nxt][:, 0:o], in_=Zi[cur][:, 0:o]),
                     reads=[TZ[cur]], writes=[TZ[nxt]])
                cur = nxt
            S.op("pool", lambda e: e.memset(Spb[0][:, 0:1], 0.0), reads=[TSp], writes=[TSp])
            S.op("pool", lambda e: e.memset(Spb[1][:, 0:1], 0.0), reads=[TSp], writes=[TSp])
            if NCH > 1:
                S.op("dve", lambda e, cur=cur: e.tensor_copy(out=Spb[0][:, 1:NCH], in_=Zr[cur][:, 0:NCH - 1]), reads=[TZ[cur], TSp], writes=[TSp])
                S.op("dve", lambda e, cur=cur: e.tensor_copy(out=Spb[1][:, 1:NCH], in_=Zi[cur][:, 0:NCH - 1]), reads=[TZ[cur], TSp], writes=[TSp])
            def p2_tau(cc, n, tau):
                    lo, hi = 16 * cc, 16 * (cc + n)
                    j = pc[0] % 2; pc[0] += 1
                    for tp in range(tau + 1):
                        d = tau - tp
                        r_re = BU[0][:, tp * NCH + cc:tp * NCH + cc + n]; r_im = BU[1][:, tp * NCH + cc:tp * NCH + cc + n]
                        S.op("pe", lambda e, d=d, r=r_re, tp=tp: e.matmul(psR[j][:, 0:n], lhsT=DR[pr][d][:], rhs=r, start=(tp == 0), stop=False),
                             reads=[TBU[0], Tc], writes=[TpR[j]])
                        S.op("pe", lambda e, d=d, r=r_im: e.matmul(psR[j][:, 0:n], lhsT=DN[pr][d][:], rhs=r, start=False, stop=False),
                             reads=[TBU[1], Tc], writes=[TpR[j]])
                        S.op("pe", lambda e, d=d, r=r_re, tp=tp: e.matmul(psI[j][:, 0:n], lhsT=DI[pr][d][:], rhs=r, start=(tp == 0), stop=False),
                             reads=[TBU[0], Tc], writes=[TpI[j]])
                        S.op("pe", lambda e, d=d, r=r_im: e.matmul(psI[j][:, 0:n], lhsT=DR[pr][d][:], rhs=r, start=False, stop=False),
                             reads=[TBU[1], Tc], writes=[TpI[j]])
                    d = tau + 1
                    S.op("pe", lambda e, d=d: e.matmul(psR[j][:, 0:n], lhsT=DR[pr][d][:], rhs=Spb[0][:, cc:cc + n], start=False, stop=False),
                         reads=[TSp, Tc], writes=[TpR[j]])
                    S.op("pe", lambda e, d=d: e.matmul(psR[j][:, 0:n], lhsT=DN[pr][d][:], rhs=Spb[1][:, cc:cc + n], start=False, stop=True),
                         reads=[TSp, Tc], writes=[TpR[j]])
                    S.op("pe", lambda e, d=d: e.matmul(psI[j][:, 0:n], lhsT=DI[pr][d][:], rhs=Spb[0][:, cc:cc + n], start=False, stop=False),
                         reads=[TSp, Tc], writes=[TpI[j]])
                    S.op("pe", lambda e, d=d: e.matmul(psI[j][:, 0:n], lhsT=DR[pr][d][:], rhs=Spb[1][:, cc:cc + n], start=False, stop=True),
                         reads=[TSp, Tc], writes=[TpI[j]])
                    evac(stt[0][:, tau:16 * n:16], psR[j][:, 0:n], [TpR[j]], [Tst[0]])
                    evac(stt[1][:, tau:16 * n:16], psI[j][:, 0:n], [TpI[j]], [Tst[1]])

            def p2_y(cc, n, x0):
                    lo = 16 * cc
                    ntok = 16 * n
                    w = min(512, ntok - x0)
                    jy = yc[0] % 2; yc[0] += 1
                    t0 = lo + x0
                    r0, r1 = 32 * pr, 32 * pr + 32
                    S.dma("sp", usk[jy][r0:r1, 0:w], u_d[r0:r1, b, t0:t0 + w], writes=[Tusk[jy]])
                    S.op("pe", lambda e, jy=jy, x0=x0, w=w: e.matmul(psY[jy][:, 0:w], lhsT=CB[pr][0][:], rhs=stt[0][:, x0:x0 + w], start=True, stop=False),
                         reads=[Tst[0], Tc], writes=[TpY[jy]])
                    S.op("pe", lambda e, jy=jy, x0=x0, w=w: e.matmul(psY[jy][:, 0:w], lhsT=CB[pr][1][:], rhs=stt[1][:, x0:x0 + w], start=False, stop=True),
                         reads=[Tst[1], Tc], writes=[TpY[jy]])
                    S.op("dve", lambda e, jy=jy, w=w, r0=r0, r1=r1: e.scalar_tensor_tensor(
                        out=ysb[jy][r0:r1, 0:w], in0=usk[jy][r0:r1, 0:w], scalar=dsk[r0:r1, 0:1], in1=psY[jy][r0:r1, 0:w],
                        op0=ALU.mult, op1=ALU.add), reads=[Tusk[jy], TpY[jy], Tp], writes=[Tys[jy]])
                    S.dma("sp", ys_d[r0:r1, b, t0:t0 + w], ysb[jy][r0:r1, 0:w], reads=[Tys[jy]], writes=[Tout])
            for (cc, n) in coltiles:
                for tau in range(16):
                    p2_tau(cc, n, tau)
                for x0 in range(0, 16 * n, 512):
                    p2_y(cc, n, x0)

        si = 0
        for b in range(NBATCH):
            for c0 in range(0, L, 1024):
                n = min(1024, L - c0)
                i = si % 2; si += 1
                S.dma("sp", ustg[i][:, 0:n], u_d[:, b, c0:c0 + n], writes=[Tus[i]])
                S.op("pool", lambda e, i=i, n=n, c0=c0: e.tensor_copy(out=ub[:, c0:c0 + n], in_=ustg[i][:, 0:n]),
                     reads=[Tus[i]], writes=[Tub])
            for pr in range(2):
                unit(pr, b)
        S.barrier()
        S.stack = st0
    return Tout


def dn_phase(S, nc, xq_d, xk_d, xv_d, cw_d, a_d, b_d, hp_d, o_d, NC, NPAD):
    st0 = S.stack
    LP = NC * 64
    with ExitStack() as ps_:
        S.stack = ps_
        Tc = Trk("dnc")
        cw = S.sb("cw", [64, 12]); hp = S.sb("hp", [64, 2]); acol = S.sb("acol", [64, NC]); bcol = S.sb("bcol", [64, NC])
        Tp = Trk("dnp")
        S.dma("sp", cw[:], cw_d[:, :], writes=[Tp]); S.dma("sp", hp[:], hp_d[:, :], writes=[Tp])
        S.dma("sp", acol[:], a_d[:, :], writes=[Tp]); S.dma("sp", bcol[:], b_d[:, :], writes=[Tp])
        one_t = S.sb("done", [64, 1]); eps_t = S.sb("deps", [64, 1])
        S.op("pool", lambda e: e.memset(one_t[:], 1.0), writes=[Tc])
        S.op("pool", lambda e: e.memset(eps_t[:], EPS), writes=[Tc])
        iot = S.sb("dniot", [64, 64]); ident = S.sb("dnident", [64, 64]); identb = S.sb("dnidentb", [64, 64], BF16)
        triu = S.sb("triu", [64, 64]); slow = S.sb("slow", [64, 64]); uinc = S.sb("uinc", [64, 64])
        ones64 = S.sb("ones64", [64, 64]); nones64 = S.sb("nones64", [64, 64]); ones64b = S.sb("ones64b", [64, 64], BF16)
        S.op("pool", lambda e: e.iota(iot[:], pattern=[[1, 64]], base=0, channel_multiplier=-1, allow_small_or_imprecise_dtypes=True), writes=[Tc])
        S.op("dve", lambda e: e.tensor_single_scalar(out=ident[:], in_=iot[:], scalar=0.0, op=ALU.is_equal), reads=[Tc], writes=[Tc])
        S.op("dve", lambda e: e.tensor_copy(out=identb[:], in_=ident[:]), reads=[Tc], writes=[Tc])
        S.op("dve", lambda e: e.tensor_single_scalar(out=triu[:], in_=iot[:], scalar=0.0, op=ALU.is_ge), reads=[Tc], writes=[Tc])
        S.op("dve", lambda e: e.tensor_copy(out=uinc[:], in_=triu[:]), reads=[Tc], writes=[Tc])
        S.op("dve", lambda e: e.tensor_single_scalar(out=slow[:], in_=iot[:], scalar=0.0, op=ALU.is_lt), reads=[Tc], writes=[Tc])
        S.op("pool", lambda e: e.memset(ones64[:], 1.0), writes=[Tc])
        S.op("pool", lambda e: e.memset(nones64[:], -1.0), writes=[Tc])
        S.op("pool", lambda e: e.memset(ones64b[:], 1.0), writes=[Tc])

        gcol = S.sb("gcol", [64, NC]); gccol = S.sb("gccol", [64, NC]); glast = S.sb("glast", [64, NC])
        egc = S.sb("egc", [64, NC]); eglast = S.sb("eglast", [64, NC]); edec = S.sb("edec", [64, NC])
        beta = S.sb("beta", [64, NC]); nbeta = S.sb("nbeta", [64, NC]); begc = S.sb("begc", [64, NC])
        nexpA = S.sb("nexpA", [64, 1]); tmpc = S.sb("tmpc", [64, NC])
        Tg = Trk("gates")
        psG = S.ps("psG", [64, 512]); TpG = Trk()
        S.op("act", lambda e: e.activation(out=tmpc[:], in_=acol[:], func=AF.Exp, bias=hp[:, 1:2]), reads=[Tp], writes=[Tg])
        S.op("act", lambda e: e.activation(out=tmpc[:], in_=tmpc[:], func=AF.Ln, bias=one_t[:, 0:1]), reads=[Tg, Tc], writes=[Tg])
        S.op("act", lambda e: e.activation(out=nexpA[:], in_=hp[:, 0:1], func=AF.Exp), reads=[Tp], writes=[Tg])
        S.op("dve", lambda e: e.tensor_scalar(out=gcol[:], in0=tmpc[:], scalar1=nexpA[:, 0:1], scalar2=-1.0, op0=ALU.mult, op1=ALU.mult),
             reads=[Tg], writes=[Tg])
        if NPAD > 0:
            S.op("dve", lambda e: e.memset(gcol[0:NPAD, 0:1], 0.0), reads=[Tg], writes=[Tg])
        S.op("pe", lambda e: e.matmul(psG[:, 0:NC], lhsT=triu[:], rhs=gcol[:], start=True, stop=True), reads=[Tg, Tc], writes=[TpG])
        S.op("dve", lambda e: e.tensor_copy(out=gccol[:], in_=psG[:, 0:NC]), reads=[TpG], writes=[Tg])
        S.op("pe", lambda e: e.matmul(psG[:, 0:NC], lhsT=ones64[:], rhs=gcol[:], start=True, stop=True), reads=[Tg, Tc], writes=[TpG])
        S.op("dve", lambda e: e.tensor_copy(out=glast[:], in_=psG[:, 0:NC]), reads=[TpG], writes=[Tg])
        S.op("act", lambda e: e.activation(out=egc[:], in_=gccol[:], func=AF.Exp), reads=[Tg], writes=[Tg])
        S.op("act", lambda e: e.activation(out=eglast[:], in_=glast[:], func=AF.Exp), reads=[Tg], writes=[Tg])
        S.op("dve", lambda e: e.tensor_tensor(out=tmpc[:], in0=glast[:], in1=gccol[:], op=ALU.subtract), reads=[Tg], writes=[Tg])
        S.op("act", lambda e: e.activation(out=edec[:], in_=tmpc[:], func=AF.Exp), reads=[Tg], writes=[Tg])
        S.op("act", lambda e: e.activation(out=beta[:], in_=bcol[:], func=AF.Sigmoid), reads=[Tp], writes=[Tg])
        S.op("dve", lambda e: e.tensor_scalar(out=nbeta[:], in0=beta[:], scalar1=-1.0, scalar2=None, op0=ALU.mult), reads=[Tg], writes=[Tg])
        S.op("dve", lambda e: e.tensor_tensor(out=begc[:], in0=beta[:], in1=egc[:], op=ALU.mult), reads=[Tg], writes=[Tg])

        qb = S.sb("dqb", [64, LP], BF16); kb = S.sb("dkb", [64, LP], BF16); vb = S.sb("dvb", [64, LP], BF16)
        Tqkv = [Trk("dq"), Trk("dk"), Trk("dv")]
        CT = 2048
        xin = [S.sb("dxin%d" % i, [64, CT + 3]) for i in range(2)]; Txin = [Trk() for i in range(2)]
        acc = [S.sb("dacc%d" % i, [64, CT]) for i in range(2)]; Tacc = [Trk() for i in range(2)]
        sqb = S.sb("dsq", [64, CT], BF16); Tsq = Trk()
        rinv = S.sb("drinv", [64, 512]); Tri = Trk()
        psS = [S.ps("psS%d" % i, [64, 512]) for i in range(2)]; TpS = [Trk() for i in range(2)]
        cc = [0]

        def conv_tile(src, which, c0):
            n = min(CT, LP - c0)
            i = cc[0] % 2; cc[0] += 1
            dst = (qb, kb, vb)[which]
            S.dma("sp", xin[i][:, 0:n + 3], src[:, c0:c0 + n + 3], writes=[Txin[i]])
            S.op("dve", lambda e: e.tensor_scalar(out=acc[i][:, 0:n], in0=xin[i][:, 0:n], scalar1=cw[:, 4 * which:4 * which + 1],
                                                  scalar2=None, op0=ALU.mult), reads=[Txin[i], Tp], writes=[Tacc[i]])
            for j in range(1, 4):
                S.op("dve", lambda e, j=j: e.scalar_tensor_tensor(
                    out=acc[i][:, 0:n], in0=xin[i][:, j:j + n], scalar=cw[:, 4 * which + j:4 * which + j + 1], in1=acc[i][:, 0:n],
                    op0=ALU.mult, op1=ALU.add), reads=[Txin[i], Tp, Tacc[i]], writes=[Tacc[i]])
            S.op("act", lambda e: e.activation(out=acc[i][:, 0:n], in_=acc[i][:, 0:n], func=AF.Silu), reads=[Tacc[i]], writes=[Tacc[i]])
            if which == 2:
                S.op("pool", lambda e: e.tensor_copy(out=dst[:, c0:c0 + n], in_=acc[i][:, 0:n]), reads=[Tacc[i]], writes=[Tqkv[2]])
                return
            S.op("act", lambda e: e.activation(out=sqb[:, 0:n], in_=acc[i][:, 0:n], func=AF.Square), reads=[Tacc[i]], writes=[Tsq])
            for s0 in range(0, n, 512):
                w = min(512, n - s0)
                j = cc[0] % 2; cc[0] += 1
                S.op("pe", lambda e, s0=s0, w=w, j=j: e.matmul(psS[j][:, 0:w], lhsT=ones64b[:], rhs=sqb[:, s0:s0 + w], start=True, stop=True),
                     reads=[Tsq, Tc], writes=[TpS[j]])
                S.op("act", lambda e, w=w, j=j: e.activation(out=rinv[:, 0:w], in_=psS[j][:, 0:w], func=AF.Ln, bias=eps_t[:, 0:1]),
                     reads=[TpS[j], Tc], writes=[Tri])
                S.op("act", lambda e, w=w: e.activation(out=rinv[:, 0:w], in_=rinv[:, 0:w], func=AF.Exp, scale=-0.5), reads=[Tri], writes=[Tri])
                sc_ = 0.125 if which == 0 else 1.0
                S.op("dve", lambda e, s0=s0, w=w, sc_=sc_: e.scalar_tensor_tensor(
                    out=dst[:, c0 + s0:c0 + s0 + w], in0=acc[i][:, s0:s0 + w], scalar=sc_, in1=rinv[:, 0:w], op0=ALU.mult, op1=ALU.mult),
                    reads=[Tacc[i], Tri], writes=[Tqkv[which]])

        for c0 in range(0, LP, CT):
            conv_tile(xq_d, 0, c0); conv_tile(xk_d, 1, c0); conv_tile(xv_d, 2, c0)

        GS = 8
        psD = S.ps("psD", [64, 512]); TpD = Trk()
        psA = S.ps("psA", [64, 512]); TpA = Trk()
        psQ = S.ps("psQ", [64, 512]); TpQ = Trk()
        psTr = S.ps("psTr", [64, 512]); TpTr = Trk()
        psN0 = S.ps("psN", [64, 512])
        S.barrier()
        psNl = [psN0, psG]; TpNl = [Trk(), Trk()]
        psNbl = [psQ, psS[0]]; TpNbl = [Trk(), Trk()]
        psQ = psD; TpQ = TpD
        psU = psTr; TpU = TpTr
        psQ2 = psS[1]; TpSeq = Trk()
        gt = S.sb("gt", [64, GS * 64]); Tgt = Trk()
        E = S.sb("Emat", [64, GS * 64]); TE = Trk()
        EL = S.sb("EL", [64, GS * 64]); EU = S.sb("EU", [64, GS * 64]); TEm = Trk()
        NL = 2
        Yl = [[S.sb("Yk%d_%d" % (l, i), [64, 64]) for i in range(2)] for l in range(NL)]
        Xl = [[S.sb("Xk%d_%d" % (l, i), [64, 64]) for i in range(2)] for l in range(NL)]
        TYl = [[Trk() for i in range(2)] for l in range(NL)]; TXl = [[Trk() for i in range(2)] for l in range(NL)]
        Pl = [S.sb("Pm%d" % l, [64, 64]) for l in range(NL)]; TPl = [Trk() for l in range(NL)]
        TTbl = [S.sb("TTb%d" % l, [64, 64], BF16) for l in range(NL)]; TTTl = [Trk() for l in range(NL)]
        vbetal = [S.sb("vbeta%d" % l, [64, 64], BF16) for l in range(NL)]
        kbgl = [S.sb("kbg%d" % l, [64, 64], BF16) for l in range(NL)]; Tvkl = [Trk() for l in range(NL)]
        u_g = [S.sb("u_g%d" % i, [64, GS * 64]) for i in range(2)]
        wT_g = [S.sb("wT_g%d" % i, [64, GS * 64], BF16) for i in range(2)]
        attnT_g = [S.sb("attnT_g%d" % i, [64, GS * 64], BF16) for i in range(2)]
        kdec_g = [S.sb("kdec_g%d" % i, [64, GS * 64], BF16) for i in range(2)]
        Tgrp = [[Trk() for _ in range(GS)] for i in range(2)]
        Sf = S.sb("Sf", [64, 64]); Sb = S.sb("Sb", [64, 64], BF16); TS = Trk()
        vnew = S.sb("vnew", [64, 64], BF16); Tvn = Trk()
        qs_t = S.sb("qs_t", [64, 64]); Tqs = Trk()
        o_g = [S.sb("o_g%d" % i, [64, GS * 64]) for i in range(2)]; Tog = [Trk() for i in range(2)]
        Tout = Trk("dnout")
        S.op("pool", lambda e: e.memset(Sf[:], 0.0), writes=[TS])
        S.op("pool", lambda e: e.memset(Sb[:], 0.0), reads=[TS], writes=[TS])

        def pre_group_head(n0, ng, gp):
            W = ng * 64
            for c in range(ng):
                n = n0 + c
                S.op("dve", lambda e, c=c, n=n: e.tensor_scalar(out=gt[:, c * 64:(c + 1) * 64], in0=triu[:], scalar1=gcol[:, n:n + 1],
                                                                 scalar2=None, op0=ALU.mult), reads=[Tg, Tc, Tgt], writes=[Tgt])
                S.op("pe", lambda e, c=c: e.matmul(psD[:, c * 64:(c + 1) * 64], lhsT=gt[:, c * 64:(c + 1) * 64], rhs=ones64[:], start=True, stop=False),
                     reads=[Tgt, Tc], writes=[TpD])
                S.op("pe", lambda e, c=c: e.matmul(psD[:, c * 64:(c + 1) * 64], lhsT=nones64[:], rhs=gt[:, c * 64:(c + 1) * 64], start=False, stop=True),
                     reads=[Tgt, Tc], writes=[TpD])
            yield
            S.op("dve", lambda e: e.tensor_scalar(out=E[:, 0:W], in0=psD[:, 0:W], scalar1=-1.0, scalar2=None, op0=ALU.mult), reads=[TpD, TE, TEm], writes=[TE])
            S.op("dve", lambda e: e.tensor_tensor(out=E[:, 0:W], in0=E[:, 0:W], in1=psD[:, 0:W], op=ALU.min), reads=[TpD, TE], writes=[TE])
            S.op("act", lambda e: e.activation(out=E[:, 0:W], in_=E[:, 0:W], func=AF.Exp), reads=[TE], writes=[TE])
            yield
            S.op("dve", lambda e: e.tensor_tensor(out=EL[:, 0:W].rearrange("p (c f) -> p c f", f=64), in0=E[:, 0:W].rearrange("p (c f) -> p c f", f=64),
                                                  in1=slow[:].unsqueeze(1).broadcast_to([64, ng, 64]), op=ALU.mult), reads=[TE, Tc, TEm], writes=[TEm])
            S.op("dve", lambda e: e.tensor_tensor(out=EU[:, 0:W].rearrange("p (c f) -> p c f", f=64), in0=E[:, 0:W].rearrange("p (c f) -> p c f", f=64),
                                                  in1=uinc[:].unsqueeze(1).broadcast_to([64, ng, 64]), op=ALU.mult), reads=[TE, Tc, TEm], writes=[TEm])
            for c in range(ng):
                n = n0 + c
                ks = kb[:, n * 64:(n + 1) * 64]; qs = qb[:, n * 64:(n + 1) * 64]
                S.op("pe", lambda e, c=c, ks=ks: e.matmul(psA[:, c * 64:(c + 1) * 64], lhsT=ks, rhs=ks, start=True, stop=True),
                     reads=[Tqkv[1]], writes=[TpA])
                S.op("pe", lambda e, c=c, ks=ks, qs=qs: e.matmul(psQ[:, c * 64:(c + 1) * 64], lhsT=ks, rhs=qs, start=True, stop=True),
                     reads=[Tqkv[0], Tqkv[1]], writes=[TpQ])
            yield
            S.op("dve", lambda e: e.tensor_tensor(out=attnT_g[gp][:, 0:W], in0=EU[:, 0:W], in1=psQ[:, 0:W], op=ALU.mult),
                 reads=[TEm, TpQ] + Tgrp[gp], writes=Tgrp[gp])
            yield

        def pre_chunk(n, c, gp, l):
            cs = slice(c * 64, (c + 1) * 64)
            lo_ = 128 * l
            Y = Yl[l]; Xm = Xl[l]; TY = TYl[l]; TX = TXl[l]; P = Pl[l]; TP = TPl[l]; psN = psNl[l]; TpN = TpNl[l]
            TTb = TTbl[l]; TTT = TTTl[l]; vbeta = vbetal[l]; kbg = kbgl[l]; Tvk = Tvkl[l]
            psNb = psNbl[l]; TpNb = TpNbl[l]
            S.op("dve", lambda e: e.scalar_tensor_tensor(out=Y[0][:], in0=psA[:, cs], scalar=nbeta[:, n:n + 1], in1=EL[:, cs],
                                                         op0=ALU.mult, op1=ALU.mult), reads=[TpA, Tg, TEm, TY[0]], writes=[TY[0]])
            yield
            S.op("pe", lambda e: e.matmul(psN[:, 0:64], lhsT=Y[0][:], rhs=ident[:], start=True, stop=True), reads=[TY[0], Tc], writes=[TpN])
            yield
            S.op("dve", lambda e: e.tensor_copy(out=Xm[0][:], in_=psN[:, 0:64]), reads=[TpN], writes=[TX[0]])
            S.op("dve", lambda e: e.tensor_tensor(out=P[:], in0=Xm[0][:], in1=ident[:], op=ALU.add), reads=[TX[0], Tc, TP], writes=[TP])
            yield
            cur = 0
            for lv in range(5):
                nx = 1 - cur
                S.op("pe", lambda e, cur=cur: e.matmul(psN[:, 64:128], lhsT=Y[cur][:], rhs=Xm[cur][:], start=True, stop=True),
                     reads=[TY[cur], TX[cur]], writes=[TpN])
                S.op("pe", lambda e, cur=cur: e.matmul(psNb[:, 0:64], lhsT=Xm[cur][:], rhs=Y[cur][:], start=True, stop=True),
                     reads=[TY[cur], TX[cur]], writes=[TpNb])
                yield
                S.op("act", lambda e, nx=nx: e.copy(out=Xm[nx][:], in_=psN[:, 64:128]), reads=[TpN], writes=[TX[nx]])
                S.op("dve", lambda e, nx=nx: e.tensor_copy(out=Y[nx][:], in_=psNb[:, 0:64]), reads=[TpNb], writes=[TY[nx]])
                yield
                S.op("pe", lambda e, nx=nx: e.matmul(psNb[:, 64:128], lhsT=Y[nx][:], rhs=P[:], start=True, stop=True),
                     reads=[TY[nx], TP], writes=[TpNb])
                yield
                S.op("dve", lambda e: e.tensor_tensor(out=P[:], in0=P[:], in1=psNb[:, 64:128], op=ALU.add), reads=[TpNb, TP], writes=[TP])
                yield
                cur = nx
            S.op("act", lambda e: e.copy(out=TTb[:], in_=P[:]), reads=[TP], writes=[TTT])
            ks = kb[:, n * 64:(n + 1) * 64]; vs = vb[:, n * 64:(n + 1) * 64]
            S.op("pe", lambda e: e.matmul(psTr[:, lo_ + 0:lo_ + 64], lhsT=ks, rhs=identb[:], start=True, stop=True), reads=[Tqkv[1], Tc], writes=[TpTr])
            S.op("pe", lambda e: e.matmul(psTr[:, lo_ + 64:lo_ + 128], lhsT=vs, rhs=identb[:], start=True, stop=True), reads=[Tqkv[2], Tc], writes=[TpTr])
            yield
            S.op("dve", lambda e: e.tensor_scalar(out=kbg[:], in0=psTr[:, lo_ + 0:lo_ + 64], scalar1=begc[:, n:n + 1], scalar2=None, op0=ALU.mult),
                 reads=[TpTr, Tg, Tvk], writes=[Tvk])
            S.op("dve", lambda e: e.tensor_scalar(out=vbeta[:], in0=psTr[:, lo_ + 64:lo_ + 128], scalar1=beta[:, n:n + 1], scalar2=None, op0=ALU.mult),
                 reads=[TpTr, Tg, Tvk], writes=[Tvk])
            S.op("dve", lambda e: e.tensor_scalar(out=kdec_g[gp][:, cs], in0=psTr[:, lo_ + 0:lo_ + 64], scalar1=edec[:, n:n + 1], scalar2=None, op0=ALU.mult),
                 reads=[TpTr, Tg, Tgrp[gp][c]], writes=[Tgrp[gp][c]])
            yield
            S.op("pe", lambda e: e.matmul(psU[:, 256 + lo_ + 0:256 + lo_ + 64], lhsT=TTb[:], rhs=vbeta[:], start=True, stop=True), reads=[TTT, Tvk], writes=[TpU])
            S.op("pe", lambda e: e.matmul(psU[:, 256 + lo_ + 64:256 + lo_ + 128], lhsT=kbg[:], rhs=TTb[:], start=True, stop=True), reads=[TTT, Tvk], writes=[TpU])
            yield
            S.op("dve", lambda e: e.tensor_copy(out=u_g[gp][:, cs], in_=psU[:, 256 + lo_ + 0:256 + lo_ + 64]), reads=[TpU, Tgrp[gp][c]], writes=[Tgrp[gp][c]])
            S.op("dve", lambda e: e.tensor_copy(out=wT_g[gp][:, cs], in_=psU[:, 256 + lo_ + 64:256 + lo_ + 128]), reads=[TpU, Tgrp[gp][c]], writes=[Tgrp[gp][c]])
            yield

        def seq_chunk(n, c, gp, og):
            cs = slice(c * 64, (c + 1) * 64)
            qs = qb[:, n * 64:(n + 1) * 64]
            Tgc = Tgrp[gp][c]
            S.op("pe", lambda e: e.matmul(psQ2[:, 0:64], lhsT=wT_g[gp][:, cs], rhs=Sb[:], start=True, stop=True), reads=[Tgc, TS], writes=[TpSeq])
            S.op("pe", lambda e: e.matmul(psQ2[:, 64:128], lhsT=qs, rhs=Sb[:], start=True, stop=True), reads=[Tqkv[0], TS], writes=[TpSeq])
            yield
            S.op("dve", lambda e: e.tensor_tensor(out=vnew[:], in0=u_g[gp][:, cs], in1=psQ2[:, 0:64], op=ALU.subtract),
                 reads=[Tgc, TpSeq, Tvn], writes=[Tvn])
            S.op("dve", lambda e: e.tensor_scalar(out=qs_t[:], in0=psQ2[:, 64:128], scalar1=egc[:, n:n + 1], scalar2=None, op0=ALU.mult),
                 reads=[TpSeq, Tg, Tqs], writes=[Tqs])
            yield
            S.op("pe", lambda e: e.matmul(psQ2[:, 128:192], lhsT=attnT_g[gp][:, cs], rhs=vnew[:], start=True, stop=True), reads=[Tgc, Tvn], writes=[TpSeq])
            S.op("pe", lambda e: e.matmul(psQ2[:, 192:256], lhsT=kdec_g[gp][:, cs], rhs=vnew[:], start=True, stop=True), reads=[Tgc, Tvn], writes=[TpSeq])
            yield
            S.op("dve", lambda e: e.scalar_tensor_tensor(out=Sf[:], in0=Sf[:], scalar=eglast[:, n:n + 1], in1=psQ2[:, 192:256],
                                                         op0=ALU.mult, op1=ALU.add), reads=[TS, Tg, TpSeq], writes=[TS])
            S.op("act", lambda e: e.copy(out=Sb[:], in_=Sf[:]), reads=[TS], writes=[TS])
            S.op("dve", lambda e: e.tensor_tensor(out=o_g[og][:, cs], in0=qs_t[:], in1=psQ2[:, 128:192], op=ALU.add),
                 reads=[Tqs, TpSeq, Tog[og]], writes=[Tog[og]])
            yield

        def pre_gen(n0, ng, gp):
            yield from pre_group_head(n0, ng, gp)
            for c in range(0, ng, NL):
                gens = [pre_chunk(n0 + c + l, c + l, gp, l) for l in range(NL) if c + l < ng]
                while gens:
                    for g_ in list(gens):
                        try:
                            next(g_)
                        except StopIteration:
                            gens.remove(g_)
                    yield

        def seq_gen(n0, ng, gp, og):
            for c in range(ng):
                yield from seq_chunk(n0 + c, c, gp, og)
            S.dma("sp", o_d[:, n0:n0 + ng, :], o_g[og][:, 0:ng * 64].rearrange("p (c f) -> p c f", f=64), reads=[Tog[og]], writes=[Tout])

        groups = [(n0, min(GS, NC - n0)) for n0 in range(0, NC, GS)]
        for _ in pre_gen(groups[0][0], groups[0][1], 0):
            pass
        for gi, (n0, ng) in enumerate(groups):
            gp = gi % 2
            active = [seq_gen(n0, ng, gp, gi % 2)]
            if gi + 1 < len(groups):
                active.append(pre_gen(groups[gi + 1][0], groups[gi + 1][1], 1 - gp))
            while active:
                for g_ in list(active):
                    try:
                        next(g_)
                    except StopIteration:
                        active.remove(g_)
        S.barrier()
        S.stack = st0
    return Tout


def s5_phase2(S, nc, u_d, are_d, aim_d, ldt_d, bre_d, bim_d, cre_d, cim_d, dsk_d, ys_d, L, NBATCH=2):
    st0 = S.stack
    NCH = L // 16
    assert NCH * 16 == L
    TWO_PI = 2.0 * math.pi
    MAGIC = 12582912.0
    NG = 4
    with ExitStack() as ps_:
        S.stack = ps_
        Tc = Trk("s5c")
        prm = S.sb("prm", [128, 12]); Tp = Trk("prm")
        bre = S.sb("bre", [128, NG, 16]); bim = S.sb("bim", [128, NG, 16])
        cre = S.sb("cre", [128, NG, 16]); cim = S.sb("cim", [128, NG, 16])
        dsk = S.sb("dsk", [64, 1])
        S.dma("sp", prm[:, 0:4], are_d[:, :], writes=[Tp])
        S.dma("sp", prm[:, 4:8], aim_d[:, :], writes=[Tp])
        S.dma("sp", prm[:, 8:12], ldt_d[:, :], writes=[Tp])
        S.dma("sp", bre[:], bre_d[:, :, :], writes=[Tp])
        S.dma("sp", bim[:], bim_d[:, :, :], writes=[Tp])
        S.dma("sp", cre[:], cre_d[:, :, :], writes=[Tp])
        S.dma("sp", cim[:], cim_d[:, :, :], writes=[Tp])
        S.dma("sp", dsk[:], dsk_d[:, :], writes=[Tp])
        sc = S.sb("s5sc", [128, 64]); Ts = Trk("s5sc")
        dve = lambda fn: S.op("dve", fn, reads=[Tp, Ts, Tc], writes=[Ts])
        DT = sc[:, 0:4]; ARD = sc[:, 4:8]; AID = sc[:, 8:12]; X = sc[:, 12:16]; DEN = sc[:, 16:20]; RDEN = sc[:, 20:24]
        CFR = sc[:, 24:28]; CFI = sc[:, 28:32]; T1 = sc[:, 32:36]; T2 = sc[:, 36:40]
        ARE = prm[:, 0:4]; AIM = prm[:, 4:8]
        S.op("act", lambda e: e.activation(out=DT, in_=prm[:, 8:12], func=AF.Exp), reads=[Tp], writes=[Ts])
        dve(lambda e: e.tensor_tensor(out=ARD, in0=ARE, in1=DT, op=ALU.mult))
        dve(lambda e: e.tensor_tensor(out=AID, in0=AIM, in1=DT, op=ALU.mult))
        NP = 17
        mag = S.sb("mag", [128, NG, NP]); ang = S.sb("ang", [128, NG, 2 * NP]); ang2 = S.sb("ang2", [128, NG, 2 * NP])
        trg = S.sb("trg", [128, NG, 2 * NP])
        pwr = S.sb("pwr", [128, NG, NP]); pwi = S.sb("pwi", [128, NG, NP]); pws = S.sb("pws", [128, NG, NP])
        for m in range(NP):
            S.op("act", lambda e, m=m: e.activation(out=mag[:, :, m], in_=ARD, func=AF.Exp, scale=float(m)), reads=[Ts], writes=[Ts])
            dve(lambda e, m=m: e.tensor_scalar(out=ang[:, :, m], in0=AID, scalar1=float(m), scalar2=0.0, op0=ALU.mult, op1=ALU.add))
            dve(lambda e, m=m: e.tensor_scalar(out=ang[:, :, NP + m], in0=AID, scalar1=float(m), scalar2=math.pi / 2, op0=ALU.mult, op1=ALU.add))

        def sincos(dst_r, dst_i, dst_s, magt, angt, ang2t, trgt, n):
            dve(lambda e: e.tensor_scalar(out=ang2t, in0=angt, scalar1=1.0 / TWO_PI, scalar2=MAGIC, op0=ALU.mult, op1=ALU.add))
            dve(lambda e: e.tensor_scalar(out=ang2t, in0=ang2t, scalar1=-MAGIC, scalar2=None, op0=ALU.add))
            dve(lambda e: e.scalar_tensor_tensor(out=ang2t, in0=ang2t, scalar=-TWO_PI, in1=angt, op0=ALU.mult, op1=ALU.add))
            dve(lambda e: e.tensor_scalar(out=ang2t, in0=ang2t, scalar1=3.141592, scalar2=-3.141592, op0=ALU.min, op1=ALU.max))
            S.op("act", lambda e: e.activation(out=trgt, in_=ang2t, func=AF.Sin), reads=[Ts], writes=[Ts])
        sincos(None, None, None, mag[:], ang[:], ang2[:], trg[:], NP)
        dve(lambda e: e.tensor_tensor(out=pwi[:], in0=mag[:], in1=trg[:, :, 0:NP], op=ALU.mult))
        dve(lambda e: e.tensor_tensor(out=pwr[:], in0=mag[:], in1=trg[:, :, NP:2 * NP], op=ALU.mult))
        dve(lambda e: e.tensor_copy(out=pws[0:64], in_=pwi[0:64]))
        dve(lambda e: e.tensor_scalar(out=pws[64:128], in0=pwi[64:128], scalar1=-1.0, scalar2=None, op0=ALU.mult))
        dve(lambda e: e.tensor_scalar(out=X, in0=pwr[:, :, 1], scalar1=-1.0, scalar2=None, op0=ALU.add))
        dve(lambda e: e.tensor_tensor(out=DEN, in0=ARE, in1=ARE, op=ALU.mult))
        dve(lambda e: e.tensor_tensor(out=T1, in0=AIM, in1=AIM, op=ALU.mult))
        dve(lambda e: e.tensor_tensor(out=DEN, in0=DEN, in1=T1, op=ALU.add))
        dve(lambda e: e.reciprocal(out=RDEN, in_=DEN))
        dve(lambda e: e.tensor_tensor(out=T1, in0=X, in1=ARE, op=ALU.mult))
        dve(lambda e: e.tensor_tensor(out=T2, in0=pwi[:, :, 1], in1=AIM, op=ALU.mult))
        dve(lambda e: e.tensor_tensor(out=T1, in0=T1, in1=T2, op=ALU.add))
        dve(lambda e: e.tensor_tensor(out=CFR, in0=T1, in1=RDEN, op=ALU.mult))
        dve(lambda e: e.tensor_tensor(out=T1, in0=pwi[:, :, 1], in1=ARE, op=ALU.mult))
        dve(lambda e: e.tensor_tensor(out=T2, in0=X, in1=AIM, op=ALU.mult))
        dve(lambda e: e.tensor_tensor(out=T1, in0=T1, in1=T2, op=ALU.subtract))
        dve(lambda e: e.tensor_tensor(out=CFI, in0=T1, in1=RDEN, op=ALU.mult))
        import os
        if os.environ.get("S5_STOP") == "1":
            S.barrier(); S.stack = st0
            return Trk()
        iot = S.sb("s5iot", [128, 128]); ident = S.sb("s5ident", [128, 128]); Jm = S.sb("s5J", [128, 128]); jt = S.sb("s5jt", [128, 128])
        S.op("pool", lambda e: e.iota(iot[:], pattern=[[1, 128]], base=0, channel_multiplier=-1, allow_small_or_imprecise_dtypes=True), writes=[Tc])
        S.op("dve", lambda e: e.tensor_single_scalar(out=ident[:], in_=iot[:], scalar=0.0, op=ALU.is_equal), reads=[Tc], writes=[Tc])
        S.op("dve", lambda e: e.tensor_single_scalar(out=Jm[:], in_=iot[:], scalar=64.0, op=ALU.is_equal), reads=[Tc], writes=[Tc])
        S.op("dve", lambda e: e.tensor_single_scalar(out=jt[:], in_=iot[:], scalar=-64.0, op=ALU.is_equal), reads=[Tc], writes=[Tc])
        S.op("dve", lambda e: e.tensor_tensor(out=Jm[:], in0=Jm[:], in1=jt[:], op=ALU.add), reads=[Tc], writes=[Tc])
        RT = [[S.sb("RT%d_%d" % (g, m), [128, 128], BF16) for m in range(NP)] for g in range(NG)]
        kk_ = 0
        for g in range(NG):
            for m in range(NP):
                S.op("dve", lambda e, g=g, m=m: e.tensor_scalar(out=jt[:], in0=Jm[:], scalar1=pws[:, g, m:m + 1], scalar2=None, op0=ALU.mult),
                     reads=[Ts, Tc], writes=[Tc])
                S.op("dve", lambda e, g=g, m=m: e.scalar_tensor_tensor(out=RT[g][m][:], in0=ident[:], scalar=pwr[:, g, m:m + 1], in1=jt[:],
                                                                      op0=ALU.mult, op1=ALU.add), reads=[Ts, Tc], writes=[Tc])
        NLV = max(1, int(math.ceil(math.log2(NCH))))
        lvr = S.sb("lvr", [128, NG, NLV]); lvi = S.sb("lvi", [128, NG, NLV]); lvs = S.sb("lvs", [128, NG, NLV])
        dve(lambda e: e.tensor_copy(out=lvr[:, :, 0], in_=pwr[:, :, 16]))
        dve(lambda e: e.tensor_copy(out=lvi[:, :, 0], in_=pwi[:, :, 16]))
        for kk in range(1, NLV):
            dve(lambda e, kk=kk: e.tensor_tensor(out=T1, in0=lvr[:, :, kk - 1], in1=lvr[:, :, kk - 1], op=ALU.mult))
            dve(lambda e, kk=kk: e.tensor_tensor(out=T2, in0=lvi[:, :, kk - 1], in1=lvi[:, :, kk - 1], op=ALU.mult))
            dve(lambda e, kk=kk: e.tensor_tensor(out=lvr[:, :, kk], in0=T1, in1=T2, op=ALU.subtract))
            dve(lambda e, kk=kk: e.tensor_tensor(out=T1, in0=lvr[:, :, kk - 1], in1=lvi[:, :, kk - 1], op=ALU.mult))
            dve(lambda e, kk=kk: e.tensor_scalar(out=lvi[:, :, kk], in0=T1, scalar1=2.0, scalar2=None, op0=ALU.mult))
        dve(lambda e: e.tensor_copy(out=lvs[0:64], in_=lvi[0:64]))
        dve(lambda e: e.tensor_scalar(out=lvs[64:128], in0=lvi[64:128], scalar1=-1.0, scalar2=None, op0=ALU.mult))
        RL = [[S.sb("RL%d_%d" % (g, k), [128, 128]) for k in range(NLV)] for g in range(NG)]
        for g in range(NG):
            for k in range(NLV):
                S.op("dve", lambda e, g=g, k=k: e.tensor_scalar(out=jt[:], in0=Jm[:], scalar1=lvs[:, g, k:k + 1], scalar2=None, op0=ALU.mult),
                     reads=[Ts, Tc], writes=[Tc])
                S.op("dve", lambda e, g=g, k=k: e.scalar_tensor_tensor(out=RL[g][k][:], in0=ident[:], scalar=lvr[:, g, k:k + 1], in1=jt[:],
                                                                      op0=ALU.mult, op1=ALU.add), reads=[Ts, Tc], writes=[Tc])
        bst = S.sb("bst", [128, 64]); tmpb = S.sb("tmpb", [128, 16]); Tbb = Trk("bst")
        BT = [S.sb("BTs%d" % g, [64, 128], BF16) for g in range(NG)]
        CS = [S.sb("CSs%d" % g, [128, 64], BF16) for g in range(NG)]
        psT = S.ps("psT", [128, 512]); TpT = Trk()
        for g in range(NG):
            c0 = 16 * g
            S.op("pool", lambda e: e.memset(bst[:], 0.0), reads=[Tbb], writes=[Tbb])
            cfr0 = sc[0:64, 24 + g:25 + g]; cfi0 = sc[0:64, 28 + g:29 + g]
            cfr1 = sc[64:128, 24 + g:25 + g]; cfi1 = sc[64:128, 28 + g:29 + g]
            S.op("dve", lambda e, g=g, cfi0=cfi0: e.tensor_scalar(out=tmpb[0:64, :], in0=bim[0:64, g, :], scalar1=cfi0, scalar2=None, op0=ALU.mult),
                 reads=[Tp, Ts, Tbb], writes=[Tbb])
            S.op("dve", lambda e, g=g, cfr0=cfr0, c0=c0: e.scalar_tensor_tensor(out=bst[0:64, c0:c0 + 16], in0=bre[0:64, g, :], scalar=cfr0,
                                                                               in1=tmpb[0:64, :], op0=ALU.mult, op1=ALU.subtract),
                 reads=[Tp, Ts, Tbb], writes=[Tbb])
            S.op("dve", lambda e, g=g, cfi1=cfi1: e.tensor_scalar(out=tmpb[64:128, :], in0=bre[64:128, g, :], scalar1=cfi1, scalar2=None, op0=ALU.mult),
                 reads=[Tp, Ts, Tbb], writes=[Tbb])
            S.op("dve", lambda e, g=g, cfr1=cfr1, c0=c0: e.scalar_tensor_tensor(out=bst[64:128, c0:c0 + 16], in0=bim[64:128, g, :], scalar=cfr1,
                                                                               in1=tmpb[64:128, :], op0=ALU.mult, op1=ALU.add),
                 reads=[Tp, Ts, Tbb], writes=[Tbb])
            S.op("pe", lambda e: e.transpose(out=psT[0:64, 0:128], in_=bst[:], identity=ident[:]), reads=[Tbb, Tc], writes=[TpT])
            S.op("act", lambda e, g=g: e.copy(out=BT[g][:], in_=psT[0:64, 0:128]), reads=[TpT], writes=[Tc])
            S.op("pool", lambda e, g=g: e.memset(CS[g][:], 0.0), writes=[Tc])
            S.op("dve", lambda e, g=g, c0=c0: e.tensor_copy(out=CS[g][0:64, c0:c0 + 16], in_=cre[0:64, g, :]), reads=[Tp, Tc], writes=[Tc])
            S.op("dve", lambda e, g=g, c0=c0: e.tensor_scalar(out=CS[g][64:128, c0:c0 + 16], in0=cim[64:128, g, :], scalar1=-1.0, scalar2=None,
                                                              op0=ALU.mult), reads=[Tp, Tc], writes=[Tc])

        ub = S.sb("ub", [64, L], BF16); Tub = Trk("ub")
        ustg = [S.sb("ustg%d" % i, [64, 1024]) for i in range(2)]; Tus = [Trk() for i in range(2)]
        BU = [S.sb("BU%d" % i, [128, L], BF16) for i in range(2)]; TBU = [Trk() for i in range(2)]
        CW = (NCH + 2) // 3
        coltiles = [(c, min(CW, NCH - c)) for c in range(0, NCH, CW)]
        stt = [S.sb("stt%d" % i, [128, CW * 16], BF16) for i in range(2)]; Tst = [Trk() for i in range(2)]
        Z = [[S.sb("Z%d_%d" % (i, k), [128, NCH]) for k in range(2)] for i in range(2)]; TZ = [[Trk() for k in range(2)] for i in range(2)]
        Spb = [S.sb("Spb%d" % i, [128, NCH], BF16) for i in range(2)]; TSp = [Trk() for i in range(2)]
        usk = [S.sb("usk%d" % i, [64, 512]) for i in range(2)]; Tusk = [Trk() for i in range(2)]
        ysb = [S.sb("ysb%d" % i, [64, 512]) for i in range(2)]; Tys = [Trk() for i in range(2)]
        psR = [S.ps("psR%d" % i, [128, 512]) for i in range(4)]; TpR = [Trk() for i in range(4)]
        psY = [S.ps("psY%d" % i, [64, 512]) for i in range(2)]; TpY = [Trk() for i in range(2)]
        Tout = Trk("s5out")
        pc = [0]; yc = [0]; ec = [0]

        def evac(dst_ap, src_ap, reads, writes):
            ec[0] += 1
            if ec[0] % 2 == 0:
                S.op("act", lambda e: e.copy(out=dst_ap, in_=src_ap), reads=reads, writes=writes)
            else:
                S.op("dve", lambda e: e.tensor_copy(out=dst_ap, in_=src_ap), reads=reads, writes=writes)

        def prep_group(g, gi):
            def bu_body(c0):
                n = min(400, L - c0)
                j = pc[0] % 4; pc[0] += 1
                S.op("pe", lambda e: e.matmul(psR[j][:, 0:n], lhsT=BT[g][:], rhs=ub[:, c0:c0 + n], start=True, stop=True),
                     reads=[Tub, Tc], writes=[TpR[j]])
                n0 = c0 // 16; nn = n // 16
                dst = BU[gi][:].rearrange("p (t n) -> p t n", t=16)[:, :, n0:n0 + nn]
                srcv = psR[j][:, 0:n].rearrange("p (n t) -> p t n", t=16)
                evac(dst, srcv, [TpR[j]], [TBU[gi]])
            for c0 in range(0, L, 400):
                bu_body(c0)

            def p1_body(cc, n):
                j = pc[0] % 4; pc[0] += 1
                for tp in range(16):
                    r = BU[gi][:, tp * NCH + cc:tp * NCH + cc + n]
                    S.op("pe", lambda e, tp=tp, r=r: e.matmul(psR[j][:, 0:n], lhsT=RT[g][15 - tp][:], rhs=r, start=(tp == 0), stop=(tp == 15)),
                         reads=[TBU[gi], Tc], writes=[TpR[j]])
                evac(Z[gi][0][:, cc:cc + n], psR[j][:, 0:n], [TpR[j]], [TZ[gi][0]])
            for (cc, n) in coltiles:
                p1_body(cc, n)
            cur = 0
            for kk in range(NLV):
                o = 1 << kk
                if o >= NCH:
                    break
                nxt = 1 - cur
                m = NCH - o

                def lvl(c0, w, cur=cur, nxt=nxt, o=o, kk=kk):
                    j = pc[0] % 4; pc[0] += 1
                    S.op("pe", lambda e: e.matmul(psR[j][:, 0:w], lhsT=RL[g][kk][:], rhs=Z[gi][cur][:, c0:c0 + w], start=True, stop=True),
                         reads=[TZ[gi][cur], Tc], writes=[TpR[j]])
                    S.op("dve", lambda e: e.tensor_tensor(out=Z[gi][nxt][:, o + c0:o + c0 + w], in0=Z[gi][cur][:, o + c0:o + c0 + w],
                                                          in1=psR[j][:, 0:w], op=ALU.add), reads=[TZ[gi][cur], TpR[j]], writes=[TZ[gi][nxt]])
                for c0 in range(0, m, 512):
                    lvl(c0, min(512, m - c0))
                S.op("pool", lambda e, cur=cur, nxt=nxt, o=o: e.tensor_copy(out=Z[gi][nxt][:, 0:o], in_=Z[gi][cur][:, 0:o]),
                     reads=[TZ[gi][cur]], writes=[TZ[gi][nxt]])
                cur = nxt
            S.op("pool", lambda e: e.memset(Spb[gi][:, 0:1], 0.0), reads=[TSp[gi]], writes=[TSp[gi]])
            if NCH > 1:
                S.op("dve", lambda e, cur=cur: e.tensor_copy(out=Spb[gi][:, 1:NCH], in_=Z[gi][cur][:, 0:NCH - 1]),
                     reads=[TZ[gi][cur], TSp[gi]], writes=[TSp[gi]])

        def p2_tau(g, gi, cc, n, tau):
            j = pc[0] % 4; pc[0] += 1
            for tp in range(tau + 1):
                r = BU[gi][:, tp * NCH + cc:tp * NCH + cc + n]
                S.op("pe", lambda e, tp=tp, r=r: e.matmul(psR[j][:, 0:n], lhsT=RT[g][tau - tp][:], rhs=r, start=(tp == 0), stop=False),
                     reads=[TBU[gi], Tc], writes=[TpR[j]])
            S.op("pe", lambda e: e.matmul(psR[j][:, 0:n], lhsT=RT[g][tau + 1][:], rhs=Spb[gi][:, cc:cc + n], start=False, stop=True),
                 reads=[TSp[gi], Tc], writes=[TpR[j]])
            evac(stt[gi][:, tau:16 * n:16], psR[j][:, 0:n], [TpR[j]], [Tst[gi]])

        def p2_y(pr, b, cc, n, x0):
            lo = 16 * cc
            w = min(512, 16 * n - x0)
            jy = yc[0] % 2; yc[0] += 1
            t0 = lo + x0
            r0, r1 = 32 * pr, 32 * pr + 32
            S.dma("sp", usk[jy][r0:r1, 0:w], u_d[r0:r1, b, t0:t0 + w], writes=[Tusk[jy]])
            for gi in range(2):
                g = 2 * pr + gi
                S.op("pe", lambda e, gi=gi, g=g: e.matmul(psY[jy][:, 0:w], lhsT=CS[g][:], rhs=stt[gi][:, x0:x0 + w], start=(gi == 0), stop=(gi == 1)),
                     reads=[Tst[gi], Tc], writes=[TpY[jy]])
            S.op("dve", lambda e: e.scalar_tensor_tensor(out=ysb[jy][r0:r1, 0:w], in0=usk[jy][r0:r1, 0:w], scalar=dsk[r0:r1, 0:1],
                                                         in1=psY[jy][r0:r1, 0:w], op0=ALU.mult, op1=ALU.add),
                 reads=[Tusk[jy], TpY[jy], Tp], writes=[Tys[jy]])
            S.dma("sp", ys_d[r0:r1, b, t0:t0 + w], ysb[jy][r0:r1, 0:w], reads=[Tys[jy]], writes=[Tout])

        si = 0
        for b in range(NBATCH):
            for c0 in range(0, L, 1024):
                n = min(1024, L - c0)
                i = si % 2; si += 1
                S.dma("sp", ustg[i][:, 0:n], u_d[:, b, c0:c0 + n], writes=[Tus[i]])
                S.op("pool", lambda e, i=i, n=n, c0=c0: e.tensor_copy(out=ub[:, c0:c0 + n], in_=ustg[i][:, 0:n]),
                     reads=[Tus[i]], writes=[Tub])
            for pr in range(2):
                for gi in range(2):
                    prep_group(2 * pr + gi, gi)
                for (cc, n) in coltiles:
                    for gi in range(2):
                        for tau in range(16):
                            p2_tau(2 * pr + gi, gi, cc, n, tau)
                    for x0 in range(0, 16 * n, 512):
                        p2_y(pr, b, cc, n, x0)
        S.barrier()
        S.stack = st0
    return Tout


from concourse.bass_utils import run_bass_kernel_spmd

N_META = 16
SEQ = 16384
LTOK = N_META + SEQ
NTOK = 4100
SB_NB = 129
SB_PAD = 112
DN_NC = 257
DN_PAD = 48


def build_mix():
    nc = bass.Bass("TRN2", target_bir_lowering=False)
    di = lambda n, s: nc.dram_tensor(n, list(s), F32, kind="ExternalInput").ap()
    do = lambda n, s: nc.dram_tensor(n, list(s), F32, kind="ExternalOutput").ap()
    LPS = SB_NB * 128
    LPD = DN_NC * 64
    qT = di("sb_q", [64, LPS]); kT = di("sb_k", [64, LPS]); vv = di("sb_v", [128, SB_NB, 64]); oT = do("sb_o", [64, LPS])
    xq = di("dn_xq", [64, LPD + 3]); xk = di("dn_xk", [64, LPD + 3]); xv = di("dn_xv", [64, LPD + 3])
    cw = di("dn_cw", [64, 12]); a_d = di("dn_a", [64, DN_NC]); b_d = di("dn_b", [64, DN_NC]); hp = di("dn_hp", [64, 2])
    dn_o = do("dn_o", [64, DN_NC, 64])
    u_d = di("s5_u", [64, 2, LTOK]); are = di("s5_are", [128, 4]); aim = di("s5_aim", [128, 4]); ldt = di("s5_ldt", [128, 4])
    bre = di("s5_bre", [128, 4, 16]); bim = di("s5_bim", [128, 4, 16]); cre = di("s5_cre", [128, 4, 16]); cim = di("s5_cim", [128, 4, 16])
    dsk = di("s5_dsk", [64, 1]); ys = do("s5_y", [64, 2, LTOK])
    with ExitStack() as st:
        S = Sched(nc, st)
        T1 = sb_phase(S, nc, qT, kT, vv, oT, SB_NB, SB_PAD)
        T2 = dn_phase(S, nc, xq, xk, xv, cw, a_d, b_d, hp, dn_o, DN_NC, DN_PAD)
        T3 = s5_phase2(S, nc, u_d, are, aim, ldt, bre, bim, cre, cim, dsk, ys, LTOK, 2)
        S.finish([T1, T2, T3])
    return nc


def _g8(g):
    return np.ascontiguousarray(np.asarray(g, np.float32).reshape(-1, 128).T)


def _pairlay(x):
    x = np.asarray(x, np.float32)
    y = np.moveaxis(x, 0, 1)
    return np.ascontiguousarray(np.concatenate([y, y], axis=0))


def _c(x):
    return np.ascontiguousarray(x, dtype=np.float32)


def _run(nc, maps):
    res = run_bass_kernel_spmd(nc, maps, core_ids=list(range(8)))
    return res.results


def kernel(**I):
    I = {k: np.asarray(v) for k, v in I.items()}
    x = I["x"]; meta = I["meta_tokens"]
    h = np.concatenate([np.broadcast_to(meta[None], (2, N_META, D)), x], axis=1)
    hT = [_c(h[c // 4, (c % 4) * NTOK:(c % 4 + 1) * NTOK].T) for c in range(8)]
    progs = {}

    def tok_launch(mode, l, hT, mix=None):
        if mode not in progs:
            progs[mode] = build_tok(mode)
        maps = []
        for c in range(8):
            m = {"h_in": hT[c]}
            if mode in ("CA", "C1"):
                b, q = c // 4, c % 4
                sl = slice(q * NTOK, (q + 1) * NTOK)
                m.update(osb=_c(mix["osb"][b][:, sl]), odn=_c(mix["odn"][b][:, sl]), dnz=_c(mix["dnz"][b][:, sl]), ys5=_c(mix["ys5"][b][:, sl]),
                         sbn=_c(np.tile(I["sb_out_norm"][l], 2).reshape(128, 1)), dnn=_c(np.tile(I["dn_out_norm"][l], 2).reshape(128, 1)),
                         wglu=_c(I["s5_w_glu"][l]), bglu=_g8(I["s5_b_glu"][l]), s5n=_g8(I["s5_out_norm"][l]), w_out=_c(I["w_out"][l]),
                         wg2=_c(I["ffn2_w_gate"][l]), wu2=_c(I["ffn2_w_up"][l]), wd2=_c(I["ffn2_w_down"][l]), n2=_g8(I["ffn2_norm"][l]))
            if mode == "CA":
                l2 = l + 1
            else:
                l2 = l
            if mode in ("A0", "CA"):
                m.update(wg1=_c(I["ffn1_w_gate"][l2]), wu1=_c(I["ffn1_w_up"][l2]), wd1=_c(I["ffn1_w_down"][l2]), n1=_g8(I["ffn1_norm"][l2]),
                         w_in=_c(I["w_in"][l2]), nmix=_g8(I["mix_norm"][l2]))
            if mode == "C1":
                m.update(nfin=_g8(I["final_norm"]))
            maps.append(m)
        return _run(progs[mode], maps)

    def mix_launch(l, projT):
        if "mix" not in progs:
            progs["mix"] = build_mix()
        maps = []
        for c in range(8):
            b, hh = c // 4, c % 4
            P = projT[b]
            m = {}
            padz = lambda r, n: _c(np.concatenate([np.zeros((r.shape[0], n), np.float32), r], axis=1))
            m["sb_q"] = padz(P[hh * 64:(hh + 1) * 64], SB_PAD)
            m["sb_k"] = padz(P[256 + hh * 64:256 + (hh + 1) * 64], SB_PAD)
            vT = padz(P[512 + hh * 64:512 + (hh + 1) * 64], SB_PAD)
            m["sb_v"] = _c(vT.T.reshape(SB_NB, 128, 64).transpose(1, 0, 2))
            o = 768
            m["dn_xq"] = padz(P[o + hh * 64:o + (hh + 1) * 64], DN_PAD + 3)
            m["dn_xk"] = padz(P[o + 256 + hh * 64:o + 256 + (hh + 1) * 64], DN_PAD + 3)
            m["dn_xv"] = padz(P[o + 512 + hh * 64:o + 512 + (hh + 1) * 64], DN_PAD + 3)
            cwl = I["dn_conv_w"][l]
            m["dn_cw"] = _c(np.concatenate([cwl[:, hh * 64:(hh + 1) * 64].T, cwl[:, 256 + hh * 64:256 + (hh + 1) * 64].T,
                                            cwl[:, 512 + hh * 64:512 + (hh + 1) * 64].T], axis=1))
            brow = P[1792 + hh]; arow = P[1796 + hh]
            col = lambda r: _c(np.concatenate([np.zeros(DN_PAD, np.float32), r]).reshape(DN_NC, 64).T)
            m["dn_a"] = col(arow); m["dn_b"] = col(brow)
            m["dn_hp"] = _c(np.tile(np.array([[I["dn_a_log"][l, hh], I["dn_dt_bias"][l, hh]]], np.float32), (64, 1)))
            g0 = 4 * c
            m["s5_u"] = _c(np.stack([projT[0][1800 + 64 * c:1800 + 64 * c + 64], projT[1][1800 + 64 * c:1800 + 64 * c + 64]], axis=1))
            m["s5_are"] = _pairlay(I["s5_a_re"][l, g0:g0 + 4]); m["s5_aim"] = _pairlay(I["s5_a_im"][l, g0:g0 + 4])
            m["s5_ldt"] = _pairlay(np.repeat(I["s5_log_dt"][l, g0:g0 + 4][:, None], 64, 1))
            m["s5_bre"] = _pairlay(I["s5_b_re"][l, g0:g0 + 4]); m["s5_bim"] = _pairlay(I["s5_b_im"][l, g0:g0 + 4])
            m["s5_cre"] = _pairlay(I["s5_c_re"][l, g0:g0 + 4].transpose(0, 2, 1)); m["s5_cim"] = _pairlay(I["s5_c_im"][l, g0:g0 + 4].transpose(0, 2, 1))
            m["s5_dsk"] = _c(I["s5_d"][l, 64 * c:64 * c + 64].reshape(64, 1))
            maps.append(m)
        res = _run(progs["mix"], maps)
        osb = [np.zeros((256, LTOK), np.float32) for _ in range(2)]
        odn = [np.zeros((256, LTOK), np.float32) for _ in range(2)]
        ys5 = [np.zeros((512, LTOK), np.float32) for _ in range(2)]
        for c in range(8):
            b, hh = c // 4, c % 4
            osb[b][hh * 64:(hh + 1) * 64] = res[c]["sb_o"][:, SB_PAD:]
            od = res[c]["dn_o"].transpose(1, 0, 2).reshape(DN_NC * 64, 64)[DN_PAD:]
            odn[b][hh * 64:(hh + 1) * 64] = od.T
            for bb in range(2):
                ys5[bb][64 * c:64 * c + 64] = res[c]["s5_y"][:, bb]
        dnz = [projT[b][1536:1792] for b in range(2)]
        return {"osb": osb, "odn": odn, "dnz": dnz, "ys5": ys5}

    def gather_proj(res):
        projT = [np.zeros((INW, LTOK), np.float32) for _ in range(2)]
        for c in range(8):
            b, q = c // 4, c % 4
            projT[b][:, q * NTOK:(q + 1) * NTOK] = res[c]["proj"][:INW]
        return projT

    r = tok_launch("A0", 0, hT)
    hT = [r[c]["h_out"] for c in range(8)]
    mix = mix_launch(0, gather_proj(r))
    r = tok_launch("CA", 0, hT, mix)
    hT = [r[c]["h_out"] for c in range(8)]
    mix = mix_launch(1, gather_proj(r))
    r = tok_launch("C1", 1, hT, mix)
    out = np.zeros((2, LTOK, D), np.float32)
    for c in range(8):
        b, q = c // 4, c % 4
        out[b, q * NTOK:(q + 1) * NTOK] = r[c]["y_out"].T
    return np.ascontiguousarray(out[:, N_META:])
```
